# Optimizing a Trainium2 kernel written in Bass

```python
import jax, jax.numpy as jnp
from jax import lax
import numpy as np

D_MODEL = 1024
BATCH = 2
SEQ = 8192
DEPTH = 1

GRID_W = 64
CTX_LEN = 256
RET_HEADS = 8
RET_QK_DIM = 64
RET_V_DIM = 128
RET_QK_WIDTH = RET_HEADS * RET_QK_DIM
RET_V_WIDTH = RET_HEADS * RET_V_DIM
CHUNK = 128
ROPE_BASE = 10000.0
FOURIER_GROUPS = 4
FOURIER_GROUP_DIM = 128
FOURIER_WIDTH = FOURIER_GROUPS * FOURIER_GROUP_DIM
Q_OFF = 0
K_OFF = Q_OFF + RET_QK_WIDTH
V_OFF = K_OFF + RET_QK_WIDTH
G_OFF = V_OFF + RET_V_WIDTH
F_OFF = G_OFF + RET_V_WIDTH
IN_COLS = F_OFF + FOURIER_WIDTH
N_BRANCHES = 2
FFN_DIM = 2816
CONV_W = 3
N_MOD = 6
EPS = 1e-6

kernel_name = "hybrid_retention_fourier_convffn_dit_block"


def rms_norm(x, w):
    x32 = x.astype(jnp.float32)
    y = x32 * lax.rsqrt(jnp.mean(x32 * x32, axis=-1, keepdims=True) + EPS)
    return (y * w.astype(jnp.float32)).astype(x.dtype)


def modulate(h, shift, scale):
    return h * (1 + scale) + shift


def split_heads(t, n_heads):
    b, l, w = t.shape
    return t.reshape(b, l, n_heads, w // n_heads).transpose(0, 2, 1, 3)


def rope_2d(t, row, col):
    n_freq = RET_QK_DIM // 4
    inv = ROPE_BASE ** (-jnp.arange(n_freq, dtype=jnp.float32) / n_freq)
    ang = jnp.concatenate([row[:, None] * inv, col[:, None] * inv], axis=-1)
    cos, sin = jnp.cos(ang), jnp.sin(ang)
    half = RET_QK_DIM // 2
    t1, t2 = t[..., :half], t[..., half:]
    return jnp.concatenate([t1 * cos - t2 * sin, t1 * sin + t2 * cos], axis=-1).astype(t.dtype)


def log_gamma(a):
    return -jnp.exp(a.astype(jnp.float32))


def chunk_retention(q, k, v, a, s0):
    b, h, l, dk = q.shape
    dv = v.shape[-1]
    n = l // CHUNK
    lg = log_gamma(a)
    qc = q.reshape(b, h, n, CHUNK, dk)
    kc = k.reshape(b, h, n, CHUNK, dk)
    vc = v.reshape(b, h, n, CHUNK, dv)
    pos = jnp.arange(CHUNK, dtype=jnp.float32)
    diff = pos[:, None] - pos[None, :]
    intra_decay = jnp.where(diff >= 0, jnp.exp(lg[:, None, None] * jnp.maximum(diff, 0.0)), 0.0)
    scores = jnp.einsum('bhncd,bhnsd->bhncs', qc, kc) * intra_decay[None, :, None]
    intra = jnp.einsum('bhncs,bhnse->bhnce', scores, vc)
    k_dec = kc * jnp.exp(lg[:, None] * (CHUNK - 1 - pos))[None, :, None, :, None]
    kv = jnp.einsum('bhnsd,bhnse->nbhde', k_dec, vc)
    chunk_decay = jnp.exp(lg * CHUNK)[None, :, None, None]

    def step(s, kv_n):
        return chunk_decay * s + kv_n, s

    s_final, s_prev = lax.scan(step, s0.astype(jnp.float32), kv)
    q_dec = qc * jnp.exp(lg[:, None] * (pos + 1))[None, :, None, :, None]
    inter = jnp.einsum('bhncd,nbhde->bhnce', q_dec, s_prev)
    return (intra + inter).reshape(b, h, l, dv), s_final


def retention_state(k, v, a):
    l = k.shape[2]
    lg = log_gamma(a)
    w = jnp.exp(lg[:, None] * (l - 1 - jnp.arange(l, dtype=jnp.float32)))
    return jnp.einsum('bhld,bhle,hl->bhde', k, v, w)


def bidir_retention(q, k, v, a_f, a_b, s_f0, s_b0):
    o_f, s_f = chunk_retention(q, k, v, a_f, s_f0)
    o_b, s_b = chunk_retention(jnp.flip(q, 2), jnp.flip(k, 2), jnp.flip(v, 2), a_b, s_b0)
    return o_f + jnp.flip(o_b, 2), s_f, s_b


def ret_readout(o, g, w_o):
    mu = jnp.mean(o, axis=-1, keepdims=True)
    var = jnp.mean(jnp.square(o - mu), axis=-1, keepdims=True)
    o = (o - mu) * lax.rsqrt(var + EPS)
    b, h, l, dv = o.shape
    o = o.transpose(0, 2, 1, 3).reshape(b, l, h * dv).astype(g.dtype)
    return (o * jax.nn.silu(g)) @ w_o


def fourier_mix(f, w_f):
    b, l, _ = f.shape
    fg = f.reshape(b, l, FOURIER_GROUPS, FOURIER_GROUP_DIM).astype(jnp.float32)
    z = jnp.fft.fft2(fg, axes=(1, 3), norm='ortho').real
    return z.reshape(b, l, FOURIER_WIDTH).astype(f.dtype) @ w_f


def merge_branches(h, ret_d, four_d, w_bg, b_bg, w_out):
    gates = jax.nn.sigmoid(h @ w_bg + b_bg)
    g_r, g_f = jnp.split(gates, N_BRANCHES, axis=-1)
    return (g_r * ret_d + g_f * four_d) @ w_out


def depthwise_conv3(u, w, b):
    up = jnp.pad(u, ((0, 0), (1, 1), (0, 0)))
    return up[:, :-2] * w[0] + up[:, 1:-1] * w[1] + up[:, 2:] * w[2] + b


def conv_ffn(h, w_up, conv_w, conv_b, w_down):
    u = depthwise_conv3(h @ w_up, conv_w, conv_b)
    a, val = jnp.split(u, 2, axis=-1)
    return (jax.nn.gelu(a) * val) @ w_down


def split_proj(p):
    return (p[..., Q_OFF:K_OFF], p[..., K_OFF:V_OFF], p[..., V_OFF:G_OFF],
            p[..., G_OFF:F_OFF], p[..., F_OFF:IN_COLS])


def setup_inputs(seed: int = 0) -> dict:
    key = jax.random.key(seed)
    ks = jax.random.split(key, 24)

    def nrm(k, shape, scale):
        return jax.random.normal(k, shape, jnp.float32) * scale

    d = D_MODEL
    base_decay = jnp.log(-jnp.log1p(-(2.0 ** (-5.0 - jnp.arange(RET_HEADS, dtype=jnp.float32)))))
    return {
        "x": nrm(ks[0], (BATCH, SEQ, d), 1.0),
        "c": nrm(ks[1], (BATCH, d), 1.0),
        "ctx": nrm(ks[2], (BATCH, CTX_LEN, d), 1.0),
        "c_ctx": nrm(ks[3], (d,), 1.0),
        "w_mod": nrm(ks[4], (DEPTH, d, N_MOD * d), 0.5 * d ** -0.5),
        "b_mod": nrm(ks[5], (DEPTH, N_MOD * d), 0.01),
        "norm1_w": 1.0 + nrm(ks[6], (DEPTH, d), 0.05),
        "w_in": nrm(ks[7], (DEPTH, d, IN_COLS), d ** -0.5),
        "ret_decay_f": base_decay[None] + nrm(ks[8], (DEPTH, RET_HEADS), 0.1),
        "ret_decay_b": base_decay[None] + nrm(ks[9], (DEPTH, RET_HEADS), 0.1),
        "w_ret_out": nrm(ks[10], (DEPTH, RET_V_WIDTH, d), RET_V_WIDTH ** -0.5),
        "w_four_out": nrm(ks[11], (DEPTH, FOURIER_WIDTH, d), FOURIER_WIDTH ** -0.5),
        "w_branch_gate": nrm(ks[12], (DEPTH, d, N_BRANCHES * d), d ** -0.5),
        "b_branch_gate": nrm(ks[13], (DEPTH, N_BRANCHES * d), 0.01),
        "w_out": nrm(ks[14], (DEPTH, d, d), d ** -0.5),
        "norm2_w": 1.0 + nrm(ks[15], (DEPTH, d), 0.05),
        "w_up": nrm(ks[16], (DEPTH, d, 2 * FFN_DIM), d ** -0.5),
        "conv_w": nrm(ks[17], (DEPTH, CONV_W, 2 * FFN_DIM), CONV_W ** -0.5),
        "conv_b": nrm(ks[18], (DEPTH, 2 * FFN_DIM), 0.01),
        "w_down": nrm(ks[19], (DEPTH, FFN_DIM, d), FFN_DIM ** -0.5),
        "final_norm_w": 1.0 + nrm(ks[20], (d,), 0.05),
    }


def reference(x, c, ctx, c_ctx, w_mod, b_mod, norm1_w, w_in, ret_decay_f, ret_decay_b,
              w_ret_out, w_four_out, w_branch_gate, b_branch_gate, w_out, norm2_w,
              w_up, conv_w, conv_b, w_down, final_norm_w):
    b, l, _ = x.shape
    rows = l // GRID_W
    row = jnp.repeat(jnp.arange(rows, dtype=jnp.float32), GRID_W)
    col = jnp.tile(jnp.arange(GRID_W, dtype=jnp.float32), rows)
    q_scale = RET_QK_DIM ** -0.5
    silu_c = jax.nn.silu(c)
    silu_cc = jax.nn.silu(c_ctx)
    xc = ctx
    for layer in range(DEPTH):
        last = layer == DEPTH - 1
        wi = w_in[layer]
        a_f, a_b = ret_decay_f[layer], ret_decay_b[layer]
        mod = silu_c @ w_mod[layer] + b_mod[layer]
        sh1, sc1, g1, sh2, sc2, g2 = [m[:, None, :] for m in jnp.split(mod, N_MOD, axis=-1)]
        mc = jnp.split(silu_cc @ w_mod[layer] + b_mod[layer], N_MOD)

        h = modulate(rms_norm(x, norm1_w[layer]), sh1, sc1)
        hc = modulate(rms_norm(xc, norm1_w[layer]), mc[0], mc[1])

        if last:
            kc = split_heads(hc @ wi[:, K_OFF:V_OFF], RET_HEADS)
            vc = split_heads(hc @ wi[:, V_OFF:G_OFF], RET_HEADS)
            s_f = retention_state(kc, vc, a_f)
            s_b = retention_state(jnp.flip(kc, 2), jnp.flip(vc, 2), a_b)
        else:
            qc, kc, vc, gc, fc = split_proj(hc @ wi)
            qc = split_heads(qc, RET_HEADS) * q_scale
            kc = split_heads(kc, RET_HEADS)
            vc = split_heads(vc, RET_HEADS)
            zeros = jnp.zeros((b, RET_HEADS, RET_QK_DIM, RET_V_DIM), jnp.float32)
            oc, s_f, s_b = bidir_retention(qc, kc, vc, a_f, a_b, zeros, zeros)
            yc = merge_branches(hc, ret_readout(oc, gc, w_ret_out[layer]),
                                fourier_mix(fc, w_four_out[layer]),
                                w_branch_gate[layer], b_branch_gate[layer], w_out[layer])
            xc = xc + mc[2] * yc
            hc2 = modulate(rms_norm(xc, norm2_w[layer]), mc[3], mc[4])
            xc = xc + mc[5] * conv_ffn(hc2, w_up[layer], conv_w[layer], conv_b[layer], w_down[layer])

        q, k, v, g, f = split_proj(h @ wi)
        q = rope_2d(split_heads(q, RET_HEADS), row, col) * q_scale
        k = rope_2d(split_heads(k, RET_HEADS), row, col)
        v = split_heads(v, RET_HEADS)
        o, _, _ = bidir_retention(q, k, v, a_f, a_b, s_f, s_b)
        y = merge_branches(h, ret_readout(o, g, w_ret_out[layer]),
                           fourier_mix(f, w_four_out[layer]),
                           w_branch_gate[layer], b_branch_gate[layer], w_out[layer])
        x = x + g1 * y
        h2 = modulate(rms_norm(x, norm2_w[layer]), sh2, sc2)
        x = x + g2 * conv_ffn(h2, w_up[layer], conv_w[layer], conv_b[layer], w_down[layer])
    return rms_norm(x, final_norm_w)
```

```python
import numpy as np
import ml_dtypes
from contextlib import ExitStack

import concourse.bass as bass
import concourse.mybir as mybir
from concourse.bass_utils import run_bass_kernel_spmd

F32 = mybir.dt.float32
BF16 = mybir.dt.bfloat16
ALU = mybir.AluOpType
AF = mybir.ActivationFunctionType
AX = mybir.AxisListType

D = 1024
SEQ = 8192
NB = 2
NCORES = 8
TOK = 2048
NT = 16
CTX = 256
H = 8
INC = 3584
FFN = 2816
NCH = 44
EPS = 1e-6
GROUPS = [[0, 1, 2, 3], [4, 5, 6, 7]]
GELU_C = 1.5957691216057308


class Tok:
    __slots__ = ("key", "val")

    def __init__(self, key, val):
        self.key = key
        self.val = val


class Buf:
    __slots__ = ("name", "w", "r", "excl")

    def __init__(self, name, excl=False):
        self.name = name
        self.w = None
        self.r = []
        self.excl = excl


class Sched:
    ENGS = ("pe", "act", "dve", "pool", "sp")

    def __init__(self, nc, stack):
        self.nc = nc
        self.stack = stack
        self.ops = {e: [] for e in self.ENGS}
        self.sems = {}
        self.cnt = {}
        self.seen = {e: {} for e in self.ENGS}
        for e in ("pe", "act", "dve", "pool"):
            self._mk(e)

    def _mk(self, key):
        if key not in self.sems:
            self.sems[key] = self.stack.enter_context(self.nc.semaphore("s_" + key))
            self.cnt[key] = 0

    def _deps(self, eng, reads, writes):
        deps = []
        for b in reads:
            if b.w is not None:
                deps.append(b.w)
        for b in writes:
            if b.w is not None:
                deps.append(b.w)
            deps.extend(b.r)
        waits = {}
        for t in deps:
            if t.key == "pe" and eng == "pe":
                continue
            if self.seen[eng].get(t.key, 0) >= t.val:
                continue
            waits[t.key] = max(waits.get(t.key, 0), t.val)
        for k, v in waits.items():
            self.seen[eng][k] = v
        return list(waits.items())

    def _commit(self, tok, reads, writes):
        for b in writes:
            b.w = tok
            b.r = []
        for b in reads:
            if b not in writes:
                b.r.append(tok)
                if len(b.r) > 64:
                    b.r = b.r[-48:]

    def op(self, eng, fn, reads=(), writes=()):
        ex = [b for b in reads if b.excl]
        if ex:
            reads = [b for b in reads if not b.excl]
            writes = list(writes) + ex
        waits = self._deps(eng, reads, writes)
        self.cnt[eng] += 1
        tok = Tok(eng, self.cnt[eng])
        self.ops[eng].append((waits, fn, eng, 1))
        self._commit(tok, reads, writes)
        return tok

    def dma(self, queue, key, out, in_, reads=(), writes=(), **kw):
        self._mk(key)
        waits = self._deps(queue, reads, writes)
        self.cnt[key] += 16
        tok = Tok(key, self.cnt[key])
        self.ops[queue].append((waits, lambda e: e.dma_start(out=out, in_=in_, **kw), key, 16))
        self._commit(tok, reads, writes)
        return tok

    def custom(self, queue, key, fn, reads=(), writes=()):
        import os
        if os.environ.get("KDBG_NOCC"):
            return None
        self._mk(key)
        waits = self._deps(queue, reads, writes)
        self.cnt[key] += 1
        tok = Tok(key, self.cnt[key])
        self.ops[queue].append((waits, fn, key, None))
        self._commit(tok, reads, writes)
        return tok

    def barrier(self, exclude=()):
        for e in self.ENGS:
            waits = []
            for k, v in self.cnt.items():
                if k == e or v == 0 or k in exclude:
                    continue
                if self.seen[e].get(k, 0) >= v:
                    continue
                self.seen[e][k] = v
                waits.append((k, v))
            if waits:
                self.ops[e].append((waits, None, None, 0))

    def emit(self):
        nc = self.nc
        handles = {"pe": "tensor", "act": "scalar", "dve": "vector", "pool": "gpsimd", "sp": "sync"}
        with nc.Block() as block:
            for e in self.ENGS:
                ops = self.ops[e]

                def body(engine, ops=ops):
                    for waits, fn, key, inc in ops:
                        for k, v in waits:
                            engine.wait_ge(self.sems[k], v)
                        if fn is None:
                            continue
                        ins = fn(engine)
                        if inc is None:
                            ins.then_inc(self.sems[key])
                        else:
                            ins.then_inc(self.sems[key], inc)

                getattr(block, handles[e])(body)


class Arena:
    def __init__(self, t, nelem):
        self.t = t
        self.n = nelem
        self.off = 0

    def reset(self, off=0):
        self.off = off

    def bf(self, nelem, parts=128):
        a = self.t[0:parts, self.off:self.off + nelem]
        self.off += nelem
        assert self.off <= self.n, (self.off, self.n)
        return a

    def f32(self, nelem, parts=128):
        return self.bf(2 * nelem, parts).bitcast(F32)


def _bf(a):
    return np.ascontiguousarray(a).astype(ml_dtypes.bfloat16)


def host_consts(j):
    c = {}
    c["ident"] = _bf(np.eye(128, dtype=np.float32))
    p = np.arange(128)
    t = (TOK * j + 128 * np.arange(NT)[None, :] + p[:, None]).astype(np.float32)
    row = np.floor(t / 64.0).astype(np.float32)
    col = (t - 64.0 * row).astype(np.float32)
    inv = (np.float32(10000.0) ** (-(np.arange(16, dtype=np.float32)) / np.float32(16))).astype(np.float32)
    ang = np.concatenate([row[:, :, None] * inv[None, None, :], col[:, :, None] * inv[None, None, :]], axis=-1)
    ang = ang.astype(np.float32)
    c["rope"] = np.concatenate([np.cos(ang), np.sin(ang)], axis=-1).astype(np.float32)
    s_ = p[:, None]
    c_ = p[None, :]
    mf = (c_ >= s_).astype(np.float32)
    mb = (c_ <= s_).astype(np.float32)
    c["mask"] = np.stack([mf, mf, mb, mb], axis=1).astype(np.float32)
    pc = np.zeros((128, 8), np.float32)
    pc[:, 0] = -(p + 1)
    pc[:, 1] = (p + 1)
    pc[:, 2] = -(128 - p)
    pc[:, 3] = (128 - p)
    pc[:, 4] = -(255 - p)
    pc[:, 5] = -(255 - 128 - p)
    pc[:, 6] = -p
    pc[:, 7] = -(128 + p)
    c["pcol"] = pc
    sel = np.zeros((128, 18), np.float32)
    sel[:, 16] = 1.0 if j == 0 else 0.0
    sel[:, 17] = 1.0 if j == 3 else 0.0
    sel[:, j] = 1.0
    sel[:, 4 + j] = 1.0
    if j > 0:
        sel[:, 8 + (j - 1)] = 1.0
    if j < 3:
        sel[:, 12 + (j + 1)] = 1.0
    c["sel"] = sel
    q = np.arange(64)
    hh, rr, mm = q // 32, (q % 32) // 8, q % 8
    n1 = 16 * rr + 8 * hh + mm
    k1 = np.arange(64)
    th = 2.0 * np.pi * ((n1[:, None] * k1[None, :]) % 64) / 64.0
    c["e64"] = _bf(np.concatenate([np.cos(th), -np.sin(th)], axis=1))
    n2 = np.arange(128)
    k2 = 32 * j + np.arange(32)
    kk = k1[None, :, None] + 64 * k2[None, None, :]
    ph = 2.0 * np.pi * ((n2[:, None, None] * kk) % 8192) / 8192.0
    twA = np.concatenate([np.cos(ph), -np.sin(ph)], axis=2)
    twB = np.concatenate([np.sin(ph), np.cos(ph)], axis=2)
    c["tw"] = _bf(np.stack([twA, twB], axis=2))
    ch = np.arange(128)
    pc2 = 2.0 * np.pi * ((ch[:, None] * ch[None, :]) % 128) / 128.0
    c["c128"] = _bf(np.stack([np.cos(pc2), np.sin(pc2)], axis=1) / 1024.0)
    c["ones"] = np.ones((128, 128), np.float32)
    c["identf"] = np.eye(128, dtype=np.float32)
    c["onesb"] = _bf(np.ones((128, 128), np.float32))
    return c


CONST_SPECS = [
    ("ident", [128, 128], BF16), ("rope", [128, 16, 64], F32), ("mask", [128, 4, 128], F32),
    ("pcol", [128, 8], F32), ("sel", [128, 18], F32), ("e64", [64, 128], BF16),
    ("tw", [128, 64, 2, 64], BF16), ("c128", [128, 2, 128], BF16), ("ones", [128, 128], F32),
    ("onesb", [128, 128], BF16), ("identf", [128, 128], F32),
]

INPUT_SPECS = [
    ("x_own", [TOK, D], F32), ("ctx", [CTX, D], F32), ("c_col", [128, 8], F32), ("cc_col", [128, 8], F32),
    ("w_mod", [D, 6 * D], F32), ("b_mod", [1, 6 * D], F32), ("norm1_w", [1, D], F32),
    ("w_in", [D, INC], F32), ("a_f", [1, H], F32), ("a_b", [1, H], F32),
    ("w_ret_out", [D, D], F32), ("w_four_out", [512, D], F32), ("w_bg", [D, 2 * D], F32),
    ("b_bg", [1, 2 * D], F32), ("w_out", [D, D], F32), ("norm2_w", [1, D], F32),
    ("w_up", [D, 2 * FFN], F32), ("conv_w", [3, 2 * FFN], F32), ("conv_b", [1, 2 * FFN], F32),
    ("w_down", [FFN, D], F32), ("final_norm_w", [1, D], F32),
]


def build_program(stage=4):
    import os
    STOP = os.environ.get("KDBG_STOP", "")
    nc = bass.Bass("TRN2", target_bir_lowering=False)
    stack = ExitStack()
    S = Sched(nc, stack)
    dr = {}
    for name, shape, dt in INPUT_SPECS + CONST_SPECS:
        dr[name] = nc.dram_tensor(name, shape, dt, kind="ExternalInput").ap()
    out_d = nc.dram_tensor("out", [TOK, D], F32, kind="ExternalOutput").ap()
    dbg_d = None
    if stage < 4:
        dbg_d = nc.dram_tensor("dbg", [128, 8 * TOK], BF16, kind="ExternalOutput").ap()
    rec_d = nc.dram_tensor("rec_scr", [NT, 128, 5120], BF16).ap()
    kv_d = nc.dram_tensor("kv_scr", [NT, 128, 1024], F32).ap()
    x1_d = nc.dram_tensor("x1_scr", [TOK, D], F32).ap()
    mod_d = nc.dram_tensor("mod_scr", [1, 6 * D + 2 * D], F32).ap()
    fin_d = [nc.dram_tensor(f"f_in{h}", [1024, 512], BF16) for h in range(2)]
    fout_d = [nc.dram_tensor(f"f_out{h}", [4096, 512], BF16) for h in range(2)]
    stin_d = nc.dram_tensor("st_in", [128, 1024], F32)
    stout_d = nc.dram_tensor("st_out", [512, 1024], F32)
    hin_d = nc.dram_tensor("halo_in", [128, 128], F32)
    hout_d = nc.dram_tensor("halo_out", [512, 128], F32)

    def sb(name, shape, dt):
        return stack.enter_context(nc.sbuf_tensor("sb_" + name, shape, dt))

    PS = [stack.enter_context(nc.psum_tensor(f"ps{i}", [128, 512], F32)) for i in range(8)]
    PB = [Buf(f"ps{i}", excl=True) for i in range(8)]

    def psbf(i):
        return PS[i][:, :].bitcast(BF16)

    ident = sb("ident", [128, 128], BF16)
    rope = sb("rope", [128, 16, 64], F32)
    mask = sb("mask", [128, 4, 128], F32)
    pcol = sb("pcol", [128, 8], F32)
    sel = sb("sel", [128, 18], F32)
    ones = sb("ones", [128, 128], F32)
    onesb = sb("onesb", [128, 128], BF16)
    identf = sb("identf", [128, 128], F32)
    sctx = sb("sctx", [128, 1024], F32)
    c128 = sb("c128", [128, 2, 128], BF16)
    e64 = sb("e64", [64, 128], BF16)
    smallf = sb("smallf", [128, 512], F32)
    smallb = sb("smallb", [128, 64], BF16)
    convc = sb("convc", [128, 4, NCH], F32)
    XH = sb("XH", [128, 8, TOK], BF16)
    R1n, R2n = 32768, 47104
    R1t = sb("R1", [128, R1n], BF16)
    R2t = sb("R2", [128, R2n], BF16)
    R1 = Arena(R1t, R1n)
    R2 = Arena(R2t, R2n)
    B_const = Buf("const")
    B_small = Buf("small")
    B_ss = Buf("ss")
    B_XH = [Buf(f"xh{t}") for t in range(NT)]

    def sf(a, b):
        return smallf[:, a:b]

    a_bc = sf(0, 16)
    ea = sf(16, 32)
    qf_sc, kf_sc, qb_sc, kb_sc = sf(32, 40), sf(40, 48), sf(48, 56), sf(56, 64)
    cxf_sc, cxb_sc = sf(64, 80), sf(80, 96)
    a_st, ea_st = sf(96, 104), sf(104, 112)
    cdp_f, cdp_b = sf(112, 180), sf(180, 248)
    silc = sf(248, 264)
    shcol = sf(264, 288)
    ss_t = sf(288, 296)
    bbg = sf(296, 312)
    bup = sf(312, 356)
    kcol = sf(356, 400)
    gst = sf(400, 464)
    silcb = smallb[:, 0:16]
    shcolb = smallb[:, 16:40]

    def TT(eng, out, in0, in1, op, reads, writes):
        return S.op(eng, lambda e: e.tensor_tensor(out=out, in0=in0, in1=in1, op=op), reads, writes)

    def TS(eng, out, in0, s1, s2, op0, op1, reads, writes):
        if s2 is None:
            return S.op(eng, lambda e: e.tensor_scalar(out=out, in0=in0, scalar1=s1, scalar2=None, op0=op0), reads, writes)
        return S.op(eng, lambda e: e.tensor_scalar(out=out, in0=in0, scalar1=s1, scalar2=s2, op0=op0, op1=op1), reads, writes)

    def STT(eng, out, in0, scalar, in1, op0, op1, reads, writes):
        return S.op(eng, lambda e: e.scalar_tensor_tensor(out=out, in0=in0, scalar=scalar, in1=in1, op0=op0, op1=op1), reads, writes)

    def CP(eng, out, in_, reads, writes):
        if eng == "act":
            return S.op("act", lambda e: e.activation(out=out, in_=in_, func=AF.Copy), reads, writes)
        return S.op(eng, lambda e: e.tensor_copy(out=out, in_=in_), reads, writes)

    def ACTF(out, in_, func, reads, writes, **kw):
        return S.op("act", lambda e: e.activation(out=out, in_=in_, func=func, **kw), reads, writes)

    def MM(out, lhsT, rhs, start, stop, reads, writes):
        return S.op("pe", lambda e: e.matmul(out, lhsT=lhsT, rhs=rhs, start=start, stop=stop), reads, writes)

    def TR(out, in_, reads, writes):
        return S.op("pe", lambda e: e.transpose(out=out, in_=in_, identity=ident[:]), reads + [B_const], writes)

    def RSQ(out, in_, c, reads, writes):
        ACTF(out, in_, AF.Sqrt, reads, writes, bias=c, scale=1.0)
        return S.op("dve", lambda e: e.reciprocal(out=out, in_=out), writes, writes)

    def bc3(ap2, n):
        return ap2.unsqueeze(2).to_broadcast([128, ap2.shape[1], n])

    R1.reset()
    Win = R1.bf(8 * INC).rearrange("p (k c) -> p k c", k=8)
    A1bc = R1.f32(1024)
    B_win = Buf("win")
    win_v = dr["w_in"].rearrange("(k p) c -> p k c", p=128)
    for k in range(8):
        S.dma("pool", "win", Win[:, k, :], win_v[:, k, :], writes=[B_win])
    for dst, name in ((ident, "ident"), (rope, "rope"), (mask, "mask"), (pcol, "pcol"), (sel, "sel"), (ones, "ones"),
                      (onesb, "onesb"), (c128, "c128"), (e64, "e64"), (identf, "identf")):
        S.dma("sp", "const", dst[:], dr[name], writes=[B_const])
    S.dma("sp", "const", a_bc[:, 0:8], dr["a_f"].partition_broadcast(128), writes=[B_const])
    S.dma("sp", "const", a_bc[:, 8:16], dr["a_b"].partition_broadcast(128), writes=[B_const])
    S.dma("sp", "const", silc[:, 0:8], dr["c_col"], writes=[B_const])
    S.dma("sp", "const", silc[:, 8:16], dr["cc_col"], writes=[B_const])
    R2.reset()
    stg = sb("stg", [64, 128], F32)
    B_stg = Buf("stg")

    def col_layout(dst, src_rows, n):
        S.dma("sp", "stg", stg[0:n, :], src_rows, writes=[B_stg])
        S.op("pe", lambda e: e.transpose(out=PS[6][:, 0:n], in_=stg[0:n, :], identity=identf[0:n, 0:n]), [B_stg, B_const], [PB[6]])
        CP("dve", dst, PS[6][:, 0:n], [PB[6]], [B_const])

    S.barrier()
    col_layout(bbg, dr["b_bg"].rearrange("o (c p) -> (o c) p", p=128), 16)
    for k in range(3):
        col_layout(convc[:, k, :], dr["conv_w"][k:k + 1, :].rearrange("o (c p) -> (o c) p", p=128), NCH)
    col_layout(convc[:, 3, :], dr["conv_b"].rearrange("o (c p) -> (o c) p", p=128), NCH)
    for di in range(2):
        for hh in range(2):
            CP("dve", a_st[64 * hh:64 * hh + 64, 4 * di:4 * di + 4],
               a_bc[64 * hh:64 * hh + 64, 8 * di:8 * di + 8].rearrange("p (a h) -> p a h", h=2)[:, :, hh], [B_const], [B_const])

    ACTF(ea, a_bc, AF.Exp, [B_const], [B_small])
    ACTF(ea_st, a_st, AF.Exp, [B_const], [B_small])
    for dst, src, col in ((qf_sc, ea[:, 0:8], 0), (kf_sc, ea[:, 0:8], 1), (qb_sc, ea[:, 8:16], 2), (kb_sc, ea[:, 8:16], 3)):
        ACTF(dst, src, AF.Exp, [B_small], [B_small], scale=pcol[:, col:col + 1])
    for tl in range(2):
        ACTF(cxf_sc[:, 8 * tl:8 * tl + 8], ea[:, 0:8], AF.Exp, [B_small], [B_small], scale=pcol[:, 4 + tl:5 + tl])
        ACTF(cxb_sc[:, 8 * tl:8 * tl + 8], ea[:, 8:16], AF.Exp, [B_small], [B_small], scale=pcol[:, 6 + tl:7 + tl])
    for n in range(17):
        ACTF(cdp_f[:, 4 * n:4 * n + 4], ea_st[:, 0:4], AF.Exp, [B_small], [B_small], scale=-128.0 * n)
        ACTF(cdp_b[:, 4 * n:4 * n + 4], ea_st[:, 4:8], AF.Exp, [B_small], [B_small], scale=-128.0 * n)
    TS("dve", qf_sc, qf_sc, 0.125, None, ALU.mult, None, [B_small], [B_small])
    TS("dve", qb_sc, qb_sc, 0.125, None, ALU.mult, None, [B_small], [B_small])
    ACTF(silc, silc, AF.Silu, [B_small], [B_small])
    CP("dve", silcb, silc, [B_small], [B_small])

    wst = [R2.f32(4096).rearrange("p (k c) -> p k c", k=8) for _ in range(2)]
    wmb = [R2.bf(4096).rearrange("p (k c) -> p k c", k=8) for _ in range(2)]
    bmod = [R2.f32(512, parts=1) for _ in range(2)]
    rowsb = [R2.f32(512, parts=1) for _ in range(4)]
    B_wst, B_wmb = [Buf("wst0"), Buf("wst1")], [Buf("wmb0"), Buf("wmb1")]
    B_bmod = [Buf("bm0"), Buf("bm1")]
    B_row = [Buf(f"row{i}") for i in range(4)]
    B_modd = Buf("modd")
    wmod_v = dr["w_mod"].rearrange("(k p) c -> p k c", p=128)
    cvt_eng = ["dve", "act"]
    nrow = [0]

    def mod_block(cb, wst, wmb, bmod, rowsb, B_wst, B_wmb, B_bmod, B_row):
        i = cb % 2
        S.dma("sp", f"wst{i}", wst[i], wmod_v[:, :, 512 * cb:512 * cb + 512], writes=[B_wst[i]])
        S.dma("sp", f"bm{i}", bmod[i], dr["b_mod"][:, 512 * cb:512 * cb + 512], writes=[B_bmod[i]])
        for half in range(2):
            CP(cvt_eng[(2 * cb + half) % len(cvt_eng)], wmb[i][:, 4 * half:4 * half + 4, :], wst[i][:, 4 * half:4 * half + 4, :], [B_wst[i]], [B_wmb[i]])
        for side in range(2 if cb < 4 else 1):
            for k in range(8):
                MM(PS[7][0:1, :], silcb[:, 8 * side + k:8 * side + k + 1], wmb[i][:, k, :], k == 0, k == 7, [B_small, B_wmb[i]], [PB[7]])
            r = nrow[0] % len(rowsb)
            nrow[0] += 1
            TT("dve", rowsb[r], PS[7][0:1, :], bmod[i], ALU.add, [PB[7], B_bmod[i]], [B_row[r]])
            off = 512 * cb if side == 0 else 6 * D + 512 * cb
            S.dma("sp", f"rowst{r}", mod_d[:, off:off + 512], rowsb[r], reads=[B_row[r]], writes=[B_modd])

    for cb in range(4):
        mod_block(cb, wst, wmb, bmod, rowsb, B_wst, B_wmb, B_bmod, B_row)
    S.barrier()
    for i, off in ((0, 0), (2, 6 * D)):
        col_layout(shcol[:, 8 * i:8 * i + 8], mod_d[:, off:off + D].rearrange("o (c p) -> (o c) p", p=128), 8)
    CP("dve", shcolb[:, 0:8], shcol[:, 0:8], [B_const], [B_small])
    CP("dve", shcolb[:, 16:24], shcol[:, 16:24], [B_const], [B_small])

    if STOP == "mod":
        S.barrier(); S.emit(); return nc
    B_bct = Buf("bct")

    R2.reset()
    xt = [R2.f32(1024) for _ in range(3)]
    hb = R2.bf(1024)
    junk = R2.bf(1024)
    qk2 = [R2.f32(1024) for _ in range(2)]
    qk_sb = qk2[0]
    off_rt1 = R2.off
    rt1, rt2 = R2.f32(1024), R2.f32(1024)
    rot = rt1
    ktok = R2.bf(1024)
    kpad = R2.bf(2048)
    qpad = R2.bf(2048)
    fsb = [R2.bf(512) for _ in range(2)]
    kvsb = [R2.f32(1024) for _ in range(2)]
    Est = R2.f32(1024)
    off_rec = R2.off
    recb = [R2.bf(5120) for _ in range(2)]
    hcT = R2.bf(2048).rearrange("p (k c) -> p k c", k=8)
    ctxv = R2.bf(1024)
    brows = R2.bf(INC, parts=2)
    lo_tmp = R2t[0:1, off_rt1:off_rt1 + INC]
    nwt = qk2[1]
    browsc = R2t[0:2, off_rec:off_rec + INC]
    A1c = R2t[:, off_rec + 5120:off_rec + 5120 + 2048].bitcast(F32)
    B_xt = [Buf("xt0"), Buf("xt1"), Buf("xt2")]
    B_hb, B_junk, B_qk = Buf("hb"), Buf("junk"), Buf("qk")
    B_qk2 = [Buf("qk2_0"), Buf("qk2_1")]
    B_rt1, B_rt2 = Buf("rt1"), Buf("rt2")
    B_rot = B_rt1
    B_ktok, B_kpad, B_qpad = Buf("ktok"), Buf("kpad"), Buf("qpad")
    B_fsb = [Buf("fsb0"), Buf("fsb1")]
    B_kvsb = [Buf("kvsb0"), Buf("kvsb1")]
    B_E = Buf("E")
    B_rec = [Buf("rec0"), Buf("rec1")]
    B_hcT, B_ctxv, B_sctx, B_bias = Buf("hcT"), Buf("ctxv"), Buf("sctx"), Buf("bias")

    S.dma("sp", "bcl", nwt, dr["norm1_w"].partition_broadcast(128), writes=[B_bct])
    S.dma("sp", "bcl", A1bc, mod_d[:, D:2 * D].partition_broadcast(128), reads=[B_modd], writes=[B_bct])
    STT("dve", A1bc, A1bc, 1.0, nwt, ALU.add, ALU.mult, [B_bct], [B_bct])
    TS("dve", A1bc, A1bc, 32.0, None, ALU.mult, None, [B_bct], [B_bct])

    def bias_rows(side, dstHL):
        lcol = 0 if side == 0 else 16
        for blk in range(7):
            if side == 1 and blk not in (1, 2, 3):
                continue
            for k in range(8):
                MM(PS[7][0:1, :], shcolb[:, lcol + k:lcol + k + 1], Win[:, k, 512 * blk:512 * blk + 512], k == 0, k == 7, [B_small, B_win], [PB[7]])
            cs = slice(512 * blk, 512 * blk + 512)
            CP("dve", dstHL[0:1, cs], PS[7][0:1, :], [PB[7]], [B_bias])
            TT("dve", lo_tmp[:, cs], PS[7][0:1, :], dstHL[0:1, cs], ALU.subtract, [PB[7], B_bias], [B_bias])
        S.dma("sp", "biaslo", dstHL[1:2, :], lo_tmp, reads=[B_bias], writes=[B_bias])

    bias_rows(0, brows)

    S.op("pool", lambda e: e.memset(kpad, 0.0), writes=[B_kpad])
    S.op("pool", lambda e: e.memset(qpad, 0.0), writes=[B_qpad])
    S.op("pool", lambda e: e.memset(Est, 0.0), writes=[B_E])

    def load_x(src_ap, slot):
        S.dma("sp", f"xt{slot}", xt[slot], src_ap, writes=[B_xt[slot]])

    def norm_part(slot, scale_bc):
        ACTF(junk, xt[slot], AF.Square, [B_xt[slot]], [B_junk, B_ss], accum_out=ss_t[:, 0:1])
        RSQ(ss_t[:, 1:2], ss_t[:, 0:1], 1024.0 * EPS, [B_ss], [B_ss])
        STT("dve", hb, xt[slot], ss_t[:, 1:2], scale_bc, ALU.mult, ALU.mult, [B_xt[slot], B_ss, B_bct], [B_hb])

    def tr_part(dstT, bdst, col0):
        pst = psbf(0)
        for k in range(8):
            TR(pst[:, 128 * k:128 * k + 128], hb[:, 128 * k:128 * k + 128], [B_hb], [PB[0]])
        CP("act", dstT[:, :, col0:col0 + 128], pst.rearrange("p (k c) -> p k c", k=8), [PB[0]], [bdst])

    def norm_transpose(slot, scale_bc, dstT, bdst, col0):
        norm_part(slot, scale_bc)
        tr_part(dstT, bdst, col0)

    def project(srcT, bsrc, col0, blocks, rows, consume):
        for i, blk in enumerate(blocks):
            bank = 1 + (i % 2)
            for k in range(8):
                MM(PS[bank][:, :], srcT[:, k, col0:col0 + 128], Win[:, k, 512 * blk:512 * blk + 512], k == 0, False, [bsrc, B_win], [PB[bank]])
            MM(PS[bank][:, :], onesb[0:2, :], rows[0:2, 512 * blk:512 * blk + 512], False, True, [B_bias, B_const], [PB[bank]])
            consume(blk, PS[bank], PB[bank])

    def k4(ap):
        return ap.rearrange("p (a h c) -> p a h c", a=4, h=2)

    def scaled_k(dirn, src_k, bsrc, sc_tile):
        kt = ktok[:, 512 * dirn:512 * dirn + 512]
        TT("dve", kt.rearrange("p (h d) -> p h d", h=8), src_k.rearrange("p (h d) -> p h d", h=8), bc3(sc_tile, 64), ALU.mult,
           [bsrc, B_small], [B_ktok])
        kp = k4(kpad[:, 1024 * dirn:1024 * dirn + 1024])
        kin = kt.rearrange("p (a h d) -> p a h d", a=4, h=2)
        for hh in range(2):
            CP("act", kp[:, :, hh, 64 * hh:64 * hh + 64], kin[:, :, hh, :], [B_ktok], [B_kpad])

    def kv_matmuls(dirn, vsrc, bv, bank):
        kp = k4(kpad[:, 1024 * dirn:1024 * dirn + 1024])
        for p4 in range(4):
            for hh in range(2):
                h = 2 * p4 + hh
                MM(PS[bank][:, 128 * p4:128 * p4 + 128], kp[:, p4, hh, :], vsrc[:, 128 * h:128 * h + 128], hh == 0, hh == 1, [B_kpad, bv], [PB[bank]])

    S.barrier()
    B_fin = [[Buf(f"fin{h}_{i}") for i in range(8)] for h in range(2)]
    B_fout = [Buf("fout0"), Buf("fout1")]
    B_recd = [Buf(f"recd{t}") for t in range(NT)]
    B_kvd = [Buf(f"kvd{t}") for t in range(NT)]

    def E3(ap):
        return ap.rearrange("p (a e) -> p a e", a=4)

    cdf_bc = bc3(cdp_f[:, 4:8], 128)
    def passA_front1(t):
        tr_part(XH, B_XH[t], 128 * t)

    def passA_front(t):
        slot = t % 2
        rb = recb[slot]

        qk_t, B_qkt = qk2[slot], B_qk2[slot]

        def consume(blk, ps, pb, t=t, rb=rb, slot=slot, qk_t=qk_t, B_qkt=B_qkt):
            if blk < 2:
                CP("act", qk_t[:, 512 * blk:512 * blk + 512], ps[:, :], [pb], [B_qkt])
            elif blk < 4:
                o0 = 3072 + 512 * (blk - 2)
                CP("act", rb[:, o0:o0 + 512], ps[:, :], [pb], [B_rec[slot]])
            elif blk < 6:
                o0 = 4096 + 512 * (blk - 4)
                ACTF(rb[:, o0:o0 + 512], ps[:, :], AF.Silu, [pb], [B_rec[slot]])
            else:
                CP("act", fsb[slot], ps[:, :], [pb], [B_fsb[slot]])
                hh_ = t // 8
                S.dma("sp", f"fst{slot}", fin_d[hh_].ap()[128 * (t % 8):128 * (t % 8) + 128, :], fsb[slot],
                      reads=[B_fsb[slot]], writes=[B_fin[hh_][t % 8]])
        project(XH, B_XH[t], 128 * t, [0, 1, 2, 3, 4, 5, 6], brows, consume)


    def passA_back(t):
        slot = t % 2
        rb = recb[slot]
        qk_t, B_qkt = qk2[slot], B_qk2[slot]
        def g4(ap):
            return ap.rearrange("p (g h c) -> p g h c", g=16, h=2)
        cosb = rope[:, t, 0:32].unsqueeze(1).to_broadcast([128, 32, 32])
        sinb = rope[:, t, 32:64].unsqueeze(1).to_broadcast([128, 16, 32])
        TT("dve", rt1.rearrange("p (g c) -> p g c", g=32), qk_t.rearrange("p (g c) -> p g c", g=32), cosb, ALU.mult, [B_qkt, B_const], [B_rt1])
        TT("dve", g4(rt2)[:, :, 0, :], g4(qk_t)[:, :, 1, :], sinb, ALU.mult, [B_qkt, B_const], [B_rt2])
        TT("dve", g4(rt2)[:, :, 1, :], g4(qk_t)[:, :, 0, :], sinb, ALU.mult, [B_qkt, B_const], [B_rt2])
        TT("dve", g4(rt1)[:, :, 0, :], g4(rt1)[:, :, 0, :], g4(rt2)[:, :, 0, :], ALU.subtract, [B_rt2], [B_rt1])
        TT("dve", g4(rt1)[:, :, 1, :], g4(rt1)[:, :, 1, :], g4(rt2)[:, :, 1, :], ALU.add, [B_rt2], [B_rt1])
        for dirn, sc in ((0, qf_sc), (1, qb_sc)):
            qp = k4(qpad[:, 1024 * dirn:1024 * dirn + 1024])
            qin = rot[:, 0:512].rearrange("p (a h d) -> p a h d", a=4, h=2)
            scv = sc.rearrange("p (a h) -> p a h", h=2)
            for hh in range(2):
                TT("dve", qp[:, :, hh, 64 * hh:64 * hh + 64], qin[:, :, hh, :], scv[:, :, hh].unsqueeze(2).to_broadcast([128, 4, 64]), ALU.mult,
                   [B_rot, B_small], [B_qpad])
        scaled_k(0, rot[:, 512:1024], B_rot, kf_sc)
        scaled_k(1, rot[:, 512:1024], B_rot, kb_sc)
        pst = [psbf(3), psbf(4), psbf(7)]
        for dirn in range(2):
            qp = k4(qpad[:, 1024 * dirn:1024 * dirn + 1024])
            for h in range(8):
                TR(pst[dirn][:, 128 * h:128 * h + 128], qp[:, h // 2, h % 2, :], [B_qpad], [PB[3 + dirn]])
        for dirn in range(2):
            for p4 in range(4):
                c0 = 512 * dirn + 128 * p4
                TR(pst[2][:, 128 * (4 * dirn + p4):128 * (4 * dirn + p4) + 128], ktok[:, c0:c0 + 128], [B_ktok], [PB[7]])
        CP("dve", rb[:, 0:1024], pst[0], [PB[3]], [B_rec[slot]])
        CP("dve", rb[:, 1024:2048], pst[1], [PB[4]], [B_rec[slot]])
        CP("dve", rb[:, 2048:3072], pst[2], [PB[7]], [B_rec[slot]])
        for dirn in range(2):
            kv_matmuls(dirn, rb[:, 3072:4096], B_rec[slot], 5 + dirn)
            CP("act", kvsb[slot][:, 512 * dirn:512 * dirn + 512], PS[5 + dirn][:, :], [PB[5 + dirn]], [B_kvsb[slot]])
        TT("pool", rt1[:, 0:512], Est[:, 0:512], kvsb[slot][:, 0:512], ALU.add, [B_E, B_kvsb[slot]], [B_rt1])
        TT("pool", E3(Est[:, 0:512]), E3(rt1[:, 0:512]), cdf_bc, ALU.mult, [B_rt1, B_small], [B_E])
        cdb_t = bc3(cdp_b[:, 4 * (t + 1):4 * (t + 1) + 4], 128)
        TT("pool", E3(rt1[:, 512:1024]), E3(kvsb[slot][:, 512:1024]), cdb_t, ALU.mult, [B_kvsb[slot], B_small], [B_rt1])
        TT("pool", Est[:, 512:1024], Est[:, 512:1024], rt1[:, 512:1024], ALU.add, [B_rt1, B_E], [B_E])
        S.dma("sp", f"rst{slot}", rec_d[t], rb, reads=[B_rec[slot]], writes=[B_recd[t]])
        S.dma("sp", f"kst{slot}", kv_d[t], kvsb[slot], reads=[B_kvsb[slot]], writes=[B_kvd[t]])
        if t % 8 == 7:
            hh_ = t // 8
            S.custom("pool", f"ccf{hh_}", lambda e, hh_=hh_: e.collective_compute(
                "AllGather", ALU.bypass, replica_groups=GROUPS, ins=[fin_d[hh_].ap().opt()], outs=[fout_d[hh_].ap().opt()]),
                reads=B_fin[hh_], writes=[B_fout[hh_]])


    for t0 in range(3):
        load_x(dr["x_own"][128 * t0:128 * t0 + 128, :], t0)
    norm_part(0, A1bc)
    passA_front1(0)
    norm_part(1, A1bc)
    passA_front(0)
    for t in range(NT):
        if t + 1 < NT:
            passA_front1(t + 1)
        if t + 2 < NT:
            norm_part((t + 2) % 3, A1bc)
        if t + 3 < NT:
            load_x(dr["x_own"][128 * (t + 3):128 * (t + 4), :], t % 3)
        if t + 1 < NT:
            passA_front(t + 1)
        passA_back(t)
    B_stin, B_stout = Buf("stin"), Buf("stout")
    S.dma("sp", "stst", stin_d.ap(), Est, reads=[B_E], writes=[B_stin])
    S.custom("pool", "ccst", lambda e: e.collective_compute(
        "AllGather", ALU.bypass, replica_groups=GROUPS, ins=[stin_d.ap().opt()], outs=[stout_d.ap().opt()]),
        reads=[B_stin], writes=[B_stout])
    S.barrier(exclude=("ccst",))
    S.dma("sp", "bcl", nwt, dr["norm1_w"].partition_broadcast(128), writes=[B_bct])
    S.dma("sp", "bcl", A1c, mod_d[:, 7 * D:8 * D].partition_broadcast(128), reads=[B_modd], writes=[B_bct])
    STT("dve", A1c, A1c, 1.0, nwt, ALU.add, ALU.mult, [B_bct], [B_bct])
    TS("dve", A1c, A1c, 32.0, None, ALU.mult, None, [B_bct], [B_bct])
    bias_rows(1, browsc)
    for tl in range(2):
        load_x(dr["ctx"][128 * tl:128 * tl + 128, :], tl)
    for tl in range(2):
        norm_transpose(tl, A1c, hcT, B_hcT, 128 * tl)

        def consume_ctx(blk, ps, pb):
            if blk == 1:
                CP("act", qk_sb[:, 512:1024], ps[:, :], [pb], [B_qk])
            else:
                CP("act", ctxv[:, 512 * (blk - 2):512 * (blk - 2) + 512], ps[:, :], [pb], [B_ctxv])
        project(hcT, B_hcT, 128 * tl, [1, 2, 3], browsc, consume_ctx)
        scaled_k(0, qk_sb[:, 512:1024], B_qk, cxf_sc[:, 8 * tl:8 * tl + 8])
        scaled_k(1, qk_sb[:, 512:1024], B_qk, cxb_sc[:, 8 * tl:8 * tl + 8])
        for dirn in range(2):
            kv_matmuls(dirn, ctxv, B_ctxv, 5 + dirn)
            dst = sctx[:, 512 * dirn:512 * dirn + 512]
            if tl == 0:
                CP("dve", dst, PS[5 + dirn][:, :], [PB[5 + dirn]], [B_sctx])
            else:
                TT("dve", dst, dst, PS[5 + dirn][:, :], ALU.add, [PB[5 + dirn], B_sctx], [B_sctx])


    S.barrier()
    R1.reset()
    SfT = R1.bf(NT * 512).rearrange("p (n c) -> p n c", n=NT)
    SbT = R1.bf(NT * 512).rearrange("p (n c) -> p n c", n=NT)
    ogT = R1.bf(8 * TOK).rearrange("p (h c) -> p h c", h=8)
    B_ST = Buf("ST")
    B_ogT = [Buf(f"ogT{t}") for t in range(NT)]
    R2.reset()
    kvall = R2.f32(NT * 1024).rearrange("p (n c) -> p n c", n=NT)
    Gst = R2.f32(4096).rearrange("p (r c) -> p r c", r=4)
    curf, curb = R2.f32(512), R2.f32(512)
    tmpf, tmpb = R2.f32(512), R2.f32(512)
    B_kvall, B_G = Buf("kvall"), Buf("G")
    B_curf, B_curb, B_tmpf, B_tmpb = Buf("curf"), Buf("curb"), Buf("tmpf"), Buf("tmpb")
    CP("dve", curf, sctx[:, 0:512], [B_sctx], [B_curf])
    CP("dve", curb, sctx[:, 512:1024], [B_sctx], [B_curb])
    for t in range(NT):
        S.dma("sp", "kvld", kvall[:, t, :], kv_d[t], reads=[B_kvd[t]], writes=[B_kvall])
    S.dma("sp", "gld", Gst, stout_d.ap().rearrange("(r p) c -> p r c", p=128), reads=[B_stout], writes=[B_G])
    cd16f = bc3(cdp_f[:, 64:68], 128)
    cd16b = bc3(cdp_b[:, 64:68], 128)
    TS("dve", tmpf, curf, sel[:, 0:1], None, ALU.mult, None, [B_curf, B_const], [B_tmpf])
    for r in range(3):
        TT("dve", E3(curf), E3(curf), cd16f, ALU.mult, [B_curf, B_small], [B_curf])
        TT("dve", curf, curf, Gst[:, r, 0:512], ALU.add, [B_curf, B_G], [B_curf])
        STT("dve", tmpf, curf, sel[:, r + 1:r + 2], tmpf, ALU.mult, ALU.add, [B_curf, B_tmpf, B_const], [B_tmpf])
    TS("dve", tmpb, curb, sel[:, 7:8], None, ALU.mult, None, [B_curb, B_const], [B_tmpb])
    for r in (3, 2, 1):
        TT("dve", E3(curb), E3(curb), cd16b, ALU.mult, [B_curb, B_small], [B_curb])
        TT("dve", curb, curb, Gst[:, r, 512:1024], ALU.add, [B_curb, B_G], [B_curb])
        STT("dve", tmpb, curb, sel[:, 4 + r - 1:4 + r], tmpb, ALU.mult, ALU.add, [B_curb, B_tmpb, B_const], [B_tmpb])
    cdb_bc = bc3(cdp_b[:, 4:8], 128)
    for n in range(NT):
        m_ = NT - 1 - n
        CP("act", SfT[:, n, :], tmpf, [B_tmpf], [B_ST])
        TT("dve", curf, tmpf, kvall[:, n, 0:512], ALU.add, [B_tmpf, B_kvall], [B_curf])
        TT("dve", E3(tmpf), E3(curf), cdf_bc, ALU.mult, [B_curf, B_small], [B_tmpf])
        CP("act", SbT[:, m_, :], tmpb, [B_tmpb], [B_ST])
        TT("dve", curb, tmpb, kvall[:, m_, 512:1024], ALU.add, [B_tmpb, B_kvall], [B_curb])
        TT("dve", E3(tmpb), E3(curb), cdb_bc, ALU.mult, [B_curb, B_small], [B_tmpb])
    S.barrier()
    if STOP == "scan":
        S.emit(); return nc

    R2.reset()
    rbB = [R2.bf(5120) for _ in range(2)]
    Pm = [R2.bf(512) for _ in range(2)]
    sq = R2.f32(1024)
    tcen = R2.f32(1024)
    praw = [R2.bf(512) for _ in range(2)]
    ogtok = R2.bf(1024)
    B_rbB = [Buf("rbB0"), Buf("rbB1")]
    B_Pm = [Buf("Pm0"), Buf("Pm1")]
    B_sq, B_tcen, B_ogtok, B_gst = Buf("sq"), Buf("tcen"), Buf("ogtok"), Buf("gst")
    B_praw = [Buf("praw0"), Buf("praw1")]
    mask3 = mask[:, :, :]
    S.dma("sp", "rld0", rbB[0], rec_d[0], reads=[B_recd[0]], writes=[B_rbB[0]])
    obanks = [(5, 6), (3, 4)]

    def passB_mm(n):
        slot = n % 2
        rb = rbB[slot]

        def qT(di, h):
            return rb[:, (8 * di + h) * 128:(8 * di + h) * 128 + 128]

        def kT(di, p4):
            c0 = 2048 + (4 * di + p4) * 128
            return rb[:, c0:c0 + 128]
        def scores(p4):
            sbank = 1 + (p4 % 2)
            psc = PS[sbank][:, :].rearrange("p (a c) -> p a c", a=4)
            for di in range(2):
                for hh in range(2):
                    MM(psc[:, 2 * di + hh, :], kT(di, p4), qT(di, 2 * p4 + hh), True, True, [B_rbB[slot]], [PB[sbank]])
            pm = Pm[p4 % 2]
            CP("act", praw[p4 % 2], PS[sbank][:, :], [PB[sbank]], [B_praw[p4 % 2]])
            TT("pool", pm.rearrange("p (a c) -> p a c", a=4), praw[p4 % 2].rearrange("p (a c) -> p a c", a=4), mask3, ALU.mult,
               [B_praw[p4 % 2], B_const], [B_Pm[p4 % 2]])

        def omm(p4):
            pm3 = Pm[p4 % 2].rearrange("p (a c) -> p a c", a=4)
            obank = obanks[n % 2][p4 // 2]
            for hh in range(2):
                h = 2 * p4 + hh
                od = PS[obank][:, 128 * (h % 4):128 * (h % 4) + 128]
                vv = rb[:, 3072 + 128 * h:3072 + 128 * h + 128]
                MM(od, pm3[:, hh, :], vv, True, False, [B_Pm[p4 % 2], B_rbB[slot]], [PB[obank]])
                MM(od, qT(0, h), SfT[:, n, 128 * p4:128 * p4 + 128], False, False, [B_rbB[slot], B_ST], [PB[obank]])
                MM(od, pm3[:, 2 + hh, :], vv, False, False, [B_Pm[p4 % 2], B_rbB[slot]], [PB[obank]])
                MM(od, qT(1, h), SbT[:, n, 128 * p4:128 * p4 + 128], False, True, [B_rbB[slot], B_ST], [PB[obank]])
        scores(0)
        scores(1)
        omm(0)
        scores(2)
        omm(1)
        scores(3)
        omm(2)
        omm(3)

    def passB_gn(n):
        slot = n % 2
        rb = rbB[slot]
        ob = obanks[n % 2]
        o3s = [PS[ob[half]][:, :].rearrange("p (h e) -> p h e", h=4) for half in range(2)]
        for half in range(2):
            hs = slice(4 * half, 4 * half + 4)
            S.op("dve", lambda e, o3=o3s[half], hs=hs: e.tensor_reduce(out=gst[:, hs], in_=o3, axis=AX.X, op=ALU.add), [PB[ob[half]]], [B_gst])
            ACTF(sq[:, 512 * half:512 * half + 512], PS[ob[half]][:, :], AF.Square, [PB[ob[half]]], [B_sq])
        TS("dve", gst[:, 16:24], gst[:, 0:8], 1.0 / 128.0, None, ALU.mult, None, [B_gst], [B_gst])
        for half in range(2):
            TT("dve", tcen[:, 512 * half:512 * half + 512].rearrange("p (h e) -> p h e", h=4), o3s[half], bc3(gst[:, 16 + 4 * half:20 + 4 * half], 128),
               ALU.subtract, [PB[ob[half]], B_gst], [B_tcen])
        TT("dve", tcen, tcen, rb[:, 4096:5120], ALU.mult, [B_tcen, B_rbB[slot]], [B_tcen])
        S.op("dve", lambda e: e.tensor_reduce(out=gst[:, 8:16], in_=sq.rearrange("p (h e) -> p h e", h=8), axis=AX.X, op=ALU.add), [B_sq], [B_gst])
        TT("dve", gst[:, 24:32], gst[:, 16:24], gst[:, 16:24], ALU.mult, [B_gst], [B_gst])
        STT("dve", gst[:, 32:40], gst[:, 8:16], 1.0 / 128.0, gst[:, 24:32], ALU.mult, ALU.subtract, [B_gst], [B_gst])
        RSQ(gst[:, 40:48], gst[:, 32:40], EPS, [B_gst], [B_gst])
        TT("dve", ogtok.rearrange("p (h e) -> p h e", h=8), tcen.rearrange("p (h e) -> p h e", h=8), bc3(gst[:, 40:48], 128), ALU.mult,
           [B_tcen, B_gst], [B_ogtok])
        pst = psbf(0)
        for h in range(8):
            TR(pst[:, 128 * h:128 * h + 128], ogtok[:, 128 * h:128 * h + 128], [B_ogtok], [PB[0]])
        CP("act", ogT[:, :, 128 * n:128 * n + 128], pst.rearrange("p (h c) -> p h c", h=8), [PB[0]], [B_ogT[n]])
        if n + 2 < NT:
            S.dma("sp", f"rld{n % 2}", rbB[n % 2], rec_d[n + 2], reads=[B_recd[n + 2]], writes=[B_rbB[n % 2]])

    S.dma("sp", "rld1", rbB[1], rec_d[1], reads=[B_recd[1]], writes=[B_rbB[1]])
    passB_mm(0)
    for n in range(NT):
        if n + 1 < NT:
            passB_mm(n + 1)
        passB_gn(n)
    S.barrier()
    if stage == 1:
        S.dma("sp", "dbg", dbg_d.rearrange("p (h c) -> p h c", h=8), ogT, reads=B_ogT)
        S.barrier()
        S.emit()
        return nc

    zT = R1t[:, 0:4 * TOK].rearrange("p (g c) -> p g c", g=4)
    B_zT = Buf("zT")
    R2.reset()
    F1 = [R2.bf(16384, parts=64).rearrange("p (n c) -> p n c", n=128) for _ in range(1)]
    Aall = R2.bf(16384).rearrange("p (c k) -> p c k", c=128)
    tw = R2.bf(64 * 128).rearrange("p (k t c) -> p k t c", k=64, t=2)
    Xsb = [R2.bf(512).rearrange("p (k t c) -> p k t c", k=8, t=2) for _ in range(2)]
    B_F1, B_A, B_tw = Buf("F1"), Buf("A"), Buf("tw")
    B_Xsb = [Buf("Xsb0"), Buf("Xsb1")]
    S.dma("sp", "twld", tw, dr["tw"], writes=[B_tw])
    wst_s = R2.f32(1024).rearrange("p (k c) -> p k c", k=8)
    wmb_s = [R2.bf(1024).rearrange("p (k c) -> p k c", k=8) for _ in range(2)]
    bmod_s = [R2.f32(128, parts=1) for _ in range(2)]
    row_s = R2.f32(128, parts=1)
    B_wsts, B_rows = Buf("wsts"), Buf("rows")
    B_wmbs = [Buf("wmbs0"), Buf("wmbs1")]
    B_bms = [Buf("bms0"), Buf("bms1")]

    def mod_prep(j_):
        c0 = 4 * 512 + 128 * j_
        S.dma("sp", "wsts", wst_s, wmod_v[:, :, c0:c0 + 128], writes=[B_wsts])
        S.dma("sp", f"bms{j_ % 2}", bmod_s[j_ % 2], dr["b_mod"][:, c0:c0 + 128], writes=[B_bms[j_ % 2]])
        CP("pool", wmb_s[j_ % 2], wst_s, [B_wsts], [B_wmbs[j_ % 2]])

    def mod_mm(j_):
        c0 = 4 * 512 + 128 * j_
        for k in range(8):
            MM(PS[7][0:1, 0:128], silcb[:, k:k + 1], wmb_s[j_ % 2][:, k, :], k == 0, k == 7, [B_small, B_wmbs[j_ % 2]], [PB[7]])
        TT("dve", row_s, PS[7][0:1, 0:128], bmod_s[j_ % 2], ALU.add, [PB[7], B_bms[j_ % 2]], [B_rows])
        S.dma("sp", "rowss", mod_d[:, c0:c0 + 128], row_s, reads=[B_rows], writes=[B_modd])

    def mod_subblock(j_):
        if j_ == 0:
            mod_prep(0)
        if j_ + 1 < 32:
            mod_prep(j_ + 1)
        mod_mm(j_)

    nsub = [0]
    for g in range(4):
        for h2 in range(2):
            src = fout_d[h2].ap().rearrange("(q n) c -> q n c", n=128)[:, :, 128 * g:128 * g + 128]
            S.dma("sp", "f1ld", F1[0][32 * h2:32 * h2 + 32, :, :], src, reads=[B_fout[h2]], writes=[B_F1])
        for c4 in range(32):
            if c4 % 8 == 0 and nsub[0] < 32:
                mod_subblock(nsub[0])
                nsub[0] += 1
            bank = 1 + (c4 % 2)
            for cc in range(4):
                ch = 4 * c4 + cc
                MM(PS[bank][:, 128 * cc:128 * cc + 128], F1[0][:, :, ch], e64[:, :], True, True, [B_F1, B_const], [PB[bank]])
            CP("act" if c4 % 2 == 0 else "dve", Aall[:, 4 * c4:4 * c4 + 4, :], PS[bank][:, :].rearrange("p (c k) -> p c k", c=4), [PB[bank]], [B_A])
        for kb in range(8):
            if kb % 2 == 0 and nsub[0] < 32:
                mod_subblock(nsub[0])
                nsub[0] += 1
            xb = 3 + (kb % 2)
            px = PS[xb][:, :].rearrange("p (k t c) -> p k t c", k=8, t=2)
            for kk in range(8):
                k1 = 8 * kb + kk
                ar = Aall[:, :, k1]
                ai = Aall[:, :, 64 + k1]
                pxk = PS[xb][:, 64 * kk:64 * kk + 64]
                MM(pxk, ar, tw[:, k1, 0, :], True, False, [B_A, B_tw], [PB[xb]])
                MM(pxk, ai, tw[:, k1, 1, :], False, True, [B_A, B_tw], [PB[xb]])
            xs = Xsb[kb % 2]
            CP("act", xs, px, [PB[xb]], [B_Xsb[kb % 2]])
            zb = 5 + (kb % 2)
            pz = PS[zb][:, 0:256].rearrange("p (k c) -> p k c", k=8)
            MM(pz, c128[:, 0, :], xs[:, :, 0, :], True, False, [B_Xsb[kb % 2], B_const], [PB[zb]])
            MM(pz, c128[:, 1, :], xs[:, :, 1, :], False, True, [B_Xsb[kb % 2], B_const], [PB[zb]])
            zdst = zT[:, g, :].rearrange("p (b a) -> p a b", a=64)[:, 8 * kb:8 * kb + 8, :]
            CP("dve", zdst, pz, [PB[zb]], [B_zT])
    S.barrier()
    col_layout(shcol[:, 8:16], mod_d[:, 3 * D:4 * D].rearrange("o (c p) -> (o c) p", p=128), 8)
    CP("dve", shcolb[:, 8:16], shcol[:, 8:16], [B_const], [B_small])
    if stage == 2:
        S.dma("sp", "dbg", dbg_d[:, 0:4 * TOK], R1t[:, 0:4 * TOK], reads=[B_zT])
        S.barrier()
        S.emit()
        return nc

    Wout = R2t[:, 37888:46080].rearrange("p (k c) -> p k c", k=8)
    B_wout = Buf("wout")
    wout_v = dr["w_out"].rearrange("(k p) c -> p k c", p=128)
    R2.reset()
    mergedT = R2.bf(8 * TOK).rearrange("p (k c) -> p k c", k=8)
    wsl = [R2.bf(28 * 128).rearrange("p (k c) -> p k c", k=28) for _ in range(2)]
    gts = [R2.f32(512) for _ in range(4)]
    m12 = [R2.f32(512) for _ in range(2)]
    B_wsl = [Buf("wsl0"), Buf("wsl1")]
    B_gts = [Buf(f"gt{i}") for i in range(4)]
    B_m12 = [Buf("m1"), Buf("m2")]
    B_mg = [Buf(f"mg{i}") for i in range(4)]
    wro_v = dr["w_ret_out"].rearrange("(k p) c -> p k c", p=128)
    wfo_v = dr["w_four_out"].rearrange("(k p) c -> p k c", p=128)
    wbg_v = dr["w_bg"].rearrange("(k p) c -> p k c", p=128)

    def load_wsl(oc):
        i = oc % 2
        cs = slice(128 * oc, 128 * oc + 128)
        S.dma("pool", f"wsl{i}", wsl[i][:, 0:8, :], wro_v[:, :, cs], writes=[B_wsl[i]])
        S.dma("pool", f"wsl{i}", wsl[i][:, 8:12, :], wfo_v[:, :, cs], writes=[B_wsl[i]])
        S.dma("pool", f"wsl{i}", wsl[i][:, 12:20, :], wbg_v[:, :, cs], writes=[B_wsl[i]])
        S.dma("pool", f"wsl{i}", wsl[i][:, 20:28, :], wbg_v[:, :, D + 128 * oc:D + 128 * oc + 128], writes=[B_wsl[i]])
    load_wsl(0)
    for k in range(8):
        S.dma("pool", "wout", Wout[:, k, :], wout_v[:, k, :], writes=[B_wout])
    B_bbg = Buf("bbg")
    for oc in range(8):
        i = oc % 2
        if oc + 1 < 8:
            load_wsl(oc + 1)
        w = wsl[i]
        for gi in range(2):
            for k in range(8):
                MM(PS[7][:, gi:gi + 1], w[:, 12 + 8 * gi + k, :], shcolb[:, k:k + 1], k == 0, k == 7, [B_wsl[i], B_small], [PB[7]])
        bcol = bbg.rearrange("p (g c) -> p g c", g=2)[:, :, oc]
        TT("dve", bcol, bcol, PS[7][:, 0:2], ALU.add, [PB[7], B_const], [B_bbg])
        for tb in range(4):
            ts_ = slice(512 * tb, 512 * tb + 512)
            xh_b = [B_XH[4 * tb + q] for q in range(4)]
            og_b = [B_ogT[4 * tb + q] for q in range(4)]
            for k in range(8):
                MM(PS[1][:, :], w[:, k, :], ogT[:, k, ts_], k == 0, k == 7, [B_wsl[i]] + og_b, [PB[1]])
            for k in range(4):
                MM(PS[2][:, :], w[:, 8 + k, :], zT[:, k, ts_], k == 0, k == 3, [B_wsl[i], B_zT], [PB[2]])
            for gi in range(2):
                for k in range(8):
                    MM(PS[3 + gi][:, :], w[:, 12 + 8 * gi + k, :], XH[:, k, ts_], k == 0, k == 7, [B_wsl[i]] + xh_b, [PB[3 + gi]])
            g0 = 2 * (tb % 2)
            for gi in range(2):
                ACTF(gts[g0 + gi], PS[3 + gi][:, :], AF.Sigmoid, [PB[3 + gi], B_bbg], [B_gts[g0 + gi]], bias=bbg[:, 8 * gi + oc:8 * gi + oc + 1])
            TT("dve", m12[0], gts[g0], PS[1][:, :], ALU.mult, [B_gts[g0], PB[1]], [B_m12[0]])
            TT("dve", m12[1], gts[g0 + 1], PS[2][:, :], ALU.mult, [B_gts[g0 + 1], PB[2]], [B_m12[1]])
            TT("pool", mergedT[:, oc, ts_], m12[0], m12[1], ALU.add, [B_m12[0], B_m12[1]], [B_mg[tb]])
    S.barrier()

    R2.reset(8 * TOK)
    xt2 = [R2.f32(1024) for _ in range(2)]
    x1t = [R2.f32(1024) for _ in range(2)]
    tmpy = R2.f32(1024)
    g1bc = R2.f32(1024)
    A2bc = R2.f32(1024)
    nw2t = R2.f32(1024)
    hb2 = R2.bf(1024)
    junk2 = R2.bf(1024)
    B_xt2 = [Buf("xt2_0"), Buf("xt2_1")]
    B_x1t = [Buf("x1t0"), Buf("x1t1")]
    B_tmpy, B_hb2, B_junk2, B_bc2 = Buf("tmpy"), Buf("hb2"), Buf("junk2"), Buf("bc2")
    B_x1d = [Buf(f"x1d{t}") for t in range(NT)]
    S.dma("sp", "bcl", g1bc, mod_d[:, 2 * D:3 * D].partition_broadcast(128), writes=[B_bc2])
    S.dma("sp", "bcl", A2bc, mod_d[:, 4 * D:5 * D].partition_broadcast(128), writes=[B_bc2])
    S.dma("sp", "bcl", nw2t, dr["norm2_w"].partition_broadcast(128), writes=[B_bc2])
    STT("dve", A2bc, A2bc, 1.0, nw2t, ALU.add, ALU.mult, [B_bc2], [B_bc2])
    TS("dve", A2bc, A2bc, 32.0, None, ALU.mult, None, [B_bc2], [B_bc2])
    Wd = R1t[:, 0:22 * D].rearrange("p (k c) -> p k c", k=22)
    B_wd = Buf("wd")
    wd_v = dr["w_down"].rearrange("(k p) c -> p k c", p=128)
    for k in range(22):
        S.dma("pool", "wd", Wd[:, k, :], wd_v[:, k, :], writes=[B_wd])
    S.dma("sp", "xt2_0", xt2[0], dr["x_own"][0:128, :], writes=[B_xt2[0]])
    ybanks = [(1, 2), (3, 4)]

    def y_mm(t):
        for half in range(2):
            bk = ybanks[t % 2][half]
            for k in range(8):
                MM(PS[bk][:, :], mergedT[:, k, 128 * t:128 * t + 128], Wout[:, k, 512 * half:512 * half + 512], k == 0, k == 7,
                   [B_mg[t // 4], B_wout], [PB[bk]])

    def y_post(t):
        slot = t % 2
        if t + 1 < NT:
            S.dma("sp", f"xt2_{(t + 1) % 2}", xt2[(t + 1) % 2], dr["x_own"][128 * (t + 1):128 * (t + 2), :], writes=[B_xt2[(t + 1) % 2]])
        for half in range(2):
            bk = ybanks[t % 2][half]
            hs = slice(512 * half, 512 * half + 512)
            TT("dve", tmpy[:, hs], PS[bk][:, :], g1bc[:, hs], ALU.mult, [PB[bk], B_bc2], [B_tmpy])
        TT("dve", x1t[slot], tmpy, xt2[slot], ALU.add, [B_tmpy, B_xt2[slot]], [B_x1t[slot]])
        S.dma("sp", f"x1st{slot}", x1_d[128 * t:128 * t + 128, :], x1t[slot], reads=[B_x1t[slot]], writes=[B_x1d[t]])
        ACTF(junk2, x1t[slot], AF.Square, [B_x1t[slot]], [B_junk2, B_ss], accum_out=ss_t[:, 2:3])
        RSQ(ss_t[:, 3:4], ss_t[:, 2:3], 1024.0 * EPS, [B_ss], [B_ss])
        STT("dve", hb2, x1t[slot], ss_t[:, 3:4], A2bc, ALU.mult, ALU.mult, [B_x1t[slot], B_ss, B_bc2], [B_hb2])
        pst = psbf(0)
        for k in range(8):
            TR(pst[:, 128 * k:128 * k + 128], hb2[:, 128 * k:128 * k + 128], [B_hb2], [PB[0]])
        CP("act", XH[:, :, 128 * t:128 * t + 128], pst.rearrange("p (k c) -> p k c", k=8), [PB[0]], [B_XH[t]])

    y_mm(0)
    for t in range(NT):
        if t + 1 < NT:
            y_mm(t + 1)
        y_post(t)
    S.barrier()
    if stage == 3:
        for t in range(NT):
            S.dma("sp", "xt2_0", xt2[0], x1_d[128 * t:128 * t + 128, :], reads=[B_x1d[t]], writes=[B_xt2[0]])
            S.dma("sp", "outst", out_d[128 * t:128 * t + 128, :], xt2[0], reads=[B_xt2[0]], writes=[])
        S.dma("sp", "dbg", dbg_d, R2t[:, 0:8 * TOK], reads=B_mg)
        S.barrier()
        S.emit()
        return nc

    R2.reset()
    USE_GELU_ACT = not os.environ.get("KDBG_GELU_SIG")
    mT = R2.bf(22 * 512).rearrange("p (k c) -> p k c", k=22)
    wup = [R2.bf(2048).rearrange("p (a k c) -> p a k c", a=2, k=8) for _ in range(3)]
    cacc = [[R2.f32(512) for _ in range(2)] for _ in range(2)]
    ga = [R2.f32(512) for _ in range(2)]
    ub = R2.f32(NCH * 8).rearrange("p (c t) -> p c t", c=NCH)
    hal = R2.f32(2 * NCH).rearrange("p (s c) -> p s c", s=2)
    hsend = R2.f32(128)
    hall = R2.f32(512).rearrange("p (r c) -> p r c", r=4)
    kcc = R2.f32(3 * NCH).rearrange("p (s c) -> p s c", s=3)
    x1r = [R2.f32(1024) for _ in range(2)]
    x2t = R2.f32(1024)
    g2bc = R2.f32(1024)
    fwbc = R2.f32(1024)
    junk3 = R2.bf(1024)
    B_wup = [Buf(f"wup{i}") for i in range(3)]
    B_cacc = [[Buf(f"ca{s_}{a}") for a in range(2)] for s_ in range(2)]
    B_ga = [Buf("ga0"), Buf("ga1")]
    B_mT, B_ub, B_hal, B_hsend, B_hall, B_kcc = Buf("mT"), Buf("ub"), Buf("hal"), Buf("hsend"), Buf("hall"), Buf("kcc")
    B_x1r = [Buf("x1r0"), Buf("x1r1")]
    B_x2t, B_bc3, B_junk3 = Buf("x2t"), Buf("bc3"), Buf("junk3")
    B_hin, B_hout = Buf("hin"), Buf("hout")
    S.dma("sp", "bcl", g2bc, mod_d[:, 5 * D:6 * D].partition_broadcast(128), writes=[B_bc3])
    S.dma("sp", "bcl", fwbc, dr["final_norm_w"].partition_broadcast(128), writes=[B_bc3])
    wup_v = dr["w_up"].rearrange("(k p) c -> p k c", p=128)
    nload = [0]

    def load_wup(i):
        s_ = nload[0] % 3
        nload[0] += 1
        S.dma("pool", f"wup{s_}", wup[s_][:, 0, :, :], wup_v[:, :, 128 * i:128 * i + 128], writes=[B_wup[s_]])
        S.dma("pool", f"wup{s_}", wup[s_][:, 1, :, :], wup_v[:, :, FFN + 128 * i:FFN + 128 * i + 128], writes=[B_wup[s_]])
        return s_

    xb8 = XH[:, :, :].rearrange("p k (b c) -> p k b c", c=512)
    bnd = R2.bf(72).rearrange("p (k c) -> p k c", k=8)
    B_bnd = Buf("bnd")
    CP("dve", bnd[:, :, 0], shcolb[:, 8:16], [B_small], [B_bnd])
    CP("dve", bnd[:, :, 1:5], xb8[:, :, :, 0], B_XH, [B_bnd])
    CP("dve", bnd[:, :, 5:9], xb8[:, :, :, 511], B_XH, [B_bnd])
    B_bup = Buf("bup")
    TT("dve", kcc[:, 0, :], convc[:, 0, :], convc[:, 1, :], ALU.add, [B_const], [B_kcc])
    TT("dve", kcc[:, 0, :], kcc[:, 0, :], convc[:, 2, :], ALU.add, [B_const, B_kcc], [B_kcc])
    S.op("pool", lambda e: e.memset(hsend, 0.0), writes=[B_hsend])

    def halo_exchange():
        CP("dve", hsend[:, 0:NCH], ub[:, :, 0], [B_ub], [B_hsend])
        CP("dve", hsend[:, NCH:2 * NCH], ub[:, :, 7], [B_ub], [B_hsend])
        S.dma("sp", "hst", hin_d.ap(), hsend, reads=[B_hsend], writes=[B_hin])
        S.custom("pool", "cch", lambda e: e.collective_compute(
            "AllGather", ALU.bypass, replica_groups=GROUPS, ins=[hin_d.ap().opt()], outs=[hout_d.ap().opt()]),
            reads=[B_hin], writes=[B_hout])
        S.dma("sp", "hld", hall, hout_d.ap().rearrange("(r p) c -> p r c", p=128), reads=[B_hout], writes=[B_hall])
        for side, (c0, s0) in enumerate(((NCH, 8), (0, 12))):
            TS("dve", hal[:, side, :], hall[:, 0, c0:c0 + NCH], sel[:, s0:s0 + 1], None, ALU.mult, None, [B_hall, B_const], [B_hal])
            for r in range(1, 4):
                STT("dve", hal[:, side, :], hall[:, r, c0:c0 + NCH], sel[:, s0 + r:s0 + r + 1], hal[:, side, :], ALU.mult, ALU.add,
                    [B_hall, B_hal, B_const], [B_hal])
            STT("dve", hal[:, side, :], kcc[:, 2, :], sel[:, 16 + side:17 + side], hal[:, side, :], ALU.mult, ALU.add, [B_kcc, B_hal, B_const], [B_hal])

    for tb in (1, 2, 0, 3):
        ts_ = slice(512 * tb, 512 * tb + 512)
        xh_b = [B_XH[4 * tb + q] for q in range(4)]
        pend = [load_wup(0), load_wup(1)]
        S.dma("sp", "x1r0", x1r[0], x1_d[512 * tb:512 * tb + 128, :], reads=[B_x1d[4 * tb]], writes=[B_x1r[0]])
        for i in range(22):
            s_ = pend.pop(0)
            if i + 2 < 22:
                pend.append(load_wup(i + 2))
            us = i % 2
            for a in range(2):
                ch = i + 22 * a
                bank = 1 + 2 * us + a
                if tb == 1:
                    for k in range(8):
                        MM(PS[7][:, 0:9], wup[s_][:, a, k, :], bnd[:, k, :], k == 0, k == 7, [B_wup[s_], B_bnd], [PB[7]])
                    CP("act", ub[:, ch, :].rearrange("p (b e) -> p e b", e=2), PS[7][:, 1:9].rearrange("p (e b) -> p e b", e=2), [PB[7]], [B_ub])
                    CP("dve", bup[:, ch:ch + 1], PS[7][:, 0:1], [PB[7]], [B_bup])
                    STT("dve", kcc[:, 1, ch:ch + 1], bup[:, ch:ch + 1], kcc[:, 0, ch:ch + 1], convc[:, 3, ch:ch + 1], ALU.mult, ALU.add,
                        [B_bup, B_kcc, B_const], [B_kcc])
                    TS("dve", kcc[:, 2, ch:ch + 1], bup[:, ch:ch + 1], -1.0, None, ALU.mult, None, [B_bup], [B_kcc])
                for k in range(8):
                    MM(PS[bank][:, :], wup[s_][:, a, k, :], XH[:, k, ts_], k == 0, k == 7, [B_wup[s_]] + xh_b, [PB[bank]])
                ca = cacc[us][a]
                bca = B_cacc[us][a]
                w0c, w1c, w2c = convc[:, 0, ch:ch + 1], convc[:, 1, ch:ch + 1], convc[:, 2, ch:ch + 1]
                ACTF(ca, PS[bank][:, :], AF.Identity, [PB[bank], B_kcc, B_const], [bca], scale=w1c, bias=kcc[:, 1, ch:ch + 1])
                STT("dve", ca[:, 1:512], PS[bank][:, 0:511], w0c, ca[:, 1:512], ALU.mult, ALU.add, [PB[bank], bca, B_const], [bca])
                STT("dve", ca[:, 0:511], PS[bank][:, 1:512], w2c, ca[:, 0:511], ALU.mult, ALU.add, [PB[bank], bca, B_const], [bca])
                if tb == 0:
                    pl, bpl = hal[:, 0, ch:ch + 1], B_hal
                else:
                    pl, bpl = ub[:, ch, 2 * tb - 1:2 * tb], B_ub
                if tb == 3:
                    pr, bpr = hal[:, 1, ch:ch + 1], B_hal
                else:
                    pr, bpr = ub[:, ch, 2 * tb + 2:2 * tb + 3], B_ub
                STT("dve", ca[:, 0:1], pl, w0c, ca[:, 0:1], ALU.mult, ALU.add, [bpl, bca, B_const], [bca])
                STT("dve", ca[:, 511:512], pr, w2c, ca[:, 511:512], ALU.mult, ALU.add, [bpr, bca, B_const], [bca])
            ca, cv = cacc[us][0], cacc[us][1]
            if USE_GELU_ACT:
                ACTF(ga[us], ca, AF.Gelu_apprx_tanh, [B_cacc[us][0]], [B_ga[us]])
            else:
                TT("dve", ga[us], ca, ca, ALU.mult, [B_cacc[us][0]], [B_ga[us]])
                TS("dve", ga[us], ga[us], 0.044715, 1.0, ALU.mult, ALU.add, [B_ga[us]], [B_ga[us]])
                TT("dve", ga[us], ga[us], ca, ALU.mult, [B_ga[us], B_cacc[us][0]], [B_ga[us]])
                ACTF(ga[us], ga[us], AF.Sigmoid, [B_ga[us]], [B_ga[us]], scale=GELU_C)
                TT("dve", ga[us], ga[us], ca, ALU.mult, [B_ga[us], B_cacc[us][0]], [B_ga[us]])
            TT("pool", mT[:, i, :], cv, ga[us], ALU.mult, [B_cacc[us][1], B_ga[us]], [B_mT])
        if tb == 1:
            halo_exchange()
        for q in range(4):
            t = 4 * tb + q
            slot = q % 2
            if q + 1 < 4:
                S.dma("sp", f"x1r{(q + 1) % 2}", x1r[(q + 1) % 2], x1_d[128 * (t + 1):128 * (t + 2), :], reads=[B_x1d[t + 1]], writes=[B_x1r[(q + 1) % 2]])
            for half in range(2):
                for k in range(22):
                    MM(PS[5 + half][:, :], mT[:, k, 128 * q:128 * q + 128], Wd[:, k, 512 * half:512 * half + 512], k == 0, k == 21,
                       [B_mT, B_wd], [PB[5 + half]])
            for half in range(2):
                hs = slice(512 * half, 512 * half + 512)
                TT("dve", x2t[:, hs], PS[5 + half][:, :], g2bc[:, hs], ALU.mult, [PB[5 + half], B_bc3], [B_x2t])
            TT("dve", x2t, x2t, x1r[slot], ALU.add, [B_x2t, B_x1r[slot]], [B_x2t])
            ACTF(junk3, x2t, AF.Square, [B_x2t], [B_junk3, B_ss], accum_out=ss_t[:, 4:5])
            RSQ(ss_t[:, 5:6], ss_t[:, 4:5], 1024.0 * EPS, [B_ss], [B_ss])
            TS("dve", ss_t[:, 5:6], ss_t[:, 5:6], 32.0, None, ALU.mult, None, [B_ss], [B_ss])
            STT("dve", x1r[slot], x2t, ss_t[:, 5:6], fwbc, ALU.mult, ALU.mult, [B_x2t, B_ss, B_bc3], [B_x1r[slot]])
            S.dma("sp", f"ost{slot}", out_d[128 * t:128 * t + 128, :], x1r[slot], reads=[B_x1r[slot]], writes=[])
    S.barrier()
    S.emit()
    return nc


_CACHE = {}


def _in_maps(inputs):
    g = lambda k: np.asarray(inputs[k], dtype=np.float32)
    x, c, ctx, c_ctx = g("x"), g("c"), g("ctx"), g("c_ctx")
    shared = {
        "w_mod": g("w_mod")[0], "b_mod": g("b_mod"), "norm1_w": g("norm1_w"), "w_in": g("w_in")[0],
        "a_f": g("ret_decay_f"), "a_b": g("ret_decay_b"), "w_ret_out": g("w_ret_out")[0],
        "w_four_out": g("w_four_out")[0], "w_bg": g("w_branch_gate")[0], "b_bg": g("b_branch_gate"),
        "w_out": g("w_out")[0], "norm2_w": g("norm2_w"), "w_up": g("w_up")[0], "conv_w": g("conv_w")[0],
        "conv_b": g("conv_b"), "w_down": g("w_down")[0], "final_norm_w": g("final_norm_w").reshape(1, D),
        "cc_col": np.ascontiguousarray(c_ctx.reshape(8, 128).T),
    }
    shared = {k: np.ascontiguousarray(v) for k, v in shared.items()}
    consts = [host_consts(j) for j in range(4)]
    maps = []
    for core in range(NCORES):
        b, j = core // 4, core % 4
        m = dict(shared)
        m["x_own"] = np.ascontiguousarray(x[b, TOK * j:TOK * (j + 1)])
        m["ctx"] = np.ascontiguousarray(ctx[b])
        m["c_col"] = np.ascontiguousarray(c[b].reshape(8, 128).T)
        m.update(consts[j])
        maps.append(m)
    return maps


def kernel(**inputs):
    if "nc" not in _CACHE:
        _CACHE["nc"] = build_program(4)
    nc = _CACHE["nc"]
    res = run_bass_kernel_spmd(nc, _in_maps(inputs), core_ids=list(range(NCORES)))
    out = np.empty((NB, SEQ, D), np.float32)
    for core in range(NCORES):
        b, j = core // 4, core % 4
        out[b, TOK * j:TOK * (j + 1)] = np.asarray(res.results[core]["out"], dtype=np.float32)
    return out
```

```python
import numpy as np
import ml_dtypes
from contextlib import ExitStack

import concourse.bass as bass
import concourse.mybir as mybir
from concourse.bass_utils import run_bass_kernel_spmd

F32 = mybir.dt.float32
BF16 = mybir.dt.bfloat16
ALU = mybir.AluOpType
AF = mybir.ActivationFunctionType
AX = mybir.AxisListType

D = 1024
SEQ = 8192
NB = 2
NCORES = 8
TOK = 2048
NT = 16
CTX = 256
H = 8
INC = 3584
FFN = 2816
NCH = 44
EPS = 1e-6
GROUPS = [[0, 1, 2, 3], [4, 5, 6, 7]]
GELU_C = 1.5957691216057308


class Tok:
    __slots__ = ("key", "val")

    def __init__(self, key, val):
        self.key = key
        self.val = val


class Buf:
    __slots__ = ("name", "w", "r", "excl")

    def __init__(self, name, excl=False):
        self.name = name
        self.w = None
        self.r = []
        self.excl = excl


class Sched:
    ENGS = ("pe", "act", "dve", "pool", "sp")

    def __init__(self, nc, stack):
        self.nc = nc
        self.stack = stack
        self.ops = {e: [] for e in self.ENGS}
        self.sems = {}
        self.cnt = {}
        self.seen = {e: {} for e in self.ENGS}
        for e in ("pe", "act", "dve", "pool"):
            self._mk(e)

    def _mk(self, key):
        if key not in self.sems:
            self.sems[key] = self.stack.enter_context(self.nc.semaphore("s_" + key))
            self.cnt[key] = 0

    def _deps(self, eng, reads, writes):
        deps = []
        for b in reads:
            if b.w is not None:
                deps.append(b.w)
        for b in writes:
            if b.w is not None:
                deps.append(b.w)
            deps.extend(b.r)
        waits = {}
        for t in deps:
            if t.key == "pe" and eng == "pe":
                continue
            if self.seen[eng].get(t.key, 0) >= t.val:
                continue
            waits[t.key] = max(waits.get(t.key, 0), t.val)
        for k, v in waits.items():
            self.seen[eng][k] = v
        return list(waits.items())

    def _commit(self, tok, reads, writes):
        for b in writes:
            b.w = tok
            b.r = []
        for b in reads:
            if b not in writes:
                b.r.append(tok)
                if len(b.r) > 64:
                    b.r = b.r[-48:]

    def op(self, eng, fn, reads=(), writes=()):
        ex = [b for b in reads if b.excl]
        if ex:
            reads = [b for b in reads if not b.excl]
            writes = list(writes) + ex
        waits = self._deps(eng, reads, writes)
        self.cnt[eng] += 1
        tok = Tok(eng, self.cnt[eng])
        self.ops[eng].append((waits, fn, eng, 1))
        self._commit(tok, reads, writes)
        return tok

    def dma(self, queue, key, out, in_, reads=(), writes=(), **kw):
        self._mk(key)
        waits = self._deps(queue, reads, writes)
        self.cnt[key] += 16
        tok = Tok(key, self.cnt[key])
        self.ops[queue].append((waits, lambda e: e.dma_start(out=out, in_=in_, **kw), key, 16))
        self._commit(tok, reads, writes)
        return tok

    def custom(self, queue, key, fn, reads=(), writes=()):
        import os
        if os.environ.get("KDBG_NOCC"):
            return None
        self._mk(key)
        waits = self._deps(queue, reads, writes)
        self.cnt[key] += 1
        tok = Tok(key, self.cnt[key])
        self.ops[queue].append((waits, fn, key, None))
        self._commit(tok, reads, writes)
        return tok

    def barrier(self, exclude=()):
        for e in self.ENGS:
            waits = []
            for k, v in self.cnt.items():
                if k == e or v == 0 or k in exclude:
                    continue
                if self.seen[e].get(k, 0) >= v:
                    continue
                self.seen[e][k] = v
                waits.append((k, v))
            if waits:
                self.ops[e].append((waits, None, None, 0))

    def emit(self):
        nc = self.nc
        handles = {"pe": "tensor", "act": "scalar", "dve": "vector", "pool": "gpsimd", "sp": "sync"}
        with nc.Block() as block:
            for e in self.ENGS:
                ops = self.ops[e]

                def body(engine, ops=ops):
                    for waits, fn, key, inc in ops:
                        for k, v in waits:
                            engine.wait_ge(self.sems[k], v)
                        if fn is None:
                            continue
                        ins = fn(engine)
                        if inc is None:
                            ins.then_inc(self.sems[key])
                        else:
                            ins.then_inc(self.sems[key], inc)

                getattr(block, handles[e])(body)


class Arena:
    def __init__(self, t, nelem):
        self.t = t
        self.n = nelem
        self.off = 0

    def reset(self, off=0):
        self.off = off

    def bf(self, nelem, parts=128):
        a = self.t[0:parts, self.off:self.off + nelem]
        self.off += nelem
        assert self.off <= self.n, (self.off, self.n)
        return a

    def f32(self, nelem, parts=128):
        return self.bf(2 * nelem, parts).bitcast(F32)


def _bf(a):
    return np.ascontiguousarray(a).astype(ml_dtypes.bfloat16)


def host_consts(j):
    c = {}
    c["ident"] = _bf(np.eye(128, dtype=np.float32))
    p = np.arange(128)
    t = (TOK * j + 128 * np.arange(NT)[None, :] + p[:, None]).astype(np.float32)
    row = np.floor(t / 64.0).astype(np.float32)
    col = (t - 64.0 * row).astype(np.float32)
    inv = (np.float32(10000.0) ** (-(np.arange(16, dtype=np.float32)) / np.float32(16))).astype(np.float32)
    ang = np.concatenate([row[:, :, None] * inv[None, None, :], col[:, :, None] * inv[None, None, :]], axis=-1)
    ang = ang.astype(np.float32)
    c["rope"] = np.concatenate([np.cos(ang), np.sin(ang)], axis=-1).astype(np.float32)
    s_ = p[:, None]
    c_ = p[None, :]
    mf = (c_ >= s_).astype(np.float32)
    mb = (c_ <= s_).astype(np.float32)
    c["mask"] = np.stack([mf, mf, mb, mb], axis=1).astype(np.float32)
    pc = np.zeros((128, 8), np.float32)
    pc[:, 0] = -(p + 1)
    pc[:, 1] = (p + 1)
    pc[:, 2] = -(128 - p)
    pc[:, 3] = (128 - p)
    pc[:, 4] = -(255 - p)
    pc[:, 5] = -(255 - 128 - p)
    pc[:, 6] = -p
    pc[:, 7] = -(128 + p)
    c["pcol"] = pc
    sel = np.zeros((128, 18), np.float32)
    sel[:, 16] = 1.0 if j == 0 else 0.0
    sel[:, 17] = 1.0 if j == 3 else 0.0
    sel[:, j] = 1.0
    sel[:, 4 + j] = 1.0
    if j > 0:
        sel[:, 8 + (j - 1)] = 1.0
    if j < 3:
        sel[:, 12 + (j + 1)] = 1.0
    c["sel"] = sel
    q = np.arange(64)
    hh, rr, mm = q // 32, (q % 32) // 8, q % 8
    n1 = 16 * rr + 8 * hh + mm
    k1 = np.arange(64)
    th = 2.0 * np.pi * ((n1[:, None] * k1[None, :]) % 64) / 64.0
    c["e64"] = _bf(np.concatenate([np.cos(th), -np.sin(th)], axis=1))
    n2 = np.arange(128)
    k2 = 32 * j + np.arange(32)
    kk = k1[None, :, None] + 64 * k2[None, None, :]
    ph = 2.0 * np.pi * ((n2[:, None, None] * kk) % 8192) / 8192.0
    twA = np.concatenate([np.cos(ph), -np.sin(ph)], axis=2)
    twB = np.concatenate([np.sin(ph), np.cos(ph)], axis=2)
    c["tw"] = _bf(np.stack([twA, twB], axis=2))
    ch = np.arange(128)
    pc2 = 2.0 * np.pi * ((ch[:, None] * ch[None, :]) % 128) / 128.0
    c["c128"] = _bf(np.stack([np.cos(pc2), np.sin(pc2)], axis=1) / 1024.0)
    c["ones"] = np.ones((128, 128), np.float32)
    c["identf"] = np.eye(128, dtype=np.float32)
    c["onesb"] = _bf(np.ones((128, 128), np.float32))
    return c


CONST_SPECS = [
    ("ident", [128, 128], BF16), ("rope", [128, 16, 64], F32), ("mask", [128, 4, 128], F32),
    ("pcol", [128, 8], F32), ("sel", [128, 18], F32), ("e64", [64, 128], BF16),
    ("tw", [128, 64, 2, 64], BF16), ("c128", [128, 2, 128], BF16), ("ones", [128, 128], F32),
    ("onesb", [128, 128], BF16), ("identf", [128, 128], F32),
]

INPUT_SPECS = [
    ("x_own", [TOK, D], F32), ("ctx", [CTX, D], F32), ("c_col", [128, 8], F32), ("cc_col", [128, 8], F32),
    ("w_mod", [D, 6 * D], F32), ("b_mod", [1, 6 * D], F32), ("norm1_w", [1, D], F32),
    ("w_in", [D, INC], F32), ("a_f", [1, H], F32), ("a_b", [1, H], F32),
    ("w_ret_out", [D, D], F32), ("w_four_out", [512, D], F32), ("w_bg", [D, 2 * D], F32),
    ("b_bg", [1, 2 * D], F32), ("w_out", [D, D], F32), ("norm2_w", [1, D], F32),
    ("w_up", [D, 2 * FFN], F32), ("conv_w", [3, 2 * FFN], F32), ("conv_b", [1, 2 * FFN], F32),
    ("w_down", [FFN, D], F32), ("final_norm_w", [1, D], F32),
]


def build_program(stage=4):
    import os
    STOP = os.environ.get("KDBG_STOP", "")
    nc = bass.Bass("TRN2", target_bir_lowering=False)
    stack = ExitStack()
    S = Sched(nc, stack)
    dr = {}
    for name, shape, dt in INPUT_SPECS + CONST_SPECS:
        dr[name] = nc.dram_tensor(name, shape, dt, kind="ExternalInput").ap()
    out_d = nc.dram_tensor("out", [TOK, D], F32, kind="ExternalOutput").ap()
    dbg_d = None
    if stage < 4:
        dbg_d = nc.dram_tensor("dbg", [128, 8 * TOK], BF16, kind="ExternalOutput").ap()
    rec_d = nc.dram_tensor("rec_scr", [NT, 128, 5120], BF16).ap()
    kv_d = nc.dram_tensor("kv_scr", [NT, 128, 1024], F32).ap()
    x1_d = nc.dram_tensor("x1_scr", [TOK, D], F32).ap()
    mod_d = nc.dram_tensor("mod_scr", [1, 6 * D + 2 * D], F32).ap()
    fin_d = [nc.dram_tensor(f"f_in{h}", [1024, 512], BF16) for h in range(2)]
    fout_d = [nc.dram_tensor(f"f_out{h}", [4096, 512], BF16) for h in range(2)]
    stin_d = nc.dram_tensor("st_in", [128, 1024], F32)
    stout_d = nc.dram_tensor("st_out", [512, 1024], F32)
    hin_d = nc.dram_tensor("halo_in", [128, 128], F32)
    hout_d = nc.dram_tensor("halo_out", [512, 128], F32)

    def sb(name, shape, dt):
        return stack.enter_context(nc.sbuf_tensor("sb_" + name, shape, dt))

    PS = [stack.enter_context(nc.psum_tensor(f"ps{i}", [128, 512], F32)) for i in range(8)]
    PB = [Buf(f"ps{i}", excl=True) for i in range(8)]

    def psbf(i):
        return PS[i][:, :].bitcast(BF16)

    ident = sb("ident", [128, 128], BF16)
    rope = sb("rope", [128, 16, 64], F32)
    mask = sb("mask", [128, 4, 128], F32)
    pcol = sb("pcol", [128, 8], F32)
    sel = sb("sel", [128, 18], F32)
    ones = sb("ones", [128, 128], F32)
    onesb = sb("onesb", [128, 128], BF16)
    identf = sb("identf", [128, 128], F32)
    sctx = sb("sctx", [128, 1024], F32)
    c128 = sb("c128", [128, 2, 128], BF16)
    e64 = sb("e64", [64, 128], BF16)
    smallf = sb("smallf", [128, 512], F32)
    smallb = sb("smallb", [128, 64], BF16)
    convc = sb("convc", [128, 4, NCH], F32)
    XH = sb("XH", [128, 8, TOK], BF16)
    R1n, R2n = 32768, 47104
    R1t = sb("R1", [128, R1n], BF16)
    R2t = sb("R2", [128, R2n], BF16)
    R1 = Arena(R1t, R1n)
    R2 = Arena(R2t, R2n)
    B_const = Buf("const")
    B_small = Buf("small")
    B_ss = Buf("ss")
    B_XH = [Buf(f"xh{t}") for t in range(NT)]

    def sf(a, b):
        return smallf[:, a:b]

    a_bc = sf(0, 16)
    ea = sf(16, 32)
    qf_sc, kf_sc, qb_sc, kb_sc = sf(32, 40), sf(40, 48), sf(48, 56), sf(56, 64)
    cxf_sc, cxb_sc = sf(64, 80), sf(80, 96)
    a_st, ea_st = sf(96, 104), sf(104, 112)
    cdp_f, cdp_b = sf(112, 180), sf(180, 248)
    silc = sf(248, 264)
    shcol = sf(264, 288)
    ss_t = sf(288, 296)
    bbg = sf(296, 312)
    bup = sf(312, 356)
    kcol = sf(356, 400)
    gst = sf(400, 464)
    silcb = smallb[:, 0:16]
    shcolb = smallb[:, 16:40]

    def TT(eng, out, in0, in1, op, reads, writes):
        return S.op(eng, lambda e: e.tensor_tensor(out=out, in0=in0, in1=in1, op=op), reads, writes)

    def TS(eng, out, in0, s1, s2, op0, op1, reads, writes):
        if s2 is None:
            return S.op(eng, lambda e: e.tensor_scalar(out=out, in0=in0, scalar1=s1, scalar2=None, op0=op0), reads, writes)
        return S.op(eng, lambda e: e.tensor_scalar(out=out, in0=in0, scalar1=s1, scalar2=s2, op0=op0, op1=op1), reads, writes)

    def STT(eng, out, in0, scalar, in1, op0, op1, reads, writes):
        return S.op(eng, lambda e: e.scalar_tensor_tensor(out=out, in0=in0, scalar=scalar, in1=in1, op0=op0, op1=op1), reads, writes)

    def CP(eng, out, in_, reads, writes):
        if eng == "act":
            return S.op("act", lambda e: e.activation(out=out, in_=in_, func=AF.Copy), reads, writes)
        return S.op(eng, lambda e: e.tensor_copy(out=out, in_=in_), reads, writes)

    def ACTF(out, in_, func, reads, writes, **kw):
        return S.op("act", lambda e: e.activation(out=out, in_=in_, func=func, **kw), reads, writes)

    def MM(out, lhsT, rhs, start, stop, reads, writes):
        return S.op("pe", lambda e: e.matmul(out, lhsT=lhsT, rhs=rhs, start=start, stop=stop), reads, writes)

    def TR(out, in_, reads, writes):
        return S.op("pe", lambda e: e.transpose(out=out, in_=in_, identity=ident[:]), reads + [B_const], writes)

    def RSQ(out, in_, c, reads, writes):
        ACTF(out, in_, AF.Sqrt, reads, writes, bias=c, scale=1.0)
        return S.op("dve", lambda e: e.reciprocal(out=out, in_=out), writes, writes)

    def bc3(ap2, n):
        return ap2.unsqueeze(2).to_broadcast([128, ap2.shape[1], n])

    R1.reset()
    Win = R1.bf(8 * INC).rearrange("p (k c) -> p k c", k=8)
    A1bc = R1.f32(1024)
    B_win = Buf("win")
    win_v = dr["w_in"].rearrange("(k p) c -> p k c", p=128)
    for k in range(8):
        S.dma("pool", "win", Win[:, k, :], win_v[:, k, :], writes=[B_win])
    for dst, name in ((ident, "ident"), (rope, "rope"), (mask, "mask"), (pcol, "pcol"), (sel, "sel"), (ones, "ones"),
                      (onesb, "onesb"), (c128, "c128"), (e64, "e64"), (identf, "identf")):
        S.dma("sp", "const", dst[:], dr[name], writes=[B_const])
    S.dma("sp", "const", a_bc[:, 0:8], dr["a_f"].partition_broadcast(128), writes=[B_const])
    S.dma("sp", "const", a_bc[:, 8:16], dr["a_b"].partition_broadcast(128), writes=[B_const])
    S.dma("sp", "const", silc[:, 0:8], dr["c_col"], writes=[B_const])
    S.dma("sp", "const", silc[:, 8:16], dr["cc_col"], writes=[B_const])
    R2.reset()
    stg = sb("stg", [64, 128], F32)
    B_stg = Buf("stg")

    def col_layout(dst, src_rows, n):
        S.dma("sp", "stg", stg[0:n, :], src_rows, writes=[B_stg])
        S.op("pe", lambda e: e.transpose(out=PS[6][:, 0:n], in_=stg[0:n, :], identity=identf[0:n, 0:n]), [B_stg, B_const], [PB[6]])
        CP("dve", dst, PS[6][:, 0:n], [PB[6]], [B_const])

    S.barrier()
    col_layout(bbg, dr["b_bg"].rearrange("o (c p) -> (o c) p", p=128), 16)
    for k in range(3):
        col_layout(convc[:, k, :], dr["conv_w"][k:k + 1, :].rearrange("o (c p) -> (o c) p", p=128), NCH)
    col_layout(convc[:, 3, :], dr["conv_b"].rearrange("o (c p) -> (o c) p", p=128), NCH)
    for di in range(2):
        for hh in range(2):
            CP("dve", a_st[64 * hh:64 * hh + 64, 4 * di:4 * di + 4],
               a_bc[64 * hh:64 * hh + 64, 8 * di:8 * di + 8].rearrange("p (a h) -> p a h", h=2)[:, :, hh], [B_const], [B_const])

    ACTF(ea, a_bc, AF.Exp, [B_const], [B_small])
    ACTF(ea_st, a_st, AF.Exp, [B_const], [B_small])
    for dst, src, col in ((qf_sc, ea[:, 0:8], 0), (kf_sc, ea[:, 0:8], 1), (qb_sc, ea[:, 8:16], 2), (kb_sc, ea[:, 8:16], 3)):
        ACTF(dst, src, AF.Exp, [B_small], [B_small], scale=pcol[:, col:col + 1])
    for tl in range(2):
        ACTF(cxf_sc[:, 8 * tl:8 * tl + 8], ea[:, 0:8], AF.Exp, [B_small], [B_small], scale=pcol[:, 4 + tl:5 + tl])
        ACTF(cxb_sc[:, 8 * tl:8 * tl + 8], ea[:, 8:16], AF.Exp, [B_small], [B_small], scale=pcol[:, 6 + tl:7 + tl])
    for n in range(17):
        ACTF(cdp_f[:, 4 * n:4 * n + 4], ea_st[:, 0:4], AF.Exp, [B_small], [B_small], scale=-128.0 * n)
        ACTF(cdp_b[:, 4 * n:4 * n + 4], ea_st[:, 4:8], AF.Exp, [B_small], [B_small], scale=-128.0 * n)
    TS("dve", qf_sc, qf_sc, 0.125, None, ALU.mult, None, [B_small], [B_small])
    TS("dve", qb_sc, qb_sc, 0.125, None, ALU.mult, None, [B_small], [B_small])
    ACTF(silc, silc, AF.Silu, [B_small], [B_small])
    CP("dve", silcb, silc, [B_small], [B_small])

    wst = [R2.f32(4096).rearrange("p (k c) -> p k c", k=8) for _ in range(2)]
    wmb = [R2.bf(4096).rearrange("p (k c) -> p k c", k=8) for _ in range(2)]
    bmod = [R2.f32(512, parts=1) for _ in range(2)]
    rowsb = [R2.f32(512, parts=1) for _ in range(4)]
    B_wst, B_wmb = [Buf("wst0"), Buf("wst1")], [Buf("wmb0"), Buf("wmb1")]
    B_bmod = [Buf("bm0"), Buf("bm1")]
    B_row = [Buf(f"row{i}") for i in range(4)]
    B_modd = Buf("modd")
    wmod_v = dr["w_mod"].rearrange("(k p) c -> p k c", p=128)
    cvt_eng = ["dve", "act"]
    nrow = [0]

    def mod_block(cb, wst, wmb, bmod, rowsb, B_wst, B_wmb, B_bmod, B_row):
        i = cb % 2
        S.dma("sp", f"wst{i}", wst[i], wmod_v[:, :, 512 * cb:512 * cb + 512], writes=[B_wst[i]])
        S.dma("sp", f"bm{i}", bmod[i], dr["b_mod"][:, 512 * cb:512 * cb + 512], writes=[B_bmod[i]])
        for half in range(2):
            CP(cvt_eng[(2 * cb + half) % len(cvt_eng)], wmb[i][:, 4 * half:4 * half + 4, :], wst[i][:, 4 * half:4 * half + 4, :], [B_wst[i]], [B_wmb[i]])
        for side in range(2 if cb < 4 else 1):
            for k in range(8):
                MM(PS[7][0:1, :], silcb[:, 8 * side + k:8 * side + k + 1], wmb[i][:, k, :], k == 0, k == 7, [B_small, B_wmb[i]], [PB[7]])
            r = nrow[0] % len(rowsb)
            nrow[0] += 1
            TT("dve", rowsb[r], PS[7][0:1, :], bmod[i], ALU.add, [PB[7], B_bmod[i]], [B_row[r]])
            off = 512 * cb if side == 0 else 6 * D + 512 * cb
            S.dma("sp", f"rowst{r}", mod_d[:, off:off + 512], rowsb[r], reads=[B_row[r]], writes=[B_modd])

    for cb in range(4):
        mod_block(cb, wst, wmb, bmod, rowsb, B_wst, B_wmb, B_bmod, B_row)
    S.barrier()
    for i, off in ((0, 0), (2, 6 * D)):
        col_layout(shcol[:, 8 * i:8 * i + 8], mod_d[:, off:off + D].rearrange("o (c p) -> (o c) p", p=128), 8)
    CP("dve", shcolb[:, 0:8], shcol[:, 0:8], [B_const], [B_small])
    CP("dve", shcolb[:, 16:24], shcol[:, 16:24], [B_const], [B_small])

    if STOP == "mod":
        S.barrier(); S.emit(); return nc
    B_bct = Buf("bct")

    R2.reset()
    xt = [R2.f32(1024) for _ in range(3)]
    hb = R2.bf(1024)
    junk = R2.bf(1024)
    qk2 = [R2.f32(1024) for _ in range(2)]
    qk_sb = qk2[0]
    off_rt1 = R2.off
    rt1, rt2 = R2.f32(1024), R2.f32(1024)
    rot = rt1
    ktok = R2.bf(1024)
    kpad = R2.bf(2048)
    qpad = R2.bf(2048)
    fsb = [R2.bf(512) for _ in range(2)]
    kvsb = [R2.f32(1024) for _ in range(2)]
    Est = R2.f32(1024)
    off_rec = R2.off
    recb = [R2.bf(5120) for _ in range(2)]
    hcT = R2.bf(2048).rearrange("p (k c) -> p k c", k=8)
    ctxv = R2.bf(1024)
    brows = R2.bf(INC, parts=2)
    lo_tmp = R2t[0:1, off_rt1:off_rt1 + INC]
    nwt = qk2[1]
    browsc = R2t[0:2, off_rec:off_rec + INC]
    A1c = R2t[:, off_rec + 5120:off_rec + 5120 + 2048].bitcast(F32)
    B_xt = [Buf("xt0"), Buf("xt1"), Buf("xt2")]
    B_hb, B_junk, B_qk = Buf("hb"), Buf("junk"), Buf("qk")
    B_qk2 = [Buf("qk2_0"), Buf("qk2_1")]
    B_rt1, B_rt2 = Buf("rt1"), Buf("rt2")
    B_rot = B_rt1
    B_ktok, B_kpad, B_qpad = Buf("ktok"), Buf("kpad"), Buf("qpad")
    B_fsb = [Buf("fsb0"), Buf("fsb1")]
    B_kvsb = [Buf("kvsb0"), Buf("kvsb1")]
    B_E = Buf("E")
    B_rec = [Buf("rec0"), Buf("rec1")]
    B_hcT, B_ctxv, B_sctx, B_bias = Buf("hcT"), Buf("ctxv"), Buf("sctx"), Buf("bias")

    S.dma("sp", "bcl", nwt, dr["norm1_w"].partition_broadcast(128), writes=[B_bct])
    S.dma("sp", "bcl", A1bc, mod_d[:, D:2 * D].partition_broadcast(128), reads=[B_modd], writes=[B_bct])
    STT("dve", A1bc, A1bc, 1.0, nwt, ALU.add, ALU.mult, [B_bct], [B_bct])
    TS("dve", A1bc, A1bc, 32.0, None, ALU.mult, None, [B_bct], [B_bct])

    def bias_rows(side, dstHL):
        lcol = 0 if side == 0 else 16
        for blk in range(7):
            if side == 1 and blk not in (1, 2, 3):
                continue
            for k in range(8):
                MM(PS[7][0:1, :], shcolb[:, lcol + k:lcol + k + 1], Win[:, k, 512 * blk:512 * blk + 512], k == 0, k == 7, [B_small, B_win], [PB[7]])
            cs = slice(512 * blk, 512 * blk + 512)
            CP("dve", dstHL[0:1, cs], PS[7][0:1, :], [PB[7]], [B_bias])
            TT("dve", lo_tmp[:, cs], PS[7][0:1, :], dstHL[0:1, cs], ALU.subtract, [PB[7], B_bias], [B_bias])
        S.dma("sp", "biaslo", dstHL[1:2, :], lo_tmp, reads=[B_bias], writes=[B_bias])

    bias_rows(0, brows)

    S.op("pool", lambda e: e.memset(kpad, 0.0), writes=[B_kpad])
    S.op("pool", lambda e: e.memset(qpad, 0.0), writes=[B_qpad])
    S.op("pool", lambda e: e.memset(Est, 0.0), writes=[B_E])

    def load_x(src_ap, slot):
        S.dma("sp", f"xt{slot}", xt[slot], src_ap, writes=[B_xt[slot]])

    def norm_part(slot, scale_bc):
        ACTF(junk, xt[slot], AF.Square, [B_xt[slot]], [B_junk, B_ss], accum_out=ss_t[:, 0:1])
        RSQ(ss_t[:, 1:2], ss_t[:, 0:1], 1024.0 * EPS, [B_ss], [B_ss])
        STT("dve", hb, xt[slot], ss_t[:, 1:2], scale_bc, ALU.mult, ALU.mult, [B_xt[slot], B_ss, B_bct], [B_hb])

    def tr_part(dstT, bdst, col0):
        pst = psbf(0)
        for k in range(8):
            TR(pst[:, 128 * k:128 * k + 128], hb[:, 128 * k:128 * k + 128], [B_hb], [PB[0]])
        CP("act", dstT[:, :, col0:col0 + 128], pst.rearrange("p (k c) -> p k c", k=8), [PB[0]], [bdst])

    def norm_transpose(slot, scale_bc, dstT, bdst, col0):
        norm_part(slot, scale_bc)
        tr_part(dstT, bdst, col0)

    def project(srcT, bsrc, col0, blocks, rows, consume):
        for i, blk in enumerate(blocks):
            bank = 1 + (i % 2)
            for k in range(8):
                MM(PS[bank][:, :], srcT[:, k, col0:col0 + 128], Win[:, k, 512 * blk:512 * blk + 512], k == 0, False, [bsrc, B_win], [PB[bank]])
            MM(PS[bank][:, :], onesb[0:2, :], rows[0:2, 512 * blk:512 * blk + 512], False, True, [B_bias, B_const], [PB[bank]])
            consume(blk, PS[bank], PB[bank])

    def k4(ap):
        return ap.rearrange("p (a h c) -> p a h c", a=4, h=2)

    def scaled_k(dirn, src_k, bsrc, sc_tile):
        kt = ktok[:, 512 * dirn:512 * dirn + 512]
        TT("dve", kt.rearrange("p (h d) -> p h d", h=8), src_k.rearrange("p (h d) -> p h d", h=8), bc3(sc_tile, 64), ALU.mult,
           [bsrc, B_small], [B_ktok])
        kp = k4(kpad[:, 1024 * dirn:1024 * dirn + 1024])
        kin = kt.rearrange("p (a h d) -> p a h d", a=4, h=2)
        for hh in range(2):
            CP("act", kp[:, :, hh, 64 * hh:64 * hh + 64], kin[:, :, hh, :], [B_ktok], [B_kpad])

    def kv_matmuls(dirn, vsrc, bv, bank):
        kp = k4(kpad[:, 1024 * dirn:1024 * dirn + 1024])
        for p4 in range(4):
            for hh in range(2):
                h = 2 * p4 + hh
                MM(PS[bank][:, 128 * p4:128 * p4 + 128], kp[:, p4, hh, :], vsrc[:, 128 * h:128 * h + 128], hh == 0, hh == 1, [B_kpad, bv], [PB[bank]])

    S.barrier()
    B_fin = [[Buf(f"fin{h}_{i}") for i in range(8)] for h in range(2)]
    B_fout = [Buf("fout0"), Buf("fout1")]
    B_recd = [Buf(f"recd{t}") for t in range(NT)]
    B_kvd = [Buf(f"kvd{t}") for t in range(NT)]

    def E3(ap):
        return ap.rearrange("p (a e) -> p a e", a=4)

    cdf_bc = bc3(cdp_f[:, 4:8], 128)
    def passA_front1(t):
        tr_part(XH, B_XH[t], 128 * t)

    def passA_front(t):
        slot = t % 2
        rb = recb[slot]

        qk_t, B_qkt = qk2[slot], B_qk2[slot]

        def consume(blk, ps, pb, t=t, rb=rb, slot=slot, qk_t=qk_t, B_qkt=B_qkt):
            if blk < 2:
                CP("act", qk_t[:, 512 * blk:512 * blk + 512], ps[:, :], [pb], [B_qkt])
            elif blk < 4:
                o0 = 3072 + 512 * (blk - 2)
                CP("act", rb[:, o0:o0 + 512], ps[:, :], [pb], [B_rec[slot]])
            elif blk < 6:
                o0 = 4096 + 512 * (blk - 4)
                ACTF(rb[:, o0:o0 + 512], ps[:, :], AF.Silu, [pb], [B_rec[slot]])
            else:
                CP("act", fsb[slot], ps[:, :], [pb], [B_fsb[slot]])
                hh_ = t // 8
                S.dma("sp", f"fst{slot}", fin_d[hh_].ap()[128 * (t % 8):128 * (t % 8) + 128, :], fsb[slot],
                      reads=[B_fsb[slot]], writes=[B_fin[hh_][t % 8]])
        project(XH, B_XH[t], 128 * t, [0, 1, 2, 3, 4, 5, 6], brows, consume)


    def passA_back(t):
        slot = t % 2
        rb = recb[slot]
        qk_t, B_qkt = qk2[slot], B_qk2[slot]
        def g4(ap):
            return ap.rearrange("p (g h c) -> p g h c", g=16, h=2)
        cosb = rope[:, t, 0:32].unsqueeze(1).to_broadcast([128, 32, 32])
        sinb = rope[:, t, 32:64].unsqueeze(1).to_broadcast([128, 16, 32])
        TT("dve", rt1.rearrange("p (g c) -> p g c", g=32), qk_t.rearrange("p (g c) -> p g c", g=32), cosb, ALU.mult, [B_qkt, B_const], [B_rt1])
        TT("dve", g4(rt2)[:, :, 0, :], g4(qk_t)[:, :, 1, :], sinb, ALU.mult, [B_qkt, B_const], [B_rt2])
        TT("dve", g4(rt2)[:, :, 1, :], g4(qk_t)[:, :, 0, :], sinb, ALU.mult, [B_qkt, B_const], [B_rt2])
        TT("dve", g4(rt1)[:, :, 0, :], g4(rt1)[:, :, 0, :], g4(rt2)[:, :, 0, :], ALU.subtract, [B_rt2], [B_rt1])
        TT("dve", g4(rt1)[:, :, 1, :], g4(rt1)[:, :, 1, :], g4(rt2)[:, :, 1, :], ALU.add, [B_rt2], [B_rt1])
        for dirn, sc in ((0, qf_sc), (1, qb_sc)):
            qp = k4(qpad[:, 1024 * dirn:1024 * dirn + 1024])
            qin = rot[:, 0:512].rearrange("p (a h d) -> p a h d", a=4, h=2)
            scv = sc.rearrange("p (a h) -> p a h", h=2)
            for hh in range(2):
                TT("dve", qp[:, :, hh, 64 * hh:64 * hh + 64], qin[:, :, hh, :], scv[:, :, hh].unsqueeze(2).to_broadcast([128, 4, 64]), ALU.mult,
                   [B_rot, B_small], [B_qpad])
        scaled_k(0, rot[:, 512:1024], B_rot, kf_sc)
        scaled_k(1, rot[:, 512:1024], B_rot, kb_sc)
        pst = [psbf(3), psbf(4), psbf(7)]
        for dirn in range(2):
            qp = k4(qpad[:, 1024 * dirn:1024 * dirn + 1024])
            for h in range(8):
                TR(pst[dirn][:, 128 * h:128 * h + 128], qp[:, h // 2, h % 2, :], [B_qpad], [PB[3 + dirn]])
        for dirn in range(2):
            for p4 in range(4):
                c0 = 512 * dirn + 128 * p4
                TR(pst[2][:, 128 * (4 * dirn + p4):128 * (4 * dirn + p4) + 128], ktok[:, c0:c0 + 128], [B_ktok], [PB[7]])
        CP("dve", rb[:, 0:1024], pst[0], [PB[3]], [B_rec[slot]])
        CP("dve", rb[:, 1024:2048], pst[1], [PB[4]], [B_rec[slot]])
        CP("dve", rb[:, 2048:3072], pst[2], [PB[7]], [B_rec[slot]])
        for dirn in range(2):
            kv_matmuls(dirn, rb[:, 3072:4096], B_rec[slot], 5 + dirn)
            CP("act", kvsb[slot][:, 512 * dirn:512 * dirn + 512], PS[5 + dirn][:, :], [PB[5 + dirn]], [B_kvsb[slot]])
        TT("pool", rt1[:, 0:512], Est[:, 0:512], kvsb[slot][:, 0:512], ALU.add, [B_E, B_kvsb[slot]], [B_rt1])
        TT("pool", E3(Est[:, 0:512]), E3(rt1[:, 0:512]), cdf_bc, ALU.mult, [B_rt1, B_small], [B_E])
        cdb_t = bc3(cdp_b[:, 4 * (t + 1):4 * (t + 1) + 4], 128)
        TT("pool", E3(rt1[:, 512:1024]), E3(kvsb[slot][:, 512:1024]), cdb_t, ALU.mult, [B_kvsb[slot], B_small], [B_rt1])
        TT("pool", Est[:, 512:1024], Est[:, 512:1024], rt1[:, 512:1024], ALU.add, [B_rt1, B_E], [B_E])
        S.dma("sp", f"rst{slot}", rec_d[t], rb, reads=[B_rec[slot]], writes=[B_recd[t]])
        S.dma("sp", f"kst{slot}", kv_d[t], kvsb[slot], reads=[B_kvsb[slot]], writes=[B_kvd[t]])
        if t % 8 == 7:
            hh_ = t // 8
            S.custom("pool", f"ccf{hh_}", lambda e, hh_=hh_: e.collective_compute(
                "AllGather", ALU.bypass, replica_groups=GROUPS, ins=[fin_d[hh_].ap().opt()], outs=[fout_d[hh_].ap().opt()]),
                reads=B_fin[hh_], writes=[B_fout[hh_]])


    for t0 in range(3):
        load_x(dr["x_own"][128 * t0:128 * t0 + 128, :], t0)
    norm_part(0, A1bc)
    passA_front1(0)
    norm_part(1, A1bc)
    passA_front(0)
    for t in range(NT):
        if t + 1 < NT:
            passA_front1(t + 1)
        if t + 2 < NT:
            norm_part((t + 2) % 3, A1bc)
        if t + 3 < NT:
            load_x(dr["x_own"][128 * (t + 3):128 * (t + 4), :], t % 3)
        if t + 1 < NT:
            passA_front(t + 1)
        passA_back(t)
    B_stin, B_stout = Buf("stin"), Buf("stout")
    S.dma("sp", "stst", stin_d.ap(), Est, reads=[B_E], writes=[B_stin])
    S.custom("pool", "ccst", lambda e: e.collective_compute(
        "AllGather", ALU.bypass, replica_groups=GROUPS, ins=[stin_d.ap().opt()], outs=[stout_d.ap().opt()]),
        reads=[B_stin], writes=[B_stout])
    S.barrier(exclude=("ccst",))
    S.dma("sp", "bcl", nwt, dr["norm1_w"].partition_broadcast(128), writes=[B_bct])
    S.dma("sp", "bcl", A1c, mod_d[:, 7 * D:8 * D].partition_broadcast(128), reads=[B_modd], writes=[B_bct])
    STT("dve", A1c, A1c, 1.0, nwt, ALU.add, ALU.mult, [B_bct], [B_bct])
    TS("dve", A1c, A1c, 32.0, None, ALU.mult, None, [B_bct], [B_bct])
    bias_rows(1, browsc)
    for tl in range(2):
        load_x(dr["ctx"][128 * tl:128 * tl + 128, :], tl)
    for tl in range(2):
        norm_transpose(tl, A1c, hcT, B_hcT, 128 * tl)

        def consume_ctx(blk, ps, pb):
            if blk == 1:
                CP("act", qk_sb[:, 512:1024], ps[:, :], [pb], [B_qk])
            else:
                CP("act", ctxv[:, 512 * (blk - 2):512 * (blk - 2) + 512], ps[:, :], [pb], [B_ctxv])
        project(hcT, B_hcT, 128 * tl, [1, 2, 3], browsc, consume_ctx)
        scaled_k(0, qk_sb[:, 512:1024], B_qk, cxf_sc[:, 8 * tl:8 * tl + 8])
        scaled_k(1, qk_sb[:, 512:1024], B_qk, cxb_sc[:, 8 * tl:8 * tl + 8])
        for dirn in range(2):
            kv_matmuls(dirn, ctxv, B_ctxv, 5 + dirn)
            dst = sctx[:, 512 * dirn:512 * dirn + 512]
            if tl == 0:
                CP("dve", dst, PS[5 + dirn][:, :], [PB[5 + dirn]], [B_sctx])
            else:
                TT("dve", dst, dst, PS[5 + dirn][:, :], ALU.add, [PB[5 + dirn], B_sctx], [B_sctx])


    S.barrier()
    R1.reset()
    SfT = R1.bf(NT * 512).rearrange("p (n c) -> p n c", n=NT)
    SbT = R1.bf(NT * 512).rearrange("p (n c) -> p n c", n=NT)
    ogT = R1.bf(8 * TOK).rearrange("p (h c) -> p h c", h=8)
    B_ST = Buf("ST")
    B_ogT = [Buf(f"ogT{t}") for t in range(NT)]
    R2.reset()
    kvall = R2.f32(NT * 1024).rearrange("p (n c) -> p n c", n=NT)
    Gst = R2.f32(4096).rearrange("p (r c) -> p r c", r=4)
    curf, curb = R2.f32(512), R2.f32(512)
    tmpf, tmpb = R2.f32(512), R2.f32(512)
    B_kvall, B_G = Buf("kvall"), Buf("G")
    B_curf, B_curb, B_tmpf, B_tmpb = Buf("curf"), Buf("curb"), Buf("tmpf"), Buf("tmpb")
    CP("dve", curf, sctx[:, 0:512], [B_sctx], [B_curf])
    CP("dve", curb, sctx[:, 512:1024], [B_sctx], [B_curb])
    for t in range(NT):
        S.dma("sp", "kvld", kvall[:, t, :], kv_d[t], reads=[B_kvd[t]], writes=[B_kvall])
    S.dma("sp", "gld", Gst, stout_d.ap().rearrange("(r p) c -> p r c", p=128), reads=[B_stout], writes=[B_G])
    cd16f = bc3(cdp_f[:, 64:68], 128)
    cd16b = bc3(cdp_b[:, 64:68], 128)
    TS("dve", tmpf, curf, sel[:, 0:1], None, ALU.mult, None, [B_curf, B_const], [B_tmpf])
    for r in range(3):
        TT("dve", E3(curf), E3(curf), cd16f, ALU.mult, [B_curf, B_small], [B_curf])
        TT("dve", curf, curf, Gst[:, r, 0:512], ALU.add, [B_curf, B_G], [B_curf])
        STT("dve", tmpf, curf, sel[:, r + 1:r + 2], tmpf, ALU.mult, ALU.add, [B_curf, B_tmpf, B_const], [B_tmpf])
    TS("dve", tmpb, curb, sel[:, 7:8], None, ALU.mult, None, [B_curb, B_const], [B_tmpb])
    for r in (3, 2, 1):
        TT("dve", E3(curb), E3(curb), cd16b, ALU.mult, [B_curb, B_small], [B_curb])
        TT("dve", curb, curb, Gst[:, r, 512:1024], ALU.add, [B_curb, B_G], [B_curb])
        STT("dve", tmpb, curb, sel[:, 4 + r - 1:4 + r], tmpb, ALU.mult, ALU.add, [B_curb, B_tmpb, B_const], [B_tmpb])
    cdb_bc = bc3(cdp_b[:, 4:8], 128)
    for n in range(NT):
        m_ = NT - 1 - n
        CP("act", SfT[:, n, :], tmpf, [B_tmpf], [B_ST])
        TT("dve", curf, tmpf, kvall[:, n, 0:512], ALU.add, [B_tmpf, B_kvall], [B_curf])
        TT("dve", E3(tmpf), E3(curf), cdf_bc, ALU.mult, [B_curf, B_small], [B_tmpf])
        CP("act", SbT[:, m_, :], tmpb, [B_tmpb], [B_ST])
        TT("dve", curb, tmpb, kvall[:, m_, 512:1024], ALU.add, [B_tmpb, B_kvall], [B_curb])
        TT("dve", E3(tmpb), E3(curb), cdb_bc, ALU.mult, [B_curb, B_small], [B_tmpb])
    S.barrier()
    if STOP == "scan":
        S.emit(); return nc

    R2.reset()
    rbB = [R2.bf(5120) for _ in range(2)]
    Pm = [R2.bf(512) for _ in range(2)]
    sq = R2.f32(1024)
    tcen = R2.f32(1024)
    praw = [R2.bf(512) for _ in range(2)]
    ogtok = R2.bf(1024)
    B_rbB = [Buf("rbB0"), Buf("rbB1")]
    B_Pm = [Buf("Pm0"), Buf("Pm1")]
    B_sq, B_tcen, B_ogtok, B_gst = Buf("sq"), Buf("tcen"), Buf("ogtok"), Buf("gst")
    B_praw = [Buf("praw0"), Buf("praw1")]
    mask3 = mask[:, :, :]
    S.dma("sp", "rld0", rbB[0], rec_d[0], reads=[B_recd[0]], writes=[B_rbB[0]])
    obanks = [(5, 6), (3, 4)]

    def passB_mm(n):
        slot = n % 2
        rb = rbB[slot]

        def qT(di, h):
            return rb[:, (8 * di + h) * 128:(8 * di + h) * 128 + 128]

        def kT(di, p4):
            c0 = 2048 + (4 * di + p4) * 128
            return rb[:, c0:c0 + 128]
        def scores(p4):
            sbank = 1 + (p4 % 2)
            psc = PS[sbank][:, :].rearrange("p (a c) -> p a c", a=4)
            for di in range(2):
                for hh in range(2):
                    MM(psc[:, 2 * di + hh, :], kT(di, p4), qT(di, 2 * p4 + hh), True, True, [B_rbB[slot]], [PB[sbank]])
            pm = Pm[p4 % 2]
            CP("act", praw[p4 % 2], PS[sbank][:, :], [PB[sbank]], [B_praw[p4 % 2]])
            TT("pool", pm.rearrange("p (a c) -> p a c", a=4), praw[p4 % 2].rearrange("p (a c) -> p a c", a=4), mask3, ALU.mult,
               [B_praw[p4 % 2], B_const], [B_Pm[p4 % 2]])

        def omm(p4):
            pm3 = Pm[p4 % 2].rearrange("p (a c) -> p a c", a=4)
            obank = obanks[n % 2][p4 // 2]
            for hh in range(2):
                h = 2 * p4 + hh
                od = PS[obank][:, 128 * (h % 4):128 * (h % 4) + 128]
                vv = rb[:, 3072 + 128 * h:3072 + 128 * h + 128]
                MM(od, pm3[:, hh, :], vv, True, False, [B_Pm[p4 % 2], B_rbB[slot]], [PB[obank]])
                MM(od, qT(0, h), SfT[:, n, 128 * p4:128 * p4 + 128], False, False, [B_rbB[slot], B_ST], [PB[obank]])
                MM(od, pm3[:, 2 + hh, :], vv, False, False, [B_Pm[p4 % 2], B_rbB[slot]], [PB[obank]])
                MM(od, qT(1, h), SbT[:, n, 128 * p4:128 * p4 + 128], False, True, [B_rbB[slot], B_ST], [PB[obank]])
        scores(0)
        scores(1)
        omm(0)
        scores(2)
        omm(1)
        scores(3)
        omm(2)
        omm(3)

    def passB_gn(n):
        slot = n % 2
        rb = rbB[slot]
        ob = obanks[n % 2]
        o3s = [PS[ob[half]][:, :].rearrange("p (h e) -> p h e", h=4) for half in range(2)]
        for half in range(2):
            hs = slice(4 * half, 4 * half + 4)
            S.op("dve", lambda e, o3=o3s[half], hs=hs: e.tensor_reduce(out=gst[:, hs], in_=o3, axis=AX.X, op=ALU.add), [PB[ob[half]]], [B_gst])
            ACTF(sq[:, 512 * half:512 * half + 512], PS[ob[half]][:, :], AF.Square, [PB[ob[half]]], [B_sq])
        TS("dve", gst[:, 16:24], gst[:, 0:8], 1.0 / 128.0, None, ALU.mult, None, [B_gst], [B_gst])
        for half in range(2):
            TT("dve", tcen[:, 512 * half:512 * half + 512].rearrange("p (h e) -> p h e", h=4), o3s[half], bc3(gst[:, 16 + 4 * half:20 + 4 * half], 128),
               ALU.subtract, [PB[ob[half]], B_gst], [B_tcen])
        TT("dve", tcen, tcen, rb[:, 4096:5120], ALU.mult, [B_tcen, B_rbB[slot]], [B_tcen])
        S.op("dve", lambda e: e.tensor_reduce(out=gst[:, 8:16], in_=sq.rearrange("p (h e) -> p h e", h=8), axis=AX.X, op=ALU.add), [B_sq], [B_gst])
        TT("dve", gst[:, 24:32], gst[:, 16:24], gst[:, 16:24], ALU.mult, [B_gst], [B_gst])
        STT("dve", gst[:, 32:40], gst[:, 8:16], 1.0 / 128.0, gst[:, 24:32], ALU.mult, ALU.subtract, [B_gst], [B_gst])
        RSQ(gst[:, 40:48], gst[:, 32:40], EPS, [B_gst], [B_gst])
        TT("dve", ogtok.rearrange("p (h e) -> p h e", h=8), tcen.rearrange("p (h e) -> p h e", h=8), bc3(gst[:, 40:48], 128), ALU.mult,
           [B_tcen, B_gst], [B_ogtok])
        pst = psbf(0)
        for h in range(8):
            TR(pst[:, 128 * h:128 * h + 128], ogtok[:, 128 * h:128 * h + 128], [B_ogtok], [PB[0]])
        CP("act", ogT[:, :, 128 * n:128 * n + 128], pst.rearrange("p (h c) -> p h c", h=8), [PB[0]], [B_ogT[n]])
        if n + 2 < NT:
            S.dma("sp", f"rld{n % 2}", rbB[n % 2], rec_d[n + 2], reads=[B_recd[n + 2]], writes=[B_rbB[n % 2]])

    S.dma("sp", "rld1", rbB[1], rec_d[1], reads=[B_recd[1]], writes=[B_rbB[1]])
    passB_mm(0)
    for n in range(NT):
        if n + 1 < NT:
            passB_mm(n + 1)
        passB_gn(n)
    S.barrier()
    if stage == 1:
        S.dma("sp", "dbg", dbg_d.rearrange("p (h c) -> p h c", h=8), ogT, reads=B_ogT)
        S.barrier()
        S.emit()
        return nc

    zT = R1t[:, 0:4 * TOK].rearrange("p (g c) -> p g c", g=4)
    B_zT = Buf("zT")
    R2.reset()
    F1 = [R2.bf(16384, parts=64).rearrange("p (n c) -> p n c", n=128) for _ in range(1)]
    Aall = R2.bf(16384).rearrange("p (c k) -> p c k", c=128)
    tw = R2.bf(64 * 128).rearrange("p (k t c) -> p k t c", k=64, t=2)
    Xsb = [R2.bf(512).rearrange("p (k t c) -> p k t c", k=8, t=2) for _ in range(2)]
    B_F1, B_A, B_tw = Buf("F1"), Buf("A"), Buf("tw")
    B_Xsb = [Buf("Xsb0"), Buf("Xsb1")]
    S.dma("sp", "twld", tw, dr["tw"], writes=[B_tw])
    wst_s = R2.f32(1024).rearrange("p (k c) -> p k c", k=8)
    wmb_s = [R2.bf(1024).rearrange("p (k c) -> p k c", k=8) for _ in range(2)]
    bmod_s = [R2.f32(128, parts=1) for _ in range(2)]
    row_s = R2.f32(128, parts=1)
    B_wsts, B_rows = Buf("wsts"), Buf("rows")
    B_wmbs = [Buf("wmbs0"), Buf("wmbs1")]
    B_bms = [Buf("bms0"), Buf("bms1")]

    def mod_prep(j_):
        c0 = 4 * 512 + 128 * j_
        S.dma("sp", "wsts", wst_s, wmod_v[:, :, c0:c0 + 128], writes=[B_wsts])
        S.dma("sp", f"bms{j_ % 2}", bmod_s[j_ % 2], dr["b_mod"][:, c0:c0 + 128], writes=[B_bms[j_ % 2]])
        CP("pool", wmb_s[j_ % 2], wst_s, [B_wsts], [B_wmbs[j_ % 2]])

    def mod_mm(j_):
        c0 = 4 * 512 + 128 * j_
        for k in range(8):
            MM(PS[7][0:1, 0:128], silcb[:, k:k + 1], wmb_s[j_ % 2][:, k, :], k == 0, k == 7, [B_small, B_wmbs[j_ % 2]], [PB[7]])
        TT("dve", row_s, PS[7][0:1, 0:128], bmod_s[j_ % 2], ALU.add, [PB[7], B_bms[j_ % 2]], [B_rows])
        S.dma("sp", "rowss", mod_d[:, c0:c0 + 128], row_s, reads=[B_rows], writes=[B_modd])

    def mod_subblock(j_):
        if j_ == 0:
            mod_prep(0)
        if j_ + 1 < 32:
            mod_prep(j_ + 1)
        mod_mm(j_)

    nsub = [0]
    for g in range(4):
        for h2 in range(2):
            src = fout_d[h2].ap().rearrange("(q n) c -> q n c", n=128)[:, :, 128 * g:128 * g + 128]
            S.dma("sp", "f1ld", F1[0][32 * h2:32 * h2 + 32, :, :], src, reads=[B_fout[h2]], writes=[B_F1])
        for c4 in range(32):
            if c4 % 8 == 0 and nsub[0] < 32:
                mod_subblock(nsub[0])
                nsub[0] += 1
            bank = 1 + (c4 % 2)
            for cc in range(4):
                ch = 4 * c4 + cc
                MM(PS[bank][:, 128 * cc:128 * cc + 128], F1[0][:, :, ch], e64[:, :], True, True, [B_F1, B_const], [PB[bank]])
            CP("act" if c4 % 2 == 0 else "dve", Aall[:, 4 * c4:4 * c4 + 4, :], PS[bank][:, :].rearrange("p (c k) -> p c k", c=4), [PB[bank]], [B_A])
        for kb in range(8):
            if kb % 2 == 0 and nsub[0] < 32:
                mod_subblock(nsub[0])
                nsub[0] += 1
            xb = 3 + (kb % 2)
            px = PS[xb][:, :].rearrange("p (k t c) -> p k t c", k=8, t=2)
            for kk in range(8):
                k1 = 8 * kb + kk
                ar = Aall[:, :, k1]
                ai = Aall[:, :, 64 + k1]
                pxk = PS[xb][:, 64 * kk:64 * kk + 64]
                MM(pxk, ar, tw[:, k1, 0, :], True, False, [B_A, B_tw], [PB[xb]])
                MM(pxk, ai, tw[:, k1, 1, :], False, True, [B_A, B_tw], [PB[xb]])
            xs = Xsb[kb % 2]
            CP("act", xs, px, [PB[xb]], [B_Xsb[kb % 2]])
            zb = 5 + (kb % 2)
            pz = PS[zb][:, 0:256].rearrange("p (k c) -> p k c", k=8)
            MM(pz, c128[:, 0, :], xs[:, :, 0, :], True, False, [B_Xsb[kb % 2], B_const], [PB[zb]])
            MM(pz, c128[:, 1, :], xs[:, :, 1, :], False, True, [B_Xsb[kb % 2], B_const], [PB[zb]])
            zdst = zT[:, g, :].rearrange("p (b a) -> p a b", a=64)[:, 8 * kb:8 * kb + 8, :]
            CP("dve", zdst, pz, [PB[zb]], [B_zT])
    S.barrier()
    col_layout(shcol[:, 8:16], mod_d[:, 3 * D:4 * D].rearrange("o (c p) -> (o c) p", p=128), 8)
    CP("dve", shcolb[:, 8:16], shcol[:, 8:16], [B_const], [B_small])
    if stage == 2:
        S.dma("sp", "dbg", dbg_d[:, 0:4 * TOK], R1t[:, 0:4 * TOK], reads=[B_zT])
        S.barrier()
        S.emit()
        return nc

    Wout = R2t[:, 37888:46080].rearrange("p (k c) -> p k c", k=8)
    B_wout = Buf("wout")
    wout_v = dr["w_out"].rearrange("(k p) c -> p k c", p=128)
    R2.reset()
    mergedT = R2.bf(8 * TOK).rearrange("p (k c) -> p k c", k=8)
    wsl = [R2.bf(28 * 128).rearrange("p (k c) -> p k c", k=28) for _ in range(2)]
    gts = [R2.f32(512) for _ in range(4)]
    m12 = [R2.f32(512) for _ in range(2)]
    B_wsl = [Buf("wsl0"), Buf("wsl1")]
    B_gts = [Buf(f"gt{i}") for i in range(4)]
    B_m12 = [Buf("m1"), Buf("m2")]
    B_mg = [Buf(f"mg{i}") for i in range(4)]
    wro_v = dr["w_ret_out"].rearrange("(k p) c -> p k c", p=128)
    wfo_v = dr["w_four_out"].rearrange("(k p) c -> p k c", p=128)
    wbg_v = dr["w_bg"].rearrange("(k p) c -> p k c", p=128)

    def load_wsl(oc):
        i = oc % 2
        cs = slice(128 * oc, 128 * oc + 128)
        S.dma("pool", f"wsl{i}", wsl[i][:, 0:8, :], wro_v[:, :, cs], writes=[B_wsl[i]])
        S.dma("pool", f"wsl{i}", wsl[i][:, 8:12, :], wfo_v[:, :, cs], writes=[B_wsl[i]])
        S.dma("pool", f"wsl{i}", wsl[i][:, 12:20, :], wbg_v[:, :, cs], writes=[B_wsl[i]])
        S.dma("pool", f"wsl{i}", wsl[i][:, 20:28, :], wbg_v[:, :, D + 128 * oc:D + 128 * oc + 128], writes=[B_wsl[i]])
    load_wsl(0)
    for k in range(8):
        S.dma("pool", "wout", Wout[:, k, :], wout_v[:, k, :], writes=[B_wout])
    B_bbg = Buf("bbg")
    for oc in range(8):
        i = oc % 2
        if oc + 1 < 8:
            load_wsl(oc + 1)
        w = wsl[i]
        for gi in range(2):
            for k in range(8):
                MM(PS[7][:, gi:gi + 1], w[:, 12 + 8 * gi + k, :], shcolb[:, k:k + 1], k == 0, k == 7, [B_wsl[i], B_small], [PB[7]])
        bcol = bbg.rearrange("p (g c) -> p g c", g=2)[:, :, oc]
        TT("dve", bcol, bcol, PS[7][:, 0:2], ALU.add, [PB[7], B_const], [B_bbg])
        for tb in range(4):
            ts_ = slice(512 * tb, 512 * tb + 512)
            xh_b = [B_XH[4 * tb + q] for q in range(4)]
            og_b = [B_ogT[4 * tb + q] for q in range(4)]
            for k in range(8):
                MM(PS[1][:, :], w[:, k, :], ogT[:, k, ts_], k == 0, k == 7, [B_wsl[i]] + og_b, [PB[1]])
            for k in range(4):
                MM(PS[2][:, :], w[:, 8 + k, :], zT[:, k, ts_], k == 0, k == 3, [B_wsl[i], B_zT], [PB[2]])
            for gi in range(2):
                for k in range(8):
                    MM(PS[3 + gi][:, :], w[:, 12 + 8 * gi + k, :], XH[:, k, ts_], k == 0, k == 7, [B_wsl[i]] + xh_b, [PB[3 + gi]])
            g0 = 2 * (tb % 2)
            for gi in range(2):
                ACTF(gts[g0 + gi], PS[3 + gi][:, :], AF.Sigmoid, [PB[3 + gi], B_bbg], [B_gts[g0 + gi]], bias=bbg[:, 8 * gi + oc:8 * gi + oc + 1])
            TT("dve", m12[0], gts[g0], PS[1][:, :], ALU.mult, [B_gts[g0], PB[1]], [B_m12[0]])
            TT("dve", m12[1], gts[g0 + 1], PS[2][:, :], ALU.mult, [B_gts[g0 + 1], PB[2]], [B_m12[1]])
            TT("pool", mergedT[:, oc, ts_], m12[0], m12[1], ALU.add, [B_m12[0], B_m12[1]], [B_mg[tb]])
    S.barrier()

    R2.reset(8 * TOK)
    xt2 = [R2.f32(1024) for _ in range(2)]
    x1t = [R2.f32(1024) for _ in range(2)]
    tmpy = R2.f32(1024)
    g1bc = R2.f32(1024)
    A2bc = R2.f32(1024)
    nw2t = R2.f32(1024)
    hb2 = R2.bf(1024)
    junk2 = R2.bf(1024)
    B_xt2 = [Buf("xt2_0"), Buf("xt2_1")]
    B_x1t = [Buf("x1t0"), Buf("x1t1")]
    B_tmpy, B_hb2, B_junk2, B_bc2 = Buf("tmpy"), Buf("hb2"), Buf("junk2"), Buf("bc2")
    B_x1d = [Buf(f"x1d{t}") for t in range(NT)]
    S.dma("sp", "bcl", g1bc, mod_d[:, 2 * D:3 * D].partition_broadcast(128), writes=[B_bc2])
    S.dma("sp", "bcl", A2bc, mod_d[:, 4 * D:5 * D].partition_broadcast(128), writes=[B_bc2])
    S.dma("sp", "bcl", nw2t, dr["norm2_w"].partition_broadcast(128), writes=[B_bc2])
    STT("dve", A2bc, A2bc, 1.0, nw2t, ALU.add, ALU.mult, [B_bc2], [B_bc2])
    TS("dve", A2bc, A2bc, 32.0, None, ALU.mult, None, [B_bc2], [B_bc2])
    Wd = R1t[:, 0:22 * D].rearrange("p (k c) -> p k c", k=22)
    B_wd = Buf("wd")
    wd_v = dr["w_down"].rearrange("(k p) c -> p k c", p=128)
    for k in range(22):
        S.dma("pool", "wd", Wd[:, k, :], wd_v[:, k, :], writes=[B_wd])
    S.dma("sp", "xt2_0", xt2[0], dr["x_own"][0:128, :], writes=[B_xt2[0]])
    ybanks = [(1, 2), (3, 4)]

    def y_mm(t):
        for half in range(2):
            bk = ybanks[t % 2][half]
            for k in range(8):
                MM(PS[bk][:, :], mergedT[:, k, 128 * t:128 * t + 128], Wout[:, k, 512 * half:512 * half + 512], k == 0, k == 7,
                   [B_mg[t // 4], B_wout], [PB[bk]])

    def y_post(t):
        slot = t % 2
        if t + 1 < NT:
            S.dma("sp", f"xt2_{(t + 1) % 2}", xt2[(t + 1) % 2], dr["x_own"][128 * (t + 1):128 * (t + 2), :], writes=[B_xt2[(t + 1) % 2]])
        for half in range(2):
            bk = ybanks[t % 2][half]
            hs = slice(512 * half, 512 * half + 512)
            TT("dve", tmpy[:, hs], PS[bk][:, :], g1bc[:, hs], ALU.mult, [PB[bk], B_bc2], [B_tmpy])
        TT("dve", x1t[slot], tmpy, xt2[slot], ALU.add, [B_tmpy, B_xt2[slot]], [B_x1t[slot]])
        S.dma("sp", f"x1st{slot}", x1_d[128 * t:128 * t + 128, :], x1t[slot], reads=[B_x1t[slot]], writes=[B_x1d[t]])
        ACTF(junk2, x1t[slot], AF.Square, [B_x1t[slot]], [B_junk2, B_ss], accum_out=ss_t[:, 2:3])
        RSQ(ss_t[:, 3:4], ss_t[:, 2:3], 1024.0 * EPS, [B_ss], [B_ss])
        STT("dve", hb2, x1t[slot], ss_t[:, 3:4], A2bc, ALU.mult, ALU.mult, [B_x1t[slot], B_ss, B_bc2], [B_hb2])
        pst = psbf(0)
        for k in range(8):
            TR(pst[:, 128 * k:128 * k + 128], hb2[:, 128 * k:128 * k + 128], [B_hb2], [PB[0]])
        CP("act", XH[:, :, 128 * t:128 * t + 128], pst.rearrange("p (k c) -> p k c", k=8), [PB[0]], [B_XH[t]])

    y_mm(0)
    for t in range(NT):
        if t + 1 < NT:
            y_mm(t + 1)
        y_post(t)
    S.barrier()
    if stage == 3:
        for t in range(NT):
            S.dma("sp", "xt2_0", xt2[0], x1_d[128 * t:128 * t + 128, :], reads=[B_x1d[t]], writes=[B_xt2[0]])
            S.dma("sp", "outst", out_d[128 * t:128 * t + 128, :], xt2[0], reads=[B_xt2[0]], writes=[])
        S.dma("sp", "dbg", dbg_d, R2t[:, 0:8 * TOK], reads=B_mg)
        S.barrier()
        S.emit()
        return nc

    R2.reset()
    USE_GELU_ACT = not os.environ.get("KDBG_GELU_SIG")
    mT = R2.bf(22 * 512).rearrange("p (k c) -> p k c", k=22)
    wup = [R2.bf(2048).rearrange("p (a k c) -> p a k c", a=2, k=8) for _ in range(3)]
    cacc = [[R2.f32(512) for _ in range(2)] for _ in range(2)]
    ga = [R2.f32(512) for _ in range(2)]
    ub = R2.f32(NCH * 8).rearrange("p (c t) -> p c t", c=NCH)
    hal = R2.f32(2 * NCH).rearrange("p (s c) -> p s c", s=2)
    hsend = R2.f32(128)
    hall = R2.f32(512).rearrange("p (r c) -> p r c", r=4)
    kcc = R2.f32(3 * NCH).rearrange("p (s c) -> p s c", s=3)
    x1r = [R2.f32(1024) for _ in range(2)]
    x2t = R2.f32(1024)
    g2bc = R2.f32(1024)
    fwbc = R2.f32(1024)
    junk3 = R2.bf(1024)
    B_wup = [Buf(f"wup{i}") for i in range(3)]
    B_cacc = [[Buf(f"ca{s_}{a}") for a in range(2)] for s_ in range(2)]
    B_ga = [Buf("ga0"), Buf("ga1")]
    B_mT, B_ub, B_hal, B_hsend, B_hall, B_kcc = Buf("mT"), Buf("ub"), Buf("hal"), Buf("hsend"), Buf("hall"), Buf("kcc")
    B_x1r = [Buf("x1r0"), Buf("x1r1")]
    B_x2t, B_bc3, B_junk3 = Buf("x2t"), Buf("bc3"), Buf("junk3")
    B_hin, B_hout = Buf("hin"), Buf("hout")
    S.dma("sp", "bcl", g2bc, mod_d[:, 5 * D:6 * D].partition_broadcast(128), writes=[B_bc3])
    S.dma("sp", "bcl", fwbc, dr["final_norm_w"].partition_broadcast(128), writes=[B_bc3])
    wup_v = dr["w_up"].rearrange("(k p) c -> p k c", p=128)
    nload = [0]

    def load_wup(i):
        s_ = nload[0] % 3
        nload[0] += 1
        S.dma("pool", f"wup{s_}", wup[s_][:, 0, :, :], wup_v[:, :, 128 * i:128 * i + 128], writes=[B_wup[s_]])
        S.dma("pool", f"wup{s_}", wup[s_][:, 1, :, :], wup_v[:, :, FFN + 128 * i:FFN + 128 * i + 128], writes=[B_wup[s_]])
        return s_

    xb8 = XH[:, :, :].rearrange("p k (b c) -> p k b c", c=512)
    bnd = R2.bf(72).rearrange("p (k c) -> p k c", k=8)
    B_bnd = Buf("bnd")
    CP("dve", bnd[:, :, 0], shcolb[:, 8:16], [B_small], [B_bnd])
    CP("dve", bnd[:, :, 1:5], xb8[:, :, :, 0], B_XH, [B_bnd])
    CP("dve", bnd[:, :, 5:9], xb8[:, :, :, 511], B_XH, [B_bnd])
    B_bup = Buf("bup")
    TT("dve", kcc[:, 0, :], convc[:, 0, :], convc[:, 1, :], ALU.add, [B_const], [B_kcc])
    TT("dve", kcc[:, 0, :], kcc[:, 0, :], convc[:, 2, :], ALU.add, [B_const, B_kcc], [B_kcc])
    S.op("pool", lambda e: e.memset(hsend, 0.0), writes=[B_hsend])

    def halo_exchange():
        CP("dve", hsend[:, 0:NCH], ub[:, :, 0], [B_ub], [B_hsend])
        CP("dve", hsend[:, NCH:2 * NCH], ub[:, :, 7], [B_ub], [B_hsend])
        S.dma("sp", "hst", hin_d.ap(), hsend, reads=[B_hsend], writes=[B_hin])
        S.custom("pool", "cch", lambda e: e.collective_compute(
            "AllGather", ALU.bypass, replica_groups=GROUPS, ins=[hin_d.ap().opt()], outs=[hout_d.ap().opt()]),
            reads=[B_hin], writes=[B_hout])
        S.dma("sp", "hld", hall, hout_d.ap().rearrange("(r p) c -> p r c", p=128), reads=[B_hout], writes=[B_hall])
        for side, (c0, s0) in enumerate(((NCH, 8), (0, 12))):
            TS("dve", hal[:, side, :], hall[:, 0, c0:c0 + NCH], sel[:, s0:s0 + 1], None, ALU.mult, None, [B_hall, B_const], [B_hal])
            for r in range(1, 4):
                STT("dve", hal[:, side, :], hall[:, r, c0:c0 + NCH], sel[:, s0 + r:s0 + r + 1], hal[:, side, :], ALU.mult, ALU.add,
                    [B_hall, B_hal, B_const], [B_hal])
            STT("dve", hal[:, side, :], kcc[:, 2, :], sel[:, 16 + side:17 + side], hal[:, side, :], ALU.mult, ALU.add, [B_kcc, B_hal, B_const], [B_hal])

    for tb in (1, 2, 0, 3):
        ts_ = slice(512 * tb, 512 * tb + 512)
        xh_b = [B_XH[4 * tb + q] for q in range(4)]
        pend = [load_wup(0), load_wup(1)]
        S.dma("sp", "x1r0", x1r[0], x1_d[512 * tb:512 * tb + 128, :], reads=[B_x1d[4 * tb]], writes=[B_x1r[0]])
        for i in range(22):
            s_ = pend.pop(0)
            if i + 2 < 22:
                pend.append(load_wup(i + 2))
            us = i % 2
            for a in range(2):
                ch = i + 22 * a
                bank = ((1, 2), (3, 4), (0, 7))[i % 3][a]
                if tb == 1:
                    for k in range(8):
                        MM(PS[7][:, 0:9], wup[s_][:, a, k, :], bnd[:, k, :], k == 0, k == 7, [B_wup[s_], B_bnd], [PB[7]])
                    CP("act", ub[:, ch, :].rearrange("p (b e) -> p e b", e=2), PS[7][:, 1:9].rearrange("p (e b) -> p e b", e=2), [PB[7]], [B_ub])
                    CP("dve", bup[:, ch:ch + 1], PS[7][:, 0:1], [PB[7]], [B_bup])
                    STT("dve", kcc[:, 1, ch:ch + 1], bup[:, ch:ch + 1], kcc[:, 0, ch:ch + 1], convc[:, 3, ch:ch + 1], ALU.mult, ALU.add,
                        [B_bup, B_kcc, B_const], [B_kcc])
                    TS("dve", kcc[:, 2, ch:ch + 1], bup[:, ch:ch + 1], -1.0, None, ALU.mult, None, [B_bup], [B_kcc])
                for k in range(8):
                    MM(PS[bank][:, :], wup[s_][:, a, k, :], XH[:, k, ts_], k == 0, k == 7, [B_wup[s_]] + xh_b, [PB[bank]])
                ca = cacc[us][a]
                bca = B_cacc[us][a]
                w0c, w1c, w2c = convc[:, 0, ch:ch + 1], convc[:, 1, ch:ch + 1], convc[:, 2, ch:ch + 1]
                ACTF(ca, PS[bank][:, :], AF.Identity, [PB[bank], B_kcc, B_const], [bca], scale=w1c, bias=kcc[:, 1, ch:ch + 1])
                STT("dve", ca[:, 1:512], PS[bank][:, 0:511], w0c, ca[:, 1:512], ALU.mult, ALU.add, [PB[bank], bca, B_const], [bca])
                STT("dve", ca[:, 0:511], PS[bank][:, 1:512], w2c, ca[:, 0:511], ALU.mult, ALU.add, [PB[bank], bca, B_const], [bca])
                if tb == 0:
                    pl, bpl = hal[:, 0, ch:ch + 1], B_hal
                else:
                    pl, bpl = ub[:, ch, 2 * tb - 1:2 * tb], B_ub
                if tb == 3:
                    pr, bpr = hal[:, 1, ch:ch + 1], B_hal
                else:
                    pr, bpr = ub[:, ch, 2 * tb + 2:2 * tb + 3], B_ub
                STT("dve", ca[:, 0:1], pl, w0c, ca[:, 0:1], ALU.mult, ALU.add, [bpl, bca, B_const], [bca])
                STT("dve", ca[:, 511:512], pr, w2c, ca[:, 511:512], ALU.mult, ALU.add, [bpr, bca, B_const], [bca])
            ca, cv = cacc[us][0], cacc[us][1]
            if USE_GELU_ACT:
                ACTF(ga[us], ca, AF.Gelu_apprx_tanh, [B_cacc[us][0]], [B_ga[us]])
            else:
                TT("dve", ga[us], ca, ca, ALU.mult, [B_cacc[us][0]], [B_ga[us]])
                TS("dve", ga[us], ga[us], 0.044715, 1.0, ALU.mult, ALU.add, [B_ga[us]], [B_ga[us]])
                TT("dve", ga[us], ga[us], ca, ALU.mult, [B_ga[us], B_cacc[us][0]], [B_ga[us]])
                ACTF(ga[us], ga[us], AF.Sigmoid, [B_ga[us]], [B_ga[us]], scale=GELU_C)
                TT("dve", ga[us], ga[us], ca, ALU.mult, [B_ga[us], B_cacc[us][0]], [B_ga[us]])
            TT("dve", mT[:, i, :], cv, ga[us], ALU.mult, [B_cacc[us][1], B_ga[us]], [B_mT])
        if tb == 1:
            halo_exchange()
        for q in range(4):
            t = 4 * tb + q
            slot = q % 2
            if q + 1 < 4:
                S.dma("sp", f"x1r{(q + 1) % 2}", x1r[(q + 1) % 2], x1_d[128 * (t + 1):128 * (t + 2), :], reads=[B_x1d[t + 1]], writes=[B_x1r[(q + 1) % 2]])
            for half in range(2):
                for k in range(22):
                    MM(PS[5 + half][:, :], mT[:, k, 128 * q:128 * q + 128], Wd[:, k, 512 * half:512 * half + 512], k == 0, k == 21,
                       [B_mT, B_wd], [PB[5 + half]])
            for half in range(2):
                hs = slice(512 * half, 512 * half + 512)
                TT("dve", x2t[:, hs], PS[5 + half][:, :], g2bc[:, hs], ALU.mult, [PB[5 + half], B_bc3], [B_x2t])
            TT("dve", x2t, x2t, x1r[slot], ALU.add, [B_x2t, B_x1r[slot]], [B_x2t])
            ACTF(junk3, x2t, AF.Square, [B_x2t], [B_junk3, B_ss], accum_out=ss_t[:, 4:5])
            RSQ(ss_t[:, 5:6], ss_t[:, 4:5], 1024.0 * EPS, [B_ss], [B_ss])
            TS("dve", ss_t[:, 5:6], ss_t[:, 5:6], 32.0, None, ALU.mult, None, [B_ss], [B_ss])
            STT("dve", x1r[slot], x2t, ss_t[:, 5:6], fwbc, ALU.mult, ALU.mult, [B_x2t, B_ss, B_bc3], [B_x1r[slot]])
            S.dma("sp", f"ost{slot}", out_d[128 * t:128 * t + 128, :], x1r[slot], reads=[B_x1r[slot]], writes=[])
    S.barrier()
    S.emit()
    return nc


_CACHE = {}


def _in_maps(inputs):
    g = lambda k: np.asarray(inputs[k], dtype=np.float32)
    x, c, ctx, c_ctx = g("x"), g("c"), g("ctx"), g("c_ctx")
    shared = {
        "w_mod": g("w_mod")[0], "b_mod": g("b_mod"), "norm1_w": g("norm1_w"), "w_in": g("w_in")[0],
        "a_f": g("ret_decay_f"), "a_b": g("ret_decay_b"), "w_ret_out": g("w_ret_out")[0],
        "w_four_out": g("w_four_out")[0], "w_bg": g("w_branch_gate")[0], "b_bg": g("b_branch_gate"),
        "w_out": g("w_out")[0], "norm2_w": g("norm2_w"), "w_up": g("w_up")[0], "conv_w": g("conv_w")[0],
        "conv_b": g("conv_b"), "w_down": g("w_down")[0], "final_norm_w": g("final_norm_w").reshape(1, D),
        "cc_col": np.ascontiguousarray(c_ctx.reshape(8, 128).T),
    }
    shared = {k: np.ascontiguousarray(v) for k, v in shared.items()}
    consts = [host_consts(j) for j in range(4)]
    maps = []
    for core in range(NCORES):
        b, j = core // 4, core % 4
        m = dict(shared)
        m["x_own"] = np.ascontiguousarray(x[b, TOK * j:TOK * (j + 1)])
        m["ctx"] = np.ascontiguousarray(ctx[b])
        m["c_col"] = np.ascontiguousarray(c[b].reshape(8, 128).T)
        m.update(consts[j])
        maps.append(m)
    return maps


def kernel(**inputs):
    if "nc" not in _CACHE:
        _CACHE["nc"] = build_program(4)
    nc = _CACHE["nc"]
    res = run_bass_kernel_spmd(nc, _in_maps(inputs), core_ids=list(range(NCORES)))
    out = np.empty((NB, SEQ, D), np.float32)
    for core in range(NCORES):
        b, j = core // 4, core % 4
        out[b, TOK * j:TOK * (j + 1)] = np.asarray(res.results[core]["out"], dtype=np.float32)
    return out
```

```python
import numpy as np
import ml_dtypes
from contextlib import ExitStack

import concourse.bass as bass
import concourse.mybir as mybir
from concourse.bass_utils import run_bass_kernel_spmd

F32 = mybir.dt.float32
BF16 = mybir.dt.bfloat16
ALU = mybir.AluOpType
AF = mybir.ActivationFunctionType
AX = mybir.AxisListType

D = 1024
SEQ = 8192
NB = 2
NCORES = 8
TOK = 2048
NT = 16
CTX = 256
H = 8
INC = 3584
FFN = 2816
NCH = 44
EPS = 1e-6
GROUPS = [[0, 1, 2, 3], [4, 5, 6, 7]]
GELU_C = 1.5957691216057308


class Tok:
    __slots__ = ("key", "val")

    def __init__(self, key, val):
        self.key = key
        self.val = val


class Buf:
    __slots__ = ("name", "w", "r", "excl")

    def __init__(self, name, excl=False):
        self.name = name
        self.w = None
        self.r = []
        self.excl = excl


class Sched:
    ENGS = ("pe", "act", "dve", "pool", "sp")

    def __init__(self, nc, stack):
        self.nc = nc
        self.stack = stack
        self.ops = {e: [] for e in self.ENGS}
        self.sems = {}
        self.cnt = {}
        self.seen = {e: {} for e in self.ENGS}
        for e in ("pe", "act", "dve", "pool"):
            self._mk(e)

    def _mk(self, key):
        if key not in self.sems:
            self.sems[key] = self.stack.enter_context(self.nc.semaphore("s_" + key))
            self.cnt[key] = 0

    def _deps(self, eng, reads, writes):
        deps = []
        for b in reads:
            if b.w is not None:
                deps.append(b.w)
        for b in writes:
            if b.w is not None:
                deps.append(b.w)
            deps.extend(b.r)
        waits = {}
        for t in deps:
            if t.key == "pe" and eng == "pe":
                continue
            if self.seen[eng].get(t.key, 0) >= t.val:
                continue
            waits[t.key] = max(waits.get(t.key, 0), t.val)
        for k, v in waits.items():
            self.seen[eng][k] = v
        return list(waits.items())

    def _commit(self, tok, reads, writes):
        for b in writes:
            b.w = tok
            b.r = []
        for b in reads:
            if b not in writes:
                b.r.append(tok)
                if len(b.r) > 64:
                    b.r = b.r[-48:]

    def op(self, eng, fn, reads=(), writes=()):
        ex = [b for b in reads if b.excl]
        if ex:
            reads = [b for b in reads if not b.excl]
            writes = list(writes) + ex
        waits = self._deps(eng, reads, writes)
        self.cnt[eng] += 1
        tok = Tok(eng, self.cnt[eng])
        self.ops[eng].append((waits, fn, eng, 1))
        self._commit(tok, reads, writes)
        return tok

    def dma(self, queue, key, out, in_, reads=(), writes=(), **kw):
        self._mk(key)
        waits = self._deps(queue, reads, writes)
        self.cnt[key] += 16
        tok = Tok(key, self.cnt[key])
        self.ops[queue].append((waits, lambda e: e.dma_start(out=out, in_=in_, **kw), key, 16))
        self._commit(tok, reads, writes)
        return tok

    def custom(self, queue, key, fn, reads=(), writes=()):
        import os
        if os.environ.get("KDBG_NOCC"):
            return None
        self._mk(key)
        waits = self._deps(queue, reads, writes)
        self.cnt[key] += 1
        tok = Tok(key, self.cnt[key])
        self.ops[queue].append((waits, fn, key, None))
        self._commit(tok, reads, writes)
        return tok

    def barrier(self, exclude=()):
        for e in self.ENGS:
            waits = []
            for k, v in self.cnt.items():
                if k == e or v == 0 or k in exclude:
                    continue
                if self.seen[e].get(k, 0) >= v:
                    continue
                self.seen[e][k] = v
                waits.append((k, v))
            if waits:
                self.ops[e].append((waits, None, None, 0))

    def emit(self):
        nc = self.nc
        handles = {"pe": "tensor", "act": "scalar", "dve": "vector", "pool": "gpsimd", "sp": "sync"}
        with nc.Block() as block:
            for e in self.ENGS:
                ops = self.ops[e]

                def body(engine, ops=ops):
                    for waits, fn, key, inc in ops:
                        for k, v in waits:
                            engine.wait_ge(self.sems[k], v)
                        if fn is None:
                            continue
                        ins = fn(engine)
                        if inc is None:
                            ins.then_inc(self.sems[key])
                        else:
                            ins.then_inc(self.sems[key], inc)

                getattr(block, handles[e])(body)


class Arena:
    def __init__(self, t, nelem):
        self.t = t
        self.n = nelem
        self.off = 0

    def reset(self, off=0):
        self.off = off

    def bf(self, nelem, parts=128):
        a = self.t[0:parts, self.off:self.off + nelem]
        self.off += nelem
        assert self.off <= self.n, (self.off, self.n)
        return a

    def f32(self, nelem, parts=128):
        return self.bf(2 * nelem, parts).bitcast(F32)


def _bf(a):
    return np.ascontiguousarray(a).astype(ml_dtypes.bfloat16)


def host_consts(j):
    c = {}
    c["ident"] = _bf(np.eye(128, dtype=np.float32))
    p = np.arange(128)
    t = (TOK * j + 128 * np.arange(NT)[None, :] + p[:, None]).astype(np.float32)
    row = np.floor(t / 64.0).astype(np.float32)
    col = (t - 64.0 * row).astype(np.float32)
    inv = (np.float32(10000.0) ** (-(np.arange(16, dtype=np.float32)) / np.float32(16))).astype(np.float32)
    ang = np.concatenate([row[:, :, None] * inv[None, None, :], col[:, :, None] * inv[None, None, :]], axis=-1)
    ang = ang.astype(np.float32)
    c["rope"] = np.concatenate([np.cos(ang), np.sin(ang)], axis=-1).astype(np.float32)
    s_ = p[:, None]
    c_ = p[None, :]
    mf = (c_ >= s_).astype(np.float32)
    mb = (c_ <= s_).astype(np.float32)
    c["mask"] = np.stack([mf, mf, mb, mb], axis=1).astype(np.float32)
    pc = np.zeros((128, 8), np.float32)
    pc[:, 0] = -(p + 1)
    pc[:, 1] = (p + 1)
    pc[:, 2] = -(128 - p)
    pc[:, 3] = (128 - p)
    pc[:, 4] = -(255 - p)
    pc[:, 5] = -(255 - 128 - p)
    pc[:, 6] = -p
    pc[:, 7] = -(128 + p)
    c["pcol"] = pc
    sel = np.zeros((128, 18), np.float32)
    sel[:, 16] = 1.0 if j == 0 else 0.0
    sel[:, 17] = 1.0 if j == 3 else 0.0
    sel[:, j] = 1.0
    sel[:, 4 + j] = 1.0
    if j > 0:
        sel[:, 8 + (j - 1)] = 1.0
    if j < 3:
        sel[:, 12 + (j + 1)] = 1.0
    c["sel"] = sel
    q = np.arange(64)
    hh, rr, mm = q // 32, (q % 32) // 8, q % 8
    n1 = 16 * rr + 8 * hh + mm
    k1 = np.arange(64)
    th = 2.0 * np.pi * ((n1[:, None] * k1[None, :]) % 64) / 64.0
    c["e64"] = _bf(np.concatenate([np.cos(th), -np.sin(th)], axis=1))
    n2 = np.arange(128)
    k2 = 32 * j + np.arange(32)
    kk = k1[None, :, None] + 64 * k2[None, None, :]
    ph = 2.0 * np.pi * ((n2[:, None, None] * kk) % 8192) / 8192.0
    twA = np.concatenate([np.cos(ph), -np.sin(ph)], axis=2)
    twB = np.concatenate([np.sin(ph), np.cos(ph)], axis=2)
    c["tw"] = _bf(np.stack([twA, twB], axis=2))
    ch = np.arange(128)
    pc2 = 2.0 * np.pi * ((ch[:, None] * ch[None, :]) % 128) / 128.0
    c["c128"] = _bf(np.stack([np.cos(pc2), np.sin(pc2)], axis=1) / 1024.0)
    c["ones"] = np.ones((128, 128), np.float32)
    c["identf"] = np.eye(128, dtype=np.float32)
    c["onesb"] = _bf(np.ones((128, 128), np.float32))
    return c


CONST_SPECS = [
    ("ident", [128, 128], BF16), ("rope", [128, 16, 64], F32), ("mask", [128, 4, 128], F32),
    ("pcol", [128, 8], F32), ("sel", [128, 18], F32), ("e64", [64, 128], BF16),
    ("tw", [128, 64, 2, 64], BF16), ("c128", [128, 2, 128], BF16), ("ones", [128, 128], F32),
    ("onesb", [128, 128], BF16), ("identf", [128, 128], F32),
]

INPUT_SPECS = [
    ("x_own", [TOK, D], F32), ("ctx", [CTX, D], F32), ("c_col", [128, 8], F32), ("cc_col", [128, 8], F32),
    ("w_mod", [D, 6 * D], F32), ("b_mod", [1, 6 * D], F32), ("norm1_w", [1, D], F32),
    ("w_in", [D, INC], F32), ("a_f", [1, H], F32), ("a_b", [1, H], F32),
    ("w_ret_out", [D, D], F32), ("w_four_out", [512, D], F32), ("w_bg", [D, 2 * D], F32),
    ("b_bg", [1, 2 * D], F32), ("w_out", [D, D], F32), ("norm2_w", [1, D], F32),
    ("w_up", [D, 2 * FFN], F32), ("conv_w", [3, 2 * FFN], F32), ("conv_b", [1, 2 * FFN], F32),
    ("w_down", [FFN, D], F32), ("final_norm_w", [1, D], F32),
]


def build_program(stage=4):
    import os
    STOP = os.environ.get("KDBG_STOP", "")
    nc = bass.Bass("TRN2", target_bir_lowering=False)
    stack = ExitStack()
    S = Sched(nc, stack)
    dr = {}
    for name, shape, dt in INPUT_SPECS + CONST_SPECS:
        dr[name] = nc.dram_tensor(name, shape, dt, kind="ExternalInput").ap()
    out_d = nc.dram_tensor("out", [TOK, D], F32, kind="ExternalOutput").ap()
    dbg_d = None
    if stage < 4:
        dbg_d = nc.dram_tensor("dbg", [128, 8 * TOK], BF16, kind="ExternalOutput").ap()
    rec_d = nc.dram_tensor("rec_scr", [NT, 128, 5120], BF16).ap()
    kv_d = nc.dram_tensor("kv_scr", [NT, 128, 1024], F32).ap()
    x1_d = nc.dram_tensor("x1_scr", [TOK, D], F32).ap()
    mod_d = nc.dram_tensor("mod_scr", [1, 6 * D + 2 * D], F32).ap()
    fin_d = [nc.dram_tensor(f"f_in{h}", [1024, 512], BF16) for h in range(2)]
    fout_d = [nc.dram_tensor(f"f_out{h}", [4096, 512], BF16) for h in range(2)]
    stin_d = nc.dram_tensor("st_in", [128, 1024], F32)
    stout_d = nc.dram_tensor("st_out", [512, 1024], F32)
    hin_d = nc.dram_tensor("halo_in", [128, 128], F32)
    hout_d = nc.dram_tensor("halo_out", [512, 128], F32)

    def sb(name, shape, dt):
        return stack.enter_context(nc.sbuf_tensor("sb_" + name, shape, dt))

    PS = [stack.enter_context(nc.psum_tensor(f"ps{i}", [128, 512], F32)) for i in range(8)]
    PB = [Buf(f"ps{i}", excl=True) for i in range(8)]

    def psbf(i):
        return PS[i][:, :].bitcast(BF16)

    ident = sb("ident", [128, 128], BF16)
    rope = sb("rope", [128, 16, 64], F32)
    mask = sb("mask", [128, 4, 128], F32)
    pcol = sb("pcol", [128, 8], F32)
    sel = sb("sel", [128, 18], F32)
    ones = sb("ones", [128, 128], F32)
    onesb = sb("onesb", [128, 128], BF16)
    identf = sb("identf", [128, 128], F32)
    sctx = sb("sctx", [128, 1024], F32)
    c128 = sb("c128", [128, 2, 128], BF16)
    e64 = sb("e64", [64, 128], BF16)
    smallf = sb("smallf", [128, 512], F32)
    smallb = sb("smallb", [128, 64], BF16)
    convc = sb("convc", [128, 4, NCH], F32)
    XH = sb("XH", [128, 8, TOK], BF16)
    R1n, R2n = 32768, 47104
    R1t = sb("R1", [128, R1n], BF16)
    R2t = sb("R2", [128, R2n], BF16)
    R1 = Arena(R1t, R1n)
    R2 = Arena(R2t, R2n)
    B_const = Buf("const")
    B_small = Buf("small")
    B_ss = Buf("ss")
    B_XH = [Buf(f"xh{t}") for t in range(NT)]

    def sf(a, b):
        return smallf[:, a:b]

    a_bc = sf(0, 16)
    ea = sf(16, 32)
    qf_sc, kf_sc, qb_sc, kb_sc = sf(32, 40), sf(40, 48), sf(48, 56), sf(56, 64)
    cxf_sc, cxb_sc = sf(64, 80), sf(80, 96)
    a_st, ea_st = sf(96, 104), sf(104, 112)
    cdp_f, cdp_b = sf(112, 180), sf(180, 248)
    silc = sf(248, 264)
    shcol = sf(264, 288)
    ss_t = sf(288, 296)
    bbg = sf(296, 312)
    bup = sf(312, 356)
    kcol = sf(356, 400)
    gst = sf(400, 464)
    silcb = smallb[:, 0:16]
    shcolb = smallb[:, 16:40]

    def TT(eng, out, in0, in1, op, reads, writes):
        return S.op(eng, lambda e: e.tensor_tensor(out=out, in0=in0, in1=in1, op=op), reads, writes)

    def TS(eng, out, in0, s1, s2, op0, op1, reads, writes):
        if s2 is None:
            return S.op(eng, lambda e: e.tensor_scalar(out=out, in0=in0, scalar1=s1, scalar2=None, op0=op0), reads, writes)
        return S.op(eng, lambda e: e.tensor_scalar(out=out, in0=in0, scalar1=s1, scalar2=s2, op0=op0, op1=op1), reads, writes)

    def STT(eng, out, in0, scalar, in1, op0, op1, reads, writes):
        return S.op(eng, lambda e: e.scalar_tensor_tensor(out=out, in0=in0, scalar=scalar, in1=in1, op0=op0, op1=op1), reads, writes)

    def CP(eng, out, in_, reads, writes):
        if eng == "act":
            return S.op("act", lambda e: e.activation(out=out, in_=in_, func=AF.Copy), reads, writes)
        return S.op(eng, lambda e: e.tensor_copy(out=out, in_=in_), reads, writes)

    def ACTF(out, in_, func, reads, writes, **kw):
        return S.op("act", lambda e: e.activation(out=out, in_=in_, func=func, **kw), reads, writes)

    def MM(out, lhsT, rhs, start, stop, reads, writes):
        return S.op("pe", lambda e: e.matmul(out, lhsT=lhsT, rhs=rhs, start=start, stop=stop), reads, writes)

    def TR(out, in_, reads, writes):
        return S.op("pe", lambda e: e.transpose(out=out, in_=in_, identity=ident[:]), reads + [B_const], writes)

    def RSQ(out, in_, c, reads, writes):
        ACTF(out, in_, AF.Sqrt, reads, writes, bias=c, scale=1.0)
        return S.op("dve", lambda e: e.reciprocal(out=out, in_=out), writes, writes)

    def bc3(ap2, n):
        return ap2.unsqueeze(2).to_broadcast([128, ap2.shape[1], n])

    R1.reset()
    Win = R1.bf(8 * INC).rearrange("p (k c) -> p k c", k=8)
    A1bc = R1.f32(1024)
    B_win = Buf("win")
    win_v = dr["w_in"].rearrange("(k p) c -> p k c", p=128)
    for k in range(8):
        S.dma("pool", "win", Win[:, k, :], win_v[:, k, :], writes=[B_win])
    for dst, name in ((ident, "ident"), (rope, "rope"), (mask, "mask"), (pcol, "pcol"), (sel, "sel"), (ones, "ones"),
                      (onesb, "onesb"), (c128, "c128"), (e64, "e64"), (identf, "identf")):
        S.dma("sp", "const", dst[:], dr[name], writes=[B_const])
    S.dma("sp", "const", a_bc[:, 0:8], dr["a_f"].partition_broadcast(128), writes=[B_const])
    S.dma("sp", "const", a_bc[:, 8:16], dr["a_b"].partition_broadcast(128), writes=[B_const])
    S.dma("sp", "const", silc[:, 0:8], dr["c_col"], writes=[B_const])
    S.dma("sp", "const", silc[:, 8:16], dr["cc_col"], writes=[B_const])
    R2.reset()
    stg = sb("stg", [64, 128], F32)
    B_stg = Buf("stg")

    def col_layout(dst, src_rows, n):
        S.dma("sp", "stg", stg[0:n, :], src_rows, writes=[B_stg])
        S.op("pe", lambda e: e.transpose(out=PS[6][:, 0:n], in_=stg[0:n, :], identity=identf[0:n, 0:n]), [B_stg, B_const], [PB[6]])
        CP("dve", dst, PS[6][:, 0:n], [PB[6]], [B_const])

    S.barrier()
    col_layout(bbg, dr["b_bg"].rearrange("o (c p) -> (o c) p", p=128), 16)
    for k in range(3):
        col_layout(convc[:, k, :], dr["conv_w"][k:k + 1, :].rearrange("o (c p) -> (o c) p", p=128), NCH)
    col_layout(convc[:, 3, :], dr["conv_b"].rearrange("o (c p) -> (o c) p", p=128), NCH)
    for di in range(2):
        for hh in range(2):
            CP("dve", a_st[64 * hh:64 * hh + 64, 4 * di:4 * di + 4],
               a_bc[64 * hh:64 * hh + 64, 8 * di:8 * di + 8].rearrange("p (a h) -> p a h", h=2)[:, :, hh], [B_const], [B_const])

    ACTF(ea, a_bc, AF.Exp, [B_const], [B_small])
    ACTF(ea_st, a_st, AF.Exp, [B_const], [B_small])
    for dst, src, col in ((qf_sc, ea[:, 0:8], 0), (kf_sc, ea[:, 0:8], 1), (qb_sc, ea[:, 8:16], 2), (kb_sc, ea[:, 8:16], 3)):
        ACTF(dst, src, AF.Exp, [B_small], [B_small], scale=pcol[:, col:col + 1])
    for tl in range(2):
        ACTF(cxf_sc[:, 8 * tl:8 * tl + 8], ea[:, 0:8], AF.Exp, [B_small], [B_small], scale=pcol[:, 4 + tl:5 + tl])
        ACTF(cxb_sc[:, 8 * tl:8 * tl + 8], ea[:, 8:16], AF.Exp, [B_small], [B_small], scale=pcol[:, 6 + tl:7 + tl])
    for n in range(17):
        ACTF(cdp_f[:, 4 * n:4 * n + 4], ea_st[:, 0:4], AF.Exp, [B_small], [B_small], scale=-128.0 * n)
        ACTF(cdp_b[:, 4 * n:4 * n + 4], ea_st[:, 4:8], AF.Exp, [B_small], [B_small], scale=-128.0 * n)
    TS("dve", qf_sc, qf_sc, 0.125, None, ALU.mult, None, [B_small], [B_small])
    TS("dve", qb_sc, qb_sc, 0.125, None, ALU.mult, None, [B_small], [B_small])
    ACTF(silc, silc, AF.Silu, [B_small], [B_small])
    CP("dve", silcb, silc, [B_small], [B_small])

    wst = [R2.f32(4096).rearrange("p (k c) -> p k c", k=8) for _ in range(2)]
    wmb = [R2.bf(4096).rearrange("p (k c) -> p k c", k=8) for _ in range(2)]
    bmod = [R2.f32(512, parts=1) for _ in range(2)]
    rowsb = [R2.f32(512, parts=1) for _ in range(4)]
    B_wst, B_wmb = [Buf("wst0"), Buf("wst1")], [Buf("wmb0"), Buf("wmb1")]
    B_bmod = [Buf("bm0"), Buf("bm1")]
    B_row = [Buf(f"row{i}") for i in range(4)]
    B_modd = Buf("modd")
    wmod_v = dr["w_mod"].rearrange("(k p) c -> p k c", p=128)
    cvt_eng = ["dve", "act"]
    nrow = [0]

    def mod_block(cb, wst, wmb, bmod, rowsb, B_wst, B_wmb, B_bmod, B_row):
        i = cb % 2
        S.dma("sp", f"wst{i}", wst[i], wmod_v[:, :, 512 * cb:512 * cb + 512], writes=[B_wst[i]])
        S.dma("sp", f"bm{i}", bmod[i], dr["b_mod"][:, 512 * cb:512 * cb + 512], writes=[B_bmod[i]])
        for half in range(2):
            CP(cvt_eng[(2 * cb + half) % len(cvt_eng)], wmb[i][:, 4 * half:4 * half + 4, :], wst[i][:, 4 * half:4 * half + 4, :], [B_wst[i]], [B_wmb[i]])
        for side in range(2 if cb < 4 else 1):
            for k in range(8):
                MM(PS[7][0:1, :], silcb[:, 8 * side + k:8 * side + k + 1], wmb[i][:, k, :], k == 0, k == 7, [B_small, B_wmb[i]], [PB[7]])
            r = nrow[0] % len(rowsb)
            nrow[0] += 1
            TT("dve", rowsb[r], PS[7][0:1, :], bmod[i], ALU.add, [PB[7], B_bmod[i]], [B_row[r]])
            off = 512 * cb if side == 0 else 6 * D + 512 * cb
            S.dma("sp", f"rowst{r}", mod_d[:, off:off + 512], rowsb[r], reads=[B_row[r]], writes=[B_modd])

    for cb in range(4):
        mod_block(cb, wst, wmb, bmod, rowsb, B_wst, B_wmb, B_bmod, B_row)
    S.barrier()
    for i, off in ((0, 0), (2, 6 * D)):
        col_layout(shcol[:, 8 * i:8 * i + 8], mod_d[:, off:off + D].rearrange("o (c p) -> (o c) p", p=128), 8)
    CP("dve", shcolb[:, 0:8], shcol[:, 0:8], [B_const], [B_small])
    CP("dve", shcolb[:, 16:24], shcol[:, 16:24], [B_const], [B_small])

    if STOP == "mod":
        S.barrier(); S.emit(); return nc
    B_bct = Buf("bct")

    R2.reset()
    xt = [R2.f32(1024) for _ in range(3)]
    hb = R2.bf(1024)
    junk = R2.bf(1024)
    qk2 = [R2.f32(1024) for _ in range(2)]
    qk_sb = qk2[0]
    off_rt1 = R2.off
    rt1, rt2 = R2.f32(1024), R2.f32(1024)
    rot = rt1
    ktok = R2.bf(1024)
    kpad = R2.bf(2048)
    qpad = R2.bf(2048)
    fsb = [R2.bf(512) for _ in range(2)]
    kvsb = [R2.f32(1024) for _ in range(2)]
    Est = R2.f32(1024)
    off_rec = R2.off
    recb = [R2.bf(5120) for _ in range(2)]
    hcT = R2.bf(2048).rearrange("p (k c) -> p k c", k=8)
    ctxv = R2.bf(1024)
    brows = R2.bf(INC, parts=2)
    lo_tmp = R2t[0:1, off_rt1:off_rt1 + INC]
    nwt = qk2[1]
    browsc = R2t[0:2, off_rec:off_rec + INC]
    A1c = R2t[:, off_rec + 5120:off_rec + 5120 + 2048].bitcast(F32)
    B_xt = [Buf("xt0"), Buf("xt1"), Buf("xt2")]
    B_hb, B_junk, B_qk = Buf("hb"), Buf("junk"), Buf("qk")
    B_qk2 = [Buf("qk2_0"), Buf("qk2_1")]
    B_rt1, B_rt2 = Buf("rt1"), Buf("rt2")
    B_rot = B_rt1
    B_ktok, B_kpad, B_qpad = Buf("ktok"), Buf("kpad"), Buf("qpad")
    B_fsb = [Buf("fsb0"), Buf("fsb1")]
    B_kvsb = [Buf("kvsb0"), Buf("kvsb1")]
    B_E = Buf("E")
    B_rec = [Buf("rec0"), Buf("rec1")]
    B_hcT, B_ctxv, B_sctx, B_bias = Buf("hcT"), Buf("ctxv"), Buf("sctx"), Buf("bias")

    S.dma("sp", "bcl", nwt, dr["norm1_w"].partition_broadcast(128), writes=[B_bct])
    S.dma("sp", "bcl", A1bc, mod_d[:, D:2 * D].partition_broadcast(128), reads=[B_modd], writes=[B_bct])
    STT("dve", A1bc, A1bc, 1.0, nwt, ALU.add, ALU.mult, [B_bct], [B_bct])
    TS("dve", A1bc, A1bc, 32.0, None, ALU.mult, None, [B_bct], [B_bct])

    def bias_rows(side, dstHL):
        lcol = 0 if side == 0 else 16
        for blk in range(7):
            if side == 1 and blk not in (1, 2, 3):
                continue
            for k in range(8):
                MM(PS[7][0:1, :], shcolb[:, lcol + k:lcol + k + 1], Win[:, k, 512 * blk:512 * blk + 512], k == 0, k == 7, [B_small, B_win], [PB[7]])
            cs = slice(512 * blk, 512 * blk + 512)
            CP("dve", dstHL[0:1, cs], PS[7][0:1, :], [PB[7]], [B_bias])
            TT("dve", lo_tmp[:, cs], PS[7][0:1, :], dstHL[0:1, cs], ALU.subtract, [PB[7], B_bias], [B_bias])
        S.dma("sp", "biaslo", dstHL[1:2, :], lo_tmp, reads=[B_bias], writes=[B_bias])

    bias_rows(0, brows)

    S.op("pool", lambda e: e.memset(kpad, 0.0), writes=[B_kpad])
    S.op("pool", lambda e: e.memset(qpad, 0.0), writes=[B_qpad])
    S.op("pool", lambda e: e.memset(Est, 0.0), writes=[B_E])

    def load_x(src_ap, slot):
        S.dma("sp", f"xt{slot}", xt[slot], src_ap, writes=[B_xt[slot]])

    def norm_part(slot, scale_bc):
        ACTF(junk, xt[slot], AF.Square, [B_xt[slot]], [B_junk, B_ss], accum_out=ss_t[:, 0:1])
        RSQ(ss_t[:, 1:2], ss_t[:, 0:1], 1024.0 * EPS, [B_ss], [B_ss])
        STT("dve", hb, xt[slot], ss_t[:, 1:2], scale_bc, ALU.mult, ALU.mult, [B_xt[slot], B_ss, B_bct], [B_hb])

    def tr_part(dstT, bdst, col0):
        pst = psbf(0)
        for k in range(8):
            TR(pst[:, 128 * k:128 * k + 128], hb[:, 128 * k:128 * k + 128], [B_hb], [PB[0]])
        CP("act", dstT[:, :, col0:col0 + 128], pst.rearrange("p (k c) -> p k c", k=8), [PB[0]], [bdst])

    def norm_transpose(slot, scale_bc, dstT, bdst, col0):
        norm_part(slot, scale_bc)
        tr_part(dstT, bdst, col0)

    def project(srcT, bsrc, col0, blocks, rows, consume):
        for i, blk in enumerate(blocks):
            bank = 1 + (i % 2)
            for k in range(8):
                MM(PS[bank][:, :], srcT[:, k, col0:col0 + 128], Win[:, k, 512 * blk:512 * blk + 512], k == 0, False, [bsrc, B_win], [PB[bank]])
            MM(PS[bank][:, :], onesb[0:2, :], rows[0:2, 512 * blk:512 * blk + 512], False, True, [B_bias, B_const], [PB[bank]])
            consume(blk, PS[bank], PB[bank])

    def k4(ap):
        return ap.rearrange("p (a h c) -> p a h c", a=4, h=2)

    def scaled_k(dirn, src_k, bsrc, sc_tile):
        kt = ktok[:, 512 * dirn:512 * dirn + 512]
        TT("dve", kt.rearrange("p (h d) -> p h d", h=8), src_k.rearrange("p (h d) -> p h d", h=8), bc3(sc_tile, 64), ALU.mult,
           [bsrc, B_small], [B_ktok])
        kp = k4(kpad[:, 1024 * dirn:1024 * dirn + 1024])
        kin = kt.rearrange("p (a h d) -> p a h d", a=4, h=2)
        for hh in range(2):
            CP("act", kp[:, :, hh, 64 * hh:64 * hh + 64], kin[:, :, hh, :], [B_ktok], [B_kpad])

    def kv_matmuls(dirn, vsrc, bv, bank):
        kp = k4(kpad[:, 1024 * dirn:1024 * dirn + 1024])
        for p4 in range(4):
            for hh in range(2):
                h = 2 * p4 + hh
                MM(PS[bank][:, 128 * p4:128 * p4 + 128], kp[:, p4, hh, :], vsrc[:, 128 * h:128 * h + 128], hh == 0, hh == 1, [B_kpad, bv], [PB[bank]])

    S.barrier()
    B_fin = [[Buf(f"fin{h}_{i}") for i in range(8)] for h in range(2)]
    B_fout = [Buf("fout0"), Buf("fout1")]
    B_recd = [Buf(f"recd{t}") for t in range(NT)]
    B_kvd = [Buf(f"kvd{t}") for t in range(NT)]

    def E3(ap):
        return ap.rearrange("p (a e) -> p a e", a=4)

    cdf_bc = bc3(cdp_f[:, 4:8], 128)
    def passA_front1(t):
        tr_part(XH, B_XH[t], 128 * t)

    def passA_front(t):
        slot = t % 2
        rb = recb[slot]

        qk_t, B_qkt = qk2[slot], B_qk2[slot]

        def consume(blk, ps, pb, t=t, rb=rb, slot=slot, qk_t=qk_t, B_qkt=B_qkt):
            if blk < 2:
                CP("act", qk_t[:, 512 * blk:512 * blk + 512], ps[:, :], [pb], [B_qkt])
            elif blk < 4:
                o0 = 3072 + 512 * (blk - 2)
                CP("act", rb[:, o0:o0 + 512], ps[:, :], [pb], [B_rec[slot]])
            elif blk < 6:
                o0 = 4096 + 512 * (blk - 4)
                ACTF(rb[:, o0:o0 + 512], ps[:, :], AF.Silu, [pb], [B_rec[slot]])
            else:
                CP("act", fsb[slot], ps[:, :], [pb], [B_fsb[slot]])
                hh_ = t // 8
                S.dma("sp", f"fst{slot}", fin_d[hh_].ap()[128 * (t % 8):128 * (t % 8) + 128, :], fsb[slot],
                      reads=[B_fsb[slot]], writes=[B_fin[hh_][t % 8]])
        project(XH, B_XH[t], 128 * t, [0, 1, 2, 3, 4, 5, 6], brows, consume)


    def passA_back(t):
        slot = t % 2
        rb = recb[slot]
        qk_t, B_qkt = qk2[slot], B_qk2[slot]
        def g4(ap):
            return ap.rearrange("p (g h c) -> p g h c", g=16, h=2)
        cosb = rope[:, t, 0:32].unsqueeze(1).to_broadcast([128, 32, 32])
        sinb = rope[:, t, 32:64].unsqueeze(1).to_broadcast([128, 16, 32])
        TT("dve", rt1.rearrange("p (g c) -> p g c", g=32), qk_t.rearrange("p (g c) -> p g c", g=32), cosb, ALU.mult, [B_qkt, B_const], [B_rt1])
        TT("dve", g4(rt2)[:, :, 0, :], g4(qk_t)[:, :, 1, :], sinb, ALU.mult, [B_qkt, B_const], [B_rt2])
        TT("dve", g4(rt2)[:, :, 1, :], g4(qk_t)[:, :, 0, :], sinb, ALU.mult, [B_qkt, B_const], [B_rt2])
        TT("dve", g4(rt1)[:, :, 0, :], g4(rt1)[:, :, 0, :], g4(rt2)[:, :, 0, :], ALU.subtract, [B_rt2], [B_rt1])
        TT("dve", g4(rt1)[:, :, 1, :], g4(rt1)[:, :, 1, :], g4(rt2)[:, :, 1, :], ALU.add, [B_rt2], [B_rt1])
        for dirn, sc in ((0, qf_sc), (1, qb_sc)):
            qp = k4(qpad[:, 1024 * dirn:1024 * dirn + 1024])
            qin = rot[:, 0:512].rearrange("p (a h d) -> p a h d", a=4, h=2)
            scv = sc.rearrange("p (a h) -> p a h", h=2)
            for hh in range(2):
                TT("dve", qp[:, :, hh, 64 * hh:64 * hh + 64], qin[:, :, hh, :], scv[:, :, hh].unsqueeze(2).to_broadcast([128, 4, 64]), ALU.mult,
                   [B_rot, B_small], [B_qpad])
        scaled_k(0, rot[:, 512:1024], B_rot, kf_sc)
        scaled_k(1, rot[:, 512:1024], B_rot, kb_sc)
        pst = [psbf(3), psbf(4), psbf(7)]
        for dirn in range(2):
            qp = k4(qpad[:, 1024 * dirn:1024 * dirn + 1024])
            for h in range(8):
                TR(pst[dirn][:, 128 * h:128 * h + 128], qp[:, h // 2, h % 2, :], [B_qpad], [PB[3 + dirn]])
        for dirn in range(2):
            for p4 in range(4):
                c0 = 512 * dirn + 128 * p4
                TR(pst[2][:, 128 * (4 * dirn + p4):128 * (4 * dirn + p4) + 128], ktok[:, c0:c0 + 128], [B_ktok], [PB[7]])
        CP("dve", rb[:, 0:1024], pst[0], [PB[3]], [B_rec[slot]])
        CP("dve", rb[:, 1024:2048], pst[1], [PB[4]], [B_rec[slot]])
        CP("dve", rb[:, 2048:3072], pst[2], [PB[7]], [B_rec[slot]])
        for dirn in range(2):
            kv_matmuls(dirn, rb[:, 3072:4096], B_rec[slot], 5 + dirn)
            CP("act", kvsb[slot][:, 512 * dirn:512 * dirn + 512], PS[5 + dirn][:, :], [PB[5 + dirn]], [B_kvsb[slot]])
        TT("pool", rt1[:, 0:512], Est[:, 0:512], kvsb[slot][:, 0:512], ALU.add, [B_E, B_kvsb[slot]], [B_rt1])
        TT("pool", E3(Est[:, 0:512]), E3(rt1[:, 0:512]), cdf_bc, ALU.mult, [B_rt1, B_small], [B_E])
        cdb_t = bc3(cdp_b[:, 4 * (t + 1):4 * (t + 1) + 4], 128)
        TT("pool", E3(rt1[:, 512:1024]), E3(kvsb[slot][:, 512:1024]), cdb_t, ALU.mult, [B_kvsb[slot], B_small], [B_rt1])
        TT("pool", Est[:, 512:1024], Est[:, 512:1024], rt1[:, 512:1024], ALU.add, [B_rt1, B_E], [B_E])
        S.dma("sp", f"rst{slot}", rec_d[t], rb, reads=[B_rec[slot]], writes=[B_recd[t]])
        S.dma("sp", f"kst{slot}", kv_d[t], kvsb[slot], reads=[B_kvsb[slot]], writes=[B_kvd[t]])
        if t % 8 == 7:
            hh_ = t // 8
            S.custom("pool", f"ccf{hh_}", lambda e, hh_=hh_: e.collective_compute(
                "AllGather", ALU.bypass, replica_groups=GROUPS, ins=[fin_d[hh_].ap().opt()], outs=[fout_d[hh_].ap().opt()]),
                reads=B_fin[hh_], writes=[B_fout[hh_]])


    for t0 in range(3):
        load_x(dr["x_own"][128 * t0:128 * t0 + 128, :], t0)
    norm_part(0, A1bc)
    passA_front1(0)
    norm_part(1, A1bc)
    passA_front(0)
    for t in range(NT):
        if t + 1 < NT:
            passA_front1(t + 1)
        if t + 2 < NT:
            norm_part((t + 2) % 3, A1bc)
        if t + 3 < NT:
            load_x(dr["x_own"][128 * (t + 3):128 * (t + 4), :], t % 3)
        if t + 1 < NT:
            passA_front(t + 1)
        passA_back(t)
    B_stin, B_stout = Buf("stin"), Buf("stout")
    S.dma("sp", "stst", stin_d.ap(), Est, reads=[B_E], writes=[B_stin])
    S.custom("pool", "ccst", lambda e: e.collective_compute(
        "AllGather", ALU.bypass, replica_groups=GROUPS, ins=[stin_d.ap().opt()], outs=[stout_d.ap().opt()]),
        reads=[B_stin], writes=[B_stout])
    S.barrier(exclude=("ccst",))
    S.dma("sp", "bcl", nwt, dr["norm1_w"].partition_broadcast(128), writes=[B_bct])
    S.dma("sp", "bcl", A1c, mod_d[:, 7 * D:8 * D].partition_broadcast(128), reads=[B_modd], writes=[B_bct])
    STT("dve", A1c, A1c, 1.0, nwt, ALU.add, ALU.mult, [B_bct], [B_bct])
    TS("dve", A1c, A1c, 32.0, None, ALU.mult, None, [B_bct], [B_bct])
    bias_rows(1, browsc)
    for tl in range(2):
        load_x(dr["ctx"][128 * tl:128 * tl + 128, :], tl)
    for tl in range(2):
        norm_transpose(tl, A1c, hcT, B_hcT, 128 * tl)

        def consume_ctx(blk, ps, pb):
            if blk == 1:
                CP("act", qk_sb[:, 512:1024], ps[:, :], [pb], [B_qk])
            else:
                CP("act", ctxv[:, 512 * (blk - 2):512 * (blk - 2) + 512], ps[:, :], [pb], [B_ctxv])
        project(hcT, B_hcT, 128 * tl, [1, 2, 3], browsc, consume_ctx)
        scaled_k(0, qk_sb[:, 512:1024], B_qk, cxf_sc[:, 8 * tl:8 * tl + 8])
        scaled_k(1, qk_sb[:, 512:1024], B_qk, cxb_sc[:, 8 * tl:8 * tl + 8])
        for dirn in range(2):
            kv_matmuls(dirn, ctxv, B_ctxv, 5 + dirn)
            dst = sctx[:, 512 * dirn:512 * dirn + 512]
            if tl == 0:
                CP("dve", dst, PS[5 + dirn][:, :], [PB[5 + dirn]], [B_sctx])
            else:
                TT("dve", dst, dst, PS[5 + dirn][:, :], ALU.add, [PB[5 + dirn], B_sctx], [B_sctx])


    S.barrier()
    R1.reset()
    SfT = R1.bf(NT * 512).rearrange("p (n c) -> p n c", n=NT)
    SbT = R1.bf(NT * 512).rearrange("p (n c) -> p n c", n=NT)
    ogT = R1.bf(8 * TOK).rearrange("p (h c) -> p h c", h=8)
    B_ST = Buf("ST")
    B_ogT = [Buf(f"ogT{t}") for t in range(NT)]
    R2.reset()
    kvall = R2.f32(NT * 1024).rearrange("p (n c) -> p n c", n=NT)
    Gst = R2.f32(4096).rearrange("p (r c) -> p r c", r=4)
    curf, curb = R2.f32(512), R2.f32(512)
    tmpf, tmpb = R2.f32(512), R2.f32(512)
    B_kvall, B_G = Buf("kvall"), Buf("G")
    B_curf, B_curb, B_tmpf, B_tmpb = Buf("curf"), Buf("curb"), Buf("tmpf"), Buf("tmpb")
    CP("dve", curf, sctx[:, 0:512], [B_sctx], [B_curf])
    CP("dve", curb, sctx[:, 512:1024], [B_sctx], [B_curb])
    S.dma("sp", "kvld", kvall, kv_d.rearrange("n p c -> p n c"), reads=B_kvd, writes=[B_kvall])
    S.dma("sp", "gld", Gst, stout_d.ap().rearrange("(r p) c -> p r c", p=128), reads=[B_stout], writes=[B_G])
    cd16f = bc3(cdp_f[:, 64:68], 128)
    cd16b = bc3(cdp_b[:, 64:68], 128)
    TS("dve", tmpf, curf, sel[:, 0:1], None, ALU.mult, None, [B_curf, B_const], [B_tmpf])
    for r in range(3):
        TT("dve", E3(curf), E3(curf), cd16f, ALU.mult, [B_curf, B_small], [B_curf])
        TT("dve", curf, curf, Gst[:, r, 0:512], ALU.add, [B_curf, B_G], [B_curf])
        STT("dve", tmpf, curf, sel[:, r + 1:r + 2], tmpf, ALU.mult, ALU.add, [B_curf, B_tmpf, B_const], [B_tmpf])
    TS("dve", tmpb, curb, sel[:, 7:8], None, ALU.mult, None, [B_curb, B_const], [B_tmpb])
    for r in (3, 2, 1):
        TT("dve", E3(curb), E3(curb), cd16b, ALU.mult, [B_curb, B_small], [B_curb])
        TT("dve", curb, curb, Gst[:, r, 512:1024], ALU.add, [B_curb, B_G], [B_curb])
        STT("dve", tmpb, curb, sel[:, 4 + r - 1:4 + r], tmpb, ALU.mult, ALU.add, [B_curb, B_tmpb, B_const], [B_tmpb])
    cdb_bc = bc3(cdp_b[:, 4:8], 128)
    for n in range(NT):
        m_ = NT - 1 - n
        CP("act", SfT[:, n, :], tmpf, [B_tmpf], [B_ST])
        TT("dve", curf, tmpf, kvall[:, n, 0:512], ALU.add, [B_tmpf, B_kvall], [B_curf])
        TT("dve", E3(tmpf), E3(curf), cdf_bc, ALU.mult, [B_curf, B_small], [B_tmpf])
        CP("act", SbT[:, m_, :], tmpb, [B_tmpb], [B_ST])
        TT("dve", curb, tmpb, kvall[:, m_, 512:1024], ALU.add, [B_tmpb, B_kvall], [B_curb])
        TT("dve", E3(tmpb), E3(curb), cdb_bc, ALU.mult, [B_curb, B_small], [B_tmpb])
    S.barrier()
    if STOP == "scan":
        S.emit(); return nc

    R2.reset()
    rbB = [R2.bf(5120) for _ in range(2)]
    Pm = [R2.bf(512) for _ in range(2)]
    sq = R2.f32(1024)
    tcen = R2.f32(1024)
    praw = [R2.bf(512) for _ in range(2)]
    ogtok = R2.bf(1024)
    B_rbB = [Buf("rbB0"), Buf("rbB1")]
    B_Pm = [Buf("Pm0"), Buf("Pm1")]
    B_sq, B_tcen, B_ogtok, B_gst = Buf("sq"), Buf("tcen"), Buf("ogtok"), Buf("gst")
    B_praw = [Buf("praw0"), Buf("praw1")]
    mask3 = mask[:, :, :]
    S.dma("sp", "rld0", rbB[0], rec_d[0], reads=[B_recd[0]], writes=[B_rbB[0]])
    obanks = [(5, 6), (3, 4)]

    def passB_mm(n):
        slot = n % 2
        rb = rbB[slot]

        def qT(di, h):
            return rb[:, (8 * di + h) * 128:(8 * di + h) * 128 + 128]

        def kT(di, p4):
            c0 = 2048 + (4 * di + p4) * 128
            return rb[:, c0:c0 + 128]
        def scores(p4):
            sbank = 1 + (p4 % 2)
            psc = PS[sbank][:, :].rearrange("p (a c) -> p a c", a=4)
            for di in range(2):
                for hh in range(2):
                    MM(psc[:, 2 * di + hh, :], kT(di, p4), qT(di, 2 * p4 + hh), True, True, [B_rbB[slot]], [PB[sbank]])
            pm = Pm[p4 % 2]
            CP("act", praw[p4 % 2], PS[sbank][:, :], [PB[sbank]], [B_praw[p4 % 2]])
            TT("pool", pm.rearrange("p (a c) -> p a c", a=4), praw[p4 % 2].rearrange("p (a c) -> p a c", a=4), mask3, ALU.mult,
               [B_praw[p4 % 2], B_const], [B_Pm[p4 % 2]])

        def omm(p4):
            pm3 = Pm[p4 % 2].rearrange("p (a c) -> p a c", a=4)
            obank = obanks[n % 2][p4 // 2]
            for hh in range(2):
                h = 2 * p4 + hh
                od = PS[obank][:, 128 * (h % 4):128 * (h % 4) + 128]
                vv = rb[:, 3072 + 128 * h:3072 + 128 * h + 128]
                MM(od, pm3[:, hh, :], vv, True, False, [B_Pm[p4 % 2], B_rbB[slot]], [PB[obank]])
                MM(od, qT(0, h), SfT[:, n, 128 * p4:128 * p4 + 128], False, False, [B_rbB[slot], B_ST], [PB[obank]])
                MM(od, pm3[:, 2 + hh, :], vv, False, False, [B_Pm[p4 % 2], B_rbB[slot]], [PB[obank]])
                MM(od, qT(1, h), SbT[:, n, 128 * p4:128 * p4 + 128], False, True, [B_rbB[slot], B_ST], [PB[obank]])
        scores(0)
        scores(1)
        omm(0)
        scores(2)
        omm(1)
        scores(3)
        omm(2)
        omm(3)

    def passB_gn(n):
        slot = n % 2
        rb = rbB[slot]
        ob = obanks[n % 2]
        o3s = [PS[ob[half]][:, :].rearrange("p (h e) -> p h e", h=4) for half in range(2)]
        for half in range(2):
            hs = slice(4 * half, 4 * half + 4)
            S.op("dve", lambda e, o3=o3s[half], hs=hs: e.tensor_reduce(out=gst[:, hs], in_=o3, axis=AX.X, op=ALU.add), [PB[ob[half]]], [B_gst])
            ACTF(sq[:, 512 * half:512 * half + 512], PS[ob[half]][:, :], AF.Square, [PB[ob[half]]], [B_sq])
        TS("dve", gst[:, 16:24], gst[:, 0:8], 1.0 / 128.0, None, ALU.mult, None, [B_gst], [B_gst])
        for half in range(2):
            TT("dve", tcen[:, 512 * half:512 * half + 512].rearrange("p (h e) -> p h e", h=4), o3s[half], bc3(gst[:, 16 + 4 * half:20 + 4 * half], 128),
               ALU.subtract, [PB[ob[half]], B_gst], [B_tcen])
        TT("dve", tcen, tcen, rb[:, 4096:5120], ALU.mult, [B_tcen, B_rbB[slot]], [B_tcen])
        S.op("dve", lambda e: e.tensor_reduce(out=gst[:, 8:16], in_=sq.rearrange("p (h e) -> p h e", h=8), axis=AX.X, op=ALU.add), [B_sq], [B_gst])
        TT("dve", gst[:, 24:32], gst[:, 16:24], gst[:, 16:24], ALU.mult, [B_gst], [B_gst])
        STT("dve", gst[:, 32:40], gst[:, 8:16], 1.0 / 128.0, gst[:, 24:32], ALU.mult, ALU.subtract, [B_gst], [B_gst])
        RSQ(gst[:, 40:48], gst[:, 32:40], EPS, [B_gst], [B_gst])
        TT("dve", ogtok.rearrange("p (h e) -> p h e", h=8), tcen.rearrange("p (h e) -> p h e", h=8), bc3(gst[:, 40:48], 128), ALU.mult,
           [B_tcen, B_gst], [B_ogtok])
        pst = psbf(0)
        for h in range(8):
            TR(pst[:, 128 * h:128 * h + 128], ogtok[:, 128 * h:128 * h + 128], [B_ogtok], [PB[0]])
        CP("act", ogT[:, :, 128 * n:128 * n + 128], pst.rearrange("p (h c) -> p h c", h=8), [PB[0]], [B_ogT[n]])
        if n + 2 < NT:
            S.dma("sp", f"rld{n % 2}", rbB[n % 2], rec_d[n + 2], reads=[B_recd[n + 2]], writes=[B_rbB[n % 2]])

    S.dma("sp", "rld1", rbB[1], rec_d[1], reads=[B_recd[1]], writes=[B_rbB[1]])
    passB_mm(0)
    for n in range(NT):
        if n + 1 < NT:
            passB_mm(n + 1)
        passB_gn(n)
    S.barrier()
    if stage == 1:
        S.dma("sp", "dbg", dbg_d.rearrange("p (h c) -> p h c", h=8), ogT, reads=B_ogT)
        S.barrier()
        S.emit()
        return nc

    zT = R1t[:, 0:4 * TOK].rearrange("p (g c) -> p g c", g=4)
    B_zT = Buf("zT")
    R2.reset()
    F1 = [R2.bf(16384, parts=64).rearrange("p (n c) -> p n c", n=128) for _ in range(1)]
    Aall = R2.bf(16384).rearrange("p (c k) -> p c k", c=128)
    tw = R2.bf(64 * 128).rearrange("p (k t c) -> p k t c", k=64, t=2)
    Xsb = [R2.bf(512).rearrange("p (k t c) -> p k t c", k=8, t=2) for _ in range(2)]
    B_F1, B_A, B_tw = Buf("F1"), Buf("A"), Buf("tw")
    B_Xsb = [Buf("Xsb0"), Buf("Xsb1")]
    S.dma("sp", "twld", tw, dr["tw"], writes=[B_tw])
    wst_s = R2.f32(1024).rearrange("p (k c) -> p k c", k=8)
    wmb_s = [R2.bf(1024).rearrange("p (k c) -> p k c", k=8) for _ in range(2)]
    bmod_s = [R2.f32(128, parts=1) for _ in range(2)]
    row_s = R2.f32(128, parts=1)
    B_wsts, B_rows = Buf("wsts"), Buf("rows")
    B_wmbs = [Buf("wmbs0"), Buf("wmbs1")]
    B_bms = [Buf("bms0"), Buf("bms1")]

    def mod_prep(j_):
        c0 = 4 * 512 + 128 * j_
        S.dma("sp", "wsts", wst_s, wmod_v[:, :, c0:c0 + 128], writes=[B_wsts])
        S.dma("sp", f"bms{j_ % 2}", bmod_s[j_ % 2], dr["b_mod"][:, c0:c0 + 128], writes=[B_bms[j_ % 2]])
        CP("pool", wmb_s[j_ % 2], wst_s, [B_wsts], [B_wmbs[j_ % 2]])

    def mod_mm(j_):
        c0 = 4 * 512 + 128 * j_
        for k in range(8):
            MM(PS[7][0:1, 0:128], silcb[:, k:k + 1], wmb_s[j_ % 2][:, k, :], k == 0, k == 7, [B_small, B_wmbs[j_ % 2]], [PB[7]])
        TT("dve", row_s, PS[7][0:1, 0:128], bmod_s[j_ % 2], ALU.add, [PB[7], B_bms[j_ % 2]], [B_rows])
        S.dma("sp", "rowss", mod_d[:, c0:c0 + 128], row_s, reads=[B_rows], writes=[B_modd])

    def mod_subblock(j_):
        if j_ == 0:
            mod_prep(0)
        if j_ + 1 < 32:
            mod_prep(j_ + 1)
        mod_mm(j_)

    nsub = [0]
    for g in range(4):
        for h2 in range(2):
            src = fout_d[h2].ap().rearrange("(q n) c -> q n c", n=128)[:, :, 128 * g:128 * g + 128]
            S.dma("sp", "f1ld", F1[0][32 * h2:32 * h2 + 32, :, :], src, reads=[B_fout[h2]], writes=[B_F1])
        for c4 in range(32):
            if c4 % 8 == 0 and nsub[0] < 32:
                mod_subblock(nsub[0])
                nsub[0] += 1
            bank = 1 + (c4 % 2)
            for cc in range(4):
                ch = 4 * c4 + cc
                MM(PS[bank][:, 128 * cc:128 * cc + 128], F1[0][:, :, ch], e64[:, :], True, True, [B_F1, B_const], [PB[bank]])
            CP("act" if c4 % 2 == 0 else "dve", Aall[:, 4 * c4:4 * c4 + 4, :], PS[bank][:, :].rearrange("p (c k) -> p c k", c=4), [PB[bank]], [B_A])
        for kb in range(8):
            if kb % 2 == 0 and nsub[0] < 32:
                mod_subblock(nsub[0])
                nsub[0] += 1
            xb = 3 + (kb % 2)
            px = PS[xb][:, :].rearrange("p (k t c) -> p k t c", k=8, t=2)
            for kk in range(8):
                k1 = 8 * kb + kk
                ar = Aall[:, :, k1]
                ai = Aall[:, :, 64 + k1]
                pxk = PS[xb][:, 64 * kk:64 * kk + 64]
                MM(pxk, ar, tw[:, k1, 0, :], True, False, [B_A, B_tw], [PB[xb]])
                MM(pxk, ai, tw[:, k1, 1, :], False, True, [B_A, B_tw], [PB[xb]])
            xs = Xsb[kb % 2]
            CP("act", xs, px, [PB[xb]], [B_Xsb[kb % 2]])
            zb = 5 + (kb % 2)
            pz = PS[zb][:, 0:256].rearrange("p (k c) -> p k c", k=8)
            MM(pz, c128[:, 0, :], xs[:, :, 0, :], True, False, [B_Xsb[kb % 2], B_const], [PB[zb]])
            MM(pz, c128[:, 1, :], xs[:, :, 1, :], False, True, [B_Xsb[kb % 2], B_const], [PB[zb]])
            zdst = zT[:, g, :].rearrange("p (b a) -> p a b", a=64)[:, 8 * kb:8 * kb + 8, :]
            CP("dve", zdst, pz, [PB[zb]], [B_zT])
    S.barrier()
    col_layout(shcol[:, 8:16], mod_d[:, 3 * D:4 * D].rearrange("o (c p) -> (o c) p", p=128), 8)
    CP("dve", shcolb[:, 8:16], shcol[:, 8:16], [B_const], [B_small])
    if stage == 2:
        S.dma("sp", "dbg", dbg_d[:, 0:4 * TOK], R1t[:, 0:4 * TOK], reads=[B_zT])
        S.barrier()
        S.emit()
        return nc

    Wout = R2t[:, 37888:46080].rearrange("p (k c) -> p k c", k=8)
    B_wout = Buf("wout")
    wout_v = dr["w_out"].rearrange("(k p) c -> p k c", p=128)
    R2.reset()
    mergedT = R2.bf(8 * TOK).rearrange("p (k c) -> p k c", k=8)
    wsl = [R2.bf(28 * 128).rearrange("p (k c) -> p k c", k=28) for _ in range(2)]
    gts = [R2.f32(512) for _ in range(4)]
    m12 = [R2.f32(512) for _ in range(2)]
    B_wsl = [Buf("wsl0"), Buf("wsl1")]
    B_gts = [Buf(f"gt{i}") for i in range(4)]
    B_m12 = [Buf("m1"), Buf("m2")]
    B_mg = [Buf(f"mg{i}") for i in range(4)]
    wro_v = dr["w_ret_out"].rearrange("(k p) c -> p k c", p=128)
    wfo_v = dr["w_four_out"].rearrange("(k p) c -> p k c", p=128)
    wbg_v = dr["w_bg"].rearrange("(k p) c -> p k c", p=128)

    def load_wsl(oc):
        i = oc % 2
        cs = slice(128 * oc, 128 * oc + 128)
        S.dma("pool", f"wsl{i}", wsl[i][:, 0:8, :], wro_v[:, :, cs], writes=[B_wsl[i]])
        S.dma("pool", f"wsl{i}", wsl[i][:, 8:12, :], wfo_v[:, :, cs], writes=[B_wsl[i]])
        S.dma("pool", f"wsl{i}", wsl[i][:, 12:20, :], wbg_v[:, :, cs], writes=[B_wsl[i]])
        S.dma("pool", f"wsl{i}", wsl[i][:, 20:28, :], wbg_v[:, :, D + 128 * oc:D + 128 * oc + 128], writes=[B_wsl[i]])
    load_wsl(0)
    for k in range(8):
        S.dma("pool", "wout", Wout[:, k, :], wout_v[:, k, :], writes=[B_wout])
    B_bbg = Buf("bbg")
    for oc in range(8):
        i = oc % 2
        if oc + 1 < 8:
            load_wsl(oc + 1)
        w = wsl[i]
        for gi in range(2):
            for k in range(8):
                MM(PS[7][:, gi:gi + 1], w[:, 12 + 8 * gi + k, :], shcolb[:, k:k + 1], k == 0, k == 7, [B_wsl[i], B_small], [PB[7]])
        bcol = bbg.rearrange("p (g c) -> p g c", g=2)[:, :, oc]
        TT("dve", bcol, bcol, PS[7][:, 0:2], ALU.add, [PB[7], B_const], [B_bbg])
        for tb in range(4):
            ts_ = slice(512 * tb, 512 * tb + 512)
            xh_b = [B_XH[4 * tb + q] for q in range(4)]
            og_b = [B_ogT[4 * tb + q] for q in range(4)]
            for k in range(8):
                MM(PS[1][:, :], w[:, k, :], ogT[:, k, ts_], k == 0, k == 7, [B_wsl[i]] + og_b, [PB[1]])
            for k in range(4):
                MM(PS[2][:, :], w[:, 8 + k, :], zT[:, k, ts_], k == 0, k == 3, [B_wsl[i], B_zT], [PB[2]])
            for gi in range(2):
                for k in range(8):
                    MM(PS[3 + gi][:, :], w[:, 12 + 8 * gi + k, :], XH[:, k, ts_], k == 0, k == 7, [B_wsl[i]] + xh_b, [PB[3 + gi]])
            g0 = 2 * (tb % 2)
            for gi in range(2):
                ACTF(gts[g0 + gi], PS[3 + gi][:, :], AF.Sigmoid, [PB[3 + gi], B_bbg], [B_gts[g0 + gi]], bias=bbg[:, 8 * gi + oc:8 * gi + oc + 1])
            TT("dve", m12[0], gts[g0], PS[1][:, :], ALU.mult, [B_gts[g0], PB[1]], [B_m12[0]])
            TT("dve", m12[1], gts[g0 + 1], PS[2][:, :], ALU.mult, [B_gts[g0 + 1], PB[2]], [B_m12[1]])
            TT("pool", mergedT[:, oc, ts_], m12[0], m12[1], ALU.add, [B_m12[0], B_m12[1]], [B_mg[tb]])
    S.barrier()

    R2.reset(8 * TOK)
    xt2 = [R2.f32(1024) for _ in range(2)]
    x1t = [R2.f32(1024) for _ in range(2)]
    tmpy = R2.f32(1024)
    g1bc = R2.f32(1024)
    A2bc = R2.f32(1024)
    nw2t = R2.f32(1024)
    hb2 = R2.bf(1024)
    junk2 = R2.bf(1024)
    B_xt2 = [Buf("xt2_0"), Buf("xt2_1")]
    B_x1t = [Buf("x1t0"), Buf("x1t1")]
    B_tmpy, B_hb2, B_junk2, B_bc2 = Buf("tmpy"), Buf("hb2"), Buf("junk2"), Buf("bc2")
    B_x1d = [Buf(f"x1d{t}") for t in range(NT)]
    S.dma("sp", "bcl", g1bc, mod_d[:, 2 * D:3 * D].partition_broadcast(128), writes=[B_bc2])
    S.dma("sp", "bcl", A2bc, mod_d[:, 4 * D:5 * D].partition_broadcast(128), writes=[B_bc2])
    S.dma("sp", "bcl", nw2t, dr["norm2_w"].partition_broadcast(128), writes=[B_bc2])
    STT("dve", A2bc, A2bc, 1.0, nw2t, ALU.add, ALU.mult, [B_bc2], [B_bc2])
    TS("dve", A2bc, A2bc, 32.0, None, ALU.mult, None, [B_bc2], [B_bc2])
    Wd = R1t[:, 0:22 * D].rearrange("p (k c) -> p k c", k=22)
    B_wd = Buf("wd")
    wd_v = dr["w_down"].rearrange("(k p) c -> p k c", p=128)
    for k in range(22):
        S.dma("pool", "wd", Wd[:, k, :], wd_v[:, k, :], writes=[B_wd])
    S.dma("sp", "xt2_0", xt2[0], dr["x_own"][0:128, :], writes=[B_xt2[0]])
    ybanks = [(1, 2), (3, 4)]

    def y_mm(t):
        for half in range(2):
            bk = ybanks[t % 2][half]
            for k in range(8):
                MM(PS[bk][:, :], mergedT[:, k, 128 * t:128 * t + 128], Wout[:, k, 512 * half:512 * half + 512], k == 0, k == 7,
                   [B_mg[t // 4], B_wout], [PB[bk]])

    def y_post(t):
        slot = t % 2
        if t + 1 < NT:
            S.dma("sp", f"xt2_{(t + 1) % 2}", xt2[(t + 1) % 2], dr["x_own"][128 * (t + 1):128 * (t + 2), :], writes=[B_xt2[(t + 1) % 2]])
        for half in range(2):
            bk = ybanks[t % 2][half]
            hs = slice(512 * half, 512 * half + 512)
            TT("dve", tmpy[:, hs], PS[bk][:, :], g1bc[:, hs], ALU.mult, [PB[bk], B_bc2], [B_tmpy])
        TT("dve", x1t[slot], tmpy, xt2[slot], ALU.add, [B_tmpy, B_xt2[slot]], [B_x1t[slot]])
        S.dma("sp", f"x1st{slot}", x1_d[128 * t:128 * t + 128, :], x1t[slot], reads=[B_x1t[slot]], writes=[B_x1d[t]])
        ACTF(junk2, x1t[slot], AF.Square, [B_x1t[slot]], [B_junk2, B_ss], accum_out=ss_t[:, 2:3])
        RSQ(ss_t[:, 3:4], ss_t[:, 2:3], 1024.0 * EPS, [B_ss], [B_ss])
        STT("dve", hb2, x1t[slot], ss_t[:, 3:4], A2bc, ALU.mult, ALU.mult, [B_x1t[slot], B_ss, B_bc2], [B_hb2])
        pst = psbf(0)
        for k in range(8):
            TR(pst[:, 128 * k:128 * k + 128], hb2[:, 128 * k:128 * k + 128], [B_hb2], [PB[0]])
        CP("act", XH[:, :, 128 * t:128 * t + 128], pst.rearrange("p (k c) -> p k c", k=8), [PB[0]], [B_XH[t]])

    y_mm(0)
    for t in range(NT):
        if t + 1 < NT:
            y_mm(t + 1)
        y_post(t)
    S.barrier()
    if stage == 3:
        for t in range(NT):
            S.dma("sp", "xt2_0", xt2[0], x1_d[128 * t:128 * t + 128, :], reads=[B_x1d[t]], writes=[B_xt2[0]])
            S.dma("sp", "outst", out_d[128 * t:128 * t + 128, :], xt2[0], reads=[B_xt2[0]], writes=[])
        S.dma("sp", "dbg", dbg_d, R2t[:, 0:8 * TOK], reads=B_mg)
        S.barrier()
        S.emit()
        return nc

    R2.reset()
    USE_GELU_ACT = not os.environ.get("KDBG_GELU_SIG")
    mT = R2.bf(22 * 512).rearrange("p (k c) -> p k c", k=22)
    wup = [R2.bf(2048).rearrange("p (a k c) -> p a k c", a=2, k=8) for _ in range(3)]
    cacc = [[R2.f32(512) for _ in range(2)] for _ in range(2)]
    ga = [R2.f32(512) for _ in range(2)]
    ub = R2.f32(NCH * 8).rearrange("p (c t) -> p c t", c=NCH)
    hal = R2.f32(2 * NCH).rearrange("p (s c) -> p s c", s=2)
    hsend = R2.f32(128)
    hall = R2.f32(512).rearrange("p (r c) -> p r c", r=4)
    kcc = R2.f32(3 * NCH).rearrange("p (s c) -> p s c", s=3)
    x1r = [R2.f32(1024) for _ in range(2)]
    x2t = R2.f32(1024)
    g2bc = R2.f32(1024)
    fwbc = R2.f32(1024)
    junk3 = R2.bf(1024)
    B_wup = [Buf(f"wup{i}") for i in range(3)]
    B_cacc = [[Buf(f"ca{s_}{a}") for a in range(2)] for s_ in range(2)]
    B_ga = [Buf("ga0"), Buf("ga1")]
    B_mT, B_ub, B_hal, B_hsend, B_hall, B_kcc = Buf("mT"), Buf("ub"), Buf("hal"), Buf("hsend"), Buf("hall"), Buf("kcc")
    B_x1r = [Buf("x1r0"), Buf("x1r1")]
    B_x2t, B_bc3, B_junk3 = Buf("x2t"), Buf("bc3"), Buf("junk3")
    B_hin, B_hout = Buf("hin"), Buf("hout")
    S.dma("sp", "bcl", g2bc, mod_d[:, 5 * D:6 * D].partition_broadcast(128), writes=[B_bc3])
    S.dma("sp", "bcl", fwbc, dr["final_norm_w"].partition_broadcast(128), writes=[B_bc3])
    wup_v = dr["w_up"].rearrange("(k p) c -> p k c", p=128)
    nload = [0]

    def load_wup(i):
        s_ = nload[0] % 3
        nload[0] += 1
        S.dma("pool", f"wup{s_}", wup[s_][:, 0, :, :], wup_v[:, :, 128 * i:128 * i + 128], writes=[B_wup[s_]])
        S.dma("pool", f"wup{s_}", wup[s_][:, 1, :, :], wup_v[:, :, FFN + 128 * i:FFN + 128 * i + 128], writes=[B_wup[s_]])
        return s_

    xb8 = XH[:, :, :].rearrange("p k (b c) -> p k b c", c=512)
    bnd = R2.bf(72).rearrange("p (k c) -> p k c", k=8)
    B_bnd = Buf("bnd")
    CP("dve", bnd[:, :, 0], shcolb[:, 8:16], [B_small], [B_bnd])
    CP("dve", bnd[:, :, 1:5], xb8[:, :, :, 0], B_XH, [B_bnd])
    CP("dve", bnd[:, :, 5:9], xb8[:, :, :, 511], B_XH, [B_bnd])
    B_bup = Buf("bup")
    TT("dve", kcc[:, 0, :], convc[:, 0, :], convc[:, 1, :], ALU.add, [B_const], [B_kcc])
    TT("dve", kcc[:, 0, :], kcc[:, 0, :], convc[:, 2, :], ALU.add, [B_const, B_kcc], [B_kcc])
    S.op("pool", lambda e: e.memset(hsend, 0.0), writes=[B_hsend])

    def halo_exchange():
        CP("dve", hsend[:, 0:NCH], ub[:, :, 0], [B_ub], [B_hsend])
        CP("dve", hsend[:, NCH:2 * NCH], ub[:, :, 7], [B_ub], [B_hsend])
        S.dma("sp", "hst", hin_d.ap(), hsend, reads=[B_hsend], writes=[B_hin])
        S.custom("pool", "cch", lambda e: e.collective_compute(
            "AllGather", ALU.bypass, replica_groups=GROUPS, ins=[hin_d.ap().opt()], outs=[hout_d.ap().opt()]),
            reads=[B_hin], writes=[B_hout])
        S.dma("sp", "hld", hall, hout_d.ap().rearrange("(r p) c -> p r c", p=128), reads=[B_hout], writes=[B_hall])
        for side, (c0, s0) in enumerate(((NCH, 8), (0, 12))):
            TS("dve", hal[:, side, :], hall[:, 0, c0:c0 + NCH], sel[:, s0:s0 + 1], None, ALU.mult, None, [B_hall, B_const], [B_hal])
            for r in range(1, 4):
                STT("dve", hal[:, side, :], hall[:, r, c0:c0 + NCH], sel[:, s0 + r:s0 + r + 1], hal[:, side, :], ALU.mult, ALU.add,
                    [B_hall, B_hal, B_const], [B_hal])
            STT("dve", hal[:, side, :], kcc[:, 2, :], sel[:, 16 + side:17 + side], hal[:, side, :], ALU.mult, ALU.add, [B_kcc, B_hal, B_const], [B_hal])

    for tb in (1, 2, 0, 3):
        ts_ = slice(512 * tb, 512 * tb + 512)
        xh_b = [B_XH[4 * tb + q] for q in range(4)]
        pend = [load_wup(0), load_wup(1)]
        S.dma("sp", "x1r0", x1r[0], x1_d[512 * tb:512 * tb + 128, :], reads=[B_x1d[4 * tb]], writes=[B_x1r[0]])
        for i in range(22):
            s_ = pend.pop(0)
            if i + 2 < 22:
                pend.append(load_wup(i + 2))
            us = i % 2
            for a in range(2):
                ch = i + 22 * a
                bank = 1 + 2 * us + a
                if tb == 1:
                    for k in range(8):
                        MM(PS[7][:, 0:9], wup[s_][:, a, k, :], bnd[:, k, :], k == 0, k == 7, [B_wup[s_], B_bnd], [PB[7]])
                    CP("act", ub[:, ch, :].rearrange("p (b e) -> p e b", e=2), PS[7][:, 1:9].rearrange("p (e b) -> p e b", e=2), [PB[7]], [B_ub])
                    CP("dve", bup[:, ch:ch + 1], PS[7][:, 0:1], [PB[7]], [B_bup])
                    STT("dve", kcc[:, 1, ch:ch + 1], bup[:, ch:ch + 1], kcc[:, 0, ch:ch + 1], convc[:, 3, ch:ch + 1], ALU.mult, ALU.add,
                        [B_bup, B_kcc, B_const], [B_kcc])
                    TS("dve", kcc[:, 2, ch:ch + 1], bup[:, ch:ch + 1], -1.0, None, ALU.mult, None, [B_bup], [B_kcc])
                for k in range(8):
                    MM(PS[bank][:, :], wup[s_][:, a, k, :], XH[:, k, ts_], k == 0, k == 7, [B_wup[s_]] + xh_b, [PB[bank]])
                ca = cacc[us][a]
                bca = B_cacc[us][a]
                w0c, w1c, w2c = convc[:, 0, ch:ch + 1], convc[:, 1, ch:ch + 1], convc[:, 2, ch:ch + 1]
                ACTF(ca, PS[bank][:, :], AF.Identity, [PB[bank], B_kcc, B_const], [bca], scale=w1c, bias=kcc[:, 1, ch:ch + 1])
                STT("dve", ca[:, 1:512], PS[bank][:, 0:511], w0c, ca[:, 1:512], ALU.mult, ALU.add, [PB[bank], bca, B_const], [bca])
                STT("dve", ca[:, 0:511], PS[bank][:, 1:512], w2c, ca[:, 0:511], ALU.mult, ALU.add, [PB[bank], bca, B_const], [bca])
                if tb == 0:
                    pl, bpl = hal[:, 0, ch:ch + 1], B_hal
                else:
                    pl, bpl = ub[:, ch, 2 * tb - 1:2 * tb], B_ub
                if tb == 3:
                    pr, bpr = hal[:, 1, ch:ch + 1], B_hal
                else:
                    pr, bpr = ub[:, ch, 2 * tb + 2:2 * tb + 3], B_ub
                STT("dve", ca[:, 0:1], pl, w0c, ca[:, 0:1], ALU.mult, ALU.add, [bpl, bca, B_const], [bca])
                STT("dve", ca[:, 511:512], pr, w2c, ca[:, 511:512], ALU.mult, ALU.add, [bpr, bca, B_const], [bca])
            ca, cv = cacc[us][0], cacc[us][1]
            if USE_GELU_ACT:
                ACTF(ga[us], ca, AF.Gelu_apprx_tanh, [B_cacc[us][0]], [B_ga[us]])
            else:
                TT("dve", ga[us], ca, ca, ALU.mult, [B_cacc[us][0]], [B_ga[us]])
                TS("dve", ga[us], ga[us], 0.044715, 1.0, ALU.mult, ALU.add, [B_ga[us]], [B_ga[us]])
                TT("dve", ga[us], ga[us], ca, ALU.mult, [B_ga[us], B_cacc[us][0]], [B_ga[us]])
                ACTF(ga[us], ga[us], AF.Sigmoid, [B_ga[us]], [B_ga[us]], scale=GELU_C)
                TT("dve", ga[us], ga[us], ca, ALU.mult, [B_ga[us], B_cacc[us][0]], [B_ga[us]])
            TT("dve", mT[:, i, :], cv, ga[us], ALU.mult, [B_cacc[us][1], B_ga[us]], [B_mT])
        if tb == 1:
            halo_exchange()
        for q in range(4):
            t = 4 * tb + q
            slot = q % 2
            if q + 1 < 4:
                S.dma("sp", f"x1r{(q + 1) % 2}", x1r[(q + 1) % 2], x1_d[128 * (t + 1):128 * (t + 2), :], reads=[B_x1d[t + 1]], writes=[B_x1r[(q + 1) % 2]])
            for half in range(2):
                for k in range(22):
                    MM(PS[5 + half][:, :], mT[:, k, 128 * q:128 * q + 128], Wd[:, k, 512 * half:512 * half + 512], k == 0, k == 21,
                       [B_mT, B_wd], [PB[5 + half]])
            for half in range(2):
                hs = slice(512 * half, 512 * half + 512)
                TT("dve", x2t[:, hs], PS[5 + half][:, :], g2bc[:, hs], ALU.mult, [PB[5 + half], B_bc3], [B_x2t])
            TT("dve", x2t, x2t, x1r[slot], ALU.add, [B_x2t, B_x1r[slot]], [B_x2t])
            ACTF(junk3, x2t, AF.Square, [B_x2t], [B_junk3, B_ss], accum_out=ss_t[:, 4:5])
            RSQ(ss_t[:, 5:6], ss_t[:, 4:5], 1024.0 * EPS, [B_ss], [B_ss])
            TS("dve", ss_t[:, 5:6], ss_t[:, 5:6], 32.0, None, ALU.mult, None, [B_ss], [B_ss])
            STT("dve", x1r[slot], x2t, ss_t[:, 5:6], fwbc, ALU.mult, ALU.mult, [B_x2t, B_ss, B_bc3], [B_x1r[slot]])
            S.dma("sp", f"ost{slot}", out_d[128 * t:128 * t + 128, :], x1r[slot], reads=[B_x1r[slot]], writes=[])
    S.barrier()
    S.emit()
    return nc


_CACHE = {}


def _in_maps(inputs):
    g = lambda k: np.asarray(inputs[k], dtype=np.float32)
    x, c, ctx, c_ctx = g("x"), g("c"), g("ctx"), g("c_ctx")
    shared = {
        "w_mod": g("w_mod")[0], "b_mod": g("b_mod"), "norm1_w": g("norm1_w"), "w_in": g("w_in")[0],
        "a_f": g("ret_decay_f"), "a_b": g("ret_decay_b"), "w_ret_out": g("w_ret_out")[0],
        "w_four_out": g("w_four_out")[0], "w_bg": g("w_branch_gate")[0], "b_bg": g("b_branch_gate"),
        "w_out": g("w_out")[0], "norm2_w": g("norm2_w"), "w_up": g("w_up")[0], "conv_w": g("conv_w")[0],
        "conv_b": g("conv_b"), "w_down": g("w_down")[0], "final_norm_w": g("final_norm_w").reshape(1, D),
        "cc_col": np.ascontiguousarray(c_ctx.reshape(8, 128).T),
    }
    shared = {k: np.ascontiguousarray(v) for k, v in shared.items()}
    consts = [host_consts(j) for j in range(4)]
    maps = []
    for core in range(NCORES):
        b, j = core // 4, core % 4
        m = dict(shared)
        m["x_own"] = np.ascontiguousarray(x[b, TOK * j:TOK * (j + 1)])
        m["ctx"] = np.ascontiguousarray(ctx[b])
        m["c_col"] = np.ascontiguousarray(c[b].reshape(8, 128).T)
        m.update(consts[j])
        maps.append(m)
    return maps


def kernel(**inputs):
    if "nc" not in _CACHE:
        _CACHE["nc"] = build_program(4)
    nc = _CACHE["nc"]
    res = run_bass_kernel_spmd(nc, _in_maps(inputs), core_ids=list(range(NCORES)))
    out = np.empty((NB, SEQ, D), np.float32)
    for core in range(NCORES):
        b, j = core // 4, core % 4
        out[b, TOK * j:TOK * (j + 1)] = np.asarray(res.results[core]["out"], dtype=np.float32)
    return out
```

```python
import numpy as np
import ml_dtypes
from contextlib import ExitStack

import concourse.bass as bass
import concourse.mybir as mybir
from concourse.bass_utils import run_bass_kernel_spmd

F32 = mybir.dt.float32
BF16 = mybir.dt.bfloat16
ALU = mybir.AluOpType
AF = mybir.ActivationFunctionType
AX = mybir.AxisListType

D = 1024
SEQ = 8192
NB = 2
NCORES = 8
TOK = 2048
NT = 16
CTX = 256
H = 8
INC = 3584
FFN = 2816
NCH = 44
EPS = 1e-6
GROUPS = [[0, 1, 2, 3], [4, 5, 6, 7]]
GELU_C = 1.5957691216057308


class Tok:
    __slots__ = ("key", "val")

    def __init__(self, key, val):
        self.key = key
        self.val = val


class Buf:
    __slots__ = ("name", "w", "r", "excl")

    def __init__(self, name, excl=False):
        self.name = name
        self.w = None
        self.r = []
        self.excl = excl


class Sched:
    ENGS = ("pe", "act", "dve", "pool", "sp")

    def __init__(self, nc, stack):
        self.nc = nc
        self.stack = stack
        self.ops = {e: [] for e in self.ENGS}
        self.sems = {}
        self.cnt = {}
        self.seen = {e: {} for e in self.ENGS}
        for e in ("pe", "act", "dve", "pool"):
            self._mk(e)

    def _mk(self, key):
        if key not in self.sems:
            self.sems[key] = self.stack.enter_context(self.nc.semaphore("s_" + key))
            self.cnt[key] = 0

    def _deps(self, eng, reads, writes):
        deps = []
        for b in reads:
            if b.w is not None:
                deps.append(b.w)
        for b in writes:
            if b.w is not None:
                deps.append(b.w)
            deps.extend(b.r)
        waits = {}
        for t in deps:
            if t.key == "pe" and eng == "pe":
                continue
            if self.seen[eng].get(t.key, 0) >= t.val:
                continue
            waits[t.key] = max(waits.get(t.key, 0), t.val)
        for k, v in waits.items():
            self.seen[eng][k] = v
        return list(waits.items())

    def _commit(self, tok, reads, writes):
        for b in writes:
            b.w = tok
            b.r = []
        for b in reads:
            if b not in writes:
                b.r.append(tok)
                if len(b.r) > 64:
                    b.r = b.r[-48:]

    def op(self, eng, fn, reads=(), writes=()):
        ex = [b for b in reads if b.excl]
        if ex:
            reads = [b for b in reads if not b.excl]
            writes = list(writes) + ex
        waits = self._deps(eng, reads, writes)
        self.cnt[eng] += 1
        tok = Tok(eng, self.cnt[eng])
        self.ops[eng].append((waits, fn, eng, 1))
        self._commit(tok, reads, writes)
        return tok

    def dma(self, queue, key, out, in_, reads=(), writes=(), **kw):
        self._mk(key)
        waits = self._deps(queue, reads, writes)
        self.cnt[key] += 16
        tok = Tok(key, self.cnt[key])
        self.ops[queue].append((waits, lambda e: e.dma_start(out=out, in_=in_, **kw), key, 16))
        self._commit(tok, reads, writes)
        return tok

    def custom(self, queue, key, fn, reads=(), writes=()):
        import os
        if os.environ.get("KDBG_NOCC"):
            return None
        self._mk(key)
        waits = self._deps(queue, reads, writes)
        self.cnt[key] += 1
        tok = Tok(key, self.cnt[key])
        self.ops[queue].append((waits, fn, key, None))
        self._commit(tok, reads, writes)
        return tok

    def barrier(self, exclude=()):
        for e in self.ENGS:
            waits = []
            for k, v in self.cnt.items():
                if k == e or v == 0 or k in exclude:
                    continue
                if self.seen[e].get(k, 0) >= v:
                    continue
                self.seen[e][k] = v
                waits.append((k, v))
            if waits:
                self.ops[e].append((waits, None, None, 0))

    def emit(self):
        nc = self.nc
        handles = {"pe": "tensor", "act": "scalar", "dve": "vector", "pool": "gpsimd", "sp": "sync"}
        with nc.Block() as block:
            for e in self.ENGS:
                ops = self.ops[e]

                def body(engine, ops=ops):
                    for waits, fn, key, inc in ops:
                        for k, v in waits:
                            engine.wait_ge(self.sems[k], v)
                        if fn is None:
                            continue
                        ins = fn(engine)
                        if inc is None:
                            ins.then_inc(self.sems[key])
                        else:
                            ins.then_inc(self.sems[key], inc)

                getattr(block, handles[e])(body)


class Arena:
    def __init__(self, t, nelem):
        self.t = t
        self.n = nelem
        self.off = 0

    def reset(self, off=0):
        self.off = off

    def bf(self, nelem, parts=128):
        a = self.t[0:parts, self.off:self.off + nelem]
        self.off += nelem
        assert self.off <= self.n, (self.off, self.n)
        return a

    def f32(self, nelem, parts=128):
        return self.bf(2 * nelem, parts).bitcast(F32)


def _bf(a):
    return np.ascontiguousarray(a).astype(ml_dtypes.bfloat16)


def host_consts(j):
    c = {}
    c["ident"] = _bf(np.eye(128, dtype=np.float32))
    p = np.arange(128)
    t = (TOK * j + 128 * np.arange(NT)[None, :] + p[:, None]).astype(np.float32)
    row = np.floor(t / 64.0).astype(np.float32)
    col = (t - 64.0 * row).astype(np.float32)
    inv = (np.float32(10000.0) ** (-(np.arange(16, dtype=np.float32)) / np.float32(16))).astype(np.float32)
    ang = np.concatenate([row[:, :, None] * inv[None, None, :], col[:, :, None] * inv[None, None, :]], axis=-1)
    ang = ang.astype(np.float32)
    c["rope"] = np.concatenate([np.cos(ang), np.sin(ang)], axis=-1).astype(np.float32)
    s_ = p[:, None]
    c_ = p[None, :]
    mf = (c_ >= s_).astype(np.float32)
    mb = (c_ <= s_).astype(np.float32)
    c["mask"] = np.stack([mf, mf, mb, mb], axis=1).astype(np.float32)
    pc = np.zeros((128, 8), np.float32)
    pc[:, 0] = -(p + 1)
    pc[:, 1] = (p + 1)
    pc[:, 2] = -(128 - p)
    pc[:, 3] = (128 - p)
    pc[:, 4] = -(255 - p)
    pc[:, 5] = -(255 - 128 - p)
    pc[:, 6] = -p
    pc[:, 7] = -(128 + p)
    c["pcol"] = pc
    sel = np.zeros((128, 18), np.float32)
    sel[:, 16] = 1.0 if j == 0 else 0.0
    sel[:, 17] = 1.0 if j == 3 else 0.0
    sel[:, j] = 1.0
    sel[:, 4 + j] = 1.0
    if j > 0:
        sel[:, 8 + (j - 1)] = 1.0
    if j < 3:
        sel[:, 12 + (j + 1)] = 1.0
    c["sel"] = sel
    q = np.arange(64)
    hh, rr, mm = q // 32, (q % 32) // 8, q % 8
    n1 = 16 * rr + 8 * hh + mm
    k1 = np.arange(64)
    th = 2.0 * np.pi * ((n1[:, None] * k1[None, :]) % 64) / 64.0
    c["e64"] = _bf(np.concatenate([np.cos(th), -np.sin(th)], axis=1))
    n2 = np.arange(128)
    k2 = 32 * j + np.arange(32)
    kk = k1[None, :, None] + 64 * k2[None, None, :]
    ph = 2.0 * np.pi * ((n2[:, None, None] * kk) % 8192) / 8192.0
    twA = np.concatenate([np.cos(ph), -np.sin(ph)], axis=2)
    twB = np.concatenate([np.sin(ph), np.cos(ph)], axis=2)
    c["tw"] = _bf(np.stack([twA, twB], axis=2))
    ch = np.arange(128)
    pc2 = 2.0 * np.pi * ((ch[:, None] * ch[None, :]) % 128) / 128.0
    c["c128"] = _bf(np.stack([np.cos(pc2), np.sin(pc2)], axis=1) / 1024.0)
    c["ones"] = np.ones((128, 128), np.float32)
    c["identf"] = np.eye(128, dtype=np.float32)
    c["onesb"] = _bf(np.ones((128, 128), np.float32))
    return c


CONST_SPECS = [
    ("ident", [128, 128], BF16), ("rope", [128, 16, 64], F32), ("mask", [128, 4, 128], F32),
    ("pcol", [128, 8], F32), ("sel", [128, 18], F32), ("e64", [64, 128], BF16),
    ("tw", [128, 64, 2, 64], BF16), ("c128", [128, 2, 128], BF16), ("ones", [128, 128], F32),
    ("onesb", [128, 128], BF16), ("identf", [128, 128], F32),
]

INPUT_SPECS = [
    ("x_own", [TOK, D], F32), ("ctx", [CTX, D], F32), ("c_col", [128, 8], F32), ("cc_col", [128, 8], F32),
    ("w_mod", [D, 6 * D], F32), ("b_mod", [1, 6 * D], F32), ("norm1_w", [1, D], F32),
    ("w_in", [D, INC], F32), ("a_f", [1, H], F32), ("a_b", [1, H], F32),
    ("w_ret_out", [D, D], F32), ("w_four_out", [512, D], F32), ("w_bg", [D, 2 * D], F32),
    ("b_bg", [1, 2 * D], F32), ("w_out", [D, D], F32), ("norm2_w", [1, D], F32),
    ("w_up", [D, 2 * FFN], F32), ("conv_w", [3, 2 * FFN], F32), ("conv_b", [1, 2 * FFN], F32),
    ("w_down", [FFN, D], F32), ("final_norm_w", [1, D], F32),
]


def build_program(stage=4):
    import os
    STOP = os.environ.get("KDBG_STOP", "")
    nc = bass.Bass("TRN2", target_bir_lowering=False)
    stack = ExitStack()
    S = Sched(nc, stack)
    dr = {}
    for name, shape, dt in INPUT_SPECS + CONST_SPECS:
        dr[name] = nc.dram_tensor(name, shape, dt, kind="ExternalInput").ap()
    out_d = nc.dram_tensor("out", [TOK, D], F32, kind="ExternalOutput").ap()
    dbg_d = None
    if stage < 4:
        dbg_d = nc.dram_tensor("dbg", [128, 8 * TOK], BF16, kind="ExternalOutput").ap()
    rec_d = nc.dram_tensor("rec_scr", [NT, 128, 5120], BF16).ap()
    kv_d = nc.dram_tensor("kv_scr", [NT, 128, 1024], F32).ap()
    x1_d = nc.dram_tensor("x1_scr", [TOK, D], F32).ap()
    mod_d = nc.dram_tensor("mod_scr", [1, 6 * D + 2 * D], F32).ap()
    fin_d = [nc.dram_tensor(f"f_in{h}", [1024, 512], BF16) for h in range(2)]
    fout_d = [nc.dram_tensor(f"f_out{h}", [4096, 512], BF16) for h in range(2)]
    stin_d = nc.dram_tensor("st_in", [128, 1024], F32)
    stout_d = nc.dram_tensor("st_out", [512, 1024], F32)
    hin_d = nc.dram_tensor("halo_in", [128, 128], F32)
    hout_d = nc.dram_tensor("halo_out", [512, 128], F32)

    def sb(name, shape, dt):
        return stack.enter_context(nc.sbuf_tensor("sb_" + name, shape, dt))

    PS = [stack.enter_context(nc.psum_tensor(f"ps{i}", [128, 512], F32)) for i in range(8)]
    PB = [Buf(f"ps{i}", excl=True) for i in range(8)]

    def psbf(i):
        return PS[i][:, :].bitcast(BF16)

    ident = sb("ident", [128, 128], BF16)
    rope = sb("rope", [128, 16, 64], F32)
    mask = sb("mask", [128, 4, 128], F32)
    pcol = sb("pcol", [128, 8], F32)
    sel = sb("sel", [128, 18], F32)
    ones = sb("ones", [128, 128], F32)
    onesb = sb("onesb", [128, 128], BF16)
    identf = sb("identf", [128, 128], F32)
    sctx = sb("sctx", [128, 1024], F32)
    c128 = sb("c128", [128, 2, 128], BF16)
    e64 = sb("e64", [64, 128], BF16)
    smallf = sb("smallf", [128, 512], F32)
    smallb = sb("smallb", [128, 64], BF16)
    convc = sb("convc", [128, 4, NCH], F32)
    XH = sb("XH", [128, 8, TOK], BF16)
    R1n, R2n = 32768, 47104
    R1t = sb("R1", [128, R1n], BF16)
    R2t = sb("R2", [128, R2n], BF16)
    R1 = Arena(R1t, R1n)
    R2 = Arena(R2t, R2n)
    B_const = Buf("const")
    B_small = Buf("small")
    B_ss = Buf("ss")
    B_XH = [Buf(f"xh{t}") for t in range(NT)]

    def sf(a, b):
        return smallf[:, a:b]

    a_bc = sf(0, 16)
    ea = sf(16, 32)
    qf_sc, kf_sc, qb_sc, kb_sc = sf(32, 40), sf(40, 48), sf(48, 56), sf(56, 64)
    cxf_sc, cxb_sc = sf(64, 80), sf(80, 96)
    a_st, ea_st = sf(96, 104), sf(104, 112)
    cdp_f, cdp_b = sf(112, 180), sf(180, 248)
    silc = sf(248, 264)
    shcol = sf(264, 288)
    ss_t = sf(288, 296)
    bbg = sf(296, 312)
    bup = sf(312, 356)
    kcol = sf(356, 400)
    gst = sf(400, 464)
    silcb = smallb[:, 0:16]
    shcolb = smallb[:, 16:40]

    def TT(eng, out, in0, in1, op, reads, writes):
        return S.op(eng, lambda e: e.tensor_tensor(out=out, in0=in0, in1=in1, op=op), reads, writes)

    def TS(eng, out, in0, s1, s2, op0, op1, reads, writes):
        if s2 is None:
            return S.op(eng, lambda e: e.tensor_scalar(out=out, in0=in0, scalar1=s1, scalar2=None, op0=op0), reads, writes)
        return S.op(eng, lambda e: e.tensor_scalar(out=out, in0=in0, scalar1=s1, scalar2=s2, op0=op0, op1=op1), reads, writes)

    def STT(eng, out, in0, scalar, in1, op0, op1, reads, writes):
        return S.op(eng, lambda e: e.scalar_tensor_tensor(out=out, in0=in0, scalar=scalar, in1=in1, op0=op0, op1=op1), reads, writes)

    def CP(eng, out, in_, reads, writes):
        if eng == "act":
            return S.op("act", lambda e: e.activation(out=out, in_=in_, func=AF.Copy), reads, writes)
        return S.op(eng, lambda e: e.tensor_copy(out=out, in_=in_), reads, writes)

    def ACTF(out, in_, func, reads, writes, **kw):
        return S.op("act", lambda e: e.activation(out=out, in_=in_, func=func, **kw), reads, writes)

    def MM(out, lhsT, rhs, start, stop, reads, writes):
        return S.op("pe", lambda e: e.matmul(out, lhsT=lhsT, rhs=rhs, start=start, stop=stop), reads, writes)

    def TR(out, in_, reads, writes):
        return S.op("pe", lambda e: e.transpose(out=out, in_=in_, identity=ident[:]), reads + [B_const], writes)

    def RSQ(out, in_, c, reads, writes):
        ACTF(out, in_, AF.Sqrt, reads, writes, bias=c, scale=1.0)
        return S.op("dve", lambda e: e.reciprocal(out=out, in_=out), writes, writes)

    def bc3(ap2, n):
        return ap2.unsqueeze(2).to_broadcast([128, ap2.shape[1], n])

    R1.reset()
    Win = R1.bf(8 * INC).rearrange("p (k c) -> p k c", k=8)
    A1bc = R1.f32(1024)
    B_win = Buf("win")
    win_v = dr["w_in"].rearrange("(k p) c -> p k c", p=128)
    for k0 in (0, 4):
        S.dma("pool", "win", Win[:, k0:k0 + 4, :], win_v[:, k0:k0 + 4, :], writes=[B_win])
    for dst, name in ((ident, "ident"), (rope, "rope"), (mask, "mask"), (pcol, "pcol"), (sel, "sel"), (ones, "ones"),
                      (onesb, "onesb"), (c128, "c128"), (e64, "e64"), (identf, "identf")):
        S.dma("sp", "const", dst[:], dr[name], writes=[B_const])
    S.dma("sp", "const", a_bc[:, 0:8], dr["a_f"].partition_broadcast(128), writes=[B_const])
    S.dma("sp", "const", a_bc[:, 8:16], dr["a_b"].partition_broadcast(128), writes=[B_const])
    S.dma("sp", "const", silc[:, 0:8], dr["c_col"], writes=[B_const])
    S.dma("sp", "const", silc[:, 8:16], dr["cc_col"], writes=[B_const])
    R2.reset()
    stg = sb("stg", [64, 128], F32)
    B_stg = Buf("stg")

    def col_layout(dst, src_rows, n):
        S.dma("sp", "stg", stg[0:n, :], src_rows, writes=[B_stg])
        S.op("pe", lambda e: e.transpose(out=PS[6][:, 0:n], in_=stg[0:n, :], identity=identf[0:n, 0:n]), [B_stg, B_const], [PB[6]])
        CP("dve", dst, PS[6][:, 0:n], [PB[6]], [B_const])

    S.barrier()
    col_layout(bbg, dr["b_bg"].rearrange("o (c p) -> (o c) p", p=128), 16)
    for k in range(3):
        col_layout(convc[:, k, :], dr["conv_w"][k:k + 1, :].rearrange("o (c p) -> (o c) p", p=128), NCH)
    col_layout(convc[:, 3, :], dr["conv_b"].rearrange("o (c p) -> (o c) p", p=128), NCH)
    for di in range(2):
        for hh in range(2):
            CP("dve", a_st[64 * hh:64 * hh + 64, 4 * di:4 * di + 4],
               a_bc[64 * hh:64 * hh + 64, 8 * di:8 * di + 8].rearrange("p (a h) -> p a h", h=2)[:, :, hh], [B_const], [B_const])

    ACTF(ea, a_bc, AF.Exp, [B_const], [B_small])
    ACTF(ea_st, a_st, AF.Exp, [B_const], [B_small])
    for dst, src, col in ((qf_sc, ea[:, 0:8], 0), (kf_sc, ea[:, 0:8], 1), (qb_sc, ea[:, 8:16], 2), (kb_sc, ea[:, 8:16], 3)):
        ACTF(dst, src, AF.Exp, [B_small], [B_small], scale=pcol[:, col:col + 1])
    for tl in range(2):
        ACTF(cxf_sc[:, 8 * tl:8 * tl + 8], ea[:, 0:8], AF.Exp, [B_small], [B_small], scale=pcol[:, 4 + tl:5 + tl])
        ACTF(cxb_sc[:, 8 * tl:8 * tl + 8], ea[:, 8:16], AF.Exp, [B_small], [B_small], scale=pcol[:, 6 + tl:7 + tl])
    for n in range(17):
        ACTF(cdp_f[:, 4 * n:4 * n + 4], ea_st[:, 0:4], AF.Exp, [B_small], [B_small], scale=-128.0 * n)
        ACTF(cdp_b[:, 4 * n:4 * n + 4], ea_st[:, 4:8], AF.Exp, [B_small], [B_small], scale=-128.0 * n)
    TS("dve", qf_sc, qf_sc, 0.125, None, ALU.mult, None, [B_small], [B_small])
    TS("dve", qb_sc, qb_sc, 0.125, None, ALU.mult, None, [B_small], [B_small])
    ACTF(silc, silc, AF.Silu, [B_small], [B_small])
    CP("dve", silcb, silc, [B_small], [B_small])

    wst = [R2.f32(4096).rearrange("p (k c) -> p k c", k=8) for _ in range(2)]
    wmb = [R2.bf(4096).rearrange("p (k c) -> p k c", k=8) for _ in range(2)]
    bmod = [R2.f32(512, parts=1) for _ in range(2)]
    rowsb = [R2.f32(512, parts=1) for _ in range(4)]
    B_wst, B_wmb = [Buf("wst0"), Buf("wst1")], [Buf("wmb0"), Buf("wmb1")]
    B_bmod = [Buf("bm0"), Buf("bm1")]
    B_row = [Buf(f"row{i}") for i in range(4)]
    B_modd = Buf("modd")
    wmod_v = dr["w_mod"].rearrange("(k p) c -> p k c", p=128)
    cvt_eng = ["dve", "act"]
    nrow = [0]

    def mod_block(cb, wst, wmb, bmod, rowsb, B_wst, B_wmb, B_bmod, B_row):
        i = cb % 2
        S.dma("sp", f"wst{i}", wst[i], wmod_v[:, :, 512 * cb:512 * cb + 512], writes=[B_wst[i]])
        S.dma("sp", f"bm{i}", bmod[i], dr["b_mod"][:, 512 * cb:512 * cb + 512], writes=[B_bmod[i]])
        for half in range(2):
            CP(cvt_eng[(2 * cb + half) % len(cvt_eng)], wmb[i][:, 4 * half:4 * half + 4, :], wst[i][:, 4 * half:4 * half + 4, :], [B_wst[i]], [B_wmb[i]])
        for side in range(2 if cb < 4 else 1):
            for k in range(8):
                MM(PS[7][0:1, :], silcb[:, 8 * side + k:8 * side + k + 1], wmb[i][:, k, :], k == 0, k == 7, [B_small, B_wmb[i]], [PB[7]])
            r = nrow[0] % len(rowsb)
            nrow[0] += 1
            TT("dve", rowsb[r], PS[7][0:1, :], bmod[i], ALU.add, [PB[7], B_bmod[i]], [B_row[r]])
            off = 512 * cb if side == 0 else 6 * D + 512 * cb
            S.dma("sp", f"rowst{r}", mod_d[:, off:off + 512], rowsb[r], reads=[B_row[r]], writes=[B_modd])

    for cb in range(4):
        mod_block(cb, wst, wmb, bmod, rowsb, B_wst, B_wmb, B_bmod, B_row)
    S.barrier()
    for i, off in ((0, 0), (2, 6 * D)):
        col_layout(shcol[:, 8 * i:8 * i + 8], mod_d[:, off:off + D].rearrange("o (c p) -> (o c) p", p=128), 8)
    CP("dve", shcolb[:, 0:8], shcol[:, 0:8], [B_const], [B_small])
    CP("dve", shcolb[:, 16:24], shcol[:, 16:24], [B_const], [B_small])

    if STOP == "mod":
        S.barrier(); S.emit(); return nc
    B_bct = Buf("bct")

    R2.reset()
    xt = [R2.f32(1024) for _ in range(3)]
    hb = R2.bf(1024)
    junk = R2.bf(1024)
    qk2 = [R2.f32(1024) for _ in range(2)]
    qk_sb = qk2[0]
    off_rt1 = R2.off
    rt1, rt2 = R2.f32(1024), R2.f32(1024)
    rot = rt1
    ktok = R2.bf(1024)
    kpad = R2.bf(2048)
    qpad = R2.bf(2048)
    fsb = [R2.bf(512) for _ in range(2)]
    kvsb = [R2.f32(1024) for _ in range(2)]
    Est = R2.f32(1024)
    off_rec = R2.off
    recb = [R2.bf(5120) for _ in range(2)]
    hcT = R2.bf(2048).rearrange("p (k c) -> p k c", k=8)
    ctxv = R2.bf(1024)
    brows = R2.bf(INC, parts=2)
    lo_tmp = R2t[0:1, off_rt1:off_rt1 + INC]
    nwt = qk2[1]
    browsc = R2t[0:2, off_rec:off_rec + INC]
    A1c = R2t[:, off_rec + 5120:off_rec + 5120 + 2048].bitcast(F32)
    B_xt = [Buf("xt0"), Buf("xt1"), Buf("xt2")]
    B_hb, B_junk, B_qk = Buf("hb"), Buf("junk"), Buf("qk")
    B_qk2 = [Buf("qk2_0"), Buf("qk2_1")]
    B_rt1, B_rt2 = Buf("rt1"), Buf("rt2")
    B_rot = B_rt1
    B_ktok, B_kpad, B_qpad = Buf("ktok"), Buf("kpad"), Buf("qpad")
    B_fsb = [Buf("fsb0"), Buf("fsb1")]
    B_kvsb = [Buf("kvsb0"), Buf("kvsb1")]
    B_E = Buf("E")
    B_rec = [Buf("rec0"), Buf("rec1")]
    B_hcT, B_ctxv, B_sctx, B_bias = Buf("hcT"), Buf("ctxv"), Buf("sctx"), Buf("bias")

    S.dma("sp", "bcl", nwt, dr["norm1_w"].partition_broadcast(128), writes=[B_bct])
    S.dma("sp", "bcl", A1bc, mod_d[:, D:2 * D].partition_broadcast(128), reads=[B_modd], writes=[B_bct])
    STT("dve", A1bc, A1bc, 1.0, nwt, ALU.add, ALU.mult, [B_bct], [B_bct])
    TS("dve", A1bc, A1bc, 32.0, None, ALU.mult, None, [B_bct], [B_bct])

    def bias_rows(side, dstHL):
        lcol = 0 if side == 0 else 16
        for blk in range(7):
            if side == 1 and blk not in (1, 2, 3):
                continue
            for k in range(8):
                MM(PS[7][0:1, :], shcolb[:, lcol + k:lcol + k + 1], Win[:, k, 512 * blk:512 * blk + 512], k == 0, k == 7, [B_small, B_win], [PB[7]])
            cs = slice(512 * blk, 512 * blk + 512)
            CP("dve", dstHL[0:1, cs], PS[7][0:1, :], [PB[7]], [B_bias])
            TT("dve", lo_tmp[:, cs], PS[7][0:1, :], dstHL[0:1, cs], ALU.subtract, [PB[7], B_bias], [B_bias])
        S.dma("sp", "biaslo", dstHL[1:2, :], lo_tmp, reads=[B_bias], writes=[B_bias])

    bias_rows(0, brows)

    S.op("pool", lambda e: e.memset(kpad, 0.0), writes=[B_kpad])
    S.op("pool", lambda e: e.memset(qpad, 0.0), writes=[B_qpad])
    S.op("pool", lambda e: e.memset(Est, 0.0), writes=[B_E])

    def load_x(src_ap, slot):
        S.dma("sp", f"xt{slot}", xt[slot], src_ap, writes=[B_xt[slot]])

    def norm_part(slot, scale_bc):
        ACTF(junk, xt[slot], AF.Square, [B_xt[slot]], [B_junk, B_ss], accum_out=ss_t[:, 0:1])
        RSQ(ss_t[:, 1:2], ss_t[:, 0:1], 1024.0 * EPS, [B_ss], [B_ss])
        STT("dve", hb, xt[slot], ss_t[:, 1:2], scale_bc, ALU.mult, ALU.mult, [B_xt[slot], B_ss, B_bct], [B_hb])

    def tr_part(dstT, bdst, col0):
        pst = psbf(0)
        for k in range(8):
            TR(pst[:, 128 * k:128 * k + 128], hb[:, 128 * k:128 * k + 128], [B_hb], [PB[0]])
        CP("act", dstT[:, :, col0:col0 + 128], pst.rearrange("p (k c) -> p k c", k=8), [PB[0]], [bdst])

    def norm_transpose(slot, scale_bc, dstT, bdst, col0):
        norm_part(slot, scale_bc)
        tr_part(dstT, bdst, col0)

    def project(srcT, bsrc, col0, blocks, rows, consume):
        for i, blk in enumerate(blocks):
            bank = 1 + (i % 2)
            for k in range(8):
                MM(PS[bank][:, :], srcT[:, k, col0:col0 + 128], Win[:, k, 512 * blk:512 * blk + 512], k == 0, False, [bsrc, B_win], [PB[bank]])
            MM(PS[bank][:, :], onesb[0:2, :], rows[0:2, 512 * blk:512 * blk + 512], False, True, [B_bias, B_const], [PB[bank]])
            consume(blk, PS[bank], PB[bank])

    def k4(ap):
        return ap.rearrange("p (a h c) -> p a h c", a=4, h=2)

    def scaled_k(dirn, src_k, bsrc, sc_tile):
        kt = ktok[:, 512 * dirn:512 * dirn + 512]
        TT("dve", kt.rearrange("p (h d) -> p h d", h=8), src_k.rearrange("p (h d) -> p h d", h=8), bc3(sc_tile, 64), ALU.mult,
           [bsrc, B_small], [B_ktok])
        kp = k4(kpad[:, 1024 * dirn:1024 * dirn + 1024])
        kin = kt.rearrange("p (a h d) -> p a h d", a=4, h=2)
        for hh in range(2):
            CP("act", kp[:, :, hh, 64 * hh:64 * hh + 64], kin[:, :, hh, :], [B_ktok], [B_kpad])

    def kv_matmuls(dirn, vsrc, bv, bank):
        kp = k4(kpad[:, 1024 * dirn:1024 * dirn + 1024])
        for p4 in range(4):
            for hh in range(2):
                h = 2 * p4 + hh
                MM(PS[bank][:, 128 * p4:128 * p4 + 128], kp[:, p4, hh, :], vsrc[:, 128 * h:128 * h + 128], hh == 0, hh == 1, [B_kpad, bv], [PB[bank]])

    S.barrier()
    B_fin = [[Buf(f"fin{h}_{i}") for i in range(8)] for h in range(2)]
    B_fout = [Buf("fout0"), Buf("fout1")]
    B_recd = [Buf(f"recd{t}") for t in range(NT)]
    B_kvd = [Buf(f"kvd{t}") for t in range(NT)]

    def E3(ap):
        return ap.rearrange("p (a e) -> p a e", a=4)

    cdf_bc = bc3(cdp_f[:, 4:8], 128)
    def passA_front1(t):
        tr_part(XH, B_XH[t], 128 * t)

    def passA_front(t):
        slot = t % 2
        rb = recb[slot]

        qk_t, B_qkt = qk2[slot], B_qk2[slot]

        def consume(blk, ps, pb, t=t, rb=rb, slot=slot, qk_t=qk_t, B_qkt=B_qkt):
            if blk < 2:
                CP("act", qk_t[:, 512 * blk:512 * blk + 512], ps[:, :], [pb], [B_qkt])
            elif blk < 4:
                o0 = 3072 + 512 * (blk - 2)
                CP("act", rb[:, o0:o0 + 512], ps[:, :], [pb], [B_rec[slot]])
            elif blk < 6:
                o0 = 4096 + 512 * (blk - 4)
                ACTF(rb[:, o0:o0 + 512], ps[:, :], AF.Silu, [pb], [B_rec[slot]])
            else:
                CP("act", fsb[slot], ps[:, :], [pb], [B_fsb[slot]])
                hh_ = t // 8
                S.dma("sp", f"fst{slot}", fin_d[hh_].ap()[128 * (t % 8):128 * (t % 8) + 128, :], fsb[slot],
                      reads=[B_fsb[slot]], writes=[B_fin[hh_][t % 8]])
        project(XH, B_XH[t], 128 * t, [0, 1, 2, 3, 4, 5, 6], brows, consume)


    def passA_back(t):
        slot = t % 2
        rb = recb[slot]
        qk_t, B_qkt = qk2[slot], B_qk2[slot]
        def g4(ap):
            return ap.rearrange("p (g h c) -> p g h c", g=16, h=2)
        cosb = rope[:, t, 0:32].unsqueeze(1).to_broadcast([128, 32, 32])
        sinb = rope[:, t, 32:64].unsqueeze(1).to_broadcast([128, 16, 32])
        TT("dve", rt1.rearrange("p (g c) -> p g c", g=32), qk_t.rearrange("p (g c) -> p g c", g=32), cosb, ALU.mult, [B_qkt, B_const], [B_rt1])
        TT("dve", g4(rt2)[:, :, 0, :], g4(qk_t)[:, :, 1, :], sinb, ALU.mult, [B_qkt, B_const], [B_rt2])
        TT("dve", g4(rt2)[:, :, 1, :], g4(qk_t)[:, :, 0, :], sinb, ALU.mult, [B_qkt, B_const], [B_rt2])
        TT("dve", g4(rt1)[:, :, 0, :], g4(rt1)[:, :, 0, :], g4(rt2)[:, :, 0, :], ALU.subtract, [B_rt2], [B_rt1])
        TT("dve", g4(rt1)[:, :, 1, :], g4(rt1)[:, :, 1, :], g4(rt2)[:, :, 1, :], ALU.add, [B_rt2], [B_rt1])
        for dirn, sc in ((0, qf_sc), (1, qb_sc)):
            qp = k4(qpad[:, 1024 * dirn:1024 * dirn + 1024])
            qin = rot[:, 0:512].rearrange("p (a h d) -> p a h d", a=4, h=2)
            scv = sc.rearrange("p (a h) -> p a h", h=2)
            for hh in range(2):
                TT("dve", qp[:, :, hh, 64 * hh:64 * hh + 64], qin[:, :, hh, :], scv[:, :, hh].unsqueeze(2).to_broadcast([128, 4, 64]), ALU.mult,
                   [B_rot, B_small], [B_qpad])
        scaled_k(0, rot[:, 512:1024], B_rot, kf_sc)
        scaled_k(1, rot[:, 512:1024], B_rot, kb_sc)
        pst = [psbf(3), psbf(4), psbf(7)]
        for dirn in range(2):
            qp = k4(qpad[:, 1024 * dirn:1024 * dirn + 1024])
            for h in range(8):
                TR(pst[dirn][:, 128 * h:128 * h + 128], qp[:, h // 2, h % 2, :], [B_qpad], [PB[3 + dirn]])
        for dirn in range(2):
            for p4 in range(4):
                c0 = 512 * dirn + 128 * p4
                TR(pst[2][:, 128 * (4 * dirn + p4):128 * (4 * dirn + p4) + 128], ktok[:, c0:c0 + 128], [B_ktok], [PB[7]])
        CP("dve", rb[:, 0:1024], pst[0], [PB[3]], [B_rec[slot]])
        CP("dve", rb[:, 1024:2048], pst[1], [PB[4]], [B_rec[slot]])
        CP("dve", rb[:, 2048:3072], pst[2], [PB[7]], [B_rec[slot]])
        for dirn in range(2):
            kv_matmuls(dirn, rb[:, 3072:4096], B_rec[slot], 5 + dirn)
            CP("act", kvsb[slot][:, 512 * dirn:512 * dirn + 512], PS[5 + dirn][:, :], [PB[5 + dirn]], [B_kvsb[slot]])
        TT("pool", rt1[:, 0:512], Est[:, 0:512], kvsb[slot][:, 0:512], ALU.add, [B_E, B_kvsb[slot]], [B_rt1])
        TT("pool", E3(Est[:, 0:512]), E3(rt1[:, 0:512]), cdf_bc, ALU.mult, [B_rt1, B_small], [B_E])
        cdb_t = bc3(cdp_b[:, 4 * (t + 1):4 * (t + 1) + 4], 128)
        TT("pool", E3(rt1[:, 512:1024]), E3(kvsb[slot][:, 512:1024]), cdb_t, ALU.mult, [B_kvsb[slot], B_small], [B_rt1])
        TT("pool", Est[:, 512:1024], Est[:, 512:1024], rt1[:, 512:1024], ALU.add, [B_rt1, B_E], [B_E])
        S.dma("sp", f"rst{slot}", rec_d[t], rb, reads=[B_rec[slot]], writes=[B_recd[t]])
        S.dma("sp", f"kst{slot}", kv_d[t], kvsb[slot], reads=[B_kvsb[slot]], writes=[B_kvd[t]])
        if t % 8 == 7:
            hh_ = t // 8
            S.custom("pool", f"ccf{hh_}", lambda e, hh_=hh_: e.collective_compute(
                "AllGather", ALU.bypass, replica_groups=GROUPS, ins=[fin_d[hh_].ap().opt()], outs=[fout_d[hh_].ap().opt()]),
                reads=B_fin[hh_], writes=[B_fout[hh_]])


    for t0 in range(3):
        load_x(dr["x_own"][128 * t0:128 * t0 + 128, :], t0)
    norm_part(0, A1bc)
    passA_front1(0)
    norm_part(1, A1bc)
    passA_front(0)
    for t in range(NT):
        if t + 1 < NT:
            passA_front1(t + 1)
        if t + 2 < NT:
            norm_part((t + 2) % 3, A1bc)
        if t + 3 < NT:
            load_x(dr["x_own"][128 * (t + 3):128 * (t + 4), :], t % 3)
        if t + 1 < NT:
            passA_front(t + 1)
        passA_back(t)
    B_stin, B_stout = Buf("stin"), Buf("stout")
    S.dma("sp", "stst", stin_d.ap(), Est, reads=[B_E], writes=[B_stin])
    S.custom("pool", "ccst", lambda e: e.collective_compute(
        "AllGather", ALU.bypass, replica_groups=GROUPS, ins=[stin_d.ap().opt()], outs=[stout_d.ap().opt()]),
        reads=[B_stin], writes=[B_stout])
    S.barrier(exclude=("ccst",))
    S.dma("sp", "bcl", nwt, dr["norm1_w"].partition_broadcast(128), writes=[B_bct])
    S.dma("sp", "bcl", A1c, mod_d[:, 7 * D:8 * D].partition_broadcast(128), reads=[B_modd], writes=[B_bct])
    STT("dve", A1c, A1c, 1.0, nwt, ALU.add, ALU.mult, [B_bct], [B_bct])
    TS("dve", A1c, A1c, 32.0, None, ALU.mult, None, [B_bct], [B_bct])
    bias_rows(1, browsc)
    for tl in range(2):
        load_x(dr["ctx"][128 * tl:128 * tl + 128, :], tl)
    for tl in range(2):
        norm_transpose(tl, A1c, hcT, B_hcT, 128 * tl)

        def consume_ctx(blk, ps, pb):
            if blk == 1:
                CP("act", qk_sb[:, 512:1024], ps[:, :], [pb], [B_qk])
            else:
                CP("act", ctxv[:, 512 * (blk - 2):512 * (blk - 2) + 512], ps[:, :], [pb], [B_ctxv])
        project(hcT, B_hcT, 128 * tl, [1, 2, 3], browsc, consume_ctx)
        scaled_k(0, qk_sb[:, 512:1024], B_qk, cxf_sc[:, 8 * tl:8 * tl + 8])
        scaled_k(1, qk_sb[:, 512:1024], B_qk, cxb_sc[:, 8 * tl:8 * tl + 8])
        for dirn in range(2):
            kv_matmuls(dirn, ctxv, B_ctxv, 5 + dirn)
            dst = sctx[:, 512 * dirn:512 * dirn + 512]
            if tl == 0:
                CP("dve", dst, PS[5 + dirn][:, :], [PB[5 + dirn]], [B_sctx])
            else:
                TT("dve", dst, dst, PS[5 + dirn][:, :], ALU.add, [PB[5 + dirn], B_sctx], [B_sctx])


    S.barrier()
    R1.reset()
    SfT = R1.bf(NT * 512).rearrange("p (n c) -> p n c", n=NT)
    SbT = R1.bf(NT * 512).rearrange("p (n c) -> p n c", n=NT)
    ogT = R1.bf(8 * TOK).rearrange("p (h c) -> p h c", h=8)
    B_ST = Buf("ST")
    B_ogT = [Buf(f"ogT{t}") for t in range(NT)]
    R2.reset()
    kvall = R2.f32(NT * 1024).rearrange("p (n c) -> p n c", n=NT)
    Gst = R2.f32(4096).rearrange("p (r c) -> p r c", r=4)
    curf, curb = R2.f32(512), R2.f32(512)
    tmpf, tmpb = R2.f32(512), R2.f32(512)
    B_kvall, B_G = Buf("kvall"), Buf("G")
    B_curf, B_curb, B_tmpf, B_tmpb = Buf("curf"), Buf("curb"), Buf("tmpf"), Buf("tmpb")
    CP("dve", curf, sctx[:, 0:512], [B_sctx], [B_curf])
    CP("dve", curb, sctx[:, 512:1024], [B_sctx], [B_curb])
    S.dma("sp", "kvld", kvall, kv_d.rearrange("n p c -> p n c"), reads=B_kvd, writes=[B_kvall])
    S.dma("sp", "gld", Gst, stout_d.ap().rearrange("(r p) c -> p r c", p=128), reads=[B_stout], writes=[B_G])
    cd16f = bc3(cdp_f[:, 64:68], 128)
    cd16b = bc3(cdp_b[:, 64:68], 128)
    TS("dve", tmpf, curf, sel[:, 0:1], None, ALU.mult, None, [B_curf, B_const], [B_tmpf])
    for r in range(3):
        TT("dve", E3(curf), E3(curf), cd16f, ALU.mult, [B_curf, B_small], [B_curf])
        TT("dve", curf, curf, Gst[:, r, 0:512], ALU.add, [B_curf, B_G], [B_curf])
        STT("dve", tmpf, curf, sel[:, r + 1:r + 2], tmpf, ALU.mult, ALU.add, [B_curf, B_tmpf, B_const], [B_tmpf])
    TS("dve", tmpb, curb, sel[:, 7:8], None, ALU.mult, None, [B_curb, B_const], [B_tmpb])
    for r in (3, 2, 1):
        TT("dve", E3(curb), E3(curb), cd16b, ALU.mult, [B_curb, B_small], [B_curb])
        TT("dve", curb, curb, Gst[:, r, 512:1024], ALU.add, [B_curb, B_G], [B_curb])
        STT("dve", tmpb, curb, sel[:, 4 + r - 1:4 + r], tmpb, ALU.mult, ALU.add, [B_curb, B_tmpb, B_const], [B_tmpb])
    cdb_bc = bc3(cdp_b[:, 4:8], 128)
    for n in range(NT):
        m_ = NT - 1 - n
        CP("act", SfT[:, n, :], tmpf, [B_tmpf], [B_ST])
        TT("dve", curf, tmpf, kvall[:, n, 0:512], ALU.add, [B_tmpf, B_kvall], [B_curf])
        TT("dve", E3(tmpf), E3(curf), cdf_bc, ALU.mult, [B_curf, B_small], [B_tmpf])
        CP("act", SbT[:, m_, :], tmpb, [B_tmpb], [B_ST])
        TT("dve", curb, tmpb, kvall[:, m_, 512:1024], ALU.add, [B_tmpb, B_kvall], [B_curb])
        TT("dve", E3(tmpb), E3(curb), cdb_bc, ALU.mult, [B_curb, B_small], [B_tmpb])
    S.barrier()
    if STOP == "scan":
        S.emit(); return nc

    R2.reset()
    rbB = [R2.bf(5120) for _ in range(2)]
    Pm = [R2.bf(512) for _ in range(2)]
    sq = R2.f32(1024)
    tcen = R2.f32(1024)
    praw = [R2.bf(512) for _ in range(2)]
    ogtok = R2.bf(1024)
    B_rbB = [Buf("rbB0"), Buf("rbB1")]
    B_Pm = [Buf("Pm0"), Buf("Pm1")]
    B_sq, B_tcen, B_ogtok, B_gst = Buf("sq"), Buf("tcen"), Buf("ogtok"), Buf("gst")
    B_praw = [Buf("praw0"), Buf("praw1")]
    mask3 = mask[:, :, :]
    S.dma("sp", "rld0", rbB[0], rec_d[0], reads=[B_recd[0]], writes=[B_rbB[0]])
    obanks = [(5, 6), (3, 4)]

    def passB_mm(n):
        slot = n % 2
        rb = rbB[slot]

        def qT(di, h):
            return rb[:, (8 * di + h) * 128:(8 * di + h) * 128 + 128]

        def kT(di, p4):
            c0 = 2048 + (4 * di + p4) * 128
            return rb[:, c0:c0 + 128]
        def scores(p4):
            sbank = 1 + (p4 % 2)
            psc = PS[sbank][:, :].rearrange("p (a c) -> p a c", a=4)
            for di in range(2):
                for hh in range(2):
                    MM(psc[:, 2 * di + hh, :], kT(di, p4), qT(di, 2 * p4 + hh), True, True, [B_rbB[slot]], [PB[sbank]])
            pm = Pm[p4 % 2]
            CP("act", praw[p4 % 2], PS[sbank][:, :], [PB[sbank]], [B_praw[p4 % 2]])
            TT("pool", pm.rearrange("p (a c) -> p a c", a=4), praw[p4 % 2].rearrange("p (a c) -> p a c", a=4), mask3, ALU.mult,
               [B_praw[p4 % 2], B_const], [B_Pm[p4 % 2]])

        def omm(p4):
            pm3 = Pm[p4 % 2].rearrange("p (a c) -> p a c", a=4)
            obank = obanks[n % 2][p4 // 2]
            for hh in range(2):
                h = 2 * p4 + hh
                od = PS[obank][:, 128 * (h % 4):128 * (h % 4) + 128]
                vv = rb[:, 3072 + 128 * h:3072 + 128 * h + 128]
                MM(od, pm3[:, hh, :], vv, True, False, [B_Pm[p4 % 2], B_rbB[slot]], [PB[obank]])
                MM(od, qT(0, h), SfT[:, n, 128 * p4:128 * p4 + 128], False, False, [B_rbB[slot], B_ST], [PB[obank]])
                MM(od, pm3[:, 2 + hh, :], vv, False, False, [B_Pm[p4 % 2], B_rbB[slot]], [PB[obank]])
                MM(od, qT(1, h), SbT[:, n, 128 * p4:128 * p4 + 128], False, True, [B_rbB[slot], B_ST], [PB[obank]])
        scores(0)
        scores(1)
        omm(0)
        scores(2)
        omm(1)
        scores(3)
        omm(2)
        omm(3)

    def passB_gn(n):
        slot = n % 2
        rb = rbB[slot]
        ob = obanks[n % 2]
        o3s = [PS[ob[half]][:, :].rearrange("p (h e) -> p h e", h=4) for half in range(2)]
        for half in range(2):
            hs = slice(4 * half, 4 * half + 4)
            S.op("dve", lambda e, o3=o3s[half], hs=hs: e.tensor_reduce(out=gst[:, hs], in_=o3, axis=AX.X, op=ALU.add), [PB[ob[half]]], [B_gst])
            ACTF(sq[:, 512 * half:512 * half + 512], PS[ob[half]][:, :], AF.Square, [PB[ob[half]]], [B_sq])
        TS("dve", gst[:, 16:24], gst[:, 0:8], 1.0 / 128.0, None, ALU.mult, None, [B_gst], [B_gst])
        for half in range(2):
            TT("dve", tcen[:, 512 * half:512 * half + 512].rearrange("p (h e) -> p h e", h=4), o3s[half], bc3(gst[:, 16 + 4 * half:20 + 4 * half], 128),
               ALU.subtract, [PB[ob[half]], B_gst], [B_tcen])
        TT("dve", tcen, tcen, rb[:, 4096:5120], ALU.mult, [B_tcen, B_rbB[slot]], [B_tcen])
        S.op("dve", lambda e: e.tensor_reduce(out=gst[:, 8:16], in_=sq.rearrange("p (h e) -> p h e", h=8), axis=AX.X, op=ALU.add), [B_sq], [B_gst])
        TT("dve", gst[:, 24:32], gst[:, 16:24], gst[:, 16:24], ALU.mult, [B_gst], [B_gst])
        STT("dve", gst[:, 32:40], gst[:, 8:16], 1.0 / 128.0, gst[:, 24:32], ALU.mult, ALU.subtract, [B_gst], [B_gst])
        RSQ(gst[:, 40:48], gst[:, 32:40], EPS, [B_gst], [B_gst])
        TT("dve", ogtok.rearrange("p (h e) -> p h e", h=8), tcen.rearrange("p (h e) -> p h e", h=8), bc3(gst[:, 40:48], 128), ALU.mult,
           [B_tcen, B_gst], [B_ogtok])
        pst = psbf(0)
        for h in range(8):
            TR(pst[:, 128 * h:128 * h + 128], ogtok[:, 128 * h:128 * h + 128], [B_ogtok], [PB[0]])
        CP("act", ogT[:, :, 128 * n:128 * n + 128], pst.rearrange("p (h c) -> p h c", h=8), [PB[0]], [B_ogT[n]])
        if n + 2 < NT:
            S.dma("sp", f"rld{n % 2}", rbB[n % 2], rec_d[n + 2], reads=[B_recd[n + 2]], writes=[B_rbB[n % 2]])

    S.dma("sp", "rld1", rbB[1], rec_d[1], reads=[B_recd[1]], writes=[B_rbB[1]])
    passB_mm(0)
    for n in range(NT):
        if n + 1 < NT:
            passB_mm(n + 1)
        passB_gn(n)
    S.barrier()
    if stage == 1:
        S.dma("sp", "dbg", dbg_d.rearrange("p (h c) -> p h c", h=8), ogT, reads=B_ogT)
        S.barrier()
        S.emit()
        return nc

    zT = R1t[:, 0:4 * TOK].rearrange("p (g c) -> p g c", g=4)
    B_zT = Buf("zT")
    R2.reset()
    F1 = [R2.bf(16384, parts=64).rearrange("p (n c) -> p n c", n=128) for _ in range(1)]
    Aall = R2.bf(16384).rearrange("p (c k) -> p c k", c=128)
    tw = R2.bf(64 * 128).rearrange("p (k t c) -> p k t c", k=64, t=2)
    Xsb = [R2.bf(512).rearrange("p (k t c) -> p k t c", k=8, t=2) for _ in range(2)]
    B_F1, B_A, B_tw = Buf("F1"), Buf("A"), Buf("tw")
    B_Xsb = [Buf("Xsb0"), Buf("Xsb1")]
    S.dma("sp", "twld", tw, dr["tw"], writes=[B_tw])
    wst_s = R2.f32(1024).rearrange("p (k c) -> p k c", k=8)
    wmb_s = [R2.bf(1024).rearrange("p (k c) -> p k c", k=8) for _ in range(2)]
    bmod_s = [R2.f32(128, parts=1) for _ in range(2)]
    row_s = R2.f32(128, parts=1)
    B_wsts, B_rows = Buf("wsts"), Buf("rows")
    B_wmbs = [Buf("wmbs0"), Buf("wmbs1")]
    B_bms = [Buf("bms0"), Buf("bms1")]

    def mod_prep(j_):
        c0 = 4 * 512 + 128 * j_
        S.dma("sp", "wsts", wst_s, wmod_v[:, :, c0:c0 + 128], writes=[B_wsts])
        S.dma("sp", f"bms{j_ % 2}", bmod_s[j_ % 2], dr["b_mod"][:, c0:c0 + 128], writes=[B_bms[j_ % 2]])
        CP("pool", wmb_s[j_ % 2], wst_s, [B_wsts], [B_wmbs[j_ % 2]])

    def mod_mm(j_):
        c0 = 4 * 512 + 128 * j_
        for k in range(8):
            MM(PS[7][0:1, 0:128], silcb[:, k:k + 1], wmb_s[j_ % 2][:, k, :], k == 0, k == 7, [B_small, B_wmbs[j_ % 2]], [PB[7]])
        TT("dve", row_s, PS[7][0:1, 0:128], bmod_s[j_ % 2], ALU.add, [PB[7], B_bms[j_ % 2]], [B_rows])
        S.dma("sp", "rowss", mod_d[:, c0:c0 + 128], row_s, reads=[B_rows], writes=[B_modd])

    def mod_subblock(j_):
        if j_ == 0:
            mod_prep(0)
        if j_ + 1 < 32:
            mod_prep(j_ + 1)
        mod_mm(j_)

    nsub = [0]
    for g in range(4):
        for h2 in range(2):
            src = fout_d[h2].ap().rearrange("(q n) c -> q n c", n=128)[:, :, 128 * g:128 * g + 128]
            S.dma("sp", "f1ld", F1[0][32 * h2:32 * h2 + 32, :, :], src, reads=[B_fout[h2]], writes=[B_F1])
        for c4 in range(32):
            if c4 % 8 == 0 and nsub[0] < 32:
                mod_subblock(nsub[0])
                nsub[0] += 1
            bank = 1 + (c4 % 2)
            for cc in range(4):
                ch = 4 * c4 + cc
                MM(PS[bank][:, 128 * cc:128 * cc + 128], F1[0][:, :, ch], e64[:, :], True, True, [B_F1, B_const], [PB[bank]])
            CP("act" if c4 % 2 == 0 else "dve", Aall[:, 4 * c4:4 * c4 + 4, :], PS[bank][:, :].rearrange("p (c k) -> p c k", c=4), [PB[bank]], [B_A])
        for kb in range(8):
            if kb % 2 == 0 and nsub[0] < 32:
                mod_subblock(nsub[0])
                nsub[0] += 1
            xb = 3 + (kb % 2)
            px = PS[xb][:, :].rearrange("p (k t c) -> p k t c", k=8, t=2)
            for kk in range(8):
                k1 = 8 * kb + kk
                ar = Aall[:, :, k1]
                ai = Aall[:, :, 64 + k1]
                pxk = PS[xb][:, 64 * kk:64 * kk + 64]
                MM(pxk, ar, tw[:, k1, 0, :], True, False, [B_A, B_tw], [PB[xb]])
                MM(pxk, ai, tw[:, k1, 1, :], False, True, [B_A, B_tw], [PB[xb]])
            xs = Xsb[kb % 2]
            CP("act", xs, px, [PB[xb]], [B_Xsb[kb % 2]])
            zb = 5 + (kb % 2)
            pz = PS[zb][:, 0:256].rearrange("p (k c) -> p k c", k=8)
            MM(pz, c128[:, 0, :], xs[:, :, 0, :], True, False, [B_Xsb[kb % 2], B_const], [PB[zb]])
            MM(pz, c128[:, 1, :], xs[:, :, 1, :], False, True, [B_Xsb[kb % 2], B_const], [PB[zb]])
            zdst = zT[:, g, :].rearrange("p (b a) -> p a b", a=64)[:, 8 * kb:8 * kb + 8, :]
            CP("dve", zdst, pz, [PB[zb]], [B_zT])
    S.barrier()
    col_layout(shcol[:, 8:16], mod_d[:, 3 * D:4 * D].rearrange("o (c p) -> (o c) p", p=128), 8)
    CP("dve", shcolb[:, 8:16], shcol[:, 8:16], [B_const], [B_small])
    if stage == 2:
        S.dma("sp", "dbg", dbg_d[:, 0:4 * TOK], R1t[:, 0:4 * TOK], reads=[B_zT])
        S.barrier()
        S.emit()
        return nc

    Wout = R2t[:, 37888:46080].rearrange("p (k c) -> p k c", k=8)
    B_wout = Buf("wout")
    wout_v = dr["w_out"].rearrange("(k p) c -> p k c", p=128)
    R2.reset()
    mergedT = R2.bf(8 * TOK).rearrange("p (k c) -> p k c", k=8)
    wsl = [R2.bf(28 * 128).rearrange("p (k c) -> p k c", k=28) for _ in range(2)]
    gts = [R2.f32(512) for _ in range(4)]
    m12 = [R2.f32(512) for _ in range(2)]
    B_wsl = [Buf("wsl0"), Buf("wsl1")]
    B_gts = [Buf(f"gt{i}") for i in range(4)]
    B_m12 = [Buf("m1"), Buf("m2")]
    B_mg = [Buf(f"mg{i}") for i in range(4)]
    wro_v = dr["w_ret_out"].rearrange("(k p) c -> p k c", p=128)
    wfo_v = dr["w_four_out"].rearrange("(k p) c -> p k c", p=128)
    wbg_v = dr["w_bg"].rearrange("(k p) c -> p k c", p=128)

    def load_wsl(oc):
        i = oc % 2
        cs = slice(128 * oc, 128 * oc + 128)
        S.dma("pool", f"wsl{i}", wsl[i][:, 0:8, :], wro_v[:, :, cs], writes=[B_wsl[i]])
        S.dma("pool", f"wsl{i}", wsl[i][:, 8:12, :], wfo_v[:, :, cs], writes=[B_wsl[i]])
        S.dma("pool", f"wsl{i}", wsl[i][:, 12:20, :], wbg_v[:, :, cs], writes=[B_wsl[i]])
        S.dma("pool", f"wsl{i}", wsl[i][:, 20:28, :], wbg_v[:, :, D + 128 * oc:D + 128 * oc + 128], writes=[B_wsl[i]])
    load_wsl(0)
    S.dma("pool", "wout", Wout, wout_v, writes=[B_wout])
    B_bbg = Buf("bbg")
    for oc in range(8):
        i = oc % 2
        if oc + 1 < 8:
            load_wsl(oc + 1)
        w = wsl[i]
        for gi in range(2):
            for k in range(8):
                MM(PS[7][:, gi:gi + 1], w[:, 12 + 8 * gi + k, :], shcolb[:, k:k + 1], k == 0, k == 7, [B_wsl[i], B_small], [PB[7]])
        bcol = bbg.rearrange("p (g c) -> p g c", g=2)[:, :, oc]
        TT("dve", bcol, bcol, PS[7][:, 0:2], ALU.add, [PB[7], B_const], [B_bbg])
        for tb in range(4):
            ts_ = slice(512 * tb, 512 * tb + 512)
            xh_b = [B_XH[4 * tb + q] for q in range(4)]
            og_b = [B_ogT[4 * tb + q] for q in range(4)]
            for k in range(8):
                MM(PS[1][:, :], w[:, k, :], ogT[:, k, ts_], k == 0, k == 7, [B_wsl[i]] + og_b, [PB[1]])
            for k in range(4):
                MM(PS[2][:, :], w[:, 8 + k, :], zT[:, k, ts_], k == 0, k == 3, [B_wsl[i], B_zT], [PB[2]])
            for gi in range(2):
                for k in range(8):
                    MM(PS[3 + gi][:, :], w[:, 12 + 8 * gi + k, :], XH[:, k, ts_], k == 0, k == 7, [B_wsl[i]] + xh_b, [PB[3 + gi]])
            g0 = 2 * (tb % 2)
            for gi in range(2):
                ACTF(gts[g0 + gi], PS[3 + gi][:, :], AF.Sigmoid, [PB[3 + gi], B_bbg], [B_gts[g0 + gi]], bias=bbg[:, 8 * gi + oc:8 * gi + oc + 1])
            TT("dve", m12[0], gts[g0], PS[1][:, :], ALU.mult, [B_gts[g0], PB[1]], [B_m12[0]])
            TT("dve", m12[1], gts[g0 + 1], PS[2][:, :], ALU.mult, [B_gts[g0 + 1], PB[2]], [B_m12[1]])
            TT("pool", mergedT[:, oc, ts_], m12[0], m12[1], ALU.add, [B_m12[0], B_m12[1]], [B_mg[tb]])
    S.barrier()

    R2.reset(8 * TOK)
    xt2 = [R2.f32(1024) for _ in range(2)]
    x1t = [R2.f32(1024) for _ in range(2)]
    tmpy = R2.f32(1024)
    g1bc = R2.f32(1024)
    A2bc = R2.f32(1024)
    nw2t = R2.f32(1024)
    hb2 = R2.bf(1024)
    junk2 = R2.bf(1024)
    B_xt2 = [Buf("xt2_0"), Buf("xt2_1")]
    B_x1t = [Buf("x1t0"), Buf("x1t1")]
    B_tmpy, B_hb2, B_junk2, B_bc2 = Buf("tmpy"), Buf("hb2"), Buf("junk2"), Buf("bc2")
    B_x1d = [Buf(f"x1d{t}") for t in range(NT)]
    S.dma("sp", "bcl", g1bc, mod_d[:, 2 * D:3 * D].partition_broadcast(128), writes=[B_bc2])
    S.dma("sp", "bcl", A2bc, mod_d[:, 4 * D:5 * D].partition_broadcast(128), writes=[B_bc2])
    S.dma("sp", "bcl", nw2t, dr["norm2_w"].partition_broadcast(128), writes=[B_bc2])
    STT("dve", A2bc, A2bc, 1.0, nw2t, ALU.add, ALU.mult, [B_bc2], [B_bc2])
    TS("dve", A2bc, A2bc, 32.0, None, ALU.mult, None, [B_bc2], [B_bc2])
    Wd = R1t[:, 0:22 * D].rearrange("p (k c) -> p k c", k=22)
    B_wd = Buf("wd")
    wd_v = dr["w_down"].rearrange("(k p) c -> p k c", p=128)
    for k0 in (0, 11):
        S.dma("pool", "wd", Wd[:, k0:k0 + 11, :], wd_v[:, k0:k0 + 11, :], writes=[B_wd])
    S.dma("sp", "xt2_0", xt2[0], dr["x_own"][0:128, :], writes=[B_xt2[0]])
    ybanks = [(1, 2), (3, 4)]

    def y_mm(t):
        for half in range(2):
            bk = ybanks[t % 2][half]
            for k in range(8):
                MM(PS[bk][:, :], mergedT[:, k, 128 * t:128 * t + 128], Wout[:, k, 512 * half:512 * half + 512], k == 0, k == 7,
                   [B_mg[t // 4], B_wout], [PB[bk]])

    def y_post(t):
        slot = t % 2
        if t + 1 < NT:
            S.dma("sp", f"xt2_{(t + 1) % 2}", xt2[(t + 1) % 2], dr["x_own"][128 * (t + 1):128 * (t + 2), :], writes=[B_xt2[(t + 1) % 2]])
        for half in range(2):
            bk = ybanks[t % 2][half]
            hs = slice(512 * half, 512 * half + 512)
            TT("dve", tmpy[:, hs], PS[bk][:, :], g1bc[:, hs], ALU.mult, [PB[bk], B_bc2], [B_tmpy])
        TT("dve", x1t[slot], tmpy, xt2[slot], ALU.add, [B_tmpy, B_xt2[slot]], [B_x1t[slot]])
        S.dma("sp", f"x1st{slot}", x1_d[128 * t:128 * t + 128, :], x1t[slot], reads=[B_x1t[slot]], writes=[B_x1d[t]])
        ACTF(junk2, x1t[slot], AF.Square, [B_x1t[slot]], [B_junk2, B_ss], accum_out=ss_t[:, 2:3])
        RSQ(ss_t[:, 3:4], ss_t[:, 2:3], 1024.0 * EPS, [B_ss], [B_ss])
        STT("dve", hb2, x1t[slot], ss_t[:, 3:4], A2bc, ALU.mult, ALU.mult, [B_x1t[slot], B_ss, B_bc2], [B_hb2])
        pst = psbf(0)
        for k in range(8):
            TR(pst[:, 128 * k:128 * k + 128], hb2[:, 128 * k:128 * k + 128], [B_hb2], [PB[0]])
        CP("act", XH[:, :, 128 * t:128 * t + 128], pst.rearrange("p (k c) -> p k c", k=8), [PB[0]], [B_XH[t]])

    y_mm(0)
    for t in range(NT):
        if t + 1 < NT:
            y_mm(t + 1)
        y_post(t)
    S.barrier()
    if stage == 3:
        for t in range(NT):
            S.dma("sp", "xt2_0", xt2[0], x1_d[128 * t:128 * t + 128, :], reads=[B_x1d[t]], writes=[B_xt2[0]])
            S.dma("sp", "outst", out_d[128 * t:128 * t + 128, :], xt2[0], reads=[B_xt2[0]], writes=[])
        S.dma("sp", "dbg", dbg_d, R2t[:, 0:8 * TOK], reads=B_mg)
        S.barrier()
        S.emit()
        return nc

    R2.reset()
    USE_GELU_ACT = not os.environ.get("KDBG_GELU_SIG")
    mT = R2.bf(22 * 512).rearrange("p (k c) -> p k c", k=22)
    wup = [R2.bf(2048).rearrange("p (a k c) -> p a k c", a=2, k=8) for _ in range(3)]
    cacc = [[R2.f32(512) for _ in range(2)] for _ in range(2)]
    ga = [R2.f32(512) for _ in range(2)]
    ub = R2.f32(NCH * 8).rearrange("p (c t) -> p c t", c=NCH)
    hal = R2.f32(2 * NCH).rearrange("p (s c) -> p s c", s=2)
    hsend = R2.f32(128)
    hall = R2.f32(512).rearrange("p (r c) -> p r c", r=4)
    kcc = R2.f32(3 * NCH).rearrange("p (s c) -> p s c", s=3)
    x1r = [R2.f32(1024) for _ in range(2)]
    x2t = R2.f32(1024)
    g2bc = R2.f32(1024)
    fwbc = R2.f32(1024)
    junk3 = R2.bf(1024)
    B_wup = [Buf(f"wup{i}") for i in range(3)]
    B_cacc = [[Buf(f"ca{s_}{a}") for a in range(2)] for s_ in range(2)]
    B_ga = [Buf("ga0"), Buf("ga1")]
    B_mT, B_ub, B_hal, B_hsend, B_hall, B_kcc = Buf("mT"), Buf("ub"), Buf("hal"), Buf("hsend"), Buf("hall"), Buf("kcc")
    B_x1r = [Buf("x1r0"), Buf("x1r1")]
    B_x2t, B_bc3, B_junk3 = Buf("x2t"), Buf("bc3"), Buf("junk3")
    B_hin, B_hout = Buf("hin"), Buf("hout")
    S.dma("sp", "bcl", g2bc, mod_d[:, 5 * D:6 * D].partition_broadcast(128), writes=[B_bc3])
    S.dma("sp", "bcl", fwbc, dr["final_norm_w"].partition_broadcast(128), writes=[B_bc3])
    wup_v = dr["w_up"].rearrange("(k p) c -> p k c", p=128)
    nload = [0]

    def load_wup(i):
        s_ = nload[0] % 3
        nload[0] += 1
        S.dma("pool", f"wup{s_}", wup[s_][:, 0, :, :], wup_v[:, :, 128 * i:128 * i + 128], writes=[B_wup[s_]])
        S.dma("pool", f"wup{s_}", wup[s_][:, 1, :, :], wup_v[:, :, FFN + 128 * i:FFN + 128 * i + 128], writes=[B_wup[s_]])
        return s_

    xb8 = XH[:, :, :].rearrange("p k (b c) -> p k b c", c=512)
    bnd = R2.bf(72).rearrange("p (k c) -> p k c", k=8)
    B_bnd = Buf("bnd")
    CP("dve", bnd[:, :, 0], shcolb[:, 8:16], [B_small], [B_bnd])
    CP("dve", bnd[:, :, 1:5], xb8[:, :, :, 0], B_XH, [B_bnd])
    CP("dve", bnd[:, :, 5:9], xb8[:, :, :, 511], B_XH, [B_bnd])
    B_bup = Buf("bup")
    TT("dve", kcc[:, 0, :], convc[:, 0, :], convc[:, 1, :], ALU.add, [B_const], [B_kcc])
    TT("dve", kcc[:, 0, :], kcc[:, 0, :], convc[:, 2, :], ALU.add, [B_const, B_kcc], [B_kcc])
    S.op("pool", lambda e: e.memset(hsend, 0.0), writes=[B_hsend])

    def halo_exchange():
        CP("dve", hsend[:, 0:NCH], ub[:, :, 0], [B_ub], [B_hsend])
        CP("dve", hsend[:, NCH:2 * NCH], ub[:, :, 7], [B_ub], [B_hsend])
        S.dma("sp", "hst", hin_d.ap(), hsend, reads=[B_hsend], writes=[B_hin])
        S.custom("pool", "cch", lambda e: e.collective_compute(
            "AllGather", ALU.bypass, replica_groups=GROUPS, ins=[hin_d.ap().opt()], outs=[hout_d.ap().opt()]),
            reads=[B_hin], writes=[B_hout])
        S.dma("sp", "hld", hall, hout_d.ap().rearrange("(r p) c -> p r c", p=128), reads=[B_hout], writes=[B_hall])
        for side, (c0, s0) in enumerate(((NCH, 8), (0, 12))):
            TS("dve", hal[:, side, :], hall[:, 0, c0:c0 + NCH], sel[:, s0:s0 + 1], None, ALU.mult, None, [B_hall, B_const], [B_hal])
            for r in range(1, 4):
                STT("dve", hal[:, side, :], hall[:, r, c0:c0 + NCH], sel[:, s0 + r:s0 + r + 1], hal[:, side, :], ALU.mult, ALU.add,
                    [B_hall, B_hal, B_const], [B_hal])
            STT("dve", hal[:, side, :], kcc[:, 2, :], sel[:, 16 + side:17 + side], hal[:, side, :], ALU.mult, ALU.add, [B_kcc, B_hal, B_const], [B_hal])

    for tb in (1, 2, 0, 3):
        ts_ = slice(512 * tb, 512 * tb + 512)
        xh_b = [B_XH[4 * tb + q] for q in range(4)]
        pend = [load_wup(0), load_wup(1)]
        S.dma("sp", "x1r0", x1r[0], x1_d[512 * tb:512 * tb + 128, :], reads=[B_x1d[4 * tb]], writes=[B_x1r[0]])
        for i in range(22):
            s_ = pend.pop(0)
            if i + 2 < 22:
                pend.append(load_wup(i + 2))
            us = i % 2
            for a in range(2):
                ch = i + 22 * a
                bank = 1 + 2 * us + a
                if tb == 1:
                    for k in range(8):
                        MM(PS[7][:, 0:9], wup[s_][:, a, k, :], bnd[:, k, :], k == 0, k == 7, [B_wup[s_], B_bnd], [PB[7]])
                    CP("act", ub[:, ch, :].rearrange("p (b e) -> p e b", e=2), PS[7][:, 1:9].rearrange("p (e b) -> p e b", e=2), [PB[7]], [B_ub])
                    CP("dve", bup[:, ch:ch + 1], PS[7][:, 0:1], [PB[7]], [B_bup])
                    STT("dve", kcc[:, 1, ch:ch + 1], bup[:, ch:ch + 1], kcc[:, 0, ch:ch + 1], convc[:, 3, ch:ch + 1], ALU.mult, ALU.add,
                        [B_bup, B_kcc, B_const], [B_kcc])
                    TS("dve", kcc[:, 2, ch:ch + 1], bup[:, ch:ch + 1], -1.0, None, ALU.mult, None, [B_bup], [B_kcc])
                for k in range(8):
                    MM(PS[bank][:, :], wup[s_][:, a, k, :], XH[:, k, ts_], k == 0, k == 7, [B_wup[s_]] + xh_b, [PB[bank]])
                ca = cacc[us][a]
                bca = B_cacc[us][a]
                w0c, w1c, w2c = convc[:, 0, ch:ch + 1], convc[:, 1, ch:ch + 1], convc[:, 2, ch:ch + 1]
                ACTF(ca, PS[bank][:, :], AF.Identity, [PB[bank], B_kcc, B_const], [bca], scale=w1c, bias=kcc[:, 1, ch:ch + 1])
                STT("dve", ca[:, 1:512], PS[bank][:, 0:511], w0c, ca[:, 1:512], ALU.mult, ALU.add, [PB[bank], bca, B_const], [bca])
                STT("dve", ca[:, 0:511], PS[bank][:, 1:512], w2c, ca[:, 0:511], ALU.mult, ALU.add, [PB[bank], bca, B_const], [bca])
                if tb == 0:
                    pl, bpl = hal[:, 0, ch:ch + 1], B_hal
                else:
                    pl, bpl = ub[:, ch, 2 * tb - 1:2 * tb], B_ub
                if tb == 3:
                    pr, bpr = hal[:, 1, ch:ch + 1], B_hal
                else:
                    pr, bpr = ub[:, ch, 2 * tb + 2:2 * tb + 3], B_ub
                STT("dve", ca[:, 0:1], pl, w0c, ca[:, 0:1], ALU.mult, ALU.add, [bpl, bca, B_const], [bca])
                STT("dve", ca[:, 511:512], pr, w2c, ca[:, 511:512], ALU.mult, ALU.add, [bpr, bca, B_const], [bca])
            ca, cv = cacc[us][0], cacc[us][1]
            if USE_GELU_ACT:
                ACTF(ga[us], ca, AF.Gelu_apprx_tanh, [B_cacc[us][0]], [B_ga[us]])
            else:
                TT("dve", ga[us], ca, ca, ALU.mult, [B_cacc[us][0]], [B_ga[us]])
                TS("dve", ga[us], ga[us], 0.044715, 1.0, ALU.mult, ALU.add, [B_ga[us]], [B_ga[us]])
                TT("dve", ga[us], ga[us], ca, ALU.mult, [B_ga[us], B_cacc[us][0]], [B_ga[us]])
                ACTF(ga[us], ga[us], AF.Sigmoid, [B_ga[us]], [B_ga[us]], scale=GELU_C)
                TT("dve", ga[us], ga[us], ca, ALU.mult, [B_ga[us], B_cacc[us][0]], [B_ga[us]])
            TT("dve", mT[:, i, :], cv, ga[us], ALU.mult, [B_cacc[us][1], B_ga[us]], [B_mT])
        if tb == 1:
            halo_exchange()
        for q in range(4):
            t = 4 * tb + q
            slot = q % 2
            if q + 1 < 4:
                S.dma("sp", f"x1r{(q + 1) % 2}", x1r[(q + 1) % 2], x1_d[128 * (t + 1):128 * (t + 2), :], reads=[B_x1d[t + 1]], writes=[B_x1r[(q + 1) % 2]])
            for half in range(2):
                for k in range(22):
                    MM(PS[5 + half][:, :], mT[:, k, 128 * q:128 * q + 128], Wd[:, k, 512 * half:512 * half + 512], k == 0, k == 21,
                       [B_mT, B_wd], [PB[5 + half]])
            for half in range(2):
                hs = slice(512 * half, 512 * half + 512)
                TT("dve", x2t[:, hs], PS[5 + half][:, :], g2bc[:, hs], ALU.mult, [PB[5 + half], B_bc3], [B_x2t])
            TT("dve", x2t, x2t, x1r[slot], ALU.add, [B_x2t, B_x1r[slot]], [B_x2t])
            ACTF(junk3, x2t, AF.Square, [B_x2t], [B_junk3, B_ss], accum_out=ss_t[:, 4:5])
            RSQ(ss_t[:, 5:6], ss_t[:, 4:5], 1024.0 * EPS, [B_ss], [B_ss])
            TS("dve", ss_t[:, 5:6], ss_t[:, 5:6], 32.0, None, ALU.mult, None, [B_ss], [B_ss])
            STT("dve", x1r[slot], x2t, ss_t[:, 5:6], fwbc, ALU.mult, ALU.mult, [B_x2t, B_ss, B_bc3], [B_x1r[slot]])
            S.dma("sp", f"ost{slot}", out_d[128 * t:128 * t + 128, :], x1r[slot], reads=[B_x1r[slot]], writes=[])
    S.barrier()
    S.emit()
    return nc


_CACHE = {}


def _in_maps(inputs):
    g = lambda k: np.asarray(inputs[k], dtype=np.float32)
    x, c, ctx, c_ctx = g("x"), g("c"), g("ctx"), g("c_ctx")
    shared = {
        "w_mod": g("w_mod")[0], "b_mod": g("b_mod"), "norm1_w": g("norm1_w"), "w_in": g("w_in")[0],
        "a_f": g("ret_decay_f"), "a_b": g("ret_decay_b"), "w_ret_out": g("w_ret_out")[0],
        "w_four_out": g("w_four_out")[0], "w_bg": g("w_branch_gate")[0], "b_bg": g("b_branch_gate"),
        "w_out": g("w_out")[0], "norm2_w": g("norm2_w"), "w_up": g("w_up")[0], "conv_w": g("conv_w")[0],
        "conv_b": g("conv_b"), "w_down": g("w_down")[0], "final_norm_w": g("final_norm_w").reshape(1, D),
        "cc_col": np.ascontiguousarray(c_ctx.reshape(8, 128).T),
    }
    shared = {k: np.ascontiguousarray(v) for k, v in shared.items()}
    consts = [host_consts(j) for j in range(4)]
    maps = []
    for core in range(NCORES):
        b, j = core // 4, core % 4
        m = dict(shared)
        m["x_own"] = np.ascontiguousarray(x[b, TOK * j:TOK * (j + 1)])
        m["ctx"] = np.ascontiguousarray(ctx[b])
        m["c_col"] = np.ascontiguousarray(c[b].reshape(8, 128).T)
        m.update(consts[j])
        maps.append(m)
    return maps


def kernel(**inputs):
    if "nc" not in _CACHE:
        _CACHE["nc"] = build_program(4)
    nc = _CACHE["nc"]
    res = run_bass_kernel_spmd(nc, _in_maps(inputs), core_ids=list(range(NCORES)))
    out = np.empty((NB, SEQ, D), np.float32)
    for core in range(NCORES):
        b, j = core // 4, core % 4
        out[b, TOK * j:TOK * (j + 1)] = np.asarray(res.results[core]["out"], dtype=np.float32)
    return out
```

```python
import numpy as np
import ml_dtypes
from contextlib import ExitStack

import concourse.bass as bass
import concourse.mybir as mybir
from concourse.bass_utils import run_bass_kernel_spmd

F32 = mybir.dt.float32
BF16 = mybir.dt.bfloat16
ALU = mybir.AluOpType
AF = mybir.ActivationFunctionType
AX = mybir.AxisListType

D = 1024
SEQ = 8192
NB = 2
NCORES = 8
TOK = 2048
NT = 16
CTX = 256
H = 8
INC = 3584
FFN = 2816
NCH = 44
EPS = 1e-6
GROUPS = [[0, 1, 2, 3], [4, 5, 6, 7]]
GELU_C = 1.5957691216057308


class Tok:
    __slots__ = ("key", "val")

    def __init__(self, key, val):
        self.key = key
        self.val = val


class Buf:
    __slots__ = ("name", "w", "r", "excl")

    def __init__(self, name, excl=False):
        self.name = name
        self.w = None
        self.r = []
        self.excl = excl


class Sched:
    ENGS = ("pe", "act", "dve", "pool", "sp")

    def __init__(self, nc, stack):
        self.nc = nc
        self.stack = stack
        self.ops = {e: [] for e in self.ENGS}
        self.sems = {}
        self.cnt = {}
        self.seen = {e: {} for e in self.ENGS}
        for e in ("pe", "act", "dve", "pool"):
            self._mk(e)

    def _mk(self, key):
        if key not in self.sems:
            self.sems[key] = self.stack.enter_context(self.nc.semaphore("s_" + key))
            self.cnt[key] = 0

    def _deps(self, eng, reads, writes):
        deps = []
        for b in reads:
            if b.w is not None:
                deps.append(b.w)
        for b in writes:
            if b.w is not None:
                deps.append(b.w)
            deps.extend(b.r)
        waits = {}
        for t in deps:
            if t.key == "pe" and eng == "pe":
                continue
            if self.seen[eng].get(t.key, 0) >= t.val:
                continue
            waits[t.key] = max(waits.get(t.key, 0), t.val)
        for k, v in waits.items():
            self.seen[eng][k] = v
        return list(waits.items())

    def _commit(self, tok, reads, writes):
        for b in writes:
            b.w = tok
            b.r = []
        for b in reads:
            if b not in writes:
                b.r.append(tok)
                if len(b.r) > 64:
                    b.r = b.r[-48:]

    def op(self, eng, fn, reads=(), writes=()):
        ex = [b for b in reads if b.excl]
        if ex:
            reads = [b for b in reads if not b.excl]
            writes = list(writes) + ex
        waits = self._deps(eng, reads, writes)
        self.cnt[eng] += 1
        tok = Tok(eng, self.cnt[eng])
        self.ops[eng].append((waits, fn, eng, 1))
        self._commit(tok, reads, writes)
        return tok

    def dma(self, queue, key, out, in_, reads=(), writes=(), **kw):
        self._mk(key)
        waits = self._deps(queue, reads, writes)
        self.cnt[key] += 16
        tok = Tok(key, self.cnt[key])
        self.ops[queue].append((waits, lambda e: e.dma_start(out=out, in_=in_, **kw), key, 16))
        self._commit(tok, reads, writes)
        return tok

    def custom(self, queue, key, fn, reads=(), writes=()):
        import os
        if os.environ.get("KDBG_NOCC"):
            return None
        self._mk(key)
        waits = self._deps(queue, reads, writes)
        self.cnt[key] += 1
        tok = Tok(key, self.cnt[key])
        self.ops[queue].append((waits, fn, key, None))
        self._commit(tok, reads, writes)
        return tok

    def barrier(self, exclude=()):
        for e in self.ENGS:
            waits = []
            for k, v in self.cnt.items():
                if k == e or v == 0 or k in exclude:
                    continue
                if self.seen[e].get(k, 0) >= v:
                    continue
                self.seen[e][k] = v
                waits.append((k, v))
            if waits:
                self.ops[e].append((waits, None, None, 0))

    def emit(self):
        nc = self.nc
        handles = {"pe": "tensor", "act": "scalar", "dve": "vector", "pool": "gpsimd", "sp": "sync"}
        with nc.Block() as block:
            for e in self.ENGS:
                ops = self.ops[e]

                def body(engine, ops=ops):
                    for waits, fn, key, inc in ops:
                        for k, v in waits:
                            engine.wait_ge(self.sems[k], v)
                        if fn is None:
                            continue
                        ins = fn(engine)
                        if inc is None:
                            ins.then_inc(self.sems[key])
                        else:
                            ins.then_inc(self.sems[key], inc)

                getattr(block, handles[e])(body)


class Arena:
    def __init__(self, t, nelem):
        self.t = t
        self.n = nelem
        self.off = 0

    def reset(self, off=0):
        self.off = off

    def bf(self, nelem, parts=128):
        a = self.t[0:parts, self.off:self.off + nelem]
        self.off += nelem
        assert self.off <= self.n, (self.off, self.n)
        return a

    def f32(self, nelem, parts=128):
        return self.bf(2 * nelem, parts).bitcast(F32)


def _bf(a):
    return np.ascontiguousarray(a).astype(ml_dtypes.bfloat16)


def host_consts(j):
    c = {}
    c["ident"] = _bf(np.eye(128, dtype=np.float32))
    p = np.arange(128)
    t = (TOK * j + 128 * np.arange(NT)[None, :] + p[:, None]).astype(np.float32)
    row = np.floor(t / 64.0).astype(np.float32)
    col = (t - 64.0 * row).astype(np.float32)
    inv = (np.float32(10000.0) ** (-(np.arange(16, dtype=np.float32)) / np.float32(16))).astype(np.float32)
    ang = np.concatenate([row[:, :, None] * inv[None, None, :], col[:, :, None] * inv[None, None, :]], axis=-1)
    ang = ang.astype(np.float32)
    c["rope"] = np.concatenate([np.cos(ang), np.sin(ang)], axis=-1).astype(np.float32)
    s_ = p[:, None]
    c_ = p[None, :]
    mf = (c_ >= s_).astype(np.float32)
    mb = (c_ <= s_).astype(np.float32)
    c["mask"] = np.stack([mf, mf, mb, mb], axis=1).astype(np.float32)
    pc = np.zeros((128, 8), np.float32)
    pc[:, 0] = -(p + 1)
    pc[:, 1] = (p + 1)
    pc[:, 2] = -(128 - p)
    pc[:, 3] = (128 - p)
    pc[:, 4] = -(255 - p)
    pc[:, 5] = -(255 - 128 - p)
    pc[:, 6] = -p
    pc[:, 7] = -(128 + p)
    c["pcol"] = pc
    sel = np.zeros((128, 18), np.float32)
    sel[:, 16] = 1.0 if j == 0 else 0.0
    sel[:, 17] = 1.0 if j == 3 else 0.0
    sel[:, j] = 1.0
    sel[:, 4 + j] = 1.0
    if j > 0:
        sel[:, 8 + (j - 1)] = 1.0
    if j < 3:
        sel[:, 12 + (j + 1)] = 1.0
    c["sel"] = sel
    q = np.arange(64)
    hh, rr, mm = q // 32, (q % 32) // 8, q % 8
    n1 = 16 * rr + 8 * hh + mm
    k1 = np.arange(64)
    th = 2.0 * np.pi * ((n1[:, None] * k1[None, :]) % 64) / 64.0
    c["e64"] = _bf(np.concatenate([np.cos(th), -np.sin(th)], axis=1))
    n2 = np.arange(128)
    k2 = 32 * j + np.arange(32)
    kk = k1[None, :, None] + 64 * k2[None, None, :]
    ph = 2.0 * np.pi * ((n2[:, None, None] * kk) % 8192) / 8192.0
    twA = np.concatenate([np.cos(ph), -np.sin(ph)], axis=2)
    twB = np.concatenate([np.sin(ph), np.cos(ph)], axis=2)
    c["tw"] = _bf(np.stack([twA, twB], axis=2))
    ch = np.arange(128)
    pc2 = 2.0 * np.pi * ((ch[:, None] * ch[None, :]) % 128) / 128.0
    c["c128"] = _bf(np.stack([np.cos(pc2), np.sin(pc2)], axis=1) / 1024.0)
    c["ones"] = np.ones((128, 128), np.float32)
    c["identf"] = np.eye(128, dtype=np.float32)
    c["onesb"] = _bf(np.ones((128, 128), np.float32))
    return c


CONST_SPECS = [
    ("ident", [128, 128], BF16), ("rope", [128, 16, 64], F32), ("mask", [128, 4, 128], F32),
    ("pcol", [128, 8], F32), ("sel", [128, 18], F32), ("e64", [64, 128], BF16),
    ("tw", [128, 64, 2, 64], BF16), ("c128", [128, 2, 128], BF16), ("ones", [128, 128], F32),
    ("onesb", [128, 128], BF16), ("identf", [128, 128], F32),
]

INPUT_SPECS = [
    ("x_own", [TOK, D], F32), ("ctx", [CTX, D], F32), ("c_col", [128, 8], F32), ("cc_col", [128, 8], F32),
    ("w_mod", [D, 6 * D], F32), ("b_mod", [1, 6 * D], F32), ("norm1_w", [1, D], F32),
    ("w_in", [D, INC], F32), ("a_f", [1, H], F32), ("a_b", [1, H], F32),
    ("w_ret_out", [D, D], F32), ("w_four_out", [512, D], F32), ("w_bg", [D, 2 * D], F32),
    ("b_bg", [1, 2 * D], F32), ("w_out", [D, D], F32), ("norm2_w", [1, D], F32),
    ("w_up", [D, 2 * FFN], F32), ("conv_w", [3, 2 * FFN], F32), ("conv_b", [1, 2 * FFN], F32),
    ("w_down", [FFN, D], F32), ("final_norm_w", [1, D], F32),
]


def build_program(stage=4):
    import os
    STOP = os.environ.get("KDBG_STOP", "")
    nc = bass.Bass("TRN2", target_bir_lowering=False)
    stack = ExitStack()
    S = Sched(nc, stack)
    dr = {}
    for name, shape, dt in INPUT_SPECS + CONST_SPECS:
        dr[name] = nc.dram_tensor(name, shape, dt, kind="ExternalInput").ap()
    out_d = nc.dram_tensor("out", [TOK, D], F32, kind="ExternalOutput").ap()
    dbg_d = None
    if stage < 4:
        dbg_d = nc.dram_tensor("dbg", [128, 8 * TOK], BF16, kind="ExternalOutput").ap()
    rec_d = nc.dram_tensor("rec_scr", [NT, 128, 5120], BF16).ap()
    kv_d = nc.dram_tensor("kv_scr", [NT, 128, 1024], F32).ap()
    x1_d = nc.dram_tensor("x1_scr", [TOK, D], F32).ap()
    mod_d = nc.dram_tensor("mod_scr", [1, 6 * D + 2 * D], F32).ap()
    fin_d = [nc.dram_tensor(f"f_in{h}", [1024, 512], BF16) for h in range(2)]
    fout_d = [nc.dram_tensor(f"f_out{h}", [4096, 512], BF16) for h in range(2)]
    stin_d = nc.dram_tensor("st_in", [128, 1024], F32)
    stout_d = nc.dram_tensor("st_out", [512, 1024], F32)
    hin_d = nc.dram_tensor("halo_in", [128, 128], F32)
    hout_d = nc.dram_tensor("halo_out", [512, 128], F32)

    def sb(name, shape, dt):
        return stack.enter_context(nc.sbuf_tensor("sb_" + name, shape, dt))

    PS = [stack.enter_context(nc.psum_tensor(f"ps{i}", [128, 512], F32)) for i in range(8)]
    PB = [Buf(f"ps{i}", excl=True) for i in range(8)]

    def psbf(i):
        return PS[i][:, :].bitcast(BF16)

    ident = sb("ident", [128, 128], BF16)
    rope = sb("rope", [128, 16, 64], F32)
    mask = sb("mask", [128, 4, 128], F32)
    pcol = sb("pcol", [128, 8], F32)
    sel = sb("sel", [128, 18], F32)
    ones = sb("ones", [128, 128], F32)
    onesb = sb("onesb", [128, 128], BF16)
    identf = sb("identf", [128, 128], F32)
    sctx = sb("sctx", [128, 1024], F32)
    c128 = sb("c128", [128, 2, 128], BF16)
    e64 = sb("e64", [64, 128], BF16)
    smallf = sb("smallf", [128, 512], F32)
    smallb = sb("smallb", [128, 64], BF16)
    convc = sb("convc", [128, 4, NCH], F32)
    XH = sb("XH", [128, 8, TOK], BF16)
    R1n, R2n = 32768, 47104
    R1t = sb("R1", [128, R1n], BF16)
    R2t = sb("R2", [128, R2n], BF16)
    R1 = Arena(R1t, R1n)
    R2 = Arena(R2t, R2n)
    B_const = Buf("const")
    B_small = Buf("small")
    B_ss = Buf("ss")
    B_XH = [Buf(f"xh{t}") for t in range(NT)]

    def sf(a, b):
        return smallf[:, a:b]

    a_bc = sf(0, 16)
    ea = sf(16, 32)
    qf_sc, kf_sc, qb_sc, kb_sc = sf(32, 40), sf(40, 48), sf(48, 56), sf(56, 64)
    cxf_sc, cxb_sc = sf(64, 80), sf(80, 96)
    a_st, ea_st = sf(96, 104), sf(104, 112)
    cdp_f, cdp_b = sf(112, 180), sf(180, 248)
    silc = sf(248, 264)
    shcol = sf(264, 288)
    ss_t = sf(288, 296)
    bbg = sf(296, 312)
    bup = sf(312, 356)
    kcol = sf(356, 400)
    gst = sf(400, 464)
    silcb = smallb[:, 0:16]
    shcolb = smallb[:, 16:40]

    def TT(eng, out, in0, in1, op, reads, writes):
        return S.op(eng, lambda e: e.tensor_tensor(out=out, in0=in0, in1=in1, op=op), reads, writes)

    def TS(eng, out, in0, s1, s2, op0, op1, reads, writes):
        if s2 is None:
            return S.op(eng, lambda e: e.tensor_scalar(out=out, in0=in0, scalar1=s1, scalar2=None, op0=op0), reads, writes)
        return S.op(eng, lambda e: e.tensor_scalar(out=out, in0=in0, scalar1=s1, scalar2=s2, op0=op0, op1=op1), reads, writes)

    def STT(eng, out, in0, scalar, in1, op0, op1, reads, writes):
        return S.op(eng, lambda e: e.scalar_tensor_tensor(out=out, in0=in0, scalar=scalar, in1=in1, op0=op0, op1=op1), reads, writes)

    def CP(eng, out, in_, reads, writes):
        if eng == "act":
            return S.op("act", lambda e: e.activation(out=out, in_=in_, func=AF.Copy), reads, writes)
        return S.op(eng, lambda e: e.tensor_copy(out=out, in_=in_), reads, writes)

    def ACTF(out, in_, func, reads, writes, **kw):
        return S.op("act", lambda e: e.activation(out=out, in_=in_, func=func, **kw), reads, writes)

    def MM(out, lhsT, rhs, start, stop, reads, writes):
        return S.op("pe", lambda e: e.matmul(out, lhsT=lhsT, rhs=rhs, start=start, stop=stop), reads, writes)

    def TR(out, in_, reads, writes):
        return S.op("pe", lambda e: e.transpose(out=out, in_=in_, identity=ident[:]), reads + [B_const], writes)

    def RSQ(out, in_, c, reads, writes):
        ACTF(out, in_, AF.Sqrt, reads, writes, bias=c, scale=1.0)
        return S.op("dve", lambda e: e.reciprocal(out=out, in_=out), writes, writes)

    def bc3(ap2, n):
        return ap2.unsqueeze(2).to_broadcast([128, ap2.shape[1], n])

    R1.reset()
    Win = R1.bf(8 * INC).rearrange("p (k c) -> p k c", k=8)
    A1bc = R1.f32(1024)
    B_win = Buf("win")
    win_v = dr["w_in"].rearrange("(k p) c -> p k c", p=128)
    for k0 in (0, 4):
        S.dma("pool", "win", Win[:, k0:k0 + 4, :], win_v[:, k0:k0 + 4, :], writes=[B_win])
    for dst, name in ((ident, "ident"), (rope, "rope"), (mask, "mask"), (pcol, "pcol"), (sel, "sel"), (ones, "ones"),
                      (onesb, "onesb"), (c128, "c128"), (e64, "e64"), (identf, "identf")):
        S.dma("sp", "const", dst[:], dr[name], writes=[B_const])
    S.dma("sp", "const", a_bc[:, 0:8], dr["a_f"].partition_broadcast(128), writes=[B_const])
    S.dma("sp", "const", a_bc[:, 8:16], dr["a_b"].partition_broadcast(128), writes=[B_const])
    S.dma("sp", "const", silc[:, 0:8], dr["c_col"], writes=[B_const])
    S.dma("sp", "const", silc[:, 8:16], dr["cc_col"], writes=[B_const])
    R2.reset()
    stg = sb("stg", [64, 128], F32)
    B_stg = Buf("stg")

    def col_layout(dst, src_rows, n):
        S.dma("sp", "stg", stg[0:n, :], src_rows, writes=[B_stg])
        S.op("pe", lambda e: e.transpose(out=PS[6][:, 0:n], in_=stg[0:n, :], identity=identf[0:n, 0:n]), [B_stg, B_const], [PB[6]])
        CP("dve", dst, PS[6][:, 0:n], [PB[6]], [B_const])

    S.barrier()
    col_layout(bbg, dr["b_bg"].rearrange("o (c p) -> (o c) p", p=128), 16)
    for k in range(3):
        col_layout(convc[:, k, :], dr["conv_w"][k:k + 1, :].rearrange("o (c p) -> (o c) p", p=128), NCH)
    col_layout(convc[:, 3, :], dr["conv_b"].rearrange("o (c p) -> (o c) p", p=128), NCH)
    for di in range(2):
        for hh in range(2):
            CP("dve", a_st[64 * hh:64 * hh + 64, 4 * di:4 * di + 4],
               a_bc[64 * hh:64 * hh + 64, 8 * di:8 * di + 8].rearrange("p (a h) -> p a h", h=2)[:, :, hh], [B_const], [B_const])

    ACTF(ea, a_bc, AF.Exp, [B_const], [B_small])
    ACTF(ea_st, a_st, AF.Exp, [B_const], [B_small])
    for dst, src, col in ((qf_sc, ea[:, 0:8], 0), (kf_sc, ea[:, 0:8], 1), (qb_sc, ea[:, 8:16], 2), (kb_sc, ea[:, 8:16], 3)):
        ACTF(dst, src, AF.Exp, [B_small], [B_small], scale=pcol[:, col:col + 1])
    for tl in range(2):
        ACTF(cxf_sc[:, 8 * tl:8 * tl + 8], ea[:, 0:8], AF.Exp, [B_small], [B_small], scale=pcol[:, 4 + tl:5 + tl])
        ACTF(cxb_sc[:, 8 * tl:8 * tl + 8], ea[:, 8:16], AF.Exp, [B_small], [B_small], scale=pcol[:, 6 + tl:7 + tl])
    for n in range(17):
        ACTF(cdp_f[:, 4 * n:4 * n + 4], ea_st[:, 0:4], AF.Exp, [B_small], [B_small], scale=-128.0 * n)
        ACTF(cdp_b[:, 4 * n:4 * n + 4], ea_st[:, 4:8], AF.Exp, [B_small], [B_small], scale=-128.0 * n)
    TS("dve", qf_sc, qf_sc, 0.125, None, ALU.mult, None, [B_small], [B_small])
    TS("dve", qb_sc, qb_sc, 0.125, None, ALU.mult, None, [B_small], [B_small])
    ACTF(silc, silc, AF.Silu, [B_small], [B_small])
    CP("dve", silcb, silc, [B_small], [B_small])

    wst = [R2.f32(4096).rearrange("p (k c) -> p k c", k=8) for _ in range(2)]
    wmb = [R2.bf(4096).rearrange("p (k c) -> p k c", k=8) for _ in range(2)]
    bmod = [R2.f32(512, parts=1) for _ in range(2)]
    rowsb = [R2.f32(512, parts=1) for _ in range(4)]
    B_wst, B_wmb = [Buf("wst0"), Buf("wst1")], [Buf("wmb0"), Buf("wmb1")]
    B_bmod = [Buf("bm0"), Buf("bm1")]
    B_row = [Buf(f"row{i}") for i in range(4)]
    B_modd = Buf("modd")
    wmod_v = dr["w_mod"].rearrange("(k p) c -> p k c", p=128)
    cvt_eng = ["dve", "act"]
    nrow = [0]

    def mod_block(cb, wst, wmb, bmod, rowsb, B_wst, B_wmb, B_bmod, B_row):
        i = cb % 2
        S.dma("sp", f"wst{i}", wst[i], wmod_v[:, :, 512 * cb:512 * cb + 512], writes=[B_wst[i]])
        S.dma("sp", f"bm{i}", bmod[i], dr["b_mod"][:, 512 * cb:512 * cb + 512], writes=[B_bmod[i]])
        for half in range(2):
            CP(cvt_eng[(2 * cb + half) % len(cvt_eng)], wmb[i][:, 4 * half:4 * half + 4, :], wst[i][:, 4 * half:4 * half + 4, :], [B_wst[i]], [B_wmb[i]])
        for side in range(2 if cb < 4 else 1):
            for k in range(8):
                MM(PS[7][0:1, :], silcb[:, 8 * side + k:8 * side + k + 1], wmb[i][:, k, :], k == 0, k == 7, [B_small, B_wmb[i]], [PB[7]])
            r = nrow[0] % len(rowsb)
            nrow[0] += 1
            TT("dve", rowsb[r], PS[7][0:1, :], bmod[i], ALU.add, [PB[7], B_bmod[i]], [B_row[r]])
            off = 512 * cb if side == 0 else 6 * D + 512 * cb
            S.dma("sp", f"rowst{r}", mod_d[:, off:off + 512], rowsb[r], reads=[B_row[r]], writes=[B_modd])

    for cb in range(4):
        mod_block(cb, wst, wmb, bmod, rowsb, B_wst, B_wmb, B_bmod, B_row)
    S.barrier()
    for i, off in ((0, 0), (2, 6 * D)):
        col_layout(shcol[:, 8 * i:8 * i + 8], mod_d[:, off:off + D].rearrange("o (c p) -> (o c) p", p=128), 8)
    CP("dve", shcolb[:, 0:8], shcol[:, 0:8], [B_const], [B_small])
    CP("dve", shcolb[:, 16:24], shcol[:, 16:24], [B_const], [B_small])

    if STOP == "mod":
        S.barrier(); S.emit(); return nc
    B_bct = Buf("bct")

    R2.reset()
    xt = [R2.f32(1024) for _ in range(3)]
    hb = R2.bf(1024)
    junk = R2.bf(1024)
    qk2 = [R2.f32(1024) for _ in range(2)]
    qk_sb = qk2[0]
    off_rt1 = R2.off
    rt1, rt2 = R2.f32(1024), R2.f32(1024)
    rot = rt1
    ktok = R2.bf(1024)
    kpad = R2.bf(2048)
    qpad = R2.bf(2048)
    fsb = [R2.bf(512) for _ in range(2)]
    kvsb = [R2.f32(1024) for _ in range(2)]
    Est = R2.f32(1024)
    off_rec = R2.off
    recb = [R2.bf(5120) for _ in range(2)]
    hcT = R2.bf(2048).rearrange("p (k c) -> p k c", k=8)
    ctxv = R2.bf(1024)
    brows = R2.bf(INC, parts=2)
    lo_tmp = R2t[0:1, off_rt1:off_rt1 + INC]
    nwt = qk2[1]
    browsc = R2t[0:2, off_rec:off_rec + INC]
    A1c = R2t[:, off_rec + 5120:off_rec + 5120 + 2048].bitcast(F32)
    B_xt = [Buf("xt0"), Buf("xt1"), Buf("xt2")]
    B_hb, B_junk, B_qk = Buf("hb"), Buf("junk"), Buf("qk")
    B_qk2 = [Buf("qk2_0"), Buf("qk2_1")]
    B_rt1, B_rt2 = Buf("rt1"), Buf("rt2")
    B_rot = B_rt1
    B_ktok, B_kpad, B_qpad = Buf("ktok"), Buf("kpad"), Buf("qpad")
    B_fsb = [Buf("fsb0"), Buf("fsb1")]
    B_kvsb = [Buf("kvsb0"), Buf("kvsb1")]
    B_E = Buf("E")
    B_rec = [Buf("rec0"), Buf("rec1")]
    B_hcT, B_ctxv, B_sctx, B_bias = Buf("hcT"), Buf("ctxv"), Buf("sctx"), Buf("bias")

    S.dma("sp", "bcl", nwt, dr["norm1_w"].partition_broadcast(128), writes=[B_bct])
    S.dma("sp", "bcl", A1bc, mod_d[:, D:2 * D].partition_broadcast(128), reads=[B_modd], writes=[B_bct])
    STT("dve", A1bc, A1bc, 1.0, nwt, ALU.add, ALU.mult, [B_bct], [B_bct])
    TS("dve", A1bc, A1bc, 32.0, None, ALU.mult, None, [B_bct], [B_bct])

    def bias_rows(side, dstHL):
        lcol = 0 if side == 0 else 16
        for blk in range(7):
            if side == 1 and blk not in (1, 2, 3):
                continue
            for k in range(8):
                MM(PS[7][0:1, :], shcolb[:, lcol + k:lcol + k + 1], Win[:, k, 512 * blk:512 * blk + 512], k == 0, k == 7, [B_small, B_win], [PB[7]])
            cs = slice(512 * blk, 512 * blk + 512)
            CP("dve", dstHL[0:1, cs], PS[7][0:1, :], [PB[7]], [B_bias])
            TT("dve", lo_tmp[:, cs], PS[7][0:1, :], dstHL[0:1, cs], ALU.subtract, [PB[7], B_bias], [B_bias])
        S.dma("sp", "biaslo", dstHL[1:2, :], lo_tmp, reads=[B_bias], writes=[B_bias])

    bias_rows(0, brows)

    S.op("pool", lambda e: e.memset(kpad, 0.0), writes=[B_kpad])
    S.op("pool", lambda e: e.memset(qpad, 0.0), writes=[B_qpad])
    S.op("pool", lambda e: e.memset(Est, 0.0), writes=[B_E])

    def load_x(src_ap, slot):
        S.dma("sp", f"xt{slot}", xt[slot], src_ap, writes=[B_xt[slot]])

    def norm_part(slot, scale_bc):
        ACTF(junk, xt[slot], AF.Square, [B_xt[slot]], [B_junk, B_ss], accum_out=ss_t[:, 0:1])
        RSQ(ss_t[:, 1:2], ss_t[:, 0:1], 1024.0 * EPS, [B_ss], [B_ss])
        STT("dve", hb, xt[slot], ss_t[:, 1:2], scale_bc, ALU.mult, ALU.mult, [B_xt[slot], B_ss, B_bct], [B_hb])

    def tr_part(dstT, bdst, col0):
        pst = psbf(0)
        for k in range(8):
            TR(pst[:, 128 * k:128 * k + 128], hb[:, 128 * k:128 * k + 128], [B_hb], [PB[0]])
        CP("act", dstT[:, :, col0:col0 + 128], pst.rearrange("p (k c) -> p k c", k=8), [PB[0]], [bdst])

    def norm_transpose(slot, scale_bc, dstT, bdst, col0):
        norm_part(slot, scale_bc)
        tr_part(dstT, bdst, col0)

    def project(srcT, bsrc, col0, blocks, rows, consume):
        for i, blk in enumerate(blocks):
            bank = 1 + (i % 2)
            for k in range(8):
                MM(PS[bank][:, :], srcT[:, k, col0:col0 + 128], Win[:, k, 512 * blk:512 * blk + 512], k == 0, False, [bsrc, B_win], [PB[bank]])
            MM(PS[bank][:, :], onesb[0:2, :], rows[0:2, 512 * blk:512 * blk + 512], False, True, [B_bias, B_const], [PB[bank]])
            consume(blk, PS[bank], PB[bank])

    def k4(ap):
        return ap.rearrange("p (a h c) -> p a h c", a=4, h=2)

    def scaled_k(dirn, src_k, bsrc, sc_tile):
        kt = ktok[:, 512 * dirn:512 * dirn + 512]
        TT("dve", kt.rearrange("p (h d) -> p h d", h=8), src_k.rearrange("p (h d) -> p h d", h=8), bc3(sc_tile, 64), ALU.mult,
           [bsrc, B_small], [B_ktok])
        kp = k4(kpad[:, 1024 * dirn:1024 * dirn + 1024])
        kin = kt.rearrange("p (a h d) -> p a h d", a=4, h=2)
        for hh in range(2):
            CP("act", kp[:, :, hh, 64 * hh:64 * hh + 64], kin[:, :, hh, :], [B_ktok], [B_kpad])

    def kv_matmuls(dirn, vsrc, bv, bank):
        kp = k4(kpad[:, 1024 * dirn:1024 * dirn + 1024])
        for p4 in range(4):
            for hh in range(2):
                h = 2 * p4 + hh
                MM(PS[bank][:, 128 * p4:128 * p4 + 128], kp[:, p4, hh, :], vsrc[:, 128 * h:128 * h + 128], hh == 0, hh == 1, [B_kpad, bv], [PB[bank]])

    S.barrier()
    B_fin = [[Buf(f"fin{h}_{i}") for i in range(8)] for h in range(2)]
    B_fout = [Buf("fout0"), Buf("fout1")]
    B_recd = [Buf(f"recd{t}") for t in range(NT)]
    B_kvd = [Buf(f"kvd{t}") for t in range(NT)]

    def E3(ap):
        return ap.rearrange("p (a e) -> p a e", a=4)

    cdf_bc = bc3(cdp_f[:, 4:8], 128)
    def passA_front1(t):
        tr_part(XH, B_XH[t], 128 * t)

    def passA_front(t):
        slot = t % 2
        rb = recb[slot]

        qk_t, B_qkt = qk2[slot], B_qk2[slot]

        def consume(blk, ps, pb, t=t, rb=rb, slot=slot, qk_t=qk_t, B_qkt=B_qkt):
            if blk < 2:
                CP("act", qk_t[:, 512 * blk:512 * blk + 512], ps[:, :], [pb], [B_qkt])
            elif blk < 4:
                o0 = 3072 + 512 * (blk - 2)
                CP("act", rb[:, o0:o0 + 512], ps[:, :], [pb], [B_rec[slot]])
            elif blk < 6:
                o0 = 4096 + 512 * (blk - 4)
                ACTF(rb[:, o0:o0 + 512], ps[:, :], AF.Silu, [pb], [B_rec[slot]])
            else:
                CP("act", fsb[slot], ps[:, :], [pb], [B_fsb[slot]])
                hh_ = t // 8
                S.dma("sp", f"fst{slot}", fin_d[hh_].ap()[128 * (t % 8):128 * (t % 8) + 128, :], fsb[slot],
                      reads=[B_fsb[slot]], writes=[B_fin[hh_][t % 8]])
        project(XH, B_XH[t], 128 * t, [0, 1, 2, 3, 4, 5, 6], brows, consume)


    def passA_back(t):
        slot = t % 2
        rb = recb[slot]
        qk_t, B_qkt = qk2[slot], B_qk2[slot]
        def g4(ap):
            return ap.rearrange("p (g h c) -> p g h c", g=16, h=2)
        cosb = rope[:, t, 0:32].unsqueeze(1).to_broadcast([128, 32, 32])
        sinb = rope[:, t, 32:64].unsqueeze(1).to_broadcast([128, 16, 32])
        TT("dve", rt1.rearrange("p (g c) -> p g c", g=32), qk_t.rearrange("p (g c) -> p g c", g=32), cosb, ALU.mult, [B_qkt, B_const], [B_rt1])
        TT("dve", g4(rt2)[:, :, 0, :], g4(qk_t)[:, :, 1, :], sinb, ALU.mult, [B_qkt, B_const], [B_rt2])
        TT("dve", g4(rt2)[:, :, 1, :], g4(qk_t)[:, :, 0, :], sinb, ALU.mult, [B_qkt, B_const], [B_rt2])
        TT("dve", g4(rt1)[:, :, 0, :], g4(rt1)[:, :, 0, :], g4(rt2)[:, :, 0, :], ALU.subtract, [B_rt2], [B_rt1])
        TT("dve", g4(rt1)[:, :, 1, :], g4(rt1)[:, :, 1, :], g4(rt2)[:, :, 1, :], ALU.add, [B_rt2], [B_rt1])
        for dirn, sc in ((0, qf_sc), (1, qb_sc)):
            qp = k4(qpad[:, 1024 * dirn:1024 * dirn + 1024])
            qin = rot[:, 0:512].rearrange("p (a h d) -> p a h d", a=4, h=2)
            scv = sc.rearrange("p (a h) -> p a h", h=2)
            for hh in range(2):
                TT("dve", qp[:, :, hh, 64 * hh:64 * hh + 64], qin[:, :, hh, :], scv[:, :, hh].unsqueeze(2).to_broadcast([128, 4, 64]), ALU.mult,
                   [B_rot, B_small], [B_qpad])
        scaled_k(0, rot[:, 512:1024], B_rot, kf_sc)
        scaled_k(1, rot[:, 512:1024], B_rot, kb_sc)
        pst = [psbf(3), psbf(4), psbf(7)]
        for dirn in range(2):
            qp = k4(qpad[:, 1024 * dirn:1024 * dirn + 1024])
            for h in range(8):
                TR(pst[dirn][:, 128 * h:128 * h + 128], qp[:, h // 2, h % 2, :], [B_qpad], [PB[3 + dirn]])
        for dirn in range(2):
            for p4 in range(4):
                c0 = 512 * dirn + 128 * p4
                TR(pst[2][:, 128 * (4 * dirn + p4):128 * (4 * dirn + p4) + 128], ktok[:, c0:c0 + 128], [B_ktok], [PB[7]])
        CP("dve", rb[:, 0:1024], pst[0], [PB[3]], [B_rec[slot]])
        CP("dve", rb[:, 1024:2048], pst[1], [PB[4]], [B_rec[slot]])
        CP("dve", rb[:, 2048:3072], pst[2], [PB[7]], [B_rec[slot]])
        for dirn in range(2):
            kv_matmuls(dirn, rb[:, 3072:4096], B_rec[slot], 5 + dirn)
            CP("act", kvsb[slot][:, 512 * dirn:512 * dirn + 512], PS[5 + dirn][:, :], [PB[5 + dirn]], [B_kvsb[slot]])
        TT("pool", rt1[:, 0:512], Est[:, 0:512], kvsb[slot][:, 0:512], ALU.add, [B_E, B_kvsb[slot]], [B_rt1])
        TT("pool", E3(Est[:, 0:512]), E3(rt1[:, 0:512]), cdf_bc, ALU.mult, [B_rt1, B_small], [B_E])
        cdb_t = bc3(cdp_b[:, 4 * (t + 1):4 * (t + 1) + 4], 128)
        TT("pool", E3(rt1[:, 512:1024]), E3(kvsb[slot][:, 512:1024]), cdb_t, ALU.mult, [B_kvsb[slot], B_small], [B_rt1])
        TT("pool", Est[:, 512:1024], Est[:, 512:1024], rt1[:, 512:1024], ALU.add, [B_rt1, B_E], [B_E])
        S.dma("sp", f"rst{slot}", rec_d[t], rb, reads=[B_rec[slot]], writes=[B_recd[t]])
        S.dma("sp", f"kst{slot}", kv_d[t], kvsb[slot], reads=[B_kvsb[slot]], writes=[B_kvd[t]])
        if t % 8 == 7:
            hh_ = t // 8
            S.custom("pool", f"ccf{hh_}", lambda e, hh_=hh_: e.collective_compute(
                "AllGather", ALU.bypass, replica_groups=GROUPS, ins=[fin_d[hh_].ap().opt()], outs=[fout_d[hh_].ap().opt()]),
                reads=B_fin[hh_], writes=[B_fout[hh_]])


    for t0 in range(3):
        load_x(dr["x_own"][128 * t0:128 * t0 + 128, :], t0)
    norm_part(0, A1bc)
    passA_front1(0)
    norm_part(1, A1bc)
    passA_front(0)
    for t in range(NT):
        if t + 1 < NT:
            passA_front1(t + 1)
        if t + 2 < NT:
            norm_part((t + 2) % 3, A1bc)
        if t + 3 < NT:
            load_x(dr["x_own"][128 * (t + 3):128 * (t + 4), :], t % 3)
        if t + 1 < NT:
            passA_front(t + 1)
        passA_back(t)
    B_stin, B_stout = Buf("stin"), Buf("stout")
    S.dma("sp", "stst", stin_d.ap(), Est, reads=[B_E], writes=[B_stin])
    S.custom("pool", "ccst", lambda e: e.collective_compute(
        "AllGather", ALU.bypass, replica_groups=GROUPS, ins=[stin_d.ap().opt()], outs=[stout_d.ap().opt()]),
        reads=[B_stin], writes=[B_stout])
    S.barrier(exclude=("ccst",))
    S.dma("sp", "bcl", nwt, dr["norm1_w"].partition_broadcast(128), writes=[B_bct])
    S.dma("sp", "bcl", A1c, mod_d[:, 7 * D:8 * D].partition_broadcast(128), reads=[B_modd], writes=[B_bct])
    STT("dve", A1c, A1c, 1.0, nwt, ALU.add, ALU.mult, [B_bct], [B_bct])
    TS("dve", A1c, A1c, 32.0, None, ALU.mult, None, [B_bct], [B_bct])
    bias_rows(1, browsc)
    for tl in range(2):
        load_x(dr["ctx"][128 * tl:128 * tl + 128, :], tl)
    for tl in range(2):
        norm_transpose(tl, A1c, hcT, B_hcT, 128 * tl)

        def consume_ctx(blk, ps, pb):
            if blk == 1:
                CP("act", qk_sb[:, 512:1024], ps[:, :], [pb], [B_qk])
            else:
                CP("act", ctxv[:, 512 * (blk - 2):512 * (blk - 2) + 512], ps[:, :], [pb], [B_ctxv])
        project(hcT, B_hcT, 128 * tl, [1, 2, 3], browsc, consume_ctx)
        scaled_k(0, qk_sb[:, 512:1024], B_qk, cxf_sc[:, 8 * tl:8 * tl + 8])
        scaled_k(1, qk_sb[:, 512:1024], B_qk, cxb_sc[:, 8 * tl:8 * tl + 8])
        for dirn in range(2):
            kv_matmuls(dirn, ctxv, B_ctxv, 5 + dirn)
            dst = sctx[:, 512 * dirn:512 * dirn + 512]
            if tl == 0:
                CP("dve", dst, PS[5 + dirn][:, :], [PB[5 + dirn]], [B_sctx])
            else:
                TT("dve", dst, dst, PS[5 + dirn][:, :], ALU.add, [PB[5 + dirn], B_sctx], [B_sctx])


    S.barrier()
    R1.reset()
    SfT = R1.bf(NT * 512).rearrange("p (n c) -> p n c", n=NT)
    SbT = R1.bf(NT * 512).rearrange("p (n c) -> p n c", n=NT)
    ogT = R1.bf(8 * TOK).rearrange("p (h c) -> p h c", h=8)
    B_ST = Buf("ST")
    B_ogT = [Buf(f"ogT{t}") for t in range(NT)]
    R2.reset()
    kvall = R2.f32(NT * 1024).rearrange("p (n c) -> p n c", n=NT)
    Gst = R2.f32(4096).rearrange("p (r c) -> p r c", r=4)
    curf, curb = R2.f32(512), R2.f32(512)
    tmpf, tmpb = R2.f32(512), R2.f32(512)
    B_kvall, B_G = Buf("kvall"), Buf("G")
    B_curf, B_curb, B_tmpf, B_tmpb = Buf("curf"), Buf("curb"), Buf("tmpf"), Buf("tmpb")
    CP("dve", curf, sctx[:, 0:512], [B_sctx], [B_curf])
    CP("dve", curb, sctx[:, 512:1024], [B_sctx], [B_curb])
    S.dma("sp", "kvld", kvall, kv_d.rearrange("n p c -> p n c"), reads=B_kvd, writes=[B_kvall])
    S.dma("sp", "gld", Gst, stout_d.ap().rearrange("(r p) c -> p r c", p=128), reads=[B_stout], writes=[B_G])
    cd16f = bc3(cdp_f[:, 64:68], 128)
    cd16b = bc3(cdp_b[:, 64:68], 128)
    TS("dve", tmpf, curf, sel[:, 0:1], None, ALU.mult, None, [B_curf, B_const], [B_tmpf])
    for r in range(3):
        TT("dve", E3(curf), E3(curf), cd16f, ALU.mult, [B_curf, B_small], [B_curf])
        TT("dve", curf, curf, Gst[:, r, 0:512], ALU.add, [B_curf, B_G], [B_curf])
        STT("dve", tmpf, curf, sel[:, r + 1:r + 2], tmpf, ALU.mult, ALU.add, [B_curf, B_tmpf, B_const], [B_tmpf])
    TS("dve", tmpb, curb, sel[:, 7:8], None, ALU.mult, None, [B_curb, B_const], [B_tmpb])
    for r in (3, 2, 1):
        TT("dve", E3(curb), E3(curb), cd16b, ALU.mult, [B_curb, B_small], [B_curb])
        TT("dve", curb, curb, Gst[:, r, 512:1024], ALU.add, [B_curb, B_G], [B_curb])
        STT("dve", tmpb, curb, sel[:, 4 + r - 1:4 + r], tmpb, ALU.mult, ALU.add, [B_curb, B_tmpb, B_const], [B_tmpb])
    cdb_bc = bc3(cdp_b[:, 4:8], 128)
    for n in range(NT):
        m_ = NT - 1 - n
        CP("act", SfT[:, n, :], tmpf, [B_tmpf], [B_ST])
        TT("dve", curf, tmpf, kvall[:, n, 0:512], ALU.add, [B_tmpf, B_kvall], [B_curf])
        TT("dve", E3(tmpf), E3(curf), cdf_bc, ALU.mult, [B_curf, B_small], [B_tmpf])
        CP("act", SbT[:, m_, :], tmpb, [B_tmpb], [B_ST])
        TT("dve", curb, tmpb, kvall[:, m_, 512:1024], ALU.add, [B_tmpb, B_kvall], [B_curb])
        TT("dve", E3(tmpb), E3(curb), cdb_bc, ALU.mult, [B_curb, B_small], [B_tmpb])
    S.barrier()
    if STOP == "scan":
        S.emit(); return nc

    R2.reset()
    rbB = [R2.bf(5120) for _ in range(2)]
    Pm = [R2.bf(512) for _ in range(2)]
    sq = R2.f32(1024)
    tcen = R2.f32(1024)
    praw = [R2.bf(512) for _ in range(2)]
    ogtok = R2.bf(1024)
    B_rbB = [Buf("rbB0"), Buf("rbB1")]
    B_Pm = [Buf("Pm0"), Buf("Pm1")]
    B_sq, B_tcen, B_ogtok, B_gst = Buf("sq"), Buf("tcen"), Buf("ogtok"), Buf("gst")
    B_praw = [Buf("praw0"), Buf("praw1")]
    mask3 = mask[:, :, :]
    S.dma("sp", "rld0", rbB[0], rec_d[0], reads=[B_recd[0]], writes=[B_rbB[0]])
    obanks = [(5, 6), (3, 4)]

    def passB_mm(n):
        slot = n % 2
        rb = rbB[slot]

        def qT(di, h):
            return rb[:, (8 * di + h) * 128:(8 * di + h) * 128 + 128]

        def kT(di, p4):
            c0 = 2048 + (4 * di + p4) * 128
            return rb[:, c0:c0 + 128]
        def scores(p4):
            sbank = 1 + (p4 % 2)
            psc = PS[sbank][:, :].rearrange("p (a c) -> p a c", a=4)
            for di in range(2):
                for hh in range(2):
                    MM(psc[:, 2 * di + hh, :], kT(di, p4), qT(di, 2 * p4 + hh), True, True, [B_rbB[slot]], [PB[sbank]])
            pm = Pm[p4 % 2]
            CP("act", praw[p4 % 2], PS[sbank][:, :], [PB[sbank]], [B_praw[p4 % 2]])
            TT("pool", pm.rearrange("p (a c) -> p a c", a=4), praw[p4 % 2].rearrange("p (a c) -> p a c", a=4), mask3, ALU.mult,
               [B_praw[p4 % 2], B_const], [B_Pm[p4 % 2]])

        def omm(p4):
            pm3 = Pm[p4 % 2].rearrange("p (a c) -> p a c", a=4)
            obank = obanks[n % 2][p4 // 2]
            for hh in range(2):
                h = 2 * p4 + hh
                od = PS[obank][:, 128 * (h % 4):128 * (h % 4) + 128]
                vv = rb[:, 3072 + 128 * h:3072 + 128 * h + 128]
                MM(od, pm3[:, hh, :], vv, True, False, [B_Pm[p4 % 2], B_rbB[slot]], [PB[obank]])
                MM(od, qT(0, h), SfT[:, n, 128 * p4:128 * p4 + 128], False, False, [B_rbB[slot], B_ST], [PB[obank]])
                MM(od, pm3[:, 2 + hh, :], vv, False, False, [B_Pm[p4 % 2], B_rbB[slot]], [PB[obank]])
                MM(od, qT(1, h), SbT[:, n, 128 * p4:128 * p4 + 128], False, True, [B_rbB[slot], B_ST], [PB[obank]])
        scores(0)
        scores(1)
        omm(0)
        scores(2)
        omm(1)
        scores(3)
        omm(2)
        omm(3)

    def passB_gn(n):
        slot = n % 2
        rb = rbB[slot]
        ob = obanks[n % 2]
        o3s = [PS[ob[half]][:, :].rearrange("p (h e) -> p h e", h=4) for half in range(2)]
        for half in range(2):
            hs = slice(4 * half, 4 * half + 4)
            S.op("dve", lambda e, o3=o3s[half], hs=hs: e.tensor_reduce(out=gst[:, hs], in_=o3, axis=AX.X, op=ALU.add), [PB[ob[half]]], [B_gst])
            ACTF(sq[:, 512 * half:512 * half + 512], PS[ob[half]][:, :], AF.Square, [PB[ob[half]]], [B_sq])
        TS("dve", gst[:, 16:24], gst[:, 0:8], 1.0 / 128.0, None, ALU.mult, None, [B_gst], [B_gst])
        for half in range(2):
            TT("dve", tcen[:, 512 * half:512 * half + 512].rearrange("p (h e) -> p h e", h=4), o3s[half], bc3(gst[:, 16 + 4 * half:20 + 4 * half], 128),
               ALU.subtract, [PB[ob[half]], B_gst], [B_tcen])
        TT("dve", tcen, tcen, rb[:, 4096:5120], ALU.mult, [B_tcen, B_rbB[slot]], [B_tcen])
        S.op("dve", lambda e: e.tensor_reduce(out=gst[:, 8:16], in_=sq.rearrange("p (h e) -> p h e", h=8), axis=AX.X, op=ALU.add), [B_sq], [B_gst])
        TT("dve", gst[:, 24:32], gst[:, 16:24], gst[:, 16:24], ALU.mult, [B_gst], [B_gst])
        STT("dve", gst[:, 32:40], gst[:, 8:16], 1.0 / 128.0, gst[:, 24:32], ALU.mult, ALU.subtract, [B_gst], [B_gst])
        RSQ(gst[:, 40:48], gst[:, 32:40], EPS, [B_gst], [B_gst])
        TT("dve", ogtok.rearrange("p (h e) -> p h e", h=8), tcen.rearrange("p (h e) -> p h e", h=8), bc3(gst[:, 40:48], 128), ALU.mult,
           [B_tcen, B_gst], [B_ogtok])
        pst = psbf(0)
        for h in range(8):
            TR(pst[:, 128 * h:128 * h + 128], ogtok[:, 128 * h:128 * h + 128], [B_ogtok], [PB[0]])
        CP("act", ogT[:, :, 128 * n:128 * n + 128], pst.rearrange("p (h c) -> p h c", h=8), [PB[0]], [B_ogT[n]])
        if n + 2 < NT:
            S.dma("sp", f"rld{n % 2}", rbB[n % 2], rec_d[n + 2], reads=[B_recd[n + 2]], writes=[B_rbB[n % 2]])

    S.dma("sp", "rld1", rbB[1], rec_d[1], reads=[B_recd[1]], writes=[B_rbB[1]])
    passB_mm(0)
    for n in range(NT):
        if n + 1 < NT:
            passB_mm(n + 1)
        passB_gn(n)
    S.barrier()
    if stage == 1:
        S.dma("sp", "dbg", dbg_d.rearrange("p (h c) -> p h c", h=8), ogT, reads=B_ogT)
        S.barrier()
        S.emit()
        return nc

    zT = R1t[:, 0:4 * TOK].rearrange("p (g c) -> p g c", g=4)
    B_zT = Buf("zT")
    R2.reset()
    F1 = [R2.bf(16384, parts=64).rearrange("p (n c) -> p n c", n=128) for _ in range(1)]
    Aall = R2.bf(16384).rearrange("p (c k) -> p c k", c=128)
    tw = R2.bf(64 * 128).rearrange("p (k t c) -> p k t c", k=64, t=2)
    Xsb = [R2.bf(512).rearrange("p (k t c) -> p k t c", k=8, t=2) for _ in range(2)]
    B_F1, B_A, B_tw = Buf("F1"), Buf("A"), Buf("tw")
    B_Xsb = [Buf("Xsb0"), Buf("Xsb1")]
    S.dma("sp", "twld", tw, dr["tw"], writes=[B_tw])
    wst_s = R2.f32(1024).rearrange("p (k c) -> p k c", k=8)
    wmb_s = [R2.bf(1024).rearrange("p (k c) -> p k c", k=8) for _ in range(2)]
    bmod_s = [R2.f32(128, parts=1) for _ in range(2)]
    row_s = R2.f32(128, parts=1)
    B_wsts, B_rows = Buf("wsts"), Buf("rows")
    B_wmbs = [Buf("wmbs0"), Buf("wmbs1")]
    B_bms = [Buf("bms0"), Buf("bms1")]

    def mod_prep(j_):
        c0 = 4 * 512 + 128 * j_
        S.dma("sp", "wsts", wst_s, wmod_v[:, :, c0:c0 + 128], writes=[B_wsts])
        S.dma("sp", f"bms{j_ % 2}", bmod_s[j_ % 2], dr["b_mod"][:, c0:c0 + 128], writes=[B_bms[j_ % 2]])
        CP("pool", wmb_s[j_ % 2], wst_s, [B_wsts], [B_wmbs[j_ % 2]])

    def mod_mm(j_):
        c0 = 4 * 512 + 128 * j_
        for k in range(8):
            MM(PS[7][0:1, 0:128], silcb[:, k:k + 1], wmb_s[j_ % 2][:, k, :], k == 0, k == 7, [B_small, B_wmbs[j_ % 2]], [PB[7]])
        TT("dve", row_s, PS[7][0:1, 0:128], bmod_s[j_ % 2], ALU.add, [PB[7], B_bms[j_ % 2]], [B_rows])
        S.dma("sp", "rowss", mod_d[:, c0:c0 + 128], row_s, reads=[B_rows], writes=[B_modd])

    def mod_subblock(j_):
        if j_ == 0:
            mod_prep(0)
        if j_ + 1 < 32:
            mod_prep(j_ + 1)
        mod_mm(j_)

    nsub = [0]
    for g in range(4):
        for h2 in range(2):
            src = fout_d[h2].ap().rearrange("(q n) c -> q n c", n=128)[:, :, 128 * g:128 * g + 128]
            S.dma("sp", "f1ld", F1[0][32 * h2:32 * h2 + 32, :, :], src, reads=[B_fout[h2]], writes=[B_F1])
        for c4 in range(32):
            if c4 % 8 == 0 and nsub[0] < 32:
                mod_subblock(nsub[0])
                nsub[0] += 1
            bank = 1 + (c4 % 2)
            for cc in range(4):
                ch = 4 * c4 + cc
                MM(PS[bank][:, 128 * cc:128 * cc + 128], F1[0][:, :, ch], e64[:, :], True, True, [B_F1, B_const], [PB[bank]])
            CP("act" if c4 % 2 == 0 else "dve", Aall[:, 4 * c4:4 * c4 + 4, :], PS[bank][:, :].rearrange("p (c k) -> p c k", c=4), [PB[bank]], [B_A])
        for kb in range(8):
            if kb % 2 == 0 and nsub[0] < 32:
                mod_subblock(nsub[0])
                nsub[0] += 1
            xb = 3 + (kb % 2)
            px = PS[xb][:, :].rearrange("p (k t c) -> p k t c", k=8, t=2)
            for kk in range(8):
                k1 = 8 * kb + kk
                ar = Aall[:, :, k1]
                ai = Aall[:, :, 64 + k1]
                pxk = PS[xb][:, 64 * kk:64 * kk + 64]
                MM(pxk, ar, tw[:, k1, 0, :], True, False, [B_A, B_tw], [PB[xb]])
                MM(pxk, ai, tw[:, k1, 1, :], False, True, [B_A, B_tw], [PB[xb]])
            xs = Xsb[kb % 2]
            CP("act", xs, px, [PB[xb]], [B_Xsb[kb % 2]])
            zb = 5 + (kb % 2)
            pz = PS[zb][:, 0:256].rearrange("p (k c) -> p k c", k=8)
            MM(pz, c128[:, 0, :], xs[:, :, 0, :], True, False, [B_Xsb[kb % 2], B_const], [PB[zb]])
            MM(pz, c128[:, 1, :], xs[:, :, 1, :], False, True, [B_Xsb[kb % 2], B_const], [PB[zb]])
            zdst = zT[:, g, :].rearrange("p (b a) -> p a b", a=64)[:, 8 * kb:8 * kb + 8, :]
            CP("dve", zdst, pz, [PB[zb]], [B_zT])
    S.barrier()
    col_layout(shcol[:, 8:16], mod_d[:, 3 * D:4 * D].rearrange("o (c p) -> (o c) p", p=128), 8)
    CP("dve", shcolb[:, 8:16], shcol[:, 8:16], [B_const], [B_small])
    if stage == 2:
        S.dma("sp", "dbg", dbg_d[:, 0:4 * TOK], R1t[:, 0:4 * TOK], reads=[B_zT])
        S.barrier()
        S.emit()
        return nc

    Wout = R2t[:, 37888:46080].rearrange("p (k c) -> p k c", k=8)
    B_wout = Buf("wout")
    wout_v = dr["w_out"].rearrange("(k p) c -> p k c", p=128)
    R2.reset()
    mergedT = R2.bf(8 * TOK).rearrange("p (k c) -> p k c", k=8)
    wsl = [R2.bf(28 * 128).rearrange("p (k c) -> p k c", k=28) for _ in range(2)]
    gts = [R2.f32(512) for _ in range(4)]
    m12 = [R2.f32(512) for _ in range(2)]
    B_wsl = [Buf("wsl0"), Buf("wsl1")]
    B_gts = [Buf(f"gt{i}") for i in range(4)]
    B_m12 = [Buf("m1"), Buf("m2")]
    B_mg = [Buf(f"mg{i}") for i in range(4)]
    wro_v = dr["w_ret_out"].rearrange("(k p) c -> p k c", p=128)
    wfo_v = dr["w_four_out"].rearrange("(k p) c -> p k c", p=128)
    wbg_v = dr["w_bg"].rearrange("(k p) c -> p k c", p=128)

    def load_wsl(oc):
        i = oc % 2
        cs = slice(128 * oc, 128 * oc + 128)
        S.dma("pool", f"wsl{i}", wsl[i][:, 0:8, :], wro_v[:, :, cs], writes=[B_wsl[i]])
        S.dma("pool", f"wsl{i}", wsl[i][:, 8:12, :], wfo_v[:, :, cs], writes=[B_wsl[i]])
        S.dma("pool", f"wsl{i}", wsl[i][:, 12:20, :], wbg_v[:, :, cs], writes=[B_wsl[i]])
        S.dma("pool", f"wsl{i}", wsl[i][:, 20:28, :], wbg_v[:, :, D + 128 * oc:D + 128 * oc + 128], writes=[B_wsl[i]])
    load_wsl(0)
    S.dma("pool", "wout", Wout, wout_v, writes=[B_wout])
    B_bbg = Buf("bbg")
    for oc in range(8):
        i = oc % 2
        if oc + 1 < 8:
            load_wsl(oc + 1)
        w = wsl[i]
        for gi in range(2):
            for k in range(8):
                MM(PS[7][:, gi:gi + 1], w[:, 12 + 8 * gi + k, :], shcolb[:, k:k + 1], k == 0, k == 7, [B_wsl[i], B_small], [PB[7]])
        bcol = bbg.rearrange("p (g c) -> p g c", g=2)[:, :, oc]
        TT("dve", bcol, bcol, PS[7][:, 0:2], ALU.add, [PB[7], B_const], [B_bbg])
        for tb in range(4):
            ts_ = slice(512 * tb, 512 * tb + 512)
            xh_b = [B_XH[4 * tb + q] for q in range(4)]
            og_b = [B_ogT[4 * tb + q] for q in range(4)]
            for k in range(8):
                MM(PS[1][:, :], w[:, k, :], ogT[:, k, ts_], k == 0, k == 7, [B_wsl[i]] + og_b, [PB[1]])
            for k in range(4):
                MM(PS[2][:, :], w[:, 8 + k, :], zT[:, k, ts_], k == 0, k == 3, [B_wsl[i], B_zT], [PB[2]])
            for gi in range(2):
                for k in range(8):
                    MM(PS[3 + gi][:, :], w[:, 12 + 8 * gi + k, :], XH[:, k, ts_], k == 0, k == 7, [B_wsl[i]] + xh_b, [PB[3 + gi]])
            g0 = 2 * (tb % 2)
            for gi in range(2):
                ACTF(gts[g0 + gi], PS[3 + gi][:, :], AF.Sigmoid, [PB[3 + gi], B_bbg], [B_gts[g0 + gi]], bias=bbg[:, 8 * gi + oc:8 * gi + oc + 1])
            TT("dve", m12[0], gts[g0], PS[1][:, :], ALU.mult, [B_gts[g0], PB[1]], [B_m12[0]])
            TT("dve", m12[1], gts[g0 + 1], PS[2][:, :], ALU.mult, [B_gts[g0 + 1], PB[2]], [B_m12[1]])
            TT("pool", mergedT[:, oc, ts_], m12[0], m12[1], ALU.add, [B_m12[0], B_m12[1]], [B_mg[tb]])
    S.barrier()

    R2.reset(8 * TOK)
    xt2 = [R2.f32(1024) for _ in range(2)]
    x1t = [R2.f32(1024) for _ in range(2)]
    tmpy = R2.f32(1024)
    g1bc = R2.f32(1024)
    A2bc = R2.f32(1024)
    nw2t = R2.f32(1024)
    hb2 = R2.bf(1024)
    junk2 = R2.bf(1024)
    B_xt2 = [Buf("xt2_0"), Buf("xt2_1")]
    B_x1t = [Buf("x1t0"), Buf("x1t1")]
    B_tmpy, B_hb2, B_junk2, B_bc2 = Buf("tmpy"), Buf("hb2"), Buf("junk2"), Buf("bc2")
    B_x1d = [Buf(f"x1d{t}") for t in range(NT)]
    S.dma("sp", "bcl", g1bc, mod_d[:, 2 * D:3 * D].partition_broadcast(128), writes=[B_bc2])
    S.dma("sp", "bcl", A2bc, mod_d[:, 4 * D:5 * D].partition_broadcast(128), writes=[B_bc2])
    S.dma("sp", "bcl", nw2t, dr["norm2_w"].partition_broadcast(128), writes=[B_bc2])
    STT("dve", A2bc, A2bc, 1.0, nw2t, ALU.add, ALU.mult, [B_bc2], [B_bc2])
    TS("dve", A2bc, A2bc, 32.0, None, ALU.mult, None, [B_bc2], [B_bc2])
    Wd = R1t[:, 0:22 * D].rearrange("p (k c) -> p k c", k=22)
    B_wd = Buf("wd")
    wd_v = dr["w_down"].rearrange("(k p) c -> p k c", p=128)
    for k0 in (0, 11):
        S.dma("pool", "wd", Wd[:, k0:k0 + 11, :], wd_v[:, k0:k0 + 11, :], writes=[B_wd])
    S.dma("sp", "xt2_0", xt2[0], dr["x_own"][0:128, :], writes=[B_xt2[0]])
    ybanks = [(1, 2), (3, 4)]

    def y_mm(t):
        for half in range(2):
            bk = ybanks[t % 2][half]
            for k in range(8):
                MM(PS[bk][:, :], mergedT[:, k, 128 * t:128 * t + 128], Wout[:, k, 512 * half:512 * half + 512], k == 0, k == 7,
                   [B_mg[t // 4], B_wout], [PB[bk]])

    def y_post(t):
        slot = t % 2
        if t + 1 < NT:
            S.dma("sp", f"xt2_{(t + 1) % 2}", xt2[(t + 1) % 2], dr["x_own"][128 * (t + 1):128 * (t + 2), :], writes=[B_xt2[(t + 1) % 2]])
        for half in range(2):
            bk = ybanks[t % 2][half]
            hs = slice(512 * half, 512 * half + 512)
            TT("dve", tmpy[:, hs], PS[bk][:, :], g1bc[:, hs], ALU.mult, [PB[bk], B_bc2], [B_tmpy])
        TT("dve", x1t[slot], tmpy, xt2[slot], ALU.add, [B_tmpy, B_xt2[slot]], [B_x1t[slot]])
        S.dma("sp", f"x1st{slot}", x1_d[128 * t:128 * t + 128, :], x1t[slot], reads=[B_x1t[slot]], writes=[B_x1d[t]])
        ACTF(junk2, x1t[slot], AF.Square, [B_x1t[slot]], [B_junk2, B_ss], accum_out=ss_t[:, 2:3])
        RSQ(ss_t[:, 3:4], ss_t[:, 2:3], 1024.0 * EPS, [B_ss], [B_ss])
        STT("dve", hb2, x1t[slot], ss_t[:, 3:4], A2bc, ALU.mult, ALU.mult, [B_x1t[slot], B_ss, B_bc2], [B_hb2])
        pst = psbf(0)
        for k in range(8):
            TR(pst[:, 128 * k:128 * k + 128], hb2[:, 128 * k:128 * k + 128], [B_hb2], [PB[0]])
        CP("act", XH[:, :, 128 * t:128 * t + 128], pst.rearrange("p (k c) -> p k c", k=8), [PB[0]], [B_XH[t]])

    y_mm(0)
    for t in range(NT):
        if t + 1 < NT:
            y_mm(t + 1)
        y_post(t)
    S.barrier()
    if stage == 3:
        for t in range(NT):
            S.dma("sp", "xt2_0", xt2[0], x1_d[128 * t:128 * t + 128, :], reads=[B_x1d[t]], writes=[B_xt2[0]])
            S.dma("sp", "outst", out_d[128 * t:128 * t + 128, :], xt2[0], reads=[B_xt2[0]], writes=[])
        S.dma("sp", "dbg", dbg_d, R2t[:, 0:8 * TOK], reads=B_mg)
        S.barrier()
        S.emit()
        return nc

    R2.reset()
    USE_GELU_ACT = not os.environ.get("KDBG_GELU_SIG")
    mT = R2.bf(22 * 512).rearrange("p (k c) -> p k c", k=22)
    wup = [R2.bf(2048).rearrange("p (a k c) -> p a k c", a=2, k=8) for _ in range(3)]
    cacc = [[R2.f32(512) for _ in range(2)] for _ in range(2)]
    ga = [R2.f32(512) for _ in range(2)]
    ub = R2.f32(NCH * 8).rearrange("p (c t) -> p c t", c=NCH)
    hal = R2.f32(2 * NCH).rearrange("p (s c) -> p s c", s=2)
    hsend = R2.f32(128)
    hall = R2.f32(512).rearrange("p (r c) -> p r c", r=4)
    kcc = R2.f32(3 * NCH).rearrange("p (s c) -> p s c", s=3)
    x1r = [R2.f32(1024) for _ in range(2)]
    x2t = R2.f32(1024)
    g2bc = R2.f32(1024)
    fwbc = R2.f32(1024)
    junk3 = R2.bf(1024)
    B_wup = [Buf(f"wup{i}") for i in range(3)]
    B_cacc = [[Buf(f"ca{s_}{a}") for a in range(2)] for s_ in range(2)]
    B_ga = [Buf("ga0"), Buf("ga1")]
    B_mT, B_ub, B_hal, B_hsend, B_hall, B_kcc = Buf("mT"), Buf("ub"), Buf("hal"), Buf("hsend"), Buf("hall"), Buf("kcc")
    B_x1r = [Buf("x1r0"), Buf("x1r1")]
    B_x2t, B_bc3, B_junk3 = Buf("x2t"), Buf("bc3"), Buf("junk3")
    B_hin, B_hout = Buf("hin"), Buf("hout")
    S.dma("sp", "bcl", g2bc, mod_d[:, 5 * D:6 * D].partition_broadcast(128), writes=[B_bc3])
    S.dma("sp", "bcl", fwbc, dr["final_norm_w"].partition_broadcast(128), writes=[B_bc3])
    wup_v = dr["w_up"].rearrange("(k p) c -> p k c", p=128)
    nload = [0]

    def load_wup(i):
        s_ = nload[0] % 3
        nload[0] += 1
        S.dma("pool", f"wup{s_}", wup[s_], wup_v.rearrange("p k (a c) -> p a k c", a=2)[:, :, :, 128 * i:128 * i + 128], writes=[B_wup[s_]])
        return s_

    xb8 = XH[:, :, :].rearrange("p k (b c) -> p k b c", c=512)
    bnd = R2.bf(72).rearrange("p (k c) -> p k c", k=8)
    B_bnd = Buf("bnd")
    CP("dve", bnd[:, :, 0], shcolb[:, 8:16], [B_small], [B_bnd])
    CP("dve", bnd[:, :, 1:5], xb8[:, :, :, 0], B_XH, [B_bnd])
    CP("dve", bnd[:, :, 5:9], xb8[:, :, :, 511], B_XH, [B_bnd])
    B_bup = Buf("bup")
    TT("dve", kcc[:, 0, :], convc[:, 0, :], convc[:, 1, :], ALU.add, [B_const], [B_kcc])
    TT("dve", kcc[:, 0, :], kcc[:, 0, :], convc[:, 2, :], ALU.add, [B_const, B_kcc], [B_kcc])
    S.op("pool", lambda e: e.memset(hsend, 0.0), writes=[B_hsend])

    def halo_exchange():
        CP("dve", hsend[:, 0:NCH], ub[:, :, 0], [B_ub], [B_hsend])
        CP("dve", hsend[:, NCH:2 * NCH], ub[:, :, 7], [B_ub], [B_hsend])
        S.dma("sp", "hst", hin_d.ap(), hsend, reads=[B_hsend], writes=[B_hin])
        S.custom("pool", "cch", lambda e: e.collective_compute(
            "AllGather", ALU.bypass, replica_groups=GROUPS, ins=[hin_d.ap().opt()], outs=[hout_d.ap().opt()]),
            reads=[B_hin], writes=[B_hout])
        S.dma("sp", "hld", hall, hout_d.ap().rearrange("(r p) c -> p r c", p=128), reads=[B_hout], writes=[B_hall])
        for side, (c0, s0) in enumerate(((NCH, 8), (0, 12))):
            TS("dve", hal[:, side, :], hall[:, 0, c0:c0 + NCH], sel[:, s0:s0 + 1], None, ALU.mult, None, [B_hall, B_const], [B_hal])
            for r in range(1, 4):
                STT("dve", hal[:, side, :], hall[:, r, c0:c0 + NCH], sel[:, s0 + r:s0 + r + 1], hal[:, side, :], ALU.mult, ALU.add,
                    [B_hall, B_hal, B_const], [B_hal])
            STT("dve", hal[:, side, :], kcc[:, 2, :], sel[:, 16 + side:17 + side], hal[:, side, :], ALU.mult, ALU.add, [B_kcc, B_hal, B_const], [B_hal])

    for tb in (1, 2, 0, 3):
        ts_ = slice(512 * tb, 512 * tb + 512)
        xh_b = [B_XH[4 * tb + q] for q in range(4)]
        pend = [load_wup(0), load_wup(1)]
        S.dma("sp", "x1r0", x1r[0], x1_d[512 * tb:512 * tb + 128, :], reads=[B_x1d[4 * tb]], writes=[B_x1r[0]])
        for i in range(22):
            s_ = pend.pop(0)
            if i + 2 < 22:
                pend.append(load_wup(i + 2))
            us = i % 2
            for a in range(2):
                ch = i + 22 * a
                bank = 1 + 2 * us + a
                if tb == 1:
                    for k in range(8):
                        MM(PS[7][:, 0:9], wup[s_][:, a, k, :], bnd[:, k, :], k == 0, k == 7, [B_wup[s_], B_bnd], [PB[7]])
                    CP("act", ub[:, ch, :].rearrange("p (b e) -> p e b", e=2), PS[7][:, 1:9].rearrange("p (e b) -> p e b", e=2), [PB[7]], [B_ub])
                    CP("dve", bup[:, ch:ch + 1], PS[7][:, 0:1], [PB[7]], [B_bup])
                    STT("dve", kcc[:, 1, ch:ch + 1], bup[:, ch:ch + 1], kcc[:, 0, ch:ch + 1], convc[:, 3, ch:ch + 1], ALU.mult, ALU.add,
                        [B_bup, B_kcc, B_const], [B_kcc])
                    TS("dve", kcc[:, 2, ch:ch + 1], bup[:, ch:ch + 1], -1.0, None, ALU.mult, None, [B_bup], [B_kcc])
                for k in range(8):
                    MM(PS[bank][:, :], wup[s_][:, a, k, :], XH[:, k, ts_], k == 0, k == 7, [B_wup[s_]] + xh_b, [PB[bank]])
                ca = cacc[us][a]
                bca = B_cacc[us][a]
                w0c, w1c, w2c = convc[:, 0, ch:ch + 1], convc[:, 1, ch:ch + 1], convc[:, 2, ch:ch + 1]
                ACTF(ca, PS[bank][:, :], AF.Identity, [PB[bank], B_kcc, B_const], [bca], scale=w1c, bias=kcc[:, 1, ch:ch + 1])
                STT("dve", ca[:, 1:512], PS[bank][:, 0:511], w0c, ca[:, 1:512], ALU.mult, ALU.add, [PB[bank], bca, B_const], [bca])
                STT("dve", ca[:, 0:511], PS[bank][:, 1:512], w2c, ca[:, 0:511], ALU.mult, ALU.add, [PB[bank], bca, B_const], [bca])
                if tb == 0:
                    pl, bpl = hal[:, 0, ch:ch + 1], B_hal
                else:
                    pl, bpl = ub[:, ch, 2 * tb - 1:2 * tb], B_ub
                if tb == 3:
                    pr, bpr = hal[:, 1, ch:ch + 1], B_hal
                else:
                    pr, bpr = ub[:, ch, 2 * tb + 2:2 * tb + 3], B_ub
                STT("dve", ca[:, 0:1], pl, w0c, ca[:, 0:1], ALU.mult, ALU.add, [bpl, bca, B_const], [bca])
                STT("dve", ca[:, 511:512], pr, w2c, ca[:, 511:512], ALU.mult, ALU.add, [bpr, bca, B_const], [bca])
            ca, cv = cacc[us][0], cacc[us][1]
            if USE_GELU_ACT:
                ACTF(ga[us], ca, AF.Gelu_apprx_tanh, [B_cacc[us][0]], [B_ga[us]])
            else:
                TT("dve", ga[us], ca, ca, ALU.mult, [B_cacc[us][0]], [B_ga[us]])
                TS("dve", ga[us], ga[us], 0.044715, 1.0, ALU.mult, ALU.add, [B_ga[us]], [B_ga[us]])
                TT("dve", ga[us], ga[us], ca, ALU.mult, [B_ga[us], B_cacc[us][0]], [B_ga[us]])
                ACTF(ga[us], ga[us], AF.Sigmoid, [B_ga[us]], [B_ga[us]], scale=GELU_C)
                TT("dve", ga[us], ga[us], ca, ALU.mult, [B_ga[us], B_cacc[us][0]], [B_ga[us]])
            TT("dve", mT[:, i, :], cv, ga[us], ALU.mult, [B_cacc[us][1], B_ga[us]], [B_mT])
        if tb == 1:
            halo_exchange()
        for q in range(4):
            t = 4 * tb + q
            slot = q % 2
            if q + 1 < 4:
                S.dma("sp", f"x1r{(q + 1) % 2}", x1r[(q + 1) % 2], x1_d[128 * (t + 1):128 * (t + 2), :], reads=[B_x1d[t + 1]], writes=[B_x1r[(q + 1) % 2]])
            for half in range(2):
                for k in range(22):
                    MM(PS[5 + half][:, :], mT[:, k, 128 * q:128 * q + 128], Wd[:, k, 512 * half:512 * half + 512], k == 0, k == 21,
                       [B_mT, B_wd], [PB[5 + half]])
            for half in range(2):
                hs = slice(512 * half, 512 * half + 512)
                TT("dve", x2t[:, hs], PS[5 + half][:, :], g2bc[:, hs], ALU.mult, [PB[5 + half], B_bc3], [B_x2t])
            TT("dve", x2t, x2t, x1r[slot], ALU.add, [B_x2t, B_x1r[slot]], [B_x2t])
            ACTF(junk3, x2t, AF.Square, [B_x2t], [B_junk3, B_ss], accum_out=ss_t[:, 4:5])
            RSQ(ss_t[:, 5:6], ss_t[:, 4:5], 1024.0 * EPS, [B_ss], [B_ss])
            TS("dve", ss_t[:, 5:6], ss_t[:, 5:6], 32.0, None, ALU.mult, None, [B_ss], [B_ss])
            STT("dve", x1r[slot], x2t, ss_t[:, 5:6], fwbc, ALU.mult, ALU.mult, [B_x2t, B_ss, B_bc3], [B_x1r[slot]])
            S.dma("sp", f"ost{slot}", out_d[128 * t:128 * t + 128, :], x1r[slot], reads=[B_x1r[slot]], writes=[])
    S.barrier()
    S.emit()
    return nc


_CACHE = {}


def _in_maps(inputs):
    g = lambda k: np.asarray(inputs[k], dtype=np.float32)
    x, c, ctx, c_ctx = g("x"), g("c"), g("ctx"), g("c_ctx")
    shared = {
        "w_mod": g("w_mod")[0], "b_mod": g("b_mod"), "norm1_w": g("norm1_w"), "w_in": g("w_in")[0],
        "a_f": g("ret_decay_f"), "a_b": g("ret_decay_b"), "w_ret_out": g("w_ret_out")[0],
        "w_four_out": g("w_four_out")[0], "w_bg": g("w_branch_gate")[0], "b_bg": g("b_branch_gate"),
        "w_out": g("w_out")[0], "norm2_w": g("norm2_w"), "w_up": g("w_up")[0], "conv_w": g("conv_w")[0],
        "conv_b": g("conv_b"), "w_down": g("w_down")[0], "final_norm_w": g("final_norm_w").reshape(1, D),
        "cc_col": np.ascontiguousarray(c_ctx.reshape(8, 128).T),
    }
    shared = {k: np.ascontiguousarray(v) for k, v in shared.items()}
    consts = [host_consts(j) for j in range(4)]
    maps = []
    for core in range(NCORES):
        b, j = core // 4, core % 4
        m = dict(shared)
        m["x_own"] = np.ascontiguousarray(x[b, TOK * j:TOK * (j + 1)])
        m["ctx"] = np.ascontiguousarray(ctx[b])
        m["c_col"] = np.ascontiguousarray(c[b].reshape(8, 128).T)
        m.update(consts[j])
        maps.append(m)
    return maps


def kernel(**inputs):
    if "nc" not in _CACHE:
        _CACHE["nc"] = build_program(4)
    nc = _CACHE["nc"]
    res = run_bass_kernel_spmd(nc, _in_maps(inputs), core_ids=list(range(NCORES)))
    out = np.empty((NB, SEQ, D), np.float32)
    for core in range(NCORES):
        b, j = core // 4, core % 4
        out[b, TOK * j:TOK * (j + 1)] = np.asarray(res.results[core]["out"], dtype=np.float32)
    return out
```

```python
import numpy as np
import ml_dtypes
from contextlib import ExitStack

import concourse.bass as bass
import concourse.mybir as mybir
from concourse.bass_utils import run_bass_kernel_spmd

F32 = mybir.dt.float32
BF16 = mybir.dt.bfloat16
ALU = mybir.AluOpType
AF = mybir.ActivationFunctionType
AX = mybir.AxisListType

D = 1024
SEQ = 8192
NB = 2
NCORES = 8
TOK = 2048
NT = 16
CTX = 256
H = 8
INC = 3584
FFN = 2816
NCH = 44
EPS = 1e-6
GROUPS = [[0, 1, 2, 3], [4, 5, 6, 7]]
GELU_C = 1.5957691216057308


class Tok:
    __slots__ = ("key", "val")

    def __init__(self, key, val):
        self.key = key
        self.val = val


class Buf:
    __slots__ = ("name", "w", "r", "excl")

    def __init__(self, name, excl=False):
        self.name = name
        self.w = None
        self.r = []
        self.excl = excl


class Sched:
    ENGS = ("pe", "act", "dve", "pool", "sp")

    def __init__(self, nc, stack):
        self.nc = nc
        self.stack = stack
        self.ops = {e: [] for e in self.ENGS}
        self.sems = {}
        self.cnt = {}
        self.seen = {e: {} for e in self.ENGS}
        for e in ("pe", "act", "dve", "pool"):
            self._mk(e)

    def _mk(self, key):
        if key not in self.sems:
            self.sems[key] = self.stack.enter_context(self.nc.semaphore("s_" + key))
            self.cnt[key] = 0

    def _deps(self, eng, reads, writes):
        deps = []
        for b in reads:
            if b.w is not None:
                deps.append(b.w)
        for b in writes:
            if b.w is not None:
                deps.append(b.w)
            deps.extend(b.r)
        waits = {}
        for t in deps:
            if t.key == "pe" and eng == "pe":
                continue
            if self.seen[eng].get(t.key, 0) >= t.val:
                continue
            waits[t.key] = max(waits.get(t.key, 0), t.val)
        for k, v in waits.items():
            self.seen[eng][k] = v
        return list(waits.items())

    def _commit(self, tok, reads, writes):
        for b in writes:
            b.w = tok
            b.r = []
        for b in reads:
            if b not in writes:
                b.r.append(tok)
                if len(b.r) > 64:
                    b.r = b.r[-48:]

    def op(self, eng, fn, reads=(), writes=()):
        ex = [b for b in reads if b.excl]
        if ex:
            reads = [b for b in reads if not b.excl]
            writes = list(writes) + ex
        waits = self._deps(eng, reads, writes)
        self.cnt[eng] += 1
        tok = Tok(eng, self.cnt[eng])
        self.ops[eng].append((waits, fn, eng, 1))
        self._commit(tok, reads, writes)
        return tok

    def dma(self, queue, key, out, in_, reads=(), writes=(), **kw):
        self._mk(key)
        waits = self._deps(queue, reads, writes)
        self.cnt[key] += 16
        tok = Tok(key, self.cnt[key])
        self.ops[queue].append((waits, lambda e: e.dma_start(out=out, in_=in_, **kw), key, 16))
        self._commit(tok, reads, writes)
        return tok

    def custom(self, queue, key, fn, reads=(), writes=()):
        import os
        if os.environ.get("KDBG_NOCC"):
            return None
        self._mk(key)
        waits = self._deps(queue, reads, writes)
        self.cnt[key] += 1
        tok = Tok(key, self.cnt[key])
        self.ops[queue].append((waits, fn, key, None))
        self._commit(tok, reads, writes)
        return tok

    def barrier(self, exclude=()):
        for e in self.ENGS:
            waits = []
            for k, v in self.cnt.items():
                if k == e or v == 0 or k in exclude:
                    continue
                if self.seen[e].get(k, 0) >= v:
                    continue
                self.seen[e][k] = v
                waits.append((k, v))
            if waits:
                self.ops[e].append((waits, None, None, 0))

    def emit(self):
        nc = self.nc
        handles = {"pe": "tensor", "act": "scalar", "dve": "vector", "pool": "gpsimd", "sp": "sync"}
        with nc.Block() as block:
            for e in self.ENGS:
                ops = self.ops[e]

                def body(engine, ops=ops):
                    for waits, fn, key, inc in ops:
                        for k, v in waits:
                            engine.wait_ge(self.sems[k], v)
                        if fn is None:
                            continue
                        ins = fn(engine)
                        if inc is None:
                            ins.then_inc(self.sems[key])
                        else:
                            ins.then_inc(self.sems[key], inc)

                getattr(block, handles[e])(body)


class Arena:
    def __init__(self, t, nelem):
        self.t = t
        self.n = nelem
        self.off = 0

    def reset(self, off=0):
        self.off = off

    def bf(self, nelem, parts=128):
        a = self.t[0:parts, self.off:self.off + nelem]
        self.off += nelem
        assert self.off <= self.n, (self.off, self.n)
        return a

    def f32(self, nelem, parts=128):
        return self.bf(2 * nelem, parts).bitcast(F32)


def _bf(a):
    return np.ascontiguousarray(a).astype(ml_dtypes.bfloat16)


def host_consts(j):
    c = {}
    c["ident"] = _bf(np.eye(128, dtype=np.float32))
    p = np.arange(128)
    t = (TOK * j + 128 * np.arange(NT)[None, :] + p[:, None]).astype(np.float32)
    row = np.floor(t / 64.0).astype(np.float32)
    col = (t - 64.0 * row).astype(np.float32)
    inv = (np.float32(10000.0) ** (-(np.arange(16, dtype=np.float32)) / np.float32(16))).astype(np.float32)
    ang = np.concatenate([row[:, :, None] * inv[None, None, :], col[:, :, None] * inv[None, None, :]], axis=-1)
    ang = ang.astype(np.float32)
    c["rope"] = np.concatenate([np.cos(ang), np.sin(ang)], axis=-1).astype(np.float32)
    s_ = p[:, None]
    c_ = p[None, :]
    mf = (c_ >= s_).astype(np.float32)
    mb = (c_ <= s_).astype(np.float32)
    c["mask"] = np.stack([mf, mf, mb, mb], axis=1).astype(np.float32)
    pc = np.zeros((128, 8), np.float32)
    pc[:, 0] = -(p + 1)
    pc[:, 1] = (p + 1)
    pc[:, 2] = -(128 - p)
    pc[:, 3] = (128 - p)
    pc[:, 4] = -(255 - p)
    pc[:, 5] = -(255 - 128 - p)
    pc[:, 6] = -p
    pc[:, 7] = -(128 + p)
    c["pcol"] = pc
    sel = np.zeros((128, 18), np.float32)
    sel[:, 16] = 1.0 if j == 0 else 0.0
    sel[:, 17] = 1.0 if j == 3 else 0.0
    sel[:, j] = 1.0
    sel[:, 4 + j] = 1.0
    if j > 0:
        sel[:, 8 + (j - 1)] = 1.0
    if j < 3:
        sel[:, 12 + (j + 1)] = 1.0
    c["sel"] = sel
    q = np.arange(64)
    hh, rr, mm = q // 32, (q % 32) // 8, q % 8
    n1 = 16 * rr + 8 * hh + mm
    k1 = np.arange(64)
    th = 2.0 * np.pi * ((n1[:, None] * k1[None, :]) % 64) / 64.0
    c["e64"] = _bf(np.concatenate([np.cos(th), -np.sin(th)], axis=1))
    n2 = np.arange(128)
    k2 = 32 * j + np.arange(32)
    kk = k1[None, :, None] + 64 * k2[None, None, :]
    ph = 2.0 * np.pi * ((n2[:, None, None] * kk) % 8192) / 8192.0
    twA = np.concatenate([np.cos(ph), -np.sin(ph)], axis=2)
    twB = np.concatenate([np.sin(ph), np.cos(ph)], axis=2)
    c["tw"] = _bf(np.stack([twA, twB], axis=2))
    ch = np.arange(128)
    pc2 = 2.0 * np.pi * ((ch[:, None] * ch[None, :]) % 128) / 128.0
    c["c128"] = _bf(np.stack([np.cos(pc2), np.sin(pc2)], axis=1) / 1024.0)
    c["ones"] = np.ones((128, 128), np.float32)
    c["identf"] = np.eye(128, dtype=np.float32)
    c["onesb"] = _bf(np.ones((128, 128), np.float32))
    return c


CONST_SPECS = [
    ("ident", [128, 128], BF16), ("rope", [128, 16, 64], F32), ("mask", [128, 4, 128], F32),
    ("pcol", [128, 8], F32), ("sel", [128, 18], F32), ("e64", [64, 128], BF16),
    ("tw", [128, 64, 2, 64], BF16), ("c128", [128, 2, 128], BF16), ("ones", [128, 128], F32),
    ("onesb", [128, 128], BF16), ("identf", [128, 128], F32),
]

INPUT_SPECS = [
    ("x_own", [TOK, D], F32), ("ctx", [CTX, D], F32), ("c_col", [128, 8], F32), ("cc_col", [128, 8], F32),
    ("w_mod", [D, 6 * D], F32), ("b_mod", [1, 6 * D], F32), ("norm1_w", [1, D], F32),
    ("w_in", [D, INC], F32), ("a_f", [1, H], F32), ("a_b", [1, H], F32),
    ("w_ret_out", [D, D], F32), ("w_four_out", [512, D], F32), ("w_bg", [D, 2 * D], F32),
    ("b_bg", [1, 2 * D], F32), ("w_out", [D, D], F32), ("norm2_w", [1, D], F32),
    ("w_up", [D, 2 * FFN], F32), ("conv_w", [3, 2 * FFN], F32), ("conv_b", [1, 2 * FFN], F32),
    ("w_down", [FFN, D], F32), ("final_norm_w", [1, D], F32),
]


def build_program(stage=4):
    import os
    STOP = os.environ.get("KDBG_STOP", "")
    nc = bass.Bass("TRN2", target_bir_lowering=False)
    stack = ExitStack()
    S = Sched(nc, stack)
    dr = {}
    for name, shape, dt in INPUT_SPECS + CONST_SPECS:
        dr[name] = nc.dram_tensor(name, shape, dt, kind="ExternalInput").ap()
    out_d = nc.dram_tensor("out", [TOK, D], F32, kind="ExternalOutput").ap()
    dbg_d = None
    if stage < 4:
        dbg_d = nc.dram_tensor("dbg", [128, 8 * TOK], BF16, kind="ExternalOutput").ap()
    rec_d = nc.dram_tensor("rec_scr", [NT, 128, 5120], BF16).ap()
    kv_d = nc.dram_tensor("kv_scr", [NT, 128, 1024], F32).ap()
    x1_d = nc.dram_tensor("x1_scr", [TOK, D], F32).ap()
    mod_d = nc.dram_tensor("mod_scr", [1, 6 * D + 2 * D], F32).ap()
    fin_d = [nc.dram_tensor(f"f_in{h}", [1024, 512], BF16) for h in range(2)]
    fout_d = [nc.dram_tensor(f"f_out{h}", [4096, 512], BF16) for h in range(2)]
    stin_d = nc.dram_tensor("st_in", [128, 1024], F32)
    stout_d = nc.dram_tensor("st_out", [512, 1024], F32)
    hin_d = nc.dram_tensor("halo_in", [128, 128], F32)
    hout_d = nc.dram_tensor("halo_out", [512, 128], F32)

    def sb(name, shape, dt):
        return stack.enter_context(nc.sbuf_tensor("sb_" + name, shape, dt))

    PS = [stack.enter_context(nc.psum_tensor(f"ps{i}", [128, 512], F32)) for i in range(8)]
    PB = [Buf(f"ps{i}", excl=True) for i in range(8)]

    def psbf(i):
        return PS[i][:, :].bitcast(BF16)

    ident = sb("ident", [128, 128], BF16)
    rope = sb("rope", [128, 16, 64], F32)
    mask = sb("mask", [128, 4, 128], F32)
    pcol = sb("pcol", [128, 8], F32)
    sel = sb("sel", [128, 18], F32)
    ones = sb("ones", [128, 128], F32)
    onesb = sb("onesb", [128, 128], BF16)
    identf = sb("identf", [128, 128], F32)
    sctx = sb("sctx", [128, 1024], F32)
    c128 = sb("c128", [128, 2, 128], BF16)
    e64 = sb("e64", [64, 128], BF16)
    smallf = sb("smallf", [128, 512], F32)
    smallb = sb("smallb", [128, 64], BF16)
    convc = sb("convc", [128, 4, NCH], F32)
    XH = sb("XH", [128, 8, TOK], BF16)
    R1n, R2n = 32768, 47104
    R1t = sb("R1", [128, R1n], BF16)
    R2t = sb("R2", [128, R2n], BF16)
    R1 = Arena(R1t, R1n)
    R2 = Arena(R2t, R2n)
    B_const = Buf("const")
    B_small = Buf("small")
    B_ss = Buf("ss")
    B_XH = [Buf(f"xh{t}") for t in range(NT)]

    def sf(a, b):
        return smallf[:, a:b]

    a_bc = sf(0, 16)
    ea = sf(16, 32)
    qf_sc, kf_sc, qb_sc, kb_sc = sf(32, 40), sf(40, 48), sf(48, 56), sf(56, 64)
    cxf_sc, cxb_sc = sf(64, 80), sf(80, 96)
    a_st, ea_st = sf(96, 104), sf(104, 112)
    cdp_f, cdp_b = sf(112, 180), sf(180, 248)
    silc = sf(248, 264)
    shcol = sf(264, 288)
    ss_t = sf(288, 296)
    bbg = sf(296, 312)
    bup = sf(312, 356)
    kcol = sf(356, 400)
    gst = sf(400, 464)
    silcb = smallb[:, 0:16]
    shcolb = smallb[:, 16:40]

    def TT(eng, out, in0, in1, op, reads, writes):
        return S.op(eng, lambda e: e.tensor_tensor(out=out, in0=in0, in1=in1, op=op), reads, writes)

    def TS(eng, out, in0, s1, s2, op0, op1, reads, writes):
        if s2 is None:
            return S.op(eng, lambda e: e.tensor_scalar(out=out, in0=in0, scalar1=s1, scalar2=None, op0=op0), reads, writes)
        return S.op(eng, lambda e: e.tensor_scalar(out=out, in0=in0, scalar1=s1, scalar2=s2, op0=op0, op1=op1), reads, writes)

    def STT(eng, out, in0, scalar, in1, op0, op1, reads, writes):
        return S.op(eng, lambda e: e.scalar_tensor_tensor(out=out, in0=in0, scalar=scalar, in1=in1, op0=op0, op1=op1), reads, writes)

    def CP(eng, out, in_, reads, writes):
        if eng == "act":
            return S.op("act", lambda e: e.activation(out=out, in_=in_, func=AF.Copy), reads, writes)
        return S.op(eng, lambda e: e.tensor_copy(out=out, in_=in_), reads, writes)

    def ACTF(out, in_, func, reads, writes, **kw):
        return S.op("act", lambda e: e.activation(out=out, in_=in_, func=func, **kw), reads, writes)

    def MM(out, lhsT, rhs, start, stop, reads, writes):
        return S.op("pe", lambda e: e.matmul(out, lhsT=lhsT, rhs=rhs, start=start, stop=stop), reads, writes)

    def TR(out, in_, reads, writes):
        return S.op("pe", lambda e: e.transpose(out=out, in_=in_, identity=ident[:]), reads + [B_const], writes)

    def RSQ(out, in_, c, reads, writes):
        ACTF(out, in_, AF.Sqrt, reads, writes, bias=c, scale=1.0)
        return S.op("dve", lambda e: e.reciprocal(out=out, in_=out), writes, writes)

    def bc3(ap2, n):
        return ap2.unsqueeze(2).to_broadcast([128, ap2.shape[1], n])

    R1.reset()
    Win = R1.bf(8 * INC).rearrange("p (k c) -> p k c", k=8)
    A1bc = R1.f32(1024)
    B_win = Buf("win")
    win_v = dr["w_in"].rearrange("(k p) c -> p k c", p=128)
    for k0 in (0, 4):
        S.dma("pool", "win", Win[:, k0:k0 + 4, :], win_v[:, k0:k0 + 4, :], writes=[B_win])
    for dst, name in ((ident, "ident"), (rope, "rope"), (mask, "mask"), (pcol, "pcol"), (sel, "sel"), (ones, "ones"),
                      (onesb, "onesb"), (c128, "c128"), (e64, "e64"), (identf, "identf")):
        S.dma("sp", "const", dst[:], dr[name], writes=[B_const])
    S.dma("sp", "const", a_bc[:, 0:8], dr["a_f"].partition_broadcast(128), writes=[B_const])
    S.dma("sp", "const", a_bc[:, 8:16], dr["a_b"].partition_broadcast(128), writes=[B_const])
    S.dma("sp", "const", silc[:, 0:8], dr["c_col"], writes=[B_const])
    S.dma("sp", "const", silc[:, 8:16], dr["cc_col"], writes=[B_const])
    R2.reset()
    stg = sb("stg", [64, 128], F32)
    B_stg = Buf("stg")

    def col_layout(dst, src_rows, n):
        S.dma("sp", "stg", stg[0:n, :], src_rows, writes=[B_stg])
        S.op("pe", lambda e: e.transpose(out=PS[6][:, 0:n], in_=stg[0:n, :], identity=identf[0:n, 0:n]), [B_stg, B_const], [PB[6]])
        CP("dve", dst, PS[6][:, 0:n], [PB[6]], [B_const])

    S.barrier()
    col_layout(bbg, dr["b_bg"].rearrange("o (c p) -> (o c) p", p=128), 16)
    for k in range(3):
        col_layout(convc[:, k, :], dr["conv_w"][k:k + 1, :].rearrange("o (c p) -> (o c) p", p=128), NCH)
    col_layout(convc[:, 3, :], dr["conv_b"].rearrange("o (c p) -> (o c) p", p=128), NCH)
    for di in range(2):
        for hh in range(2):
            CP("dve", a_st[64 * hh:64 * hh + 64, 4 * di:4 * di + 4],
               a_bc[64 * hh:64 * hh + 64, 8 * di:8 * di + 8].rearrange("p (a h) -> p a h", h=2)[:, :, hh], [B_const], [B_const])

    ACTF(ea, a_bc, AF.Exp, [B_const], [B_small])
    ACTF(ea_st, a_st, AF.Exp, [B_const], [B_small])
    for dst, src, col in ((qf_sc, ea[:, 0:8], 0), (kf_sc, ea[:, 0:8], 1), (qb_sc, ea[:, 8:16], 2), (kb_sc, ea[:, 8:16], 3)):
        ACTF(dst, src, AF.Exp, [B_small], [B_small], scale=pcol[:, col:col + 1])
    for tl in range(2):
        ACTF(cxf_sc[:, 8 * tl:8 * tl + 8], ea[:, 0:8], AF.Exp, [B_small], [B_small], scale=pcol[:, 4 + tl:5 + tl])
        ACTF(cxb_sc[:, 8 * tl:8 * tl + 8], ea[:, 8:16], AF.Exp, [B_small], [B_small], scale=pcol[:, 6 + tl:7 + tl])
    for n in range(17):
        ACTF(cdp_f[:, 4 * n:4 * n + 4], ea_st[:, 0:4], AF.Exp, [B_small], [B_small], scale=-128.0 * n)
        ACTF(cdp_b[:, 4 * n:4 * n + 4], ea_st[:, 4:8], AF.Exp, [B_small], [B_small], scale=-128.0 * n)
    TS("dve", qf_sc, qf_sc, 0.125, None, ALU.mult, None, [B_small], [B_small])
    TS("dve", qb_sc, qb_sc, 0.125, None, ALU.mult, None, [B_small], [B_small])
    ACTF(silc, silc, AF.Silu, [B_small], [B_small])
    CP("dve", silcb, silc, [B_small], [B_small])

    wst = [R2.f32(4096).rearrange("p (k c) -> p k c", k=8) for _ in range(2)]
    wmb = [R2.bf(4096).rearrange("p (k c) -> p k c", k=8) for _ in range(2)]
    bmod = [R2.f32(512, parts=1) for _ in range(2)]
    rowsb = [R2.f32(512, parts=1) for _ in range(4)]
    B_wst, B_wmb = [Buf("wst0"), Buf("wst1")], [Buf("wmb0"), Buf("wmb1")]
    B_bmod = [Buf("bm0"), Buf("bm1")]
    B_row = [Buf(f"row{i}") for i in range(4)]
    B_modd = Buf("modd")
    wmod_v = dr["w_mod"].rearrange("(k p) c -> p k c", p=128)
    cvt_eng = ["dve", "act"]
    nrow = [0]

    def mod_block(cb, wst, wmb, bmod, rowsb, B_wst, B_wmb, B_bmod, B_row):
        i = cb % 2
        S.dma("sp", f"wst{i}", wst[i], wmod_v[:, :, 512 * cb:512 * cb + 512], writes=[B_wst[i]])
        S.dma("sp", f"bm{i}", bmod[i], dr["b_mod"][:, 512 * cb:512 * cb + 512], writes=[B_bmod[i]])
        for half in range(2):
            CP(cvt_eng[(2 * cb + half) % len(cvt_eng)], wmb[i][:, 4 * half:4 * half + 4, :], wst[i][:, 4 * half:4 * half + 4, :], [B_wst[i]], [B_wmb[i]])
        for side in range(2 if cb < 4 else 1):
            for k in range(8):
                MM(PS[7][0:1, :], silcb[:, 8 * side + k:8 * side + k + 1], wmb[i][:, k, :], k == 0, k == 7, [B_small, B_wmb[i]], [PB[7]])
            r = nrow[0] % len(rowsb)
            nrow[0] += 1
            TT("dve", rowsb[r], PS[7][0:1, :], bmod[i], ALU.add, [PB[7], B_bmod[i]], [B_row[r]])
            off = 512 * cb if side == 0 else 6 * D + 512 * cb
            S.dma("sp", f"rowst{r}", mod_d[:, off:off + 512], rowsb[r], reads=[B_row[r]], writes=[B_modd])

    for cb in range(4):
        mod_block(cb, wst, wmb, bmod, rowsb, B_wst, B_wmb, B_bmod, B_row)
    S.barrier()
    for i, off in ((0, 0), (2, 6 * D)):
        col_layout(shcol[:, 8 * i:8 * i + 8], mod_d[:, off:off + D].rearrange("o (c p) -> (o c) p", p=128), 8)
    CP("dve", shcolb[:, 0:8], shcol[:, 0:8], [B_const], [B_small])
    CP("dve", shcolb[:, 16:24], shcol[:, 16:24], [B_const], [B_small])

    if STOP == "mod":
        S.barrier(); S.emit(); return nc
    B_bct = Buf("bct")

    R2.reset()
    xt = [R2.f32(1024) for _ in range(3)]
    hb = R2.bf(1024)
    junk = R2.bf(1024)
    qk2 = [R2.f32(1024) for _ in range(2)]
    qk_sb = qk2[0]
    off_rt1 = R2.off
    rt1, rt2 = R2.f32(1024), R2.f32(1024)
    rot = rt1
    ktok = R2.bf(1024)
    kpad = R2.bf(2048)
    qpad = R2.bf(2048)
    fsb = [R2.bf(512) for _ in range(2)]
    kvsb = [R2.f32(1024) for _ in range(2)]
    Est = R2.f32(1024)
    off_rec = R2.off
    recb = [R2.bf(5120) for _ in range(2)]
    hcT = R2.bf(2048).rearrange("p (k c) -> p k c", k=8)
    ctxv = R2.bf(1024)
    brows = R2.bf(INC, parts=2)
    lo_tmp = R2t[0:1, off_rt1:off_rt1 + INC]
    nwt = qk2[1]
    browsc = R2t[0:2, off_rec:off_rec + INC]
    A1c = R2t[:, off_rec + 5120:off_rec + 5120 + 2048].bitcast(F32)
    B_xt = [Buf("xt0"), Buf("xt1"), Buf("xt2")]
    B_hb, B_junk, B_qk = Buf("hb"), Buf("junk"), Buf("qk")
    B_qk2 = [Buf("qk2_0"), Buf("qk2_1")]
    B_rt1, B_rt2 = Buf("rt1"), Buf("rt2")
    B_rot = B_rt1
    B_ktok, B_kpad, B_qpad = Buf("ktok"), Buf("kpad"), Buf("qpad")
    B_fsb = [Buf("fsb0"), Buf("fsb1")]
    B_kvsb = [Buf("kvsb0"), Buf("kvsb1")]
    B_E = Buf("E")
    B_rec = [Buf("rec0"), Buf("rec1")]
    B_hcT, B_ctxv, B_sctx, B_bias = Buf("hcT"), Buf("ctxv"), Buf("sctx"), Buf("bias")

    S.dma("sp", "bcl", nwt, dr["norm1_w"].partition_broadcast(128), writes=[B_bct])
    S.dma("sp", "bcl", A1bc, mod_d[:, D:2 * D].partition_broadcast(128), reads=[B_modd], writes=[B_bct])
    STT("dve", A1bc, A1bc, 1.0, nwt, ALU.add, ALU.mult, [B_bct], [B_bct])
    TS("dve", A1bc, A1bc, 32.0, None, ALU.mult, None, [B_bct], [B_bct])

    def bias_rows(side, dstHL):
        lcol = 0 if side == 0 else 16
        for blk in range(7):
            if side == 1 and blk not in (1, 2, 3):
                continue
            for k in range(8):
                MM(PS[7][0:1, :], shcolb[:, lcol + k:lcol + k + 1], Win[:, k, 512 * blk:512 * blk + 512], k == 0, k == 7, [B_small, B_win], [PB[7]])
            cs = slice(512 * blk, 512 * blk + 512)
            CP("dve", dstHL[0:1, cs], PS[7][0:1, :], [PB[7]], [B_bias])
            TT("dve", lo_tmp[:, cs], PS[7][0:1, :], dstHL[0:1, cs], ALU.subtract, [PB[7], B_bias], [B_bias])
        S.dma("sp", "biaslo", dstHL[1:2, :], lo_tmp, reads=[B_bias], writes=[B_bias])

    bias_rows(0, brows)

    S.op("pool", lambda e: e.memset(kpad, 0.0), writes=[B_kpad])
    S.op("pool", lambda e: e.memset(qpad, 0.0), writes=[B_qpad])
    S.op("pool", lambda e: e.memset(Est, 0.0), writes=[B_E])

    def load_x(src_ap, slot):
        S.dma("sp", f"xt{slot}", xt[slot], src_ap, writes=[B_xt[slot]])

    def norm_part(slot, scale_bc):
        ACTF(junk, xt[slot], AF.Square, [B_xt[slot]], [B_junk, B_ss], accum_out=ss_t[:, 0:1])
        RSQ(ss_t[:, 1:2], ss_t[:, 0:1], 1024.0 * EPS, [B_ss], [B_ss])
        STT("dve", hb, xt[slot], ss_t[:, 1:2], scale_bc, ALU.mult, ALU.mult, [B_xt[slot], B_ss, B_bct], [B_hb])

    def tr_part(dstT, bdst, col0):
        pst = psbf(0)
        for k in range(8):
            TR(pst[:, 128 * k:128 * k + 128], hb[:, 128 * k:128 * k + 128], [B_hb], [PB[0]])
        CP("act", dstT[:, :, col0:col0 + 128], pst.rearrange("p (k c) -> p k c", k=8), [PB[0]], [bdst])

    def norm_transpose(slot, scale_bc, dstT, bdst, col0):
        norm_part(slot, scale_bc)
        tr_part(dstT, bdst, col0)

    def project(srcT, bsrc, col0, blocks, rows, consume):
        for i, blk in enumerate(blocks):
            bank = 1 + (i % 2)
            for k in range(8):
                MM(PS[bank][:, :], srcT[:, k, col0:col0 + 128], Win[:, k, 512 * blk:512 * blk + 512], k == 0, False, [bsrc, B_win], [PB[bank]])
            MM(PS[bank][:, :], onesb[0:2, :], rows[0:2, 512 * blk:512 * blk + 512], False, True, [B_bias, B_const], [PB[bank]])
            consume(blk, PS[bank], PB[bank])

    def k4(ap):
        return ap.rearrange("p (a h c) -> p a h c", a=4, h=2)

    def scaled_k(dirn, src_k, bsrc, sc_tile):
        kt = ktok[:, 512 * dirn:512 * dirn + 512]
        TT("dve", kt.rearrange("p (h d) -> p h d", h=8), src_k.rearrange("p (h d) -> p h d", h=8), bc3(sc_tile, 64), ALU.mult,
           [bsrc, B_small], [B_ktok])
        kp = k4(kpad[:, 1024 * dirn:1024 * dirn + 1024])
        kin = kt.rearrange("p (a h d) -> p a h d", a=4, h=2)
        for hh in range(2):
            CP("act", kp[:, :, hh, 64 * hh:64 * hh + 64], kin[:, :, hh, :], [B_ktok], [B_kpad])

    def kv_matmuls(dirn, vsrc, bv, bank):
        kp = k4(kpad[:, 1024 * dirn:1024 * dirn + 1024])
        for p4 in range(4):
            for hh in range(2):
                h = 2 * p4 + hh
                MM(PS[bank][:, 128 * p4:128 * p4 + 128], kp[:, p4, hh, :], vsrc[:, 128 * h:128 * h + 128], hh == 0, hh == 1, [B_kpad, bv], [PB[bank]])

    S.barrier()
    B_fin = [[Buf(f"fin{h}_{i}") for i in range(8)] for h in range(2)]
    B_fout = [Buf("fout0"), Buf("fout1")]
    B_recd = [Buf(f"recd{t}") for t in range(NT)]
    B_kvd = [Buf(f"kvd{t}") for t in range(NT)]

    def E3(ap):
        return ap.rearrange("p (a e) -> p a e", a=4)

    cdf_bc = bc3(cdp_f[:, 4:8], 128)
    def passA_front1(t):
        tr_part(XH, B_XH[t], 128 * t)

    def passA_front(t):
        slot = t % 2
        rb = recb[slot]

        qk_t, B_qkt = qk2[slot], B_qk2[slot]

        def consume(blk, ps, pb, t=t, rb=rb, slot=slot, qk_t=qk_t, B_qkt=B_qkt):
            if blk < 2:
                CP("act", qk_t[:, 512 * blk:512 * blk + 512], ps[:, :], [pb], [B_qkt])
            elif blk < 4:
                o0 = 3072 + 512 * (blk - 2)
                CP("act", rb[:, o0:o0 + 512], ps[:, :], [pb], [B_rec[slot]])
            elif blk < 6:
                o0 = 4096 + 512 * (blk - 4)
                ACTF(rb[:, o0:o0 + 512], ps[:, :], AF.Silu, [pb], [B_rec[slot]])
            else:
                CP("act", fsb[slot], ps[:, :], [pb], [B_fsb[slot]])
                hh_ = t // 8
                S.dma("sp", f"fst{slot}", fin_d[hh_].ap()[128 * (t % 8):128 * (t % 8) + 128, :], fsb[slot],
                      reads=[B_fsb[slot]], writes=[B_fin[hh_][t % 8]])
        project(XH, B_XH[t], 128 * t, [0, 1, 2, 3, 4, 5, 6], brows, consume)


    def passA_back(t):
        slot = t % 2
        rb = recb[slot]
        qk_t, B_qkt = qk2[slot], B_qk2[slot]
        def g4(ap):
            return ap.rearrange("p (g h c) -> p g h c", g=16, h=2)
        cosb = rope[:, t, 0:32].unsqueeze(1).to_broadcast([128, 32, 32])
        sinb = rope[:, t, 32:64].unsqueeze(1).to_broadcast([128, 16, 32])
        TT("dve", rt1.rearrange("p (g c) -> p g c", g=32), qk_t.rearrange("p (g c) -> p g c", g=32), cosb, ALU.mult, [B_qkt, B_const], [B_rt1])
        TT("dve", g4(rt2)[:, :, 0, :], g4(qk_t)[:, :, 1, :], sinb, ALU.mult, [B_qkt, B_const], [B_rt2])
        TT("dve", g4(rt2)[:, :, 1, :], g4(qk_t)[:, :, 0, :], sinb, ALU.mult, [B_qkt, B_const], [B_rt2])
        TT("dve", g4(rt1)[:, :, 0, :], g4(rt1)[:, :, 0, :], g4(rt2)[:, :, 0, :], ALU.subtract, [B_rt2], [B_rt1])
        TT("dve", g4(rt1)[:, :, 1, :], g4(rt1)[:, :, 1, :], g4(rt2)[:, :, 1, :], ALU.add, [B_rt2], [B_rt1])
        for dirn, sc in ((0, qf_sc), (1, qb_sc)):
            qp = k4(qpad[:, 1024 * dirn:1024 * dirn + 1024])
            qin = rot[:, 0:512].rearrange("p (a h d) -> p a h d", a=4, h=2)
            scv = sc.rearrange("p (a h) -> p a h", h=2)
            for hh in range(2):
                TT("dve", qp[:, :, hh, 64 * hh:64 * hh + 64], qin[:, :, hh, :], scv[:, :, hh].unsqueeze(2).to_broadcast([128, 4, 64]), ALU.mult,
                   [B_rot, B_small], [B_qpad])
        scaled_k(0, rot[:, 512:1024], B_rot, kf_sc)
        scaled_k(1, rot[:, 512:1024], B_rot, kb_sc)
        pst = [psbf(3), psbf(4), psbf(7)]
        for dirn in range(2):
            qp = k4(qpad[:, 1024 * dirn:1024 * dirn + 1024])
            for h in range(8):
                TR(pst[dirn][:, 128 * h:128 * h + 128], qp[:, h // 2, h % 2, :], [B_qpad], [PB[3 + dirn]])
        for dirn in range(2):
            for p4 in range(4):
                c0 = 512 * dirn + 128 * p4
                TR(pst[2][:, 128 * (4 * dirn + p4):128 * (4 * dirn + p4) + 128], ktok[:, c0:c0 + 128], [B_ktok], [PB[7]])
        CP("dve", rb[:, 0:1024], pst[0], [PB[3]], [B_rec[slot]])
        CP("dve", rb[:, 1024:2048], pst[1], [PB[4]], [B_rec[slot]])
        CP("dve", rb[:, 2048:3072], pst[2], [PB[7]], [B_rec[slot]])
        for dirn in range(2):
            kv_matmuls(dirn, rb[:, 3072:4096], B_rec[slot], 5 + dirn)
            CP("act", kvsb[slot][:, 512 * dirn:512 * dirn + 512], PS[5 + dirn][:, :], [PB[5 + dirn]], [B_kvsb[slot]])
        TT("pool", rt1[:, 0:512], Est[:, 0:512], kvsb[slot][:, 0:512], ALU.add, [B_E, B_kvsb[slot]], [B_rt1])
        TT("pool", E3(Est[:, 0:512]), E3(rt1[:, 0:512]), cdf_bc, ALU.mult, [B_rt1, B_small], [B_E])
        cdb_t = bc3(cdp_b[:, 4 * (t + 1):4 * (t + 1) + 4], 128)
        TT("pool", E3(rt1[:, 512:1024]), E3(kvsb[slot][:, 512:1024]), cdb_t, ALU.mult, [B_kvsb[slot], B_small], [B_rt1])
        TT("pool", Est[:, 512:1024], Est[:, 512:1024], rt1[:, 512:1024], ALU.add, [B_rt1, B_E], [B_E])
        S.dma("sp", f"rst{slot}", rec_d[t], rb, reads=[B_rec[slot]], writes=[B_recd[t]])
        S.dma("sp", f"kst{slot}", kv_d[t], kvsb[slot], reads=[B_kvsb[slot]], writes=[B_kvd[t]])
        if t % 8 == 7:
            hh_ = t // 8
            S.custom("pool", f"ccf{hh_}", lambda e, hh_=hh_: e.collective_compute(
                "AllGather", ALU.bypass, replica_groups=GROUPS, ins=[fin_d[hh_].ap().opt()], outs=[fout_d[hh_].ap().opt()]),
                reads=B_fin[hh_], writes=[B_fout[hh_]])


    for t0 in range(3):
        load_x(dr["x_own"][128 * t0:128 * t0 + 128, :], t0)
    norm_part(0, A1bc)
    passA_front1(0)
    norm_part(1, A1bc)
    passA_front(0)
    for t in range(NT):
        if t + 1 < NT:
            passA_front1(t + 1)
        if t + 2 < NT:
            norm_part((t + 2) % 3, A1bc)
        if t + 3 < NT:
            load_x(dr["x_own"][128 * (t + 3):128 * (t + 4), :], t % 3)
        if t + 1 < NT:
            passA_front(t + 1)
        passA_back(t)
    B_stin, B_stout = Buf("stin"), Buf("stout")
    S.dma("sp", "stst", stin_d.ap(), Est, reads=[B_E], writes=[B_stin])
    S.custom("pool", "ccst", lambda e: e.collective_compute(
        "AllGather", ALU.bypass, replica_groups=GROUPS, ins=[stin_d.ap().opt()], outs=[stout_d.ap().opt()]),
        reads=[B_stin], writes=[B_stout])
    S.barrier(exclude=("ccst",))
    S.dma("sp", "bcl", nwt, dr["norm1_w"].partition_broadcast(128), writes=[B_bct])
    S.dma("sp", "bcl", A1c, mod_d[:, 7 * D:8 * D].partition_broadcast(128), reads=[B_modd], writes=[B_bct])
    STT("dve", A1c, A1c, 1.0, nwt, ALU.add, ALU.mult, [B_bct], [B_bct])
    TS("dve", A1c, A1c, 32.0, None, ALU.mult, None, [B_bct], [B_bct])
    bias_rows(1, browsc)
    for tl in range(2):
        load_x(dr["ctx"][128 * tl:128 * tl + 128, :], tl)
    for tl in range(2):
        norm_transpose(tl, A1c, hcT, B_hcT, 128 * tl)

        def consume_ctx(blk, ps, pb):
            if blk == 1:
                CP("act", qk_sb[:, 512:1024], ps[:, :], [pb], [B_qk])
            else:
                CP("act", ctxv[:, 512 * (blk - 2):512 * (blk - 2) + 512], ps[:, :], [pb], [B_ctxv])
        project(hcT, B_hcT, 128 * tl, [1, 2, 3], browsc, consume_ctx)
        scaled_k(0, qk_sb[:, 512:1024], B_qk, cxf_sc[:, 8 * tl:8 * tl + 8])
        scaled_k(1, qk_sb[:, 512:1024], B_qk, cxb_sc[:, 8 * tl:8 * tl + 8])
        for dirn in range(2):
            kv_matmuls(dirn, ctxv, B_ctxv, 5 + dirn)
            dst = sctx[:, 512 * dirn:512 * dirn + 512]
            if tl == 0:
                CP("dve", dst, PS[5 + dirn][:, :], [PB[5 + dirn]], [B_sctx])
            else:
                TT("dve", dst, dst, PS[5 + dirn][:, :], ALU.add, [PB[5 + dirn], B_sctx], [B_sctx])


    S.barrier()
    R1.reset()
    SfT = R1.bf(NT * 512).rearrange("p (n c) -> p n c", n=NT)
    SbT = R1.bf(NT * 512).rearrange("p (n c) -> p n c", n=NT)
    ogT = R1.bf(8 * TOK).rearrange("p (h c) -> p h c", h=8)
    B_ST = Buf("ST")
    B_ogT = [Buf(f"ogT{t}") for t in range(NT)]
    R2.reset()
    kvall = R2.f32(NT * 1024).rearrange("p (n c) -> p n c", n=NT)
    Gst = R2.f32(4096).rearrange("p (r c) -> p r c", r=4)
    curf, curb = R2.f32(512), R2.f32(512)
    tmpf, tmpb = R2.f32(512), R2.f32(512)
    B_kvall, B_G = Buf("kvall"), Buf("G")
    B_curf, B_curb, B_tmpf, B_tmpb = Buf("curf"), Buf("curb"), Buf("tmpf"), Buf("tmpb")
    CP("dve", curf, sctx[:, 0:512], [B_sctx], [B_curf])
    CP("dve", curb, sctx[:, 512:1024], [B_sctx], [B_curb])
    S.dma("sp", "kvld", kvall, kv_d.rearrange("n p c -> p n c"), reads=B_kvd, writes=[B_kvall])
    S.dma("sp", "gld", Gst, stout_d.ap().rearrange("(r p) c -> p r c", p=128), reads=[B_stout], writes=[B_G])
    cd16f = bc3(cdp_f[:, 64:68], 128)
    cd16b = bc3(cdp_b[:, 64:68], 128)
    TS("dve", tmpf, curf, sel[:, 0:1], None, ALU.mult, None, [B_curf, B_const], [B_tmpf])
    for r in range(3):
        TT("dve", E3(curf), E3(curf), cd16f, ALU.mult, [B_curf, B_small], [B_curf])
        TT("dve", curf, curf, Gst[:, r, 0:512], ALU.add, [B_curf, B_G], [B_curf])
        STT("dve", tmpf, curf, sel[:, r + 1:r + 2], tmpf, ALU.mult, ALU.add, [B_curf, B_tmpf, B_const], [B_tmpf])
    TS("dve", tmpb, curb, sel[:, 7:8], None, ALU.mult, None, [B_curb, B_const], [B_tmpb])
    for r in (3, 2, 1):
        TT("dve", E3(curb), E3(curb), cd16b, ALU.mult, [B_curb, B_small], [B_curb])
        TT("dve", curb, curb, Gst[:, r, 512:1024], ALU.add, [B_curb, B_G], [B_curb])
        STT("dve", tmpb, curb, sel[:, 4 + r - 1:4 + r], tmpb, ALU.mult, ALU.add, [B_curb, B_tmpb, B_const], [B_tmpb])
    cdb_bc = bc3(cdp_b[:, 4:8], 128)
    for n in range(NT):
        m_ = NT - 1 - n
        CP("act", SfT[:, n, :], tmpf, [B_tmpf], [B_ST])
        TT("dve", curf, tmpf, kvall[:, n, 0:512], ALU.add, [B_tmpf, B_kvall], [B_curf])
        TT("dve", E3(tmpf), E3(curf), cdf_bc, ALU.mult, [B_curf, B_small], [B_tmpf])
        CP("act", SbT[:, m_, :], tmpb, [B_tmpb], [B_ST])
        TT("dve", curb, tmpb, kvall[:, m_, 512:1024], ALU.add, [B_tmpb, B_kvall], [B_curb])
        TT("dve", E3(tmpb), E3(curb), cdb_bc, ALU.mult, [B_curb, B_small], [B_tmpb])
    S.barrier()
    if STOP == "scan":
        S.emit(); return nc

    R2.reset()
    rbB = [R2.bf(5120) for _ in range(2)]
    Pm = [R2.bf(512) for _ in range(2)]
    sq = R2.f32(1024)
    tcen = R2.f32(1024)
    praw = [R2.bf(512) for _ in range(2)]
    ogtok = R2.bf(1024)
    B_rbB = [Buf("rbB0"), Buf("rbB1")]
    B_Pm = [Buf("Pm0"), Buf("Pm1")]
    B_sq, B_tcen, B_ogtok, B_gst = Buf("sq"), Buf("tcen"), Buf("ogtok"), Buf("gst")
    B_praw = [Buf("praw0"), Buf("praw1")]
    mask3 = mask[:, :, :]
    S.dma("sp", "rld0", rbB[0], rec_d[0], reads=[B_recd[0]], writes=[B_rbB[0]])
    obanks = [(5, 6), (3, 4)]

    def passB_mm(n):
        slot = n % 2
        rb = rbB[slot]

        def qT(di, h):
            return rb[:, (8 * di + h) * 128:(8 * di + h) * 128 + 128]

        def kT(di, p4):
            c0 = 2048 + (4 * di + p4) * 128
            return rb[:, c0:c0 + 128]
        def scores(p4):
            sbank = 1 + (p4 % 2)
            psc = PS[sbank][:, :].rearrange("p (a c) -> p a c", a=4)
            for di in range(2):
                for hh in range(2):
                    MM(psc[:, 2 * di + hh, :], kT(di, p4), qT(di, 2 * p4 + hh), True, True, [B_rbB[slot]], [PB[sbank]])
            pm = Pm[p4 % 2]
            CP("act", praw[p4 % 2], PS[sbank][:, :], [PB[sbank]], [B_praw[p4 % 2]])
            TT("pool", pm.rearrange("p (a c) -> p a c", a=4), praw[p4 % 2].rearrange("p (a c) -> p a c", a=4), mask3, ALU.mult,
               [B_praw[p4 % 2], B_const], [B_Pm[p4 % 2]])

        def omm(p4):
            pm3 = Pm[p4 % 2].rearrange("p (a c) -> p a c", a=4)
            obank = obanks[n % 2][p4 // 2]
            for hh in range(2):
                h = 2 * p4 + hh
                od = PS[obank][:, 128 * (h % 4):128 * (h % 4) + 128]
                vv = rb[:, 3072 + 128 * h:3072 + 128 * h + 128]
                MM(od, pm3[:, hh, :], vv, True, False, [B_Pm[p4 % 2], B_rbB[slot]], [PB[obank]])
                MM(od, qT(0, h), SfT[:, n, 128 * p4:128 * p4 + 128], False, False, [B_rbB[slot], B_ST], [PB[obank]])
                MM(od, pm3[:, 2 + hh, :], vv, False, False, [B_Pm[p4 % 2], B_rbB[slot]], [PB[obank]])
                MM(od, qT(1, h), SbT[:, n, 128 * p4:128 * p4 + 128], False, True, [B_rbB[slot], B_ST], [PB[obank]])
        scores(0)
        scores(1)
        omm(0)
        scores(2)
        omm(1)
        scores(3)
        omm(2)
        omm(3)

    def passB_gn(n):
        slot = n % 2
        rb = rbB[slot]
        ob = obanks[n % 2]
        o3s = [PS[ob[half]][:, :].rearrange("p (h e) -> p h e", h=4) for half in range(2)]
        for half in range(2):
            hs = slice(4 * half, 4 * half + 4)
            S.op("dve", lambda e, o3=o3s[half], hs=hs: e.tensor_reduce(out=gst[:, hs], in_=o3, axis=AX.X, op=ALU.add), [PB[ob[half]]], [B_gst])
            ACTF(sq[:, 512 * half:512 * half + 512], PS[ob[half]][:, :], AF.Square, [PB[ob[half]]], [B_sq])
        TS("dve", gst[:, 16:24], gst[:, 0:8], 1.0 / 128.0, None, ALU.mult, None, [B_gst], [B_gst])
        for half in range(2):
            TT("dve", tcen[:, 512 * half:512 * half + 512].rearrange("p (h e) -> p h e", h=4), o3s[half], bc3(gst[:, 16 + 4 * half:20 + 4 * half], 128),
               ALU.subtract, [PB[ob[half]], B_gst], [B_tcen])
        TT("dve", tcen, tcen, rb[:, 4096:5120], ALU.mult, [B_tcen, B_rbB[slot]], [B_tcen])
        S.op("dve", lambda e: e.tensor_reduce(out=gst[:, 8:16], in_=sq.rearrange("p (h e) -> p h e", h=8), axis=AX.X, op=ALU.add), [B_sq], [B_gst])
        TT("dve", gst[:, 24:32], gst[:, 16:24], gst[:, 16:24], ALU.mult, [B_gst], [B_gst])
        STT("dve", gst[:, 32:40], gst[:, 8:16], 1.0 / 128.0, gst[:, 24:32], ALU.mult, ALU.subtract, [B_gst], [B_gst])
        RSQ(gst[:, 40:48], gst[:, 32:40], EPS, [B_gst], [B_gst])
        TT("dve", ogtok.rearrange("p (h e) -> p h e", h=8), tcen.rearrange("p (h e) -> p h e", h=8), bc3(gst[:, 40:48], 128), ALU.mult,
           [B_tcen, B_gst], [B_ogtok])
        pst = psbf(0)
        for h in range(8):
            TR(pst[:, 128 * h:128 * h + 128], ogtok[:, 128 * h:128 * h + 128], [B_ogtok], [PB[0]])
        CP("act", ogT[:, :, 128 * n:128 * n + 128], pst.rearrange("p (h c) -> p h c", h=8), [PB[0]], [B_ogT[n]])
        if n + 2 < NT:
            S.dma("sp", f"rld{n % 2}", rbB[n % 2], rec_d[n + 2], reads=[B_recd[n + 2]], writes=[B_rbB[n % 2]])

    S.dma("sp", "rld1", rbB[1], rec_d[1], reads=[B_recd[1]], writes=[B_rbB[1]])
    passB_mm(0)
    for n in range(NT):
        if n + 1 < NT:
            passB_mm(n + 1)
        passB_gn(n)
    S.barrier()
    if stage == 1:
        S.dma("sp", "dbg", dbg_d.rearrange("p (h c) -> p h c", h=8), ogT, reads=B_ogT)
        S.barrier()
        S.emit()
        return nc

    zT = R1t[:, 0:4 * TOK].rearrange("p (g c) -> p g c", g=4)
    B_zT = Buf("zT")
    R2.reset()
    F1 = [R2.bf(16384, parts=64).rearrange("p (n c) -> p n c", n=128) for _ in range(1)]
    Aall = R2.bf(16384).rearrange("p (c k) -> p c k", c=128)
    tw = R2.bf(64 * 128).rearrange("p (k t c) -> p k t c", k=64, t=2)
    Xsb = [R2.bf(512).rearrange("p (k t c) -> p k t c", k=8, t=2) for _ in range(2)]
    B_F1, B_A, B_tw = Buf("F1"), Buf("A"), Buf("tw")
    B_Xsb = [Buf("Xsb0"), Buf("Xsb1")]
    S.dma("sp", "twld", tw, dr["tw"], writes=[B_tw])
    wst_s = R2.f32(1024).rearrange("p (k c) -> p k c", k=8)
    wmb_s = [R2.bf(1024).rearrange("p (k c) -> p k c", k=8) for _ in range(2)]
    bmod_s = [R2.f32(128, parts=1) for _ in range(2)]
    row_s = R2.f32(128, parts=1)
    B_wsts, B_rows = Buf("wsts"), Buf("rows")
    B_wmbs = [Buf("wmbs0"), Buf("wmbs1")]
    B_bms = [Buf("bms0"), Buf("bms1")]

    def mod_prep(j_):
        c0 = 4 * 512 + 128 * j_
        S.dma("sp", "wsts", wst_s, wmod_v[:, :, c0:c0 + 128], writes=[B_wsts])
        S.dma("sp", f"bms{j_ % 2}", bmod_s[j_ % 2], dr["b_mod"][:, c0:c0 + 128], writes=[B_bms[j_ % 2]])
        CP("pool", wmb_s[j_ % 2], wst_s, [B_wsts], [B_wmbs[j_ % 2]])

    def mod_mm(j_):
        c0 = 4 * 512 + 128 * j_
        for k in range(8):
            MM(PS[7][0:1, 0:128], silcb[:, k:k + 1], wmb_s[j_ % 2][:, k, :], k == 0, k == 7, [B_small, B_wmbs[j_ % 2]], [PB[7]])
        TT("dve", row_s, PS[7][0:1, 0:128], bmod_s[j_ % 2], ALU.add, [PB[7], B_bms[j_ % 2]], [B_rows])
        S.dma("sp", "rowss", mod_d[:, c0:c0 + 128], row_s, reads=[B_rows], writes=[B_modd])

    def mod_subblock(j_):
        if j_ == 0:
            mod_prep(0)
        if j_ + 1 < 32:
            mod_prep(j_ + 1)
        mod_mm(j_)

    nsub = [0]
    for g in range(4):
        for h2 in range(2):
            src = fout_d[h2].ap().rearrange("(q n) c -> q n c", n=128)[:, :, 128 * g:128 * g + 128]
            S.dma("sp", "f1ld", F1[0][32 * h2:32 * h2 + 32, :, :], src, reads=[B_fout[h2]], writes=[B_F1])
        for c4 in range(32):
            if c4 % 8 == 0 and nsub[0] < 32:
                mod_subblock(nsub[0])
                nsub[0] += 1
            bank = 1 + (c4 % 2)
            for cc in range(4):
                ch = 4 * c4 + cc
                MM(PS[bank][:, 128 * cc:128 * cc + 128], F1[0][:, :, ch], e64[:, :], True, True, [B_F1, B_const], [PB[bank]])
            CP("act" if c4 % 2 == 0 else "dve", Aall[:, 4 * c4:4 * c4 + 4, :], PS[bank][:, :].rearrange("p (c k) -> p c k", c=4), [PB[bank]], [B_A])
        for kb in range(8):
            if kb % 2 == 0 and nsub[0] < 32:
                mod_subblock(nsub[0])
                nsub[0] += 1
            xb = 3 + (kb % 2)
            px = PS[xb][:, :].rearrange("p (k t c) -> p k t c", k=8, t=2)
            for kk in range(8):
                k1 = 8 * kb + kk
                ar = Aall[:, :, k1]
                ai = Aall[:, :, 64 + k1]
                pxk = PS[xb][:, 64 * kk:64 * kk + 64]
                MM(pxk, ar, tw[:, k1, 0, :], True, False, [B_A, B_tw], [PB[xb]])
                MM(pxk, ai, tw[:, k1, 1, :], False, True, [B_A, B_tw], [PB[xb]])
            xs = Xsb[kb % 2]
            CP("act", xs, px, [PB[xb]], [B_Xsb[kb % 2]])
            zb = 5 + (kb % 2)
            pz = PS[zb][:, 0:256].rearrange("p (k c) -> p k c", k=8)
            MM(pz, c128[:, 0, :], xs[:, :, 0, :], True, False, [B_Xsb[kb % 2], B_const], [PB[zb]])
            MM(pz, c128[:, 1, :], xs[:, :, 1, :], False, True, [B_Xsb[kb % 2], B_const], [PB[zb]])
            zdst = zT[:, g, :].rearrange("p (b a) -> p a b", a=64)[:, 8 * kb:8 * kb + 8, :]
            CP("dve", zdst, pz, [PB[zb]], [B_zT])
    S.barrier()
    col_layout(shcol[:, 8:16], mod_d[:, 3 * D:4 * D].rearrange("o (c p) -> (o c) p", p=128), 8)
    CP("dve", shcolb[:, 8:16], shcol[:, 8:16], [B_const], [B_small])
    if stage == 2:
        S.dma("sp", "dbg", dbg_d[:, 0:4 * TOK], R1t[:, 0:4 * TOK], reads=[B_zT])
        S.barrier()
        S.emit()
        return nc

    Wout = R2t[:, 37888:46080].rearrange("p (k c) -> p k c", k=8)
    B_wout = Buf("wout")
    wout_v = dr["w_out"].rearrange("(k p) c -> p k c", p=128)
    R2.reset()
    mergedT = R2.bf(8 * TOK).rearrange("p (k c) -> p k c", k=8)
    wsl = [R2.bf(28 * 128).rearrange("p (k c) -> p k c", k=28) for _ in range(2)]
    gts = [R2.f32(512) for _ in range(4)]
    m12 = [R2.f32(512) for _ in range(2)]
    B_wsl = [Buf("wsl0"), Buf("wsl1")]
    B_gts = [Buf(f"gt{i}") for i in range(4)]
    B_m12 = [Buf("m1"), Buf("m2")]
    B_mg = [Buf(f"mg{i}") for i in range(4)]
    wro_v = dr["w_ret_out"].rearrange("(k p) c -> p k c", p=128)
    wfo_v = dr["w_four_out"].rearrange("(k p) c -> p k c", p=128)
    wbg_v = dr["w_bg"].rearrange("(k p) c -> p k c", p=128)

    def load_wsl(oc):
        i = oc % 2
        cs = slice(128 * oc, 128 * oc + 128)
        S.dma("pool", f"wsl{i}", wsl[i][:, 0:8, :], wro_v[:, :, cs], writes=[B_wsl[i]])
        S.dma("pool", f"wsl{i}", wsl[i][:, 8:12, :], wfo_v[:, :, cs], writes=[B_wsl[i]])
        S.dma("pool", f"wsl{i}", wsl[i][:, 12:28, :].rearrange("p (g k) c -> p g k c", g=2),
              wbg_v.rearrange("p k (g c) -> p g k c", g=2)[:, :, :, cs], writes=[B_wsl[i]])
    load_wsl(0)
    S.dma("pool", "wout", Wout, wout_v, writes=[B_wout])
    B_bbg = Buf("bbg")
    for oc in range(8):
        i = oc % 2
        if oc + 1 < 8:
            load_wsl(oc + 1)
        w = wsl[i]
        for gi in range(2):
            for k in range(8):
                MM(PS[7][:, gi:gi + 1], w[:, 12 + 8 * gi + k, :], shcolb[:, k:k + 1], k == 0, k == 7, [B_wsl[i], B_small], [PB[7]])
        bcol = bbg.rearrange("p (g c) -> p g c", g=2)[:, :, oc]
        TT("dve", bcol, bcol, PS[7][:, 0:2], ALU.add, [PB[7], B_const], [B_bbg])
        for tb in range(4):
            ts_ = slice(512 * tb, 512 * tb + 512)
            xh_b = [B_XH[4 * tb + q] for q in range(4)]
            og_b = [B_ogT[4 * tb + q] for q in range(4)]
            for k in range(8):
                MM(PS[1][:, :], w[:, k, :], ogT[:, k, ts_], k == 0, k == 7, [B_wsl[i]] + og_b, [PB[1]])
            for k in range(4):
                MM(PS[2][:, :], w[:, 8 + k, :], zT[:, k, ts_], k == 0, k == 3, [B_wsl[i], B_zT], [PB[2]])
            for gi in range(2):
                for k in range(8):
                    MM(PS[3 + gi][:, :], w[:, 12 + 8 * gi + k, :], XH[:, k, ts_], k == 0, k == 7, [B_wsl[i]] + xh_b, [PB[3 + gi]])
            g0 = 2 * (tb % 2)
            for gi in range(2):
                ACTF(gts[g0 + gi], PS[3 + gi][:, :], AF.Sigmoid, [PB[3 + gi], B_bbg], [B_gts[g0 + gi]], bias=bbg[:, 8 * gi + oc:8 * gi + oc + 1])
            TT("dve", m12[0], gts[g0], PS[1][:, :], ALU.mult, [B_gts[g0], PB[1]], [B_m12[0]])
            TT("dve", m12[1], gts[g0 + 1], PS[2][:, :], ALU.mult, [B_gts[g0 + 1], PB[2]], [B_m12[1]])
            TT("pool", mergedT[:, oc, ts_], m12[0], m12[1], ALU.add, [B_m12[0], B_m12[1]], [B_mg[tb]])
    S.barrier()

    R2.reset(8 * TOK)
    xt2 = [R2.f32(1024) for _ in range(2)]
    x1t = [R2.f32(1024) for _ in range(2)]
    tmpy = R2.f32(1024)
    g1bc = R2.f32(1024)
    A2bc = R2.f32(1024)
    nw2t = R2.f32(1024)
    hb2 = R2.bf(1024)
    junk2 = R2.bf(1024)
    B_xt2 = [Buf("xt2_0"), Buf("xt2_1")]
    B_x1t = [Buf("x1t0"), Buf("x1t1")]
    B_tmpy, B_hb2, B_junk2, B_bc2 = Buf("tmpy"), Buf("hb2"), Buf("junk2"), Buf("bc2")
    B_x1d = [Buf(f"x1d{t}") for t in range(NT)]
    S.dma("sp", "bcl", g1bc, mod_d[:, 2 * D:3 * D].partition_broadcast(128), writes=[B_bc2])
    S.dma("sp", "bcl", A2bc, mod_d[:, 4 * D:5 * D].partition_broadcast(128), writes=[B_bc2])
    S.dma("sp", "bcl", nw2t, dr["norm2_w"].partition_broadcast(128), writes=[B_bc2])
    STT("dve", A2bc, A2bc, 1.0, nw2t, ALU.add, ALU.mult, [B_bc2], [B_bc2])
    TS("dve", A2bc, A2bc, 32.0, None, ALU.mult, None, [B_bc2], [B_bc2])
    Wd = R1t[:, 0:22 * D].rearrange("p (k c) -> p k c", k=22)
    B_wd = Buf("wd")
    wd_v = dr["w_down"].rearrange("(k p) c -> p k c", p=128)
    for k0 in (0, 11):
        S.dma("pool", "wd", Wd[:, k0:k0 + 11, :], wd_v[:, k0:k0 + 11, :], writes=[B_wd])
    S.dma("sp", "xt2_0", xt2[0], dr["x_own"][0:128, :], writes=[B_xt2[0]])
    ybanks = [(1, 2), (3, 4)]

    def y_mm(t):
        for half in range(2):
            bk = ybanks[t % 2][half]
            for k in range(8):
                MM(PS[bk][:, :], mergedT[:, k, 128 * t:128 * t + 128], Wout[:, k, 512 * half:512 * half + 512], k == 0, k == 7,
                   [B_mg[t // 4], B_wout], [PB[bk]])

    def y_post(t):
        slot = t % 2
        if t + 1 < NT:
            S.dma("sp", f"xt2_{(t + 1) % 2}", xt2[(t + 1) % 2], dr["x_own"][128 * (t + 1):128 * (t + 2), :], writes=[B_xt2[(t + 1) % 2]])
        for half in range(2):
            bk = ybanks[t % 2][half]
            hs = slice(512 * half, 512 * half + 512)
            TT("dve", tmpy[:, hs], PS[bk][:, :], g1bc[:, hs], ALU.mult, [PB[bk], B_bc2], [B_tmpy])
        TT("dve", x1t[slot], tmpy, xt2[slot], ALU.add, [B_tmpy, B_xt2[slot]], [B_x1t[slot]])
        S.dma("sp", f"x1st{slot}", x1_d[128 * t:128 * t + 128, :], x1t[slot], reads=[B_x1t[slot]], writes=[B_x1d[t]])
        ACTF(junk2, x1t[slot], AF.Square, [B_x1t[slot]], [B_junk2, B_ss], accum_out=ss_t[:, 2:3])
        RSQ(ss_t[:, 3:4], ss_t[:, 2:3], 1024.0 * EPS, [B_ss], [B_ss])
        STT("dve", hb2, x1t[slot], ss_t[:, 3:4], A2bc, ALU.mult, ALU.mult, [B_x1t[slot], B_ss, B_bc2], [B_hb2])
        pst = psbf(0)
        for k in range(8):
            TR(pst[:, 128 * k:128 * k + 128], hb2[:, 128 * k:128 * k + 128], [B_hb2], [PB[0]])
        CP("act", XH[:, :, 128 * t:128 * t + 128], pst.rearrange("p (k c) -> p k c", k=8), [PB[0]], [B_XH[t]])

    y_mm(0)
    for t in range(NT):
        if t + 1 < NT:
            y_mm(t + 1)
        y_post(t)
    S.barrier()
    if stage == 3:
        for t in range(NT):
            S.dma("sp", "xt2_0", xt2[0], x1_d[128 * t:128 * t + 128, :], reads=[B_x1d[t]], writes=[B_xt2[0]])
            S.dma("sp", "outst", out_d[128 * t:128 * t + 128, :], xt2[0], reads=[B_xt2[0]], writes=[])
        S.dma("sp", "dbg", dbg_d, R2t[:, 0:8 * TOK], reads=B_mg)
        S.barrier()
        S.emit()
        return nc

    R2.reset()
    USE_GELU_ACT = not os.environ.get("KDBG_GELU_SIG")
    mT = R2.bf(22 * 512).rearrange("p (k c) -> p k c", k=22)
    wup = [R2.bf(2048).rearrange("p (a k c) -> p a k c", a=2, k=8) for _ in range(3)]
    cacc = [[R2.f32(512) for _ in range(2)] for _ in range(2)]
    ga = [R2.f32(512) for _ in range(2)]
    ub = R2.f32(NCH * 8).rearrange("p (c t) -> p c t", c=NCH)
    hal = R2.f32(2 * NCH).rearrange("p (s c) -> p s c", s=2)
    hsend = R2.f32(128)
    hall = R2.f32(512).rearrange("p (r c) -> p r c", r=4)
    kcc = R2.f32(3 * NCH).rearrange("p (s c) -> p s c", s=3)
    x1r = [R2.f32(1024) for _ in range(2)]
    x2t = R2.f32(1024)
    g2bc = R2.f32(1024)
    fwbc = R2.f32(1024)
    junk3 = R2.bf(1024)
    B_wup = [Buf(f"wup{i}") for i in range(3)]
    B_cacc = [[Buf(f"ca{s_}{a}") for a in range(2)] for s_ in range(2)]
    B_ga = [Buf("ga0"), Buf("ga1")]
    B_mT, B_ub, B_hal, B_hsend, B_hall, B_kcc = Buf("mT"), Buf("ub"), Buf("hal"), Buf("hsend"), Buf("hall"), Buf("kcc")
    B_x1r = [Buf("x1r0"), Buf("x1r1")]
    B_x2t, B_bc3, B_junk3 = Buf("x2t"), Buf("bc3"), Buf("junk3")
    B_hin, B_hout = Buf("hin"), Buf("hout")
    S.dma("sp", "bcl", g2bc, mod_d[:, 5 * D:6 * D].partition_broadcast(128), writes=[B_bc3])
    S.dma("sp", "bcl", fwbc, dr["final_norm_w"].partition_broadcast(128), writes=[B_bc3])
    wup_v = dr["w_up"].rearrange("(k p) c -> p k c", p=128)
    nload = [0]

    def load_wup(i):
        s_ = nload[0] % 3
        nload[0] += 1
        S.dma("pool", f"wup{s_}", wup[s_], wup_v.rearrange("p k (a c) -> p a k c", a=2)[:, :, :, 128 * i:128 * i + 128], writes=[B_wup[s_]])
        return s_

    xb8 = XH[:, :, :].rearrange("p k (b c) -> p k b c", c=512)
    bnd = R2.bf(72).rearrange("p (k c) -> p k c", k=8)
    B_bnd = Buf("bnd")
    CP("dve", bnd[:, :, 0], shcolb[:, 8:16], [B_small], [B_bnd])
    CP("dve", bnd[:, :, 1:5], xb8[:, :, :, 0], B_XH, [B_bnd])
    CP("dve", bnd[:, :, 5:9], xb8[:, :, :, 511], B_XH, [B_bnd])
    B_bup = Buf("bup")
    TT("dve", kcc[:, 0, :], convc[:, 0, :], convc[:, 1, :], ALU.add, [B_const], [B_kcc])
    TT("dve", kcc[:, 0, :], kcc[:, 0, :], convc[:, 2, :], ALU.add, [B_const, B_kcc], [B_kcc])
    S.op("pool", lambda e: e.memset(hsend, 0.0), writes=[B_hsend])

    def halo_exchange():
        CP("dve", hsend[:, 0:NCH], ub[:, :, 0], [B_ub], [B_hsend])
        CP("dve", hsend[:, NCH:2 * NCH], ub[:, :, 7], [B_ub], [B_hsend])
        S.dma("sp", "hst", hin_d.ap(), hsend, reads=[B_hsend], writes=[B_hin])
        S.custom("pool", "cch", lambda e: e.collective_compute(
            "AllGather", ALU.bypass, replica_groups=GROUPS, ins=[hin_d.ap().opt()], outs=[hout_d.ap().opt()]),
            reads=[B_hin], writes=[B_hout])
        S.dma("sp", "hld", hall, hout_d.ap().rearrange("(r p) c -> p r c", p=128), reads=[B_hout], writes=[B_hall])
        for side, (c0, s0) in enumerate(((NCH, 8), (0, 12))):
            TS("dve", hal[:, side, :], hall[:, 0, c0:c0 + NCH], sel[:, s0:s0 + 1], None, ALU.mult, None, [B_hall, B_const], [B_hal])
            for r in range(1, 4):
                STT("dve", hal[:, side, :], hall[:, r, c0:c0 + NCH], sel[:, s0 + r:s0 + r + 1], hal[:, side, :], ALU.mult, ALU.add,
                    [B_hall, B_hal, B_const], [B_hal])
            STT("dve", hal[:, side, :], kcc[:, 2, :], sel[:, 16 + side:17 + side], hal[:, side, :], ALU.mult, ALU.add, [B_kcc, B_hal, B_const], [B_hal])

    for tb in (1, 2, 0, 3):
        ts_ = slice(512 * tb, 512 * tb + 512)
        xh_b = [B_XH[4 * tb + q] for q in range(4)]
        pend = [load_wup(0), load_wup(1)]
        S.dma("sp", "x1r0", x1r[0], x1_d[512 * tb:512 * tb + 128, :], reads=[B_x1d[4 * tb]], writes=[B_x1r[0]])
        for i in range(22):
            s_ = pend.pop(0)
            if i + 2 < 22:
                pend.append(load_wup(i + 2))
            us = i % 2
            for a in range(2):
                ch = i + 22 * a
                bank = 1 + 2 * us + a
                if tb == 1:
                    for k in range(8):
                        MM(PS[7][:, 0:9], wup[s_][:, a, k, :], bnd[:, k, :], k == 0, k == 7, [B_wup[s_], B_bnd], [PB[7]])
                    CP("act", ub[:, ch, :].rearrange("p (b e) -> p e b", e=2), PS[7][:, 1:9].rearrange("p (e b) -> p e b", e=2), [PB[7]], [B_ub])
                    CP("dve", bup[:, ch:ch + 1], PS[7][:, 0:1], [PB[7]], [B_bup])
                    STT("dve", kcc[:, 1, ch:ch + 1], bup[:, ch:ch + 1], kcc[:, 0, ch:ch + 1], convc[:, 3, ch:ch + 1], ALU.mult, ALU.add,
                        [B_bup, B_kcc, B_const], [B_kcc])
                    TS("dve", kcc[:, 2, ch:ch + 1], bup[:, ch:ch + 1], -1.0, None, ALU.mult, None, [B_bup], [B_kcc])
                for k in range(8):
                    MM(PS[bank][:, :], wup[s_][:, a, k, :], XH[:, k, ts_], k == 0, k == 7, [B_wup[s_]] + xh_b, [PB[bank]])
                ca = cacc[us][a]
                bca = B_cacc[us][a]
                w0c, w1c, w2c = convc[:, 0, ch:ch + 1], convc[:, 1, ch:ch + 1], convc[:, 2, ch:ch + 1]
                ACTF(ca, PS[bank][:, :], AF.Identity, [PB[bank], B_kcc, B_const], [bca], scale=w1c, bias=kcc[:, 1, ch:ch + 1])
                STT("dve", ca[:, 1:512], PS[bank][:, 0:511], w0c, ca[:, 1:512], ALU.mult, ALU.add, [PB[bank], bca, B_const], [bca])
                STT("dve", ca[:, 0:511], PS[bank][:, 1:512], w2c, ca[:, 0:511], ALU.mult, ALU.add, [PB[bank], bca, B_const], [bca])
                if tb == 0:
                    pl, bpl = hal[:, 0, ch:ch + 1], B_hal
                else:
                    pl, bpl = ub[:, ch, 2 * tb - 1:2 * tb], B_ub
                if tb == 3:
                    pr, bpr = hal[:, 1, ch:ch + 1], B_hal
                else:
                    pr, bpr = ub[:, ch, 2 * tb + 2:2 * tb + 3], B_ub
                STT("dve", ca[:, 0:1], pl, w0c, ca[:, 0:1], ALU.mult, ALU.add, [bpl, bca, B_const], [bca])
                STT("dve", ca[:, 511:512], pr, w2c, ca[:, 511:512], ALU.mult, ALU.add, [bpr, bca, B_const], [bca])
            ca, cv = cacc[us][0], cacc[us][1]
            if USE_GELU_ACT:
                ACTF(ga[us], ca, AF.Gelu_apprx_tanh, [B_cacc[us][0]], [B_ga[us]])
            else:
                TT("dve", ga[us], ca, ca, ALU.mult, [B_cacc[us][0]], [B_ga[us]])
                TS("dve", ga[us], ga[us], 0.044715, 1.0, ALU.mult, ALU.add, [B_ga[us]], [B_ga[us]])
                TT("dve", ga[us], ga[us], ca, ALU.mult, [B_ga[us], B_cacc[us][0]], [B_ga[us]])
                ACTF(ga[us], ga[us], AF.Sigmoid, [B_ga[us]], [B_ga[us]], scale=GELU_C)
                TT("dve", ga[us], ga[us], ca, ALU.mult, [B_ga[us], B_cacc[us][0]], [B_ga[us]])
            TT("dve", mT[:, i, :], cv, ga[us], ALU.mult, [B_cacc[us][1], B_ga[us]], [B_mT])
        if tb == 1:
            halo_exchange()
        for q in range(4):
            t = 4 * tb + q
            slot = q % 2
            if q + 1 < 4:
                S.dma("sp", f"x1r{(q + 1) % 2}", x1r[(q + 1) % 2], x1_d[128 * (t + 1):128 * (t + 2), :], reads=[B_x1d[t + 1]], writes=[B_x1r[(q + 1) % 2]])
            for half in range(2):
                for k in range(22):
                    MM(PS[5 + half][:, :], mT[:, k, 128 * q:128 * q + 128], Wd[:, k, 512 * half:512 * half + 512], k == 0, k == 21,
                       [B_mT, B_wd], [PB[5 + half]])
            for half in range(2):
                hs = slice(512 * half, 512 * half + 512)
                TT("dve", x2t[:, hs], PS[5 + half][:, :], g2bc[:, hs], ALU.mult, [PB[5 + half], B_bc3], [B_x2t])
            TT("dve", x2t, x2t, x1r[slot], ALU.add, [B_x2t, B_x1r[slot]], [B_x2t])
            ACTF(junk3, x2t, AF.Square, [B_x2t], [B_junk3, B_ss], accum_out=ss_t[:, 4:5])
            RSQ(ss_t[:, 5:6], ss_t[:, 4:5], 1024.0 * EPS, [B_ss], [B_ss])
            TS("dve", ss_t[:, 5:6], ss_t[:, 5:6], 32.0, None, ALU.mult, None, [B_ss], [B_ss])
            STT("dve", x1r[slot], x2t, ss_t[:, 5:6], fwbc, ALU.mult, ALU.mult, [B_x2t, B_ss, B_bc3], [B_x1r[slot]])
            S.dma("sp", f"ost{slot}", out_d[128 * t:128 * t + 128, :], x1r[slot], reads=[B_x1r[slot]], writes=[])
    S.barrier()
    S.emit()
    return nc


_CACHE = {}


def _in_maps(inputs):
    g = lambda k: np.asarray(inputs[k], dtype=np.float32)
    x, c, ctx, c_ctx = g("x"), g("c"), g("ctx"), g("c_ctx")
    shared = {
        "w_mod": g("w_mod")[0], "b_mod": g("b_mod"), "norm1_w": g("norm1_w"), "w_in": g("w_in")[0],
        "a_f": g("ret_decay_f"), "a_b": g("ret_decay_b"), "w_ret_out": g("w_ret_out")[0],
        "w_four_out": g("w_four_out")[0], "w_bg": g("w_branch_gate")[0], "b_bg": g("b_branch_gate"),
        "w_out": g("w_out")[0], "norm2_w": g("norm2_w"), "w_up": g("w_up")[0], "conv_w": g("conv_w")[0],
        "conv_b": g("conv_b"), "w_down": g("w_down")[0], "final_norm_w": g("final_norm_w").reshape(1, D),
        "cc_col": np.ascontiguousarray(c_ctx.reshape(8, 128).T),
    }
    shared = {k: np.ascontiguousarray(v) for k, v in shared.items()}
    consts = [host_consts(j) for j in range(4)]
    maps = []
    for core in range(NCORES):
        b, j = core // 4, core % 4
        m = dict(shared)
        m["x_own"] = np.ascontiguousarray(x[b, TOK * j:TOK * (j + 1)])
        m["ctx"] = np.ascontiguousarray(ctx[b])
        m["c_col"] = np.ascontiguousarray(c[b].reshape(8, 128).T)
        m.update(consts[j])
        maps.append(m)
    return maps


def kernel(**inputs):
    if "nc" not in _CACHE:
        _CACHE["nc"] = build_program(4)
    nc = _CACHE["nc"]
    res = run_bass_kernel_spmd(nc, _in_maps(inputs), core_ids=list(range(NCORES)))
    out = np.empty((NB, SEQ, D), np.float32)
    for core in range(NCORES):
        b, j = core // 4, core % 4
        out[b, TOK * j:TOK * (j + 1)] = np.asarray(res.results[core]["out"], dtype=np.float32)
    return out
```
